# Optimizing a Trainium2 kernel written in Bass

```python
import math
import jax, jax.numpy as jnp
from jax import lax
import numpy as np

D_MODEL = 1024
BATCH = 4
SEQ = 4096
DEPTH = 1
DEC_BATCH = 16
DEC_SEQ = 16
PAST_LEN = 2048

CHUNK = 64
GLA_HEADS = 4
GLA_DK = 128
GLA_DV = 256
GLA_RANK = 16
GLA_TAU = 16.0
SWA_HEADS = 16
SWA_KV_HEADS = 4
SWA_HD = 64
SWA_GROUP = SWA_HEADS // SWA_KV_HEADS
WINDOW = 128
WIN_CHUNKS = WINDOW // CHUNK
SWA_KEYS = (WIN_CHUNKS + 1) * CHUNK
N_BUCKETS = 32
MAX_DISTANCE = 128
D_FF = 2816
EPS = 1e-6

GLA_QK = GLA_HEADS * GLA_DK
GLA_V = GLA_HEADS * GLA_DV
SWA_Q = SWA_HEADS * SWA_HD
SWA_KV = SWA_KV_HEADS * SWA_HD
IN_SPLITS = (GLA_QK, GLA_QK, GLA_V, GLA_V, GLA_RANK, SWA_Q, SWA_KV, SWA_KV, D_MODEL, D_MODEL)
IN_WIDTH = 2 * GLA_QK + 2 * GLA_V + GLA_RANK + SWA_Q + 2 * SWA_KV + 2 * D_MODEL

kernel_name = "hybrid_gla_swa_macaron_stream_step"


def rms_norm(x, g):
    xf = x.astype(jnp.float32)
    y = xf * lax.rsqrt(jnp.mean(xf * xf, axis=-1, keepdims=True) + EPS)
    return (y * g.astype(jnp.float32)).astype(x.dtype)


def swiglu_ffn(x, w_up, w_down):
    a, b = jnp.split(x @ w_up, 2, axis=-1)
    return (jax.nn.silu(a) * b) @ w_down


def t5_bucket(rel):
    nb = N_BUCKETS // 2
    ret = jnp.where(rel > 0, nb, 0)
    n = jnp.abs(rel)
    max_exact = nb // 2
    nf = jnp.maximum(n, 1).astype(jnp.float32)
    large = max_exact + (jnp.log(nf / max_exact) / math.log(MAX_DISTANCE / max_exact)
                         * (nb - max_exact)).astype(jnp.int32)
    large = jnp.minimum(large, nb - 1)
    return ret + jnp.where(n < max_exact, n, large)


def rel_pos_bias(rel_bias, qpos, kpos):
    bucket = t5_bucket(kpos[None, :] - qpos[:, None])
    b = jnp.transpose(rel_bias[bucket], (2, 0, 1)).astype(jnp.float32)
    return b.reshape(SWA_KV_HEADS, SWA_GROUP, qpos.shape[0], kpos.shape[0])


def window_mask(qpos, kpos):
    qc = (qpos // CHUNK)[..., :, None]
    kc = (kpos // CHUNK)[..., None, :]
    return (kpos[..., None, :] >= 0) & (kc <= qc) & (kc >= qc - WIN_CHUNKS)


def sink_attention(q, k, v, bias, sinks, mask):
    qg = q.reshape(q.shape[:-2] + (SWA_KV_HEADS, SWA_GROUP, SWA_HD))
    s = jnp.einsum('...qkgd,...skd->...kgqs', qg, k).astype(jnp.float32) * (SWA_HD ** -0.5) + bias
    s = jnp.where(mask, s, -jnp.inf)
    sink = sinks.astype(jnp.float32).reshape(SWA_KV_HEADS, SWA_GROUP, 1, 1)
    m = jnp.maximum(jnp.max(s, axis=-1, keepdims=True), sink)
    p = jnp.exp(s - m)
    probs = p / (jnp.sum(p, axis=-1, keepdims=True) + jnp.exp(sink - m))
    out = jnp.einsum('...kgqs,...skd->...qkgd', probs.astype(v.dtype), v)
    return out.reshape(out.shape[:-3] + (SWA_Q,))


def gla_chunk(S, q, k, v, g):
    L = q.shape[1]
    b = jnp.cumsum(g, axis=1)
    inter = jnp.einsum('blhk,bhkv->blhv', q * jnp.exp(b), S)
    causal = jnp.tril(jnp.ones((L, L), dtype=bool))[None, :, :, None, None]
    diff = b[:, :, None] - b[:, None, :]
    decay = jnp.exp(jnp.where(causal, diff, -jnp.inf))
    A = jnp.einsum('bthk,bshk,btshk->btsh', q, k, decay)
    intra = jnp.einsum('btsh,bshv->bthv', A, v)
    b_last = b[:, -1]
    k_dec = k * jnp.exp(b_last[:, None] - b)
    S_new = jnp.exp(b_last)[..., None] * S + jnp.einsum('bshk,bshv->bhkv', k_dec, v)
    return S_new, inter + intra


def mixer_project(h, w_in, gla_w_alpha, gla_b_alpha, q_norm, k_norm):
    lead = h.shape[:-1]
    offs = np.cumsum(IN_SPLITS)[:-1].tolist()
    gq, gk, gv, gr, ga, sq, sk, sv, gate_a, gate_b = jnp.split(h @ w_in, offs, axis=-1)
    f32 = jnp.float32
    q = gq.astype(f32).reshape(lead + (GLA_HEADS, GLA_DK)) * (GLA_DK ** -0.5)
    k = gk.astype(f32).reshape(lead + (GLA_HEADS, GLA_DK))
    v = gv.astype(f32).reshape(lead + (GLA_HEADS, GLA_DV))
    la = (jax.nn.log_sigmoid((ga @ gla_w_alpha + gla_b_alpha).astype(f32)) / GLA_TAU)
    la = la.reshape(lead + (GLA_HEADS, GLA_DK))
    sq = rms_norm(sq.reshape(lead + (SWA_HEADS, SWA_HD)), q_norm)
    sk = rms_norm(sk.reshape(lead + (SWA_KV_HEADS, SWA_HD)), k_norm)
    sv = sv.reshape(lead + (SWA_KV_HEADS, SWA_HD))
    return q, k, v, la, gr, sq, sk, sv, gate_a, gate_b


def mixer_merge(o_gla, gr, o_swa, gate_a, gate_b, gla_head_norm, w_branch, w_out):
    lead = o_swa.shape[:-1]
    o = rms_norm(o_gla, gla_head_norm).reshape(lead + (GLA_V,))
    o = (o * jax.nn.silu(gr.astype(jnp.float32))).astype(o_swa.dtype)
    y_a = o @ w_branch[:GLA_V]
    y_b = o_swa @ w_branch[GLA_V:]
    m = jax.nn.sigmoid(gate_a) * y_a + jax.nn.sigmoid(gate_b) * y_b
    return m @ w_out


def setup_inputs(seed: int = 0) -> dict:
    key = jax.random.key(seed)
    ks = jax.random.split(key, 24)
    f32 = jnp.float32

    def nrm(k, shape, scale):
        return jax.random.normal(k, shape, f32) * scale

    n_cache = min(WINDOW, PAST_LEN)
    return {
        "x_prompt": nrm(ks[0], (BATCH, SEQ, D_MODEL), 1.0),
        "x_sample": nrm(ks[1], (DEC_BATCH, DEC_SEQ, D_MODEL), 1.0),
        "cache_swa_k": nrm(ks[2], (DEPTH, DEC_BATCH, n_cache, SWA_KV_HEADS, SWA_HD), 1.0),
        "cache_swa_v": nrm(ks[3], (DEPTH, DEC_BATCH, n_cache, SWA_KV_HEADS, SWA_HD), 1.0),
        "state_gla": nrm(ks[4], (DEPTH, DEC_BATCH, GLA_HEADS, GLA_DK, GLA_DV), 0.5),
        "ffn1_norm": 1.0 + nrm(ks[5], (DEPTH, D_MODEL), 0.02),
        "ffn1_w_up": nrm(ks[6], (DEPTH, D_MODEL, 2 * D_FF), D_MODEL ** -0.5),
        "ffn1_w_down": nrm(ks[7], (DEPTH, D_FF, D_MODEL), D_FF ** -0.5),
        "mix_norm": 1.0 + nrm(ks[8], (DEPTH, D_MODEL), 0.02),
        "w_in": nrm(ks[9], (DEPTH, D_MODEL, IN_WIDTH), D_MODEL ** -0.5),
        "gla_w_alpha": nrm(ks[10], (DEPTH, GLA_RANK, GLA_QK), GLA_RANK ** -0.5),
        "gla_b_alpha": nrm(ks[11], (DEPTH, GLA_QK), 0.1),
        "gla_head_norm": 1.0 + nrm(ks[12], (DEPTH, GLA_DV), 0.02),
        "q_norm": 1.0 + nrm(ks[13], (DEPTH, SWA_HD), 0.02),
        "k_norm": 1.0 + nrm(ks[14], (DEPTH, SWA_HD), 0.02),
        "attn_sinks": nrm(ks[15], (DEPTH, SWA_HEADS), 0.5),
        "rel_bias": nrm(ks[16], (N_BUCKETS, SWA_HEADS), 0.1),
        "w_branch": nrm(ks[17], (DEPTH, GLA_V + SWA_Q, D_MODEL), GLA_V ** -0.5),
        "w_out": nrm(ks[18], (DEPTH, D_MODEL, D_MODEL), D_MODEL ** -0.5),
        "ffn2_norm": 1.0 + nrm(ks[19], (DEPTH, D_MODEL), 0.02),
        "ffn2_w_up": nrm(ks[20], (DEPTH, D_MODEL, 2 * D_FF), D_MODEL ** -0.5),
        "ffn2_w_down": nrm(ks[21], (DEPTH, D_FF, D_MODEL), D_FF ** -0.5),
        "final_norm": 1.0 + nrm(ks[22], (DEPTH, D_MODEL), 0.02),
    }


def reference(x_prompt, x_sample, cache_swa_k, cache_swa_v, state_gla,
              ffn1_norm, ffn1_w_up, ffn1_w_down, mix_norm, w_in, gla_w_alpha, gla_b_alpha,
              gla_head_norm, q_norm, k_norm, attn_sinks, rel_bias, w_branch, w_out,
              ffn2_norm, ffn2_w_up, ffn2_w_down, final_norm):
    B, S, _ = x_prompt.shape
    NC = S // CHUNK
    DB, L, _ = x_sample.shape
    n_cache = cache_swa_k.shape[2]

    q_loc = jnp.arange(CHUNK)
    k_loc = jnp.arange(SWA_KEYS) - WIN_CHUNKS * CHUNK
    blk = jnp.arange(NC)[:, None] * CHUNK
    mask_p = window_mask(blk + q_loc[None], blk + k_loc[None])[None, :, None, None]
    bias_p = rel_pos_bias(rel_bias, q_loc, k_loc)
    qpos_s = PAST_LEN + jnp.arange(L)
    kpos_s = jnp.concatenate([PAST_LEN - n_cache + jnp.arange(n_cache), qpos_s])
    mask_s = window_mask(qpos_s, kpos_s)
    bias_s = rel_pos_bias(rel_bias, qpos_s, kpos_s)

    def to_chunks(t):
        return jnp.moveaxis(t.reshape((B, NC, CHUNK) + t.shape[2:]), 1, 0)

    def gla_body(state, inp):
        return gla_chunk(state, *inp)

    xp, xs = x_prompt, x_sample
    kp_l, vp_l, sp_l, ks_l, vs_l, ss_l = [], [], [], [], [], []
    for l in range(DEPTH):
        xp = xp + 0.5 * swiglu_ffn(rms_norm(xp, ffn1_norm[l]), ffn1_w_up[l], ffn1_w_down[l])
        xs = xs + 0.5 * swiglu_ffn(rms_norm(xs, ffn1_norm[l]), ffn1_w_up[l], ffn1_w_down[l])

        q, k, v, la, gr, sq, sk, sv, ga, gb = mixer_project(
            rms_norm(xp, mix_norm[l]), w_in[l], gla_w_alpha[l], gla_b_alpha[l], q_norm[l], k_norm[l])
        S0 = jnp.zeros((B, GLA_HEADS, GLA_DK, GLA_DV), jnp.float32)
        S_fin, o_ch = lax.scan(gla_body, S0, (to_chunks(q), to_chunks(k), to_chunks(v), to_chunks(la)))
        o_gla = jnp.moveaxis(o_ch, 0, 1).reshape(B, S, GLA_HEADS, GLA_DV)
        qb = sq.reshape(B, NC, CHUNK, SWA_HEADS, SWA_HD)
        pad = jnp.zeros((B, WIN_CHUNKS, CHUNK, SWA_KV_HEADS, SWA_HD), sk.dtype)
        kpad = jnp.concatenate([pad, sk.reshape(B, NC, CHUNK, SWA_KV_HEADS, SWA_HD)], axis=1)
        vpad = jnp.concatenate([pad, sv.reshape(B, NC, CHUNK, SWA_KV_HEADS, SWA_HD)], axis=1)
        kb = jnp.concatenate([kpad[:, i:i + NC] for i in range(WIN_CHUNKS + 1)], axis=2)
        vb = jnp.concatenate([vpad[:, i:i + NC] for i in range(WIN_CHUNKS + 1)], axis=2)
        o_swa = sink_attention(qb, kb, vb, bias_p, attn_sinks[l], mask_p).reshape(B, S, SWA_Q)
        xp = xp + mixer_merge(o_gla, gr, o_swa, ga, gb, gla_head_norm[l], w_branch[l], w_out[l])
        n_keep = min(WINDOW, S)
        kp_l.append(sk[:, S - n_keep:])
        vp_l.append(sv[:, S - n_keep:])
        sp_l.append(S_fin.astype(state_gla.dtype))

        q, k, v, la, gr, sq, sk, sv, ga, gb = mixer_project(
            rms_norm(xs, mix_norm[l]), w_in[l], gla_w_alpha[l], gla_b_alpha[l], q_norm[l], k_norm[l])
        S_new, o_gla = gla_chunk(state_gla[l].astype(jnp.float32), q, k, v, la)
        k_all = jnp.concatenate([cache_swa_k[l].astype(sk.dtype), sk], axis=1)
        v_all = jnp.concatenate([cache_swa_v[l].astype(sv.dtype), sv], axis=1)
        o_swa = sink_attention(sq, k_all, v_all, bias_s, attn_sinks[l], mask_s)
        xs = xs + mixer_merge(o_gla, gr, o_swa, ga, gb, gla_head_norm[l], w_branch[l], w_out[l])
        ks_l.append(sk)
        vs_l.append(sv)
        ss_l.append(S_new.astype(state_gla.dtype))

        xp = xp + 0.5 * swiglu_ffn(rms_norm(xp, ffn2_norm[l]), ffn2_w_up[l], ffn2_w_down[l])
        xs = xs + 0.5 * swiglu_ffn(rms_norm(xs, ffn2_norm[l]), ffn2_w_up[l], ffn2_w_down[l])
        xp = rms_norm(xp, final_norm[l])
        xs = rms_norm(xs, final_norm[l])

    y_prompt = xp
    y_sample = xs
    new_k_prompt = jnp.stack(kp_l)
    new_v_prompt = jnp.stack(vp_l)
    new_gla_prompt = jnp.stack(sp_l)
    new_k_sample = jnp.stack(ks_l)
    new_v_sample = jnp.stack(vs_l)
    new_gla_sample = jnp.stack(ss_l)
    return (y_prompt, y_sample, new_k_prompt, new_v_prompt, new_gla_prompt, new_k_sample, new_v_sample, new_gla_sample)
```

```python
import numpy as np
import ml_dtypes
import concourse.bass as bass
import concourse.mybir as mybir
from concourse.bass_utils import run_bass_kernel_spmd

F32 = mybir.dt.float32
BF16 = mybir.dt.bfloat16
AF = mybir.ActivationFunctionType
ALU = mybir.AluOpType
AX = mybir.AxisListType

ENGS = ("pe", "act", "dve", "pool", "sp")
NCORES = 8
D = 1024
DFF = 2816
TOKC = 2048
T = 512
NTILE = TOKC // T
TS = 64
EPS = 1e-6
NEG = -30000.0
IN_W = 6672
C_GQ, C_GK, C_GV, C_GR, C_GA, C_SQ, C_SK, C_SV, C_GTA, C_GTB = 0, 512, 1024, 2048, 3072, 3088, 4112, 4368, 4624, 5648


class Op:
    __slots__ = ("eng", "fn", "deps", "sig", "semval", "dma", "dsem", "dval", "idx", "group")


class Sched:
    def __init__(self, nc, n_dma_sems=48):
        self.nc = nc
        self.ops = []
        self.by_eng = {e: [] for e in ENGS}
        self.last_w = {}
        self.readers = {}
        self.esem = {e: nc.alloc_semaphore(name="cnt_" + e) for e in ENGS}
        self.dsems = [nc.alloc_semaphore(name="dma%d" % i) for i in range(n_dma_sems)]
        self.dcount = [0] * n_dma_sems
        self.dlast = [None] * n_dma_sems
        self.dnext = {"sw": 0, "hw": 0}

    def add(self, eng, fn, reads=(), writes=(), dma=False, group=None):
        op = Op()
        op.group = group
        op.eng = eng
        op.fn = fn
        op.sig = False
        op.semval = None
        op.dma = dma
        op.idx = len(self.ops)
        deps = {}

        def dep(d, raw):
            if d is None:
                return
            if not d.dma and not dma and d.eng == eng:
                if eng == "pe" or not raw:
                    return
            deps[d.idx] = d

        for k in reads:
            for w in self.last_w.get(k, ()):
                dep(w, True)
        for k in writes:
            for w in self.last_w.get(k, ()):
                if group is not None and w.group == group:
                    continue
                dep(w, False)
            for r in self.readers.get(k, ()):
                dep(r, False)
        if dma:
            half = len(self.dsems) // 2
            cls = "sw" if eng == "pool" else "hw"
            j = self.dnext[cls] + (half if cls == "sw" else 0)
            self.dnext[cls] = (self.dnext[cls] + 1) % half
            if self.dlast[j] is not None:
                deps[self.dlast[j].idx] = self.dlast[j]
            self.dcount[j] += 16
            op.dsem = j
            op.dval = self.dcount[j]
            self.dlast[j] = op
        for k in reads:
            lst = self.readers.setdefault(k, [])
            if not dma:
                lst[:] = [r for r in lst if r.dma or r.eng != eng]
            lst.append(op)
        for k in writes:
            lw = self.last_w.get(k, [])
            if group is not None and lw and lw[0].group == group:
                lw.append(op)
            else:
                self.last_w[k] = [op]
                self.readers[k] = []
        op.deps = list(deps.values())
        for d in op.deps:
            if not d.dma:
                d.sig = True
        self.ops.append(op)
        self.by_eng[eng].append(op)
        return op

    def emit(self):
        nc = self.nc
        for e in ENGS:
            c = 0
            for op in self.by_eng[e]:
                if not op.dma and op.sig:
                    c += 1
                    op.semval = c
        final_d = list(self.dcount)

        def replay(e, h):
            seen = {}
            for op in self.by_eng[e]:
                need = {}
                for d in op.deps:
                    if d.dma:
                        key, val, sem = ("d", d.dsem), d.dval, self.dsems[d.dsem]
                    else:
                        key, val, sem = ("e", d.eng), d.semval, self.esem[d.eng]
                    if val > need.get(key, (0, None))[0]:
                        need[key] = (val, sem)
                for key, (val, sem) in need.items():
                    if seen.get(key, 0) >= val:
                        continue
                    seen[key] = val
                    h.wait_ge(sem, val)
                ins = op.fn(h)
                if op.dma:
                    ins.then_inc(self.dsems[op.dsem], 16)
                elif op.sig:
                    ins.then_inc(self.esem[e], 1)
            if e == "sp":
                for j, v in enumerate(final_d):
                    if v > 0:
                        h.wait_ge(self.dsems[j], v)

        with nc.Block() as block:
            @block.tensor
            def _(h):
                replay("pe", h)

            @block.scalar
            def _(h):
                replay("act", h)

            @block.vector
            def _(h):
                replay("dve", h)

            @block.gpsimd
            def _(h):
                replay("pool", h)

            @block.sync
            def _(h):
                replay("sp", h)


def _is_dram(ap):
    return "DRam" in type(ap.tensor).__name__


def K(*aps):
    out = []
    for a in aps:
        if a is None or isinstance(a, (int, float)):
            continue
        if _is_dram(a):
            continue
        out.append(a.tensor.name)
    return out


def _t5_bucket(rel):
    nb = 16
    ret = np.where(rel > 0, nb, 0)
    n = np.abs(rel)
    max_exact = nb // 2
    nf = np.maximum(n, 1).astype(np.float32)
    large = max_exact + (np.log(nf / max_exact) / np.float32(np.log(128 / max_exact)) * (nb - max_exact)).astype(np.int32)
    large = np.minimum(large, nb - 1)
    return ret + np.where(n < max_exact, n, large)


def _bias_tables(rel_bias):
    rb = np.asarray(rel_bias, np.float32)
    q = np.arange(128)
    out = np.zeros((2, 128, 4, 4, 128), np.float32)
    for kt in range(2):
        kloc = np.arange(128) + (kt - 1) * 128
        rel = kloc[:, None] - q[None, :]
        bk = _t5_bucket(rel)
        kc = np.floor_divide(kloc, 64)[:, None]
        qc = (q // 64)[None, :]
        valid = (kc <= qc) & (kc >= qc - 2)
        g = rb[bk]
        g = np.where(valid[:, :, None], g, NEG)
        out[kt] = g.reshape(128, 128, 4, 4).transpose(0, 2, 3, 1)
    bp = out.reshape(2, 128, 4, 512)
    outs = np.full((3, 128, 4, 4, 64), NEG, np.float32)
    qpos = 2048 + np.arange(16)
    for s in range(2):
        kpos = 2048 - 128 + np.arange(128)
        bk = _t5_bucket(kpos[:, None] - qpos[None, :])
        g = rb[bk].reshape(128, 16, 4, 4).transpose(0, 2, 3, 1)
        outs[s, :, :, :, 32 * s:32 * s + 16] = g
        bk2 = _t5_bucket(qpos[:, None] - qpos[None, :])
        g2 = rb[bk2].reshape(16, 16, 4, 4).transpose(0, 2, 3, 1)
        outs[2, 32 * s:32 * s + 16, :, :, 32 * s:32 * s + 16] = g2
    bs = outs.reshape(3, 128, 4, 256)
    return np.ascontiguousarray(bp, np.float32), np.ascontiguousarray(bs, np.float32)


def _consts():
    c = {}
    c["ident"] = np.eye(128, dtype=np.float32).astype(ml_dtypes.bfloat16)
    s = np.arange(128)[:, None]
    t = np.arange(128)[None, :]
    c["tri_incl"] = np.where(s <= t, -1.0 / 16, 0.0).astype(np.float32).astype(ml_dtypes.bfloat16)
    c["tri_suf"] = np.where(s > t, -1.0 / 16, 0.0).astype(np.float32).astype(ml_dtypes.bfloat16)
    c["cmask4"] = np.tile(np.where(s <= t, 1.0, 0.0).astype(np.float32), (1, 4))
    c["tri_all"] = np.full((128, 128), -1.0 / 16, np.float32).astype(ml_dtypes.bfloat16)
    s6 = np.arange(64)[:, None]
    t6 = np.arange(64)[None, :]
    same = (s6 // 32) == (t6 // 32)
    real_s = (s6 % 32) < 16
    tis = np.zeros((128, 128), np.float32)
    tis[:64, :64] = np.where(same & (s6 <= t6), -1.0 / 16, 0.0)
    tss = np.zeros((128, 128), np.float32)
    tss[:64, :64] = np.where(same & (s6 > t6) & real_s, -1.0 / 16, 0.0)
    cms = np.zeros((128, 512), np.float32)
    cms[:64, :256] = np.tile(np.where(same & (s6 <= t6), 1.0, 0.0), (1, 4))
    c["tri_incl_s"] = tis.astype(ml_dtypes.bfloat16)
    c["tri_suf_s"] = tss.astype(ml_dtypes.bfloat16)
    c["cmask4_s"] = cms
    oh = np.zeros((2, 128, 128), np.float32)
    oh[0, :, :64] = 1.0
    oh[1, :, 64:] = 1.0
    c["onesH"] = oh.astype(ml_dtypes.bfloat16)
    g4 = np.zeros((4, 512), np.float32)
    g4s = np.zeros((4, 512), np.float32)
    for g in range(4):
        g4[g, g * 128:(g + 1) * 128] = 1.0
        g4s[g, g * 64:(g + 1) * 64] = 1.0
    c["g4"] = g4.astype(ml_dtypes.bfloat16)
    c["g4s"] = g4s.astype(ml_dtypes.bfloat16)
    bo = np.zeros((128, 128), np.float32)
    bo[:64, :64] = 1.0
    bo[64:, 64:] = 1.0
    c["bo_q"] = bo.astype(ml_dtypes.bfloat16)
    c["bo_k"] = (bo / 64).astype(ml_dtypes.bfloat16)
    c["ones_mean"] = np.full((128, 128), 1.0 / 256, np.float32).astype(ml_dtypes.bfloat16)
    c["ones_row"] = np.ones((1, 128), np.float32).astype(ml_dtypes.bfloat16)
    return c


CONST_SPECS = [
    ("ident", [128, 128], BF16), ("tri_incl", [128, 128], BF16), ("tri_suf", [128, 128], BF16),
    ("cmask4", [128, 512], F32), ("tri_all", [128, 128], BF16), ("tri_incl_s", [128, 128], BF16), ("tri_suf_s", [128, 128], BF16),
    ("cmask4_s", [128, 512], F32), ("onesH", [2, 128, 128], BF16), ("g4", [4, 512], BF16), ("g4s", [4, 512], BF16),
    ("bo_q", [128, 128], BF16), ("bo_k", [128, 128], BF16), ("ones_mean", [128, 128], BF16), ("ones_row", [1, 128], BF16),
]


class Prog:
    def __init__(self, debug=None):
        self.nc = nc = bass.Bass("TRN2", target_bir_lowering=False)
        self.s = Sched(nc)
        self.planning = False
        self.plan = []
        self.wscr = None
        self.next_x = None
        self.norm_done = False
        self.x_loaded = False
        self.debug = debug or {}
        self.dbg_outs = []
        self.din = {}
        self.dout = {}
        self._declare_io()
        self._alloc()

    def _in(self, name, shape, dt=F32):
        self.din[name] = self.nc.dram_tensor(name, list(shape), dt, kind="ExternalInput").ap()
        return self.din[name]

    def _out(self, name, shape, dt=F32):
        self.dout[name] = self.nc.dram_tensor(name, list(shape), dt, kind="ExternalOutput").ap()
        return self.dout[name]

    def _declare_io(self):
        i = self._in
        i("xp", [TOKC, D]); i("xprev", [TOKC, D]); i("xs", [TS, D])
        i("ck", [2, 128, 256]); i("cv", [2, 128, 256]); i("st", [2, 4, 128, 256])
        i("ffn1_up", [D, 2 * DFF]); i("ffn1_down", [DFF, D]); i("ffn2_up", [D, 2 * DFF]); i("ffn2_down", [DFF, D])
        i("w_in", [D, IN_W]); i("w_branch", [2048, D]); i("w_out", [D, D])
        i("w_alpha", [16, 512]); i("b_alpha", [1, 512])
        i("gains_fm", [128, 24])
        i("fgain_bc", [128, D])
        i("hgain", [128, 2])
        i("qgcol", [128, 1]); i("kgcol", [128, 1]); i("kgain_bc", [128, 64])
        i("sinkT", [2, 4, 128])
        i("halo_bias", [128, 1])
        i("biasp", [2, 128, 4, 512]); i("biass", [3, 128, 4, 256])
        for n, sh, dt in CONST_SPECS:
            i(n, sh, dt)
        o = self._out
        o("y", [TOKC, D]); o("ys", [TS, D]); o("nk", [128, 256]); o("nv", [128, 256]); o("ng", [4, 128, 256])
        o("nks", [TS, 256]); o("nvs", [TS, 256]); o("ngs", [2, 4, 128, 256])

    def sb(self, name, shape, dt):
        return self.nc.alloc_sbuf_tensor("s_" + name, list(shape), dt)

    def _alloc(self):
        nc = self.nc
        sb = self.sb
        self.c = {}
        for n, sh, dt in CONST_SPECS:
            if n.endswith("_s"):
                continue
            if len(sh) == 3:
                self.c[n] = sb("c_" + n, [sh[1], sh[0], sh[2]], dt)
            else:
                self.c[n] = sb("c_" + n, sh, dt)
        self.gains_fm = sb("gains_fm", [128, 24], F32)
        self.fgain_bc = sb("fgain_bc", [128, D], F32)
        self.hgain = sb("hgain", [128, 2], F32)
        self.qgcol = sb("qgcol", [128, 1], F32)
        self.kgcol = sb("kgcol", [128, 1], F32)
        self.kgain_bc = sb("kgain_bc", [128, 64], F32)
        self.sinkT = sb("sinkT", [4, 2, 128], F32)
        self.esinkT = sb("esinkT", [4, 2, 128], BF16)
        self.halo_bias = sb("halo_bias", [128, 1], F32)
        self.walpha = sb("walpha", [16, 512], BF16)
        self.balpha = sb("balpha", [1, 512], BF16)
        self.biasT = sb("biasT", [128, 2, 4, 512], BF16)
        self.biasTs = self.biasT[:, :, :, :].rearrange("p a j n -> p (a j n)")[:, 0:3072].rearrange("p (a j n) -> p a j n", a=3, j=4)
        self.xt = [sb("xt%d" % i, [128, D], F32) for i in range(4)]
        self.hball = sb("hball", [128, 4, D], BF16)
        self.hb = [self.hball[:, i, :] for i in range(4)]
        self.osT = [self.hball[:, 2 * i:2 * i + 2, :].rearrange("p a (g t) -> p (a g) t", g=2) for i in range(2)]
        self.ss = [sb("ss%d" % i, [128, 1], F32) for i in range(4)]
        self.rstd = [sb("rstd%d" % i, [128, 1], F32) for i in range(4)]
        self.hT = [sb("hT%d" % i, [128, T], BF16) for i in range(8)]
        self.arena0 = sb("arena0", [128, 22 * T], BF16)
        self.actT = [self.arena0[:, i * T:(i + 1) * T] for i in range(22)]
        self.e1T = self.arena0[:, 0:4096].bitcast(F32).rearrange("p (h t) -> p h t", h=4)
        self.e2T = self.arena0[:, 4096:8192].bitcast(F32).rearrange("p (h t) -> p h t", h=4)
        self.qeT = self.arena0[:, 8192:10240].rearrange("p (h t) -> p h t", h=4)
        self.arena1 = sb("arena1", [128, 10240], BF16)
        self.v_tok = [self.arena1[:, i * 1024:(i + 1) * 1024] for i in range(4)]
        self.kd = [self.arena1[:, 4096 + i * 512:4096 + (i + 1) * 512] for i in range(4)]
        self.es = [self.arena1[:, 6144 + i * 1024:6144 + (i + 1) * 1024].bitcast(F32) for i in range(4)]
        self.sigA = self.arena1[:, 0:4096].rearrange("p (c t) -> p c t", c=8)
        self.sigB = self.arena1[:, 4096:8192].rearrange("p (c t) -> p c t", c=8)
        self.keT = sb("keT", [128, 4, T], BF16)
        self.gaT = sb("gaT", [16, T], BF16)
        self.sp = [sb("sp%d" % i, [128, 512], BF16) for i in range(4)]
        self.dtot = sb("dtot", [128, 4, 1], F32)
        self.sgT = sb("sgT", [128, 8, T], BF16)
        self.ATm = sb("ATm", [128, 512], BF16)
        self.ATm2 = [self.ATm, sb("ATm1", [128, 512], BF16)]
        self.Sbf_alt = sb("Sbf_alt", [128, 4, 256], BF16)
        self.o_raw = sb("o_raw", [128, 8, 128], F32)
        self.osq = sb("osq", [128, 8, 128], BF16)
        self.rstdg = sb("rstdg", [128, 512], F32)
        self.o_raw2 = [self.o_raw, sb("o_raw1", [128, 8, 128], F32)]
        self.osq2 = [self.osq, sb("osq1", [128, 8, 128], BF16)]
        self.S = [sb("S0", [128, 4, 256], F32), self.xt[1][:, :].rearrange("p (h v) -> p h v", h=4)]
        self.Sbf = [sb("Sbf0", [128, 4, 256], BF16), self.xt[2][:, 0:512].bitcast(BF16).rearrange("p (h v) -> p h v", h=4)]
        self.sqr = [sb("sqr%d" % i, [128, T], F32) for i in range(1)]
        self.sqsq = [sb("sqsq%d" % i, [128, T], BF16) for i in range(1)]
        self.rsq = [sb("rsq%d" % i, [128, T], F32) for i in range(1)]
        self.qo = [sb("qo%d" % i, [128, 4 * T], BF16) for i in range(2)]
        self.knT_cur = sb("knT_cur", [128, 2, T], BF16)
        self.knT_prev = sb("knT_prev", [128, 2, 128], BF16)
        self.vz_cur = [sb("vz_cur%d" % i, [128, 512], BF16) for i in range(4)]
        self.vz_prev = sb("vz_prev", [128, 512], BF16)
        self.PT = [sb("PT%d" % i, [128, 512], BF16) for i in range(6)]
        self.rden = sb("rden", [128, 512], F32)
        self.ksq = self.sqr[0][:, 0:256]
        self.ktok = self.sqr[0][:, 256:512]
        self.kss = sb("kss", [128, 4], F32)
        self.knew = self.rsq[0][:, 0:256]
        self.vnew = self.rsq[0][:, 256:512]
        self.mtmp = [sb("mtmp%d" % i, [128, T], F32) for i in range(2)]
        self.sa = self.mtmp
        self.junk = self.mtmp[0][:, :].bitcast(BF16)
        self.ckf = sb("ckf", [128, 256], F32)
        self.ckb = sb("ckb", [128, 256], BF16)
        self.knT_c = [sb("knT_c%d" % i, [128, 2, 128], BF16) for i in range(2)]
        self.vz_c = [sb("vz_c%d" % i, [128, 512], BF16) for i in range(2)]
        self.NSLOT = 4
        self.slots = [sb("wslot%d" % i, [128, 4096], BF16) for i in range(self.NSLOT)]
        self.psf = [nc.alloc_psum_tensor("psf%d" % i, [128, 512], F32) for i in range(6)]
        self.psb = [nc.alloc_psum_tensor("psb%d" % i, [128, 1024], BF16) for i in range(2)]

    def reset_counters(self):
        self._ng_done = False
        self._early_done = False
        self.norm_done = False
        self.x_loaded = False
        self.next_x = None
        self._pf = 0
        self._pb = 0
        self._pt = 0
        self._piece = 0
        self._rr = 0

    def pf(self):
        self._pf += 1
        return self.psf[self._pf % 6]

    def pb(self):
        self._pb += 1
        return self.psb[self._pb % 2]

    def add(self, eng, fn, reads, writes, dma=False, group=None):
        if self.planning:
            return
        self.s.add(eng, fn, reads, writes, dma, group)

    def mm(self, out, lhsT, rhs, start=True, stop=True, rkeys=None):
        self.add("pe", lambda h: h.matmul(out, lhsT=lhsT, rhs=rhs, start=start, stop=stop), K(lhsT, rhs) if rkeys is None else rkeys, K(out))

    def tr(self, out, in_, ident, rkeys=None):
        self.add("pe", lambda h: h.transpose(out, in_, ident), K(in_, ident) if rkeys is None else rkeys, K(out))

    def act(self, out, in_, func, bias=None, scale=None, accum_out=None):
        kw = {}
        if bias is not None:
            kw["bias"] = bias
        if scale is not None:
            kw["scale"] = scale
        if accum_out is not None:
            kw["accum_out"] = accum_out
        self.add("act", lambda h: h.activation(out=out, in_=in_, func=func, **kw), K(in_, bias, scale), K(out, accum_out))

    def rsqrt(self, out, in_, scale, eps):
        self.act(out, in_, AF.Ln, bias=eps, scale=scale)
        self.act(out, out, AF.Exp, scale=-0.5)

    def amul(self, out, in_, mul):
        self.add("act", lambda h: h.mul(out=out, in_=in_, mul=mul), K(in_, mul), K(out))

    def cp(self, eng, out, in_):
        if eng == "act":
            self.add("act", lambda h: h.copy(out=out, in_=in_), K(in_), K(out))
        else:
            self.add(eng, lambda h: h.tensor_copy(out=out, in_=in_), K(in_), K(out))

    def tt(self, eng, out, in0, in1, op, rkeys=None, wkeys=None):
        self.add(eng, lambda h: h.tensor_tensor(out=out, in0=in0, in1=in1, op=op), K(in0, in1) if rkeys is None else rkeys, K(out) if wkeys is None else wkeys)

    def ts(self, eng, out, in0, s1, s2, op0, op1=None, rkeys=None, wkeys=None):
        if rkeys is not None:
            self.add(eng, lambda h: h.tensor_scalar(out=out, in0=in0, scalar1=s1, scalar2=0.0, op0=op0, op1=ALU.add), rkeys, wkeys)
            return
        if op1 is None:
            if op0 == ALU.pow:
                self.add(eng, lambda h: h.tensor_scalar(out=out, in0=in0, scalar1=0.0, scalar2=s1, op0=ALU.add, op1=ALU.pow), K(in0, s1), K(out))
            else:
                self.add(eng, lambda h: h.tensor_scalar(out=out, in0=in0, scalar1=s1, scalar2=0.0, op0=op0, op1=ALU.add), K(in0, s1), K(out))
        else:
            self.add(eng, lambda h: h.tensor_scalar(out=out, in0=in0, scalar1=s1, scalar2=s2, op0=op0, op1=op1), K(in0, s1, s2), K(out))

    def stt(self, eng, out, in0, scalar, in1, op0, op1):
        self.add(eng, lambda h: h.scalar_tensor_tensor(out=out, in0=in0, scalar=scalar, in1=in1, op0=op0, op1=op1),
                 K(in0, scalar, in1), K(out))

    def memset(self, eng, ap, val):
        self.add(eng, lambda h: h.memset(ap, val), [], K(ap))

    def dma(self, eng, out, in_, reads=(), writes=(), group=None):
        self.add(eng, lambda h: h.dma_start(out=out, in_=in_), K(in_) + list(reads), K(out) + list(writes), dma=True, group=group)

    def wget(self, tag, spec, wid, nel=4096):
        if self.planning:
            self.plan.append((tag, spec, wid, nel))
            return self.slots[0]
        if self._piece == 0:
            self.first = {}
            for j, pl in enumerate(self.plan):
                self.first.setdefault(pl[2], j)
            self.widx = {w: k for k, w in enumerate(self.first)}
            if self.wscr is None:
                self.wscr = self.nc.dram_tensor("wscr", [len(self.widx), 128, 4096], BF16).ap()
        i = self._piece
        assert self.plan[i][0] == tag, (self.plan[i][0], tag)
        LA = self.NSLOT - 2
        if i == 0:
            for j in range(min(LA, len(self.plan))):
                self._issue_piece(j)
        slot = self.slots[i % self.NSLOT]
        if self.first[wid] == i and self.nuse[wid] > 1:
            self.dma("pool", self.wscr[self.widx[wid], :, :nel], slot[:, :nel], writes=[("scr", wid)])
        if i + LA < len(self.plan):
            self._issue_piece(i + LA)
        self._piece += 1
        return slot

    def _issue_piece(self, j):
        slot = self.slots[j % self.NSLOT]
        tag, spec, wid, nel = self.plan[j]
        if self.first[wid] == j:
            for dst, src in spec(slot):
                self.dma("pool", dst, src, group=("piece", j))
        else:
            self.dma("sp", slot[:, :nel], self.wscr[self.widx[wid], :, :nel], reads=[("scr", wid)])

    def dbg(self, name, ap, shape, dt=F32):
        if name not in self.debug:
            return
        if self.planning:
            return
        d = self.nc.dram_tensor("dbg_" + name, list(shape), dt, kind="ExternalOutput").ap()
        self.dbg_outs.append("dbg_" + name)
        self.dma("sp", d, ap)

    def setup(self, first_x=None):
        c = self.c
        di = self.din
        self.dma("sp", c["ident"][:], di["ident"])
        self.dma("sp", self.gains_fm[:], di["gains_fm"])
        if first_x is not None:
            self.load_x(first_x, [(j * 128, 128) for j in range(4)])
        for n, sh, dt in CONST_SPECS:
            if n == "ident":
                continue
            if n.endswith("_s"):
                continue
            if len(sh) == 3:
                self.dma("sp", c[n][:], di[n].rearrange("a p f -> p a f"))
            else:
                self.dma("sp", c[n][:], di[n])
        for n in ("fgain_bc", "hgain", "qgcol", "kgcol", "kgain_bc", "halo_bias"):
            self.dma("sp", getattr(self, n)[:], di[n])
        self.dma("sp", self.sinkT[:], di["sinkT"].rearrange("a g m -> g a m"))
        self.dma("pool", self.walpha[:], di["w_alpha"])
        self.dma("pool", self.balpha[:], di["b_alpha"])
        self.dma("pool", self.biasT[:, 0:2], di["biasp"].rearrange("a p j n -> p a j n"))
        self.act(self.esinkT[:], self.sinkT[:], AF.Exp)
        for t_ in self.vz_cur + [self.vz_prev] + self.vz_c:
            self.memset("pool", t_[:], 0.0)
        self.memset("pool", self.knT_prev[:], 0.0)
        self.memset("dve", self.S[0][:], 0.0)
        self.memset("dve", self.Sbf[0][:], 0.0)
        for t_ in self.ss:
            self.memset("dve", t_[:], 0.0)
        if self.debug.get("delay"):
            self.memset("pool", self.slots[0][:], 0.0)
            for _ in range(int(self.debug["delay"])):
                self.cp("pool", self.slots[1][:], self.slots[0][:])

    def load_x(self, src, tts):
        for i, (o, P) in enumerate(tts):
            self.dma("sp", self.xt[i][:P, :], src[o:o + P, :])

    def norm_hT(self, tts, Tn, gcol0, src=None):
        self.norm_stats(tts, src)
        self.norm_tr(tts, Tn, gcol0, subtile_major=True)

    def norm_stats(self, tts, src=None):
        xt, hb = (self.xt if src is None else src), self.hb
        for i, (o, P) in enumerate(tts):
            self.act(self.junk[:P, :], xt[i][:P, :], AF.Square, accum_out=self.ss[i][:P, 0:1])
            self.rsqrt(self.rstd[i][:P, 0:1], self.ss[i][:P, 0:1], 1.0 / D, EPS)
            self.ts("pool" if i % 2 else "dve", hb[i][:P, :], xt[i][:P, :], self.rstd[i][:P, 0:1], None, ALU.mult,
                    rkeys=K(xt[i][:], self.rstd[i][:]) + ["s_hball"], wkeys=[("hb", i)])

    def norm_tr(self, tts, Tn, gcol0, subtile_major=False):
        hb = self.hb
        ident = self.c["ident"]
        if subtile_major and len(tts) > 1:
            regs = []
            b0, b1 = self.pb(), self.pb()
            f0, f1 = self.pf()[:, :].bitcast(BF16), self.pf()[:, :].bitcast(BF16)
            for bank in (b0, b1, f0, f1):
                regs += [bank[:, 0:512], bank[:, 512:1024]]
            for i, (o, P) in enumerate(tts):
                for cc in range(8):
                    self.tr(regs[cc][:, o:o + P], hb[i][:P, cc * 128:(cc + 1) * 128], ident[:P, :P],
                            rkeys=K(ident[:]) + [("hb", i), "s_hball"])
            for cc in (0, 2, 1, 3, 4, 6, 5, 7):
                g = self.gains_fm[:, gcol0 + cc:gcol0 + cc + 1]
                if (cc // 2) % 2 == 0:
                    self.amul(self.hT[cc][:, :Tn], regs[cc][:, :Tn], g)
                else:
                    self.ts("dve", self.hT[cc][:, :Tn], regs[cc][:, :Tn], g, None, ALU.mult)
            return
        for cc in range(8):
            pT = self.pb()
            for i, (o, P) in enumerate(tts):
                self.tr(pT[:, o:o + P], hb[i][:P, cc * 128:(cc + 1) * 128], ident[:P, :P],
                        rkeys=K(ident[:]) + [("hb", i), "s_hball"])
            g = self.gains_fm[:, gcol0 + cc:gcol0 + cc + 1]
            if cc % 2 == 0:
                self.amul(self.hT[cc][:, :Tn], pT[:, :Tn], g)
            else:
                self.ts("dve", self.hT[cc][:, :Tn], pT[:, :Tn], g, None, ALU.mult)

    def ffn(self, tts, Tn, wup, wdn, gcol0, tagp, mid_hook=None, mid_hook2=None):
        if self.norm_done:
            self.norm_done = False
        else:
            self.norm_hT(tts, Tn, gcol0)
        hT = self.hT
        upv = wup.rearrange("(k p) (ab c) -> p k ab c", p=128, ab=2)
        for g in range(11):
            def spec(slot, g=g):
                sv = slot[:, :].rearrange("p (k ab c) -> p k ab c", k=8, ab=2)
                return [(sv[:, :, ab, :], upv[:, :, ab, g * 256:(g + 1) * 256]) for ab in range(2)]
            slot = self.wget((tagp, "up", g), spec, (tagp[-1], "up", g))
            wv = slot[:, :].rearrange("p (k ab c) -> p k ab c", k=8, ab=2)
            for u in range(2):
                i = 2 * g + u
                pa = self.pf()
                pb_ = self.pf()
                for k in range(8):
                    self.mm(pa[:, :Tn], wv[:, k, 0, u * 128:(u + 1) * 128], hT[k][:, :Tn], k == 0, k == 7)
                for k in range(8):
                    self.mm(pb_[:, :Tn], wv[:, k, 1, u * 128:(u + 1) * 128], hT[k][:, :Tn], k == 0, k == 7)
                sa = self.sa[i % 2]
                self.act(sa[:, :Tn], pa[:, :Tn], AF.Silu)
                self.tt("dve", self.actT[i][:, :Tn], sa[:, :Tn], pb_[:, :Tn], ALU.mult,
                        rkeys=K(sa[:], pb_[:]) + ["s_arena0"], wkeys=[("actT", i)])
        if mid_hook is not None:
            mid_hook()
        dnv = wdn.rearrange("(i p) n -> p i n", p=128)
        for nh in range(2):
            accs = [self.pf() for _ in tts]
            for cg, (c0, c1) in enumerate(((0, 8), (8, 16), (16, 22))):
                def spec(slot, nh=nh, c0=c0, c1=c1):
                    return [(slot[:, :(c1 - c0) * 512].rearrange("p (c n) -> p c n", n=512), dnv[:, c0:c1, nh * 512:(nh + 1) * 512])]
                slot = self.wget((tagp, "down", nh, cg), spec, (tagp[-1], "down", nh, cg), (c1 - c0) * 512)
                wv = slot[:, :(c1 - c0) * 512].rearrange("p (c n) -> p c n", n=512)
                for i, (o, P) in enumerate(tts):
                    for cc in range(c0, c1):
                        self.mm(accs[i][:P, :], self.actT[cc][:, o:o + P], wv[:, cc - c0, :], cc == 0, cc == 21,
                                rkeys=K(wv[:, 0, :]) + ["s_arena0", ("actT", cc)])
            for i, (o, P) in enumerate(tts):
                xs_ = self.xt[i][:P, nh * 512:(nh + 1) * 512]
                self.stt("dve", xs_, accs[i][:P, :], 0.5, xs_, ALU.mult, ALU.add)
            if nh == 0 and mid_hook2 is not None:
                mid_hook2()

    def win_piece(self, tag, c0, w):
        wv_ = self.din["w_in"].rearrange("(k p) c -> p k c", p=128)

        def spec(slot):
            return [(slot[:, :8 * w].rearrange("p (k c) -> p k c", k=8), wv_[:, :, c0:c0 + w])]
        slot = self.wget(tag, spec, tag[1:], 8 * w)
        return slot[:, :8 * w].rearrange("p (k c) -> p k c", k=8)

    def fm_proj(self, wv, col0, Tn, lhs_view=None):
        p = self.pf()
        for k in range(8):
            lhs = wv[:, k, col0:col0 + 128] if lhs_view is None else lhs_view(k)
            self.mm(p[:, :Tn], lhs, self.hT[k][:, :Tn], k == 0, k == 7)
        return p

    def tm_proj(self, wv, c0, w, o, P):
        p = self.pf()
        for k in range(8):
            self.mm(p[:P, :w], self.hT[k][:, o:o + P], wv[:, k, c0:c0 + w], k == 0, k == 7)
        return p

    def qknorm(self, ps, out, gcol, bo, eps, Tn, idx, ntt=None):
        sqsq, sqr, rs = self.sqsq[0], self.sqr[0], self.rsq[0]
        self.act(sqsq[:, :Tn], ps, AF.Square)
        pm = self.pf()
        self.mm(pm[:, :Tn], bo[:, :], sqsq[:, :Tn])
        self.rsqrt(rs[:, :Tn], pm[:, :Tn], 1.0, eps)
        if ntt is None:
            self.stt("dve", out, ps, gcol, rs[:, :Tn], ALU.mult, ALU.mult)
        else:
            self.stt("dve", out, ps.rearrange("p (t q) -> p t q", t=ntt), gcol, rs[:, :Tn].rearrange("p (t q) -> p t q", t=ntt), ALU.mult, ALU.mult)

    def mixer(self, tts, Tn, mode, tagp, seqs, tile_idx, last_tile):
        c = self.c
        sample = mode == "sample"
        L = tts[0][1]
        tri_i = c["tri_incl"]
        tri_s = c["tri_suf"]
        cmask = c["cmask4"]
        self.norm_hT(tts, Tn, 8)
        if mode == "pre" and self.next_x is not None:
            self.load_x(self.next_x, tts)
        hT = self.hT
        e1T, e2T, qeT, keT = self.e1T, self.e2T, self.qeT, self.keT
        wv = self.win_piece((tagp, "ga"), C_GA, 16)
        p = self.pf()
        for k in range(8):
            self.mm(p[:16, :Tn], wv[:, k, 0:16], hT[k][:, :Tn], k == 0, k == 7)
        self.cp("act", self.gaT[:, :Tn], p[:16, :Tn])
        wk = self.win_piece((tagp, "gk"), C_GK, 512)
        n_t = len(tts)
        pzs = []
        for i, (o, P) in enumerate(tts):
            pz = self.pf()
            self.mm(pz[:P, :], self.gaT[:, o:o + P], self.walpha[:, :], True, False)
            self.mm(pz[:P, :], c["ones_row"][0:1, :P], self.balpha[0:1, :], False, True)
            pzs.append(pz)
        for i, (o, P) in enumerate(tts):
            self.act(self.es[i][:P, :], pzs[i][:P, :], AF.Exp, scale=-1.0)
        for i, (o, P) in enumerate(tts):
            self.act(self.sp[i][:P, :], self.es[i][:P, :], AF.Ln, bias=1.0)
        for i, (o, P) in enumerate(tts):
            pk = self.tm_proj(wk, 0, 512, o, P)
            self.cp("dve", self.kd[i][:P, :], pk[:P, :])
        if mode != "pre":
            for hd in range(4):
                p = self.fm_proj(wk, hd * 128, Tn)
                self.cp("dve", keT[:, hd, :Tn], p[:, :Tn])
            wq = self.win_piece((tagp, "gq"), C_GQ, 512)
            for hd in range(4):
                p = self.fm_proj(wq, hd * 128, Tn)
                self.cp("dve", qeT[:, hd, :Tn], p[:, :Tn])
        pbts = []
        for i, (o, P) in enumerate(tts):
            pbt = self.pf()
            for hd in range(4):
                self.mm(pbt[:, hd * P:(hd + 1) * P], self.sp[i][:P, hd * 128:(hd + 1) * 128], tri_i[:P, :P])
            pbts.append(pbt)
        for i, (o, P) in enumerate(tts):
            pv3 = pbts[i][:, :4 * P].rearrange("p (h t) -> p h t", h=4)
            self.act(e1T[:, :, o:o + P], pv3, AF.Exp)
            if mode != "pre":
                self.act(e2T[:, :, o:o + P], pv3, AF.Exp, scale=-1.0)
        for pc in range(2):
            wvv = self.win_piece((tagp, "gv", pc), C_GV + pc * 512, 512)
            for i, (o, P) in enumerate(tts):
                p = self.tm_proj(wvv, 0, 512, o, P)
                self.cp("dve", self.v_tok[i][:P, pc * 512:(pc + 1) * 512], p[:P, :])
        psufs = []
        for i, (o, P) in enumerate(tts):
            psuf = self.pf()
            whole = (mode == "pre")
            self.mm(psuf[:P, :], tri_s[:P, :P], self.sp[i][:P, :], True, not (whole and i < n_t - 1))
            if whole:
                for j in range(i + 1, n_t):
                    self.mm(psuf[:P, :], c["tri_all"][:P, :P], self.sp[j][:P, :], False, j == n_t - 1)
            psufs.append(psuf)
        for i, (o, P) in enumerate(tts):
            self.act(self.es[i][:P, :], psufs[i][:P, :], AF.Exp)
        if mode != "pre":
            for pc in range(2):
                wg = self.win_piece((tagp, "gr", pc), C_GR + pc * 512, 512)
                for u in range(4):
                    p = self.fm_proj(wg, u * 128, Tn)
                    self.act(self.sgT[:, pc * 4 + u, :Tn], p[:, :Tn], AF.Silu)
        for i, (o, P) in enumerate(tts):
            self.tt("dve", self.kd[i][:P, :], self.kd[i][:P, :], self.es[i][:P, :], ALU.mult)
        if mode != "pre":
            for hd in range(4):
                self.tt("dve", keT[:, hd, :Tn], keT[:, hd, :Tn], e2T[:, hd, :Tn], ALU.mult)
            for hd in range(4):
                self.stt("dve", qeT[:, hd, :Tn], qeT[:, hd, :Tn], 128 ** -0.5, e1T[:, hd, :Tn], ALU.mult, ALU.mult)
        mstop = self.debug.get("mstop") if sample else None
        if mstop == "gpipe":
            return
        if mode == "pre":
            S, Sbf = seqs[0][2], seqs[0][3]
            pS = [self.pf(), self.pf()]
            for hd in range(4):
                for i, (o, P) in enumerate(tts):
                    self.mm(pS[hd // 2][:, (hd % 2) * 256:(hd % 2 + 1) * 256], self.kd[i][:P, hd * 128:(hd + 1) * 128],
                            self.v_tok[i][:P, hd * 256:(hd + 1) * 256], i == 0, i == n_t - 1)
            self.cp("dve", self.dtot[:, :, :], e1T[:, :, 127:128])
            for i in range(1, n_t):
                self.tt("dve", self.dtot[:, :, :], self.dtot[:, :, :], e1T[:, :, i * 128 + 127:i * 128 + 128], ALU.mult)
            for hd in range(4):
                self.stt("dve", S[:, hd, :], S[:, hd, :], self.dtot[:, hd, :], pS[hd // 2][:, (hd % 2) * 256:(hd % 2 + 1) * 256], ALU.mult, ALU.add)
            self.cp("act", Sbf[:, :, :], S[:, :, :])
        elif len(seqs) == 1:
            self.gla_pipelined(tts, seqs[0], cmask)
        else:
            for i, (o, P) in enumerate(tts):
                self.gla_chunk(i, o, P, mode, seqs, cmask)
        if mstop == "gla":
            return
        if mode == "full" and tile_idx == 0:
            self.dbg("oaT", self.sgT[:, :, :], [128, 8, T], BF16)
            self.dbg("qeT", self.qeT, [128, 4, T], BF16)
            self.dbg("keT", self.keT[:, :, :], [128, 4, T], BF16)
            self.dbg("e1T", self.e1T, [128, 4, T], F32)
            self.dbg("kd3", self.kd[3], [128, 512], BF16)
            self.dbg("vtok3", self.v_tok[3], [128, 1024], BF16)
            self.dbg("S", self.S[0][:, :, :], [128, 4, 256], F32)
        need_kv = (mode != "pre") or last_tile
        if mode != "pre":
            ntt = len(tts)
            w_in_v = self.din["w_in"].rearrange("(k p) c -> p k c", p=128)
            for pc in range(2):
                def spec(slot, pc=pc):
                    sv = slot[:, :].rearrange("p (k g two d) -> p k g two d", k=8, g=4, two=2)
                    return [(sv[:, k, :, two, :], w_in_v[:, k, C_SQ + pc * 512 + two * 256:C_SQ + pc * 512 + (two + 1) * 256].rearrange("p (g d) -> p g d", g=4))
                            for two in range(2) for k in range(8)]
                wq = self.wget((tagp, "sq", pc), spec, ("sq", pc))[:, :].rearrange("p (k c) -> p k c", k=8)
                qv = self.qo[pc][:, :ntt * 4 * L].rearrange("p (t g q) -> p t g q", t=ntt, g=4)
                for g in range(4):
                    p = self.fm_proj(wq, g * 128, Tn)
                    self.qknorm(p[:, :Tn], qv[:, :, g, :], self.qgcol[:, 0:1], c["bo_q"], 64 * EPS, Tn, g, ntt)
        if need_kv:
            wkv = self.win_piece((tagp, "sksv"), C_SK, 512)
            for P2 in range(2):
                p = self.fm_proj(wkv, P2 * 128, Tn)
                self.qknorm(p[:, :Tn], self.knT_cur[:, P2, :Tn], self.kgcol[:, 0:1], c["bo_k"], EPS, Tn, P2)
            for i, (o, P) in enumerate(tts):
                pvv = self.tm_proj(wkv, 256, 256, o, P)
                vz4 = self.vz_cur[i][:, :].rearrange("p (jp par c) -> p jp par c", jp=2, par=2)
                pv4 = pvv[:, :256].rearrange("p (jp par d) -> p jp par d", jp=2, par=2)
                for par in range(2):
                    self.cp("act", vz4[:P, :, par, par * 64:(par + 1) * 64], pv4[:P, :, par, :])
                want_out = sample or (mode == "full" and last_tile and i == len(tts) - 1)
                if want_out:
                    self.cp("act", self.vnew[:P, :], pvv[:P, :256])
                    pkk = self.tm_proj(wkv, 0, 256, o, P)
                    self.act(self.ksq[:P, :], pkk[:P, :256], AF.Square)
                    self.add("dve", lambda h, P=P: h.tensor_reduce(out=self.kss[:P, 0:4], in_=self.ksq[:P, :].rearrange("p (j d) -> p j d", j=4),
                                                                    axis=AX.X, op=ALU.add), K(self.ksq[:]), K(self.kss[:]))
                    self.rsqrt(self.kss[:P, :], self.kss[:P, :], 1.0 / 64, EPS)
                    k3 = self.ktok[:P, :].rearrange("p (j d) -> p j d", j=4)
                    self.tt("dve", k3, pkk[:P, :256].rearrange("p (j d) -> p j d", j=4),
                            self.kss[:P, 0:4].unsqueeze(2).to_broadcast([P, 4, 64]), ALU.mult)
                    self.tt("dve", self.knew[:P, :].rearrange("p (j d) -> p j d", j=4), k3,
                            self.kgain_bc[:P, :].unsqueeze(1).to_broadcast([P, 4, 64]), ALU.mult)
                    if sample:
                        self.dma("sp", self.dout["nks"], self.knew[:P, :])
                        self.dma("sp", self.dout["nvs"], self.vnew[:P, :])
                    else:
                        self.dma("sp", self.dout["nk"], self.knew[:P, :])
                        self.dma("sp", self.dout["nv"], self.vnew[:P, :])
        if mode == "pre":
            if last_tile:
                self.cp("pool", self.knT_prev[:, :, :], self.knT_cur[:, :, Tn - 128:Tn])
                self.cp("pool", self.vz_prev[:, :], self.vz_cur[len(tts) - 1][:, :])
            if self.next_x is not None:
                self.norm_hT(tts, Tn, 0)
                self.norm_done = True
            return
        if mstop == "swaproj":
            return
        gate_groups = []
        gstate = {}
        for nm, c0, dst in (("gta", C_GTA, self.sigA), ("gtb", C_GTB, self.sigB)):
            for pc in range(2):
                for u in range(4):
                    def grp(nm=nm, c0=c0, dst=dst, pc=pc, u=u):
                        if u == 0:
                            gstate["w"] = self.win_piece((tagp, nm, pc), c0 + pc * 512, 512)
                        p = self.fm_proj(gstate["w"], u * 128, Tn)
                        self.act(dst[:, pc * 4 + u, :Tn], p[:, :Tn], AF.Tanh, scale=0.5)
                    gate_groups.append(grp)

        def run_gates(k):
            for _ in range(min(k, len(gate_groups))):
                gate_groups.pop(0)()
        if sample:
            kts = []
            for sq_ in range(2):
                kts.append(dict(knT=lambda P2, rows, sq_=sq_: self.knT_c[sq_][rows, P2, :], vz=self.vz_c[sq_], bias=self.biasTs[:, sq_, :, :], nk=128, hb=None))
            kts.append(dict(knT=lambda P2, rows: self.knT_cur[rows, P2, 0:64], vz=self.vz_cur[0], bias=self.biasTs[:, 2, :, :], nk=64, hb=None))
            self.swa_block(0, 0, 1, 64, kts, c["g4s"], between=lambda: run_gates(2))
        else:
            for i, (o, P) in enumerate(tts):
                kts = []
                if i == 0:
                    hb = self.halo_bias[:, 0:1] if tile_idx == 0 else None
                    kts.append(dict(knT=lambda P2, rows: self.knT_prev[rows, P2, :], vz=self.vz_prev, bias=self.biasT[:, 0, :, :], nk=128, hb=hb))
                else:
                    kts.append(dict(knT=lambda P2, rows, o=o: self.knT_cur[rows, P2, o - 128:o], vz=self.vz_cur[i - 1], bias=self.biasT[:, 0, :, :], nk=128, hb=None))
                kts.append(dict(knT=lambda P2, rows, o=o: self.knT_cur[rows, P2, o:o + 128], vz=self.vz_cur[i], bias=self.biasT[:, 1, :, :], nk=128, hb=None))
                self.swa_block(i, o, len(tts), 128, kts, c["g4"], between=lambda: run_gates(2))
            self.cp("pool", self.knT_prev[:, :, :], self.knT_cur[:, :, Tn - 128:Tn])
            self.cp("pool", self.vz_prev[:, :], self.vz_cur[len(tts) - 1][:, :])
        if mode == "full" and tile_idx == 0:
            self.dbg("osT0", self.osT[0], [128, 4, T], BF16)
            self.dbg("osT1", self.osT[1], [128, 4, T], BF16)
            self.dbg("knT", self.knT_cur[:, :, :], [128, 2, T], BF16)
        if mstop == "swa":
            return
        run_gates(len(gate_groups))
        if mode == "full" and tile_idx == 0:
            self.dbg("sgA", self.sigA, [128, 8, T], BF16)
            self.dbg("sgB", self.sigB, [128, 8, T], BF16)
        wbr = self.din["w_branch"]
        wbA = wbr[0:1024, :].rearrange("(f p) n -> p f n", p=128)
        wbB = wbr[1024:2048, :].rearrange("(P j g d) n -> d P j g n", P=2, j=2, g=4)
        for ng in range(2):
            def specA(slot, ng=ng):
                return [(slot[:, :].rearrange("p (f n) -> p f n", f=8), wbA[:, :, ng * 512:(ng + 1) * 512])]

            def specB(slot, ng=ng):
                return [(slot[half * 64:(half + 1) * 64, :].rearrange("p (P g n) -> p P g n", P=2, g=4)[:, P2],
                         wbB[:, P2, half, :, ng * 512:(ng + 1) * 512]) for half in range(2) for P2 in range(2)]
            sA = self.wget((tagp, "wbA", ng), specA, ("wbA", ng))[:, :].rearrange("p (f n) -> p f n", f=8)
            sB = self.wget((tagp, "wbB", ng), specB, ("wbB", ng))[:, :].rearrange("p (f n) -> p f n", f=8)
            for u in range(4):
                n = ng * 4 + u
                pa = self.pf()
                for f in range(8):
                    self.mm(pa[:, :Tn], sA[:, f, u * 128:(u + 1) * 128], self.sgT[:, f, :Tn], f == 0, f == 7)
                pb_ = self.pf()
                for f in range(8):
                    self.mm(pb_[:, :Tn], sB[:, f, u * 128:(u + 1) * 128], self.osT[f // 4][:, f % 4, :Tn], f == 0, f == 7)
                t0, t1 = self.mtmp
                self.stt("dve", t0[:, :Tn], self.sigA[:, n, :Tn], 1.0, pa[:, :Tn], ALU.add, ALU.mult)
                self.stt("dve", t1[:, :Tn], self.sigB[:, n, :Tn], 1.0, pb_[:, :Tn], ALU.add, ALU.mult)
                self.tt("pool", self.sigA[:, n, :Tn], t0[:, :Tn], t1[:, :Tn], ALU.add)
        if mode == "full" and tile_idx == 0:
            self.dbg("mT", self.sigA, [128, 8, T], BF16)
        wo = self.din["w_out"].rearrange("(f p) n -> p f n", p=128)
        for nh in range(2):
            def spec(slot, nh=nh):
                return [(slot[:, :].rearrange("p (f n) -> p f n", f=8), wo[:, :, nh * 512:(nh + 1) * 512])]
            so = self.wget((tagp, "wo", nh), spec, ("wo", nh))[:, :].rearrange("p (f n) -> p f n", f=8)
            for i, (o, P) in enumerate(tts):
                p = self.pf()
                for f in range(8):
                    self.mm(p[:P, :], self.sigA[:, f, o:o + P], so[:, f, :], f == 0, f == 7)
                xs_ = self.xt[i][:P, nh * 512:(nh + 1) * 512]
                self.stt("dve", xs_, p[:P, :], 0.5, xs_, ALU.mult, ALU.add)

    def gla_pipelined(self, tts, seq, cmask):
        c = self.c
        e1T, qeT, keT = self.e1T, self.qeT, self.keT
        so, nreal, S, Sbf0 = seq
        Sb = [Sbf0, self.Sbf_alt]
        L = tts[0][1]
        n_t = len(tts)
        assert n_t % 2 == 0

        def front(i, o):
            pS = [self.pf(), self.pf()]
            for hd in range(4):
                self.mm(pS[hd // 2][:, (hd % 2) * 256:(hd % 2 + 1) * 256], self.kd[i][:L, hd * 128:(hd + 1) * 128],
                        self.v_tok[i][:L, hd * 256:(hd + 1) * 256])
            pA = self.pf()
            for hd in range(4):
                self.mm(pA[:L, hd * L:(hd + 1) * L], keT[:, hd, o:o + L], qeT[:, hd, o:o + L])
            atm = self.ATm2[i % 2]
            self.tt("dve", atm[:L, :4 * L], pA[:L, :4 * L], cmask[:L, :4 * L], ALU.mult)
            dcol = o + L - 1
            for hd in range(4):
                self.stt("dve", S[:, hd, :], S[:, hd, :], e1T[:, hd, dcol:dcol + 1], pS[hd // 2][:, (hd % 2) * 256:(hd % 2 + 1) * 256], ALU.mult, ALU.add)
            self.cp("act", Sb[(i + 1) % 2][:, :, :], S[:, :, :])
            return atm

        def tail(j):
            oj = tts[j][0]
            o_raw, osq = self.o_raw2[j % 2], self.osq2[j % 2]
            pM = self.pf()
            for hd in range(4):
                self.mm(pM[:, hd * L:(hd + 1) * L], c["ones_mean"][:, :], osq[:, 2 * hd, :L], True, False)
                self.mm(pM[:, hd * L:(hd + 1) * L], c["ones_mean"][:, :], osq[:, 2 * hd + 1, :L], False, True)
            self.rsqrt(self.rstdg[:, :4 * L], pM[:, :4 * L], 1.0, EPS)
            r3 = self.rstdg[:, :4 * L].rearrange("p (h l) -> p h l", h=4)
            o4 = o_raw[:, :, :].rearrange("p (h v) l -> p h v l", v=2)
            for vc in range(2):
                self.stt("dve", o4[:, :, vc, :L], o4[:, :, vc, :L], self.hgain[:, vc:vc + 1], r3, ALU.mult, ALU.mult)
            self.tt("dve", self.sgT[:, :, oj:oj + L], o_raw[:, :, :L], self.sgT[:, :, oj:oj + L], ALU.mult)

        nxt = front(0, tts[0][0])
        for i, (o, P) in enumerate(tts):
            atm = nxt
            Sbf = Sb[i % 2]
            pO = [self.pf(), self.pf()]
            for hd in range(4):
                for vc in range(2):
                    reg = pO[hd // 2][:, ((hd % 2) * 2 + vc) * L:((hd % 2) * 2 + vc + 1) * L]
                    self.mm(reg, self.v_tok[i][:L, hd * 256 + vc * 128:hd * 256 + (vc + 1) * 128], atm[:L, hd * L:(hd + 1) * L], True, False)
                    self.mm(reg, Sbf[:, hd, vc * 128:(vc + 1) * 128], qeT[:, hd, o:o + L], False, True)
            o_raw, osq = self.o_raw2[i % 2], self.osq2[i % 2]
            for b in range(2):
                pv = pO[b][:, :4 * L].rearrange("p (c l) -> p c l", c=4)
                self.cp("act", o_raw[:, 4 * b:4 * b + 4, :L], pv)
                self.act(osq[:, 4 * b:4 * b + 4, :L], pv, AF.Square)
            if i + 1 < n_t:
                nxt = front(i + 1, tts[i + 1][0])
            if i >= 1:
                tail(i - 1)
        tail(n_t - 1)

    def gla_chunk(self, i, o, L, mode, seqs, cmask):
        c = self.c
        e1T, qeT, keT = self.e1T, self.qeT, self.keT
        nseq = len(seqs)
        blk = L // nseq
        def state_mm(si):
            so = seqs[si][0]
            pS = [self.pf(), self.pf()]
            for hd in range(4):
                self.mm(pS[hd // 2][:, (hd % 2) * 256:(hd % 2 + 1) * 256], self.kd[i][so:so + blk, hd * 128:(hd + 1) * 128],
                        self.v_tok[i][so:so + blk, hd * 256:(hd + 1) * 256])
            return pS
        hoist = nseq == 1
        pSs = [state_mm(0)] if hoist else None
        pA = self.pf()
        for hd in range(4):
            self.mm(pA[:L, hd * L:(hd + 1) * L], keT[:, hd, o:o + L], qeT[:, hd, o:o + L])
        self.tt("dve", self.ATm[:L, :4 * L], pA[:L, :4 * L], cmask[:L, :4 * L], ALU.mult)
        pO = [self.pf(), self.pf()]
        for hd in range(4):
            for vc in range(2):
                reg = pO[hd // 2][:, ((hd % 2) * 2 + vc) * L:((hd % 2) * 2 + vc + 1) * L]
                self.mm(reg, self.v_tok[i][:L, hd * 256 + vc * 128:hd * 256 + (vc + 1) * 128], self.ATm[:L, hd * L:(hd + 1) * L], True, False)
                for si, (so, nreal, S, Sbf) in enumerate(seqs):
                    self.mm(reg[:, so:so + blk], Sbf[:, hd, vc * 128:(vc + 1) * 128], qeT[:, hd, o + so:o + so + blk], False, si == nseq - 1)
        for si, (so, nreal, S, Sbf) in enumerate(seqs):
            pS = pSs[si] if hoist else state_mm(si)
            dcol = o + so + nreal - 1
            for hd in range(4):
                self.stt("dve", S[:, hd, :], S[:, hd, :], e1T[:, hd, dcol:dcol + 1], pS[hd // 2][:, (hd % 2) * 256:(hd % 2 + 1) * 256], ALU.mult, ALU.add)
            self.cp("act", Sbf[:, :, :], S[:, :, :])
        for b in range(2):
            pv = pO[b][:, :4 * L].rearrange("p (c l) -> p c l", c=4)
            self.cp("act", self.o_raw[:, 4 * b:4 * b + 4, :L], pv)
            self.act(self.osq[:, 4 * b:4 * b + 4, :L], pv, AF.Square)
        pM = self.pf()
        for hd in range(4):
            self.mm(pM[:, hd * L:(hd + 1) * L], c["ones_mean"][:, :], self.osq[:, 2 * hd, :L], True, False)
            self.mm(pM[:, hd * L:(hd + 1) * L], c["ones_mean"][:, :], self.osq[:, 2 * hd + 1, :L], False, True)
        self.rsqrt(self.rstdg[:, :4 * L], pM[:, :4 * L], 1.0, EPS)
        r3 = self.rstdg[:, :4 * L].rearrange("p (h l) -> p h l", h=4)
        o4 = self.o_raw[:, :, :].rearrange("p (h v) l -> p h v l", v=2)
        for vc in range(2):
            self.stt("dve", o4[:, :, vc, :L], o4[:, :, vc, :L], self.hgain[:, vc:vc + 1], r3, ALU.mult, ALU.mult)
        self.tt("dve", self.sgT[:, :, o:o + L], self.o_raw[:, :, :L], self.sgT[:, :, o:o + L], ALU.mult)

    def swa_block(self, ti, q0, ntt, nq, kts, g4, between=None):
        c = self.c
        N = 4 * nq
        for P2 in range(2):
            if between is not None:
                between()
            pts = []
            for kt in kts:
                nk = kt["nk"]
                pss = []
                for half in range(2):
                    rows = slice(half * 64, half * 64 + 64)
                    rhs_q = self.qo[P2][rows, ti * 4 * nq:(ti + 1) * 4 * nq]
                    ps = self.pf()
                    self.mm(ps[:nk, :N], kt["knT"](P2, rows), rhs_q, True, False)
                    pss.append(ps)
                for half in range(2):
                    self.mm(pss[half][:nk, :N], c["ident"][:, :nk], kt["bias"][:, 2 * P2 + half, :N], False, True)
                for half in range(2):
                    ps = pss[half]
                    self._pt += 1
                    pt = self.PT[self._pt % 6]
                    if kt["hb"] is not None:
                        self.act(pt[:nk, :N], ps[:nk, :N], AF.Exp, bias=kt["hb"][:nk, :])
                    else:
                        self.act(pt[:nk, :N], ps[:nk, :N], AF.Exp)
                    pts.append((pt, kt, half))
            pO = self.pf()
            pD = self.pf()
            n = len(pts)
            for idx, (pt, kt, half) in enumerate(pts):
                nk = kt["nk"]
                vz4 = kt["vz"][:, :].rearrange("p (j c) -> p j c", j=4)
                self.mm(pO[:, :N], vz4[:nk, 2 * P2 + half, :], pt[:nk, :N], idx == 0, idx == n - 1)
            for idx, (pt, kt, half) in enumerate(pts):
                nk = kt["nk"]
                self.mm(pD[:, :N], c["onesH"][:nk, half, :], pt[:nk, :N], idx == 0, False)
            self.mm(pD[:, :N], self.esinkT[0:4, P2, :], g4[0:4, :N], False, True)
            self.add("dve", lambda h, pD=pD, N=N: h.reciprocal(out=self.rden[:, :N], in_=pD[:, :N]), K(pD[:]), K(self.rden[:]))
            self.tt("dve", self.osT[P2][:, :, q0:q0 + nq], pO[:, :N].rearrange("p (g q) -> p g q", g=4),
                    self.rden[:, :N].rearrange("p (g q) -> p g q", g=4), ALU.mult)

    def final_out(self, tts, dst):
        for i, (o, P) in enumerate(tts):
            self.act(self.junk[:P, :], self.xt[i][:P, :], AF.Square, accum_out=self.ss[i][:P, 0:1])
            self.rsqrt(self.rstd[i][:P, 0:1], self.ss[i][:P, 0:1], 1.0 / D, EPS)
            self.ts("dve", self.xt[i][:P, :], self.xt[i][:P, :], self.rstd[i][:P, 0:1], None, ALU.mult)
            self.tt("pool" if i % 2 else "dve", self.xt[i][:P, :], self.xt[i][:P, :], self.fgain_bc[:P, :], ALU.mult)
            self.dma("sp", dst[o:o + P, :], self.xt[i][:P, :])

    def body(self):
        di = self.din
        self.reset_counters()
        pre_on = not self.debug.get("skip_pre") and not self.debug.get("only_sample")
        self.setup(first_x=di["xprev"][0:T, :] if pre_on else None)
        tts = [(j * 128, 128) for j in range(4)]
        seqs_p = [(0, 128, self.S[0], self.Sbf[0])]
        stop = self.debug.get("stop")
        if not self.debug.get("skip_pre") and not self.debug.get("only_sample"):
            for t in range(NTILE):
                self.next_x = di["xprev"][(t + 1) * T:(t + 2) * T, :] if t + 1 < NTILE else di["xp"][0:T, :]
                self.ffn(tts, T, di["ffn1_up"], di["ffn1_down"], 0, ("pre", t, "f1"))
                self.mixer(tts, T, "pre", ("pre", t), seqs_p, t, t == NTILE - 1)
            self.next_x = None
        pre_ran = not self.debug.get("skip_pre") and not self.debug.get("only_sample")
        for t in range(NTILE if not self.debug.get("only_sample") else 0):
            if self.x_loaded:
                self.x_loaded = False
            elif not (t == 0 and pre_ran):
                self.load_x(di["xp"][t * T:(t + 1) * T, :], tts)
            self.ffn(tts, T, di["ffn1_up"], di["ffn1_down"], 0, ("own", t, "f1"))
            if stop == "ffn1":
                self.final_dbg_x(tts, t)
                continue
            self.mixer(tts, T, "full", ("own", t), seqs_p, t, t == NTILE - 1)
            if stop == "mixer":
                self.final_dbg_x(tts, t)
                continue
            if t == NTILE - 1 and not self.debug.get("no_sample") and not self.debug.get("no_ffn2"):
                self.dma("sp", self.dout["ng"].rearrange("h k v -> k h v"), self.S[0][:, :, :])
                self._ng_done = True
                self.sample_setup_early()
            staged = False
            if not self.debug.get("no_ffn2"):
                hook = None
                smp_next = (t == NTILE - 1 and not self.debug.get("no_sample") and not self.debug.get("no_final") and not stop)
                if smp_next:
                    tss_ = [(0, TS)]
                    xst = [self.arena1[:, 0:2048].bitcast(F32)]
                    self.dma("sp", xst[0][:TS, :], di["xs"])

                    def hook(xst=xst, tss_=tss_):
                        self.norm_stats(tss_, src=xst)

                    def hook2(tss_=tss_):
                        self.norm_tr(tss_, TS, 0)
                    staged = True
                if t + 1 < NTILE and not self.debug.get("no_final"):
                    xst = [self.arena1[:, i * 2048:(i + 1) * 2048].bitcast(F32) for i in range(4)]
                    nsrc = di["xp"][(t + 1) * T:(t + 2) * T, :]
                    for i, (o, P) in enumerate(tts):
                        self.dma("sp", xst[i][:P, :], nsrc[o:o + P, :])

                    def hook(xst=xst):
                        self.norm_stats(tts, src=xst)

                    def hook2():
                        self.norm_tr(tts, T, 0)
                    staged = True
                self.ffn(tts, T, di["ffn2_up"], di["ffn2_down"], 16, ("own", t, "f2"), mid_hook=hook, mid_hook2=hook2 if hook else None)
            if self.debug.get("no_final"):
                self.final_dbg_x(tts, t)
            else:
                self.final_out(tts, self.dout["y"][t * T:(t + 1) * T, :])
            if staged:
                for i, (o, P) in enumerate(tss_ if smp_next else tts):
                    self.cp("pool", self.xt[i][:P, :], xst[i][:P, :])
                self.norm_done = True
                self.x_loaded = True
        if not self._ng_done:
            self.dma("sp", self.dout["ng"].rearrange("h k v -> k h v"), self.S[0][:, :, :])
        if stop or self.debug.get("no_sample"):
            return
        if not self._early_done:
            self.sample_setup_early()
        tss = [(0, TS)]
        self.dma("sp", self.S[1][:, :, :], di["st"][1].rearrange("h k v -> k h v"))
        self.cp("pool", self.Sbf[1][:, :, :], self.S[1][:, :, :])
        seqs_s = [(0, 16, self.S[0], self.Sbf[0]), (32, 16, self.S[1], self.Sbf[1])]
        if self.x_loaded:
            self.x_loaded = False
        else:
            self.load_x(di["xs"], tss)
        self.sample_rest(tss, seqs_s)

    def sample_setup_early(self):
        di = self.din
        self._early_done = True
        self.dma("pool", self.biasTs, di["biass"].rearrange("a p j n -> p a j n"))
        for n in ("tri_incl", "tri_suf", "cmask4"):
            self.dma("sp", self.c[n][:], di[n + "_s"])
        self.dma("sp", self.S[0][:, :, :], di["st"][0].rearrange("h k v -> k h v"))
        self.cp("pool", self.Sbf[0][:, :, :], self.S[0][:, :, :])
        for sq_ in range(2):
            self.dma("sp", self.ckf[:, :], di["ck"][sq_])
            self.cp("dve", self.ckb[:, :], self.ckf[:, :])
            for P2 in range(2):
                pT = self.pb()
                self.tr(pT[:, 0:128], self.ckb[:, P2 * 128:(P2 + 1) * 128], self.c["ident"][:, :])
                self.cp("act", self.knT_c[sq_][:, P2, :], pT[:, 0:128])
            self.dma("sp", self.ckf[:, :], di["cv"][sq_])
            vz4 = self.vz_c[sq_][:, :].rearrange("p (jp par c) -> p jp par c", jp=2, par=2)
            cv4 = self.ckf[:, :].rearrange("p (jp par d) -> p jp par d", jp=2, par=2)
            for par in range(2):
                self.cp("dve", vz4[:, :, par, par * 64:(par + 1) * 64], cv4[:, :, par, :])

    def sample_rest(self, tss, seqs_s):
        di = self.din
        sstop = self.debug.get("sstop")
        self.ffn(tss, TS, di["ffn1_up"], di["ffn1_down"], 0, ("smp", "f1"))
        if sstop != "ffn1":
            self.mixer(tss, TS, "sample", ("smp",), seqs_s, 0, True)
            if sstop != "mixer":
                self.ffn(tss, TS, di["ffn2_up"], di["ffn2_down"], 16, ("smp", "f2"))
        self.final_out(tss, self.dout["ys"])
        for sq_ in range(2):
            self.dma("sp", self.dout["ngs"][sq_].rearrange("h k v -> k h v"), self.S[sq_][:, :, :])

    def final_dbg_x(self, tts, t):
        for i, (o, P) in enumerate(tts):
            self.dma("sp", self.dout["y"][t * T + o:t * T + o + P, :], self.xt[i][:P, :])

    def build(self):
        self.planning = True
        self.body()
        self.planning = False
        self.nuse = {}
        for pl in self.plan:
            self.nuse[pl[2]] = self.nuse.get(pl[2], 0) + 1
        self.body()
        self.s.emit()
        return self.nc


_CACHE = {}


def _get_prog(debug=None):
    key = repr(sorted((debug or {}).items()))
    if key not in _CACHE:
        p = Prog(debug)
        p.build()
        _CACHE[key] = p
    return _CACHE[key]


def make_in_maps(inp):
    f = lambda a: np.ascontiguousarray(np.asarray(a, dtype=np.float32))
    xp = f(inp["x_prompt"])
    xsm = f(inp["x_sample"])
    ck = f(inp["cache_swa_k"])[0].reshape(16, 128, 256)
    cv = f(inp["cache_swa_v"])[0].reshape(16, 128, 256)
    st = f(inp["state_gla"])[0]
    consts = _consts()
    bp, bs = _bias_tables(inp["rel_bias"])
    gains = np.stack([f(inp["ffn1_norm"])[0], f(inp["mix_norm"])[0], f(inp["ffn2_norm"])[0]])
    gains_fm = np.ascontiguousarray(gains.reshape(3, 8, 128).transpose(2, 0, 1).reshape(128, 24))
    fg = np.ascontiguousarray(np.broadcast_to(f(inp["final_norm"])[0][None, :], (128, D)))
    hg = np.ascontiguousarray(f(inp["gla_head_norm"])[0].reshape(2, 128).T)
    qg = np.ascontiguousarray(np.tile(f(inp["q_norm"])[0], 2)[:, None])
    kg = np.ascontiguousarray(np.tile(f(inp["k_norm"])[0], 2)[:, None])
    kgb = np.ascontiguousarray(np.broadcast_to(f(inp["k_norm"])[0][None, :], (128, 64)))
    sinks = f(inp["attn_sinks"])[0]
    sinkT = np.zeros((2, 4, 128), np.float32)
    for P2 in range(2):
        for g in range(4):
            sinkT[P2, g, :64] = sinks[4 * (2 * P2) + g]
            sinkT[P2, g, 64:] = sinks[4 * (2 * P2 + 1) + g]
    shared = {
        "ffn1_up": f(inp["ffn1_w_up"])[0], "ffn1_down": f(inp["ffn1_w_down"])[0],
        "ffn2_up": f(inp["ffn2_w_up"])[0], "ffn2_down": f(inp["ffn2_w_down"])[0],
        "w_in": f(inp["w_in"])[0], "w_branch": f(inp["w_branch"])[0], "w_out": f(inp["w_out"])[0],
        "w_alpha": f(inp["gla_w_alpha"])[0], "b_alpha": f(inp["gla_b_alpha"]).reshape(1, 512),
        "gains_fm": gains_fm, "fgain_bc": fg, "hgain": hg, "qgcol": qg, "kgcol": kg, "kgain_bc": kgb,
        "sinkT": sinkT, "biasp": bp, "biass": bs,
    }
    shared.update(consts)
    maps = []
    zeros_prev = np.zeros((TOKC, D), np.float32)
    for cidx in range(NCORES):
        b, half = cidx // 2, cidx % 2
        m = dict(shared)
        m["xp"] = np.ascontiguousarray(xp[b, half * TOKC:(half + 1) * TOKC])
        m["xprev"] = np.ascontiguousarray(xp[b, 0:TOKC]) if half == 1 else zeros_prev
        xs_ = np.zeros((TS, D), np.float32)
        for s_ in range(2):
            xs_[32 * s_:32 * s_ + 16] = xsm[2 * cidx + s_]
        m["xs"] = xs_
        m["ck"] = np.ascontiguousarray(ck[2 * cidx:2 * cidx + 2])
        m["cv"] = np.ascontiguousarray(cv[2 * cidx:2 * cidx + 2])
        m["st"] = np.ascontiguousarray(st[2 * cidx:2 * cidx + 2])
        m["halo_bias"] = np.full((128, 1), 0.0 if half == 1 else NEG, np.float32)
        maps.append(m)
    return maps


def run(inp, debug=None):
    prog = _get_prog(debug)
    maps = make_in_maps(inp)
    res = run_bass_kernel_spmd(prog.nc, maps, core_ids=list(range(NCORES)))
    return prog, res


def kernel(**inp):
    prog, res = run(inp)
    R = res.results
    y = np.zeros((4, 4096, D), np.float32)
    ys = np.zeros((16, 16, D), np.float32)
    nk = np.zeros((1, 4, 128, 4, 64), np.float32)
    nv = np.zeros((1, 4, 128, 4, 64), np.float32)
    ng = np.zeros((1, 4, 4, 128, 256), np.float32)
    nks = np.zeros((1, 16, 16, 4, 64), np.float32)
    nvs = np.zeros((1, 16, 16, 4, 64), np.float32)
    ngs = np.zeros((1, 16, 4, 128, 256), np.float32)
    for cidx in range(NCORES):
        b, half = cidx // 2, cidx % 2
        r = R[cidx]
        y[b, half * TOKC:(half + 1) * TOKC] = r["y"]
        if half == 1:
            nk[0, b] = r["nk"].reshape(128, 4, 64)
            nv[0, b] = r["nv"].reshape(128, 4, 64)
            ng[0, b] = r["ng"]
        for s_ in range(2):
            ys[2 * cidx + s_] = r["ys"][32 * s_:32 * s_ + 16]
            nks[0, 2 * cidx + s_] = r["nks"][32 * s_:32 * s_ + 16].reshape(16, 4, 64)
            nvs[0, 2 * cidx + s_] = r["nvs"][32 * s_:32 * s_ + 16].reshape(16, 4, 64)
            ngs[0, 2 * cidx + s_] = r["ngs"][s_]
    return (y, ys, nk, nv, ng, nks, nvs, ngs)
```

```python
import numpy as np
import ml_dtypes
import concourse.bass as bass
import concourse.mybir as mybir
from concourse.bass_utils import run_bass_kernel_spmd

F32 = mybir.dt.float32
BF16 = mybir.dt.bfloat16
AF = mybir.ActivationFunctionType
ALU = mybir.AluOpType
AX = mybir.AxisListType

ENGS = ("pe", "act", "dve", "pool", "sp")
NCORES = 8
D = 1024
DFF = 2816
TOKC = 2048
T = 512
NTILE = TOKC // T
TS = 64
EPS = 1e-6
NEG = -30000.0
IN_W = 6672
C_GQ, C_GK, C_GV, C_GR, C_GA, C_SQ, C_SK, C_SV, C_GTA, C_GTB = 0, 512, 1024, 2048, 3072, 3088, 4112, 4368, 4624, 5648


class Op:
    __slots__ = ("eng", "fn", "deps", "sig", "semval", "dma", "dsem", "dval", "idx", "group")


class Sched:
    def __init__(self, nc, n_dma_sems=48):
        self.nc = nc
        self.ops = []
        self.by_eng = {e: [] for e in ENGS}
        self.last_w = {}
        self.readers = {}
        self.esem = {e: nc.alloc_semaphore(name="cnt_" + e) for e in ENGS}
        self.dsems = [nc.alloc_semaphore(name="dma%d" % i) for i in range(n_dma_sems)]
        self.dcount = [0] * n_dma_sems
        self.dlast = [None] * n_dma_sems
        self.dnext = {"sw": 0, "hw": 0}

    def add(self, eng, fn, reads=(), writes=(), dma=False, group=None):
        op = Op()
        op.group = group
        op.eng = eng
        op.fn = fn
        op.sig = False
        op.semval = None
        op.dma = dma
        op.idx = len(self.ops)
        deps = {}

        def dep(d, raw):
            if d is None:
                return
            if not d.dma and not dma and d.eng == eng:
                if eng == "pe" or not raw:
                    return
            deps[d.idx] = d

        for k in reads:
            for w in self.last_w.get(k, ()):
                dep(w, True)
        for k in writes:
            for w in self.last_w.get(k, ()):
                if group is not None and w.group == group:
                    continue
                dep(w, False)
            for r in self.readers.get(k, ()):
                dep(r, False)
        if dma:
            half = len(self.dsems) // 2
            cls = "sw" if eng == "pool" else "hw"
            j = self.dnext[cls] + (half if cls == "sw" else 0)
            self.dnext[cls] = (self.dnext[cls] + 1) % half
            if self.dlast[j] is not None:
                deps[self.dlast[j].idx] = self.dlast[j]
            self.dcount[j] += 16
            op.dsem = j
            op.dval = self.dcount[j]
            self.dlast[j] = op
        for k in reads:
            lst = self.readers.setdefault(k, [])
            if not dma:
                lst[:] = [r for r in lst if r.dma or r.eng != eng]
            lst.append(op)
        for k in writes:
            lw = self.last_w.get(k, [])
            if group is not None and lw and lw[0].group == group:
                lw.append(op)
            else:
                self.last_w[k] = [op]
                self.readers[k] = []
        op.deps = list(deps.values())
        for d in op.deps:
            if not d.dma:
                d.sig = True
        self.ops.append(op)
        self.by_eng[eng].append(op)
        return op

    def emit(self):
        nc = self.nc
        for e in ENGS:
            c = 0
            for op in self.by_eng[e]:
                if not op.dma and op.sig:
                    c += 1
                    op.semval = c
        final_d = list(self.dcount)

        def replay(e, h):
            seen = {}
            for op in self.by_eng[e]:
                need = {}
                for d in op.deps:
                    if d.dma:
                        key, val, sem = ("d", d.dsem), d.dval, self.dsems[d.dsem]
                    else:
                        key, val, sem = ("e", d.eng), d.semval, self.esem[d.eng]
                    if val > need.get(key, (0, None))[0]:
                        need[key] = (val, sem)
                for key, (val, sem) in need.items():
                    if seen.get(key, 0) >= val:
                        continue
                    seen[key] = val
                    h.wait_ge(sem, val)
                ins = op.fn(h)
                if op.dma:
                    ins.then_inc(self.dsems[op.dsem], 16)
                elif op.sig:
                    ins.then_inc(self.esem[e], 1)
            if e == "sp":
                for j, v in enumerate(final_d):
                    if v > 0:
                        h.wait_ge(self.dsems[j], v)

        with nc.Block() as block:
            @block.tensor
            def _(h):
                replay("pe", h)

            @block.scalar
            def _(h):
                replay("act", h)

            @block.vector
            def _(h):
                replay("dve", h)

            @block.gpsimd
            def _(h):
                replay("pool", h)

            @block.sync
            def _(h):
                replay("sp", h)


def _is_dram(ap):
    return "DRam" in type(ap.tensor).__name__


def K(*aps):
    out = []
    for a in aps:
        if a is None or isinstance(a, (int, float)):
            continue
        if _is_dram(a):
            continue
        out.append(a.tensor.name)
    return out


def _t5_bucket(rel):
    nb = 16
    ret = np.where(rel > 0, nb, 0)
    n = np.abs(rel)
    max_exact = nb // 2
    nf = np.maximum(n, 1).astype(np.float32)
    large = max_exact + (np.log(nf / max_exact) / np.float32(np.log(128 / max_exact)) * (nb - max_exact)).astype(np.int32)
    large = np.minimum(large, nb - 1)
    return ret + np.where(n < max_exact, n, large)


def _bias_tables(rel_bias):
    rb = np.asarray(rel_bias, np.float32)
    q = np.arange(128)
    out = np.zeros((2, 128, 4, 4, 128), np.float32)
    for kt in range(2):
        kloc = np.arange(128) + (kt - 1) * 128
        rel = kloc[:, None] - q[None, :]
        bk = _t5_bucket(rel)
        kc = np.floor_divide(kloc, 64)[:, None]
        qc = (q // 64)[None, :]
        valid = (kc <= qc) & (kc >= qc - 2)
        g = rb[bk]
        g = np.where(valid[:, :, None], g, NEG)
        out[kt] = g.reshape(128, 128, 4, 4).transpose(0, 2, 3, 1)
    bp = out.reshape(2, 128, 4, 512)
    outs = np.full((3, 128, 4, 4, 64), NEG, np.float32)
    qpos = 2048 + np.arange(16)
    for s in range(2):
        kpos = 2048 - 128 + np.arange(128)
        bk = _t5_bucket(kpos[:, None] - qpos[None, :])
        g = rb[bk].reshape(128, 16, 4, 4).transpose(0, 2, 3, 1)
        outs[s, :, :, :, 32 * s:32 * s + 16] = g
        bk2 = _t5_bucket(qpos[:, None] - qpos[None, :])
        g2 = rb[bk2].reshape(16, 16, 4, 4).transpose(0, 2, 3, 1)
        outs[2, 32 * s:32 * s + 16, :, :, 32 * s:32 * s + 16] = g2
    bs = outs.reshape(3, 128, 4, 256)
    return np.ascontiguousarray(bp, np.float32), np.ascontiguousarray(bs, np.float32)


def _consts():
    c = {}
    c["ident"] = np.eye(128, dtype=np.float32).astype(ml_dtypes.bfloat16)
    s = np.arange(128)[:, None]
    t = np.arange(128)[None, :]
    c["tri_incl"] = np.where(s <= t, -1.0 / 16, 0.0).astype(np.float32).astype(ml_dtypes.bfloat16)
    c["tri_suf"] = np.where(s > t, -1.0 / 16, 0.0).astype(np.float32).astype(ml_dtypes.bfloat16)
    c["cmask4"] = np.tile(np.where(s <= t, 1.0, 0.0).astype(np.float32), (1, 4))
    c["tri_all"] = np.full((128, 128), -1.0 / 16, np.float32).astype(ml_dtypes.bfloat16)
    s6 = np.arange(64)[:, None]
    t6 = np.arange(64)[None, :]
    same = (s6 // 32) == (t6 // 32)
    real_s = (s6 % 32) < 16
    tis = np.zeros((128, 128), np.float32)
    tis[:64, :64] = np.where(same & (s6 <= t6), -1.0 / 16, 0.0)
    tss = np.zeros((128, 128), np.float32)
    tss[:64, :64] = np.where(same & (s6 > t6) & real_s, -1.0 / 16, 0.0)
    cms = np.zeros((128, 512), np.float32)
    cms[:64, :256] = np.tile(np.where(same & (s6 <= t6), 1.0, 0.0), (1, 4))
    c["tri_incl_s"] = tis.astype(ml_dtypes.bfloat16)
    c["tri_suf_s"] = tss.astype(ml_dtypes.bfloat16)
    c["cmask4_s"] = cms
    oh = np.zeros((2, 128, 128), np.float32)
    oh[0, :, :64] = 1.0
    oh[1, :, 64:] = 1.0
    c["onesH"] = oh.astype(ml_dtypes.bfloat16)
    g4 = np.zeros((4, 512), np.float32)
    g4s = np.zeros((4, 512), np.float32)
    for g in range(4):
        g4[g, g * 128:(g + 1) * 128] = 1.0
        g4s[g, g * 64:(g + 1) * 64] = 1.0
    c["g4"] = g4.astype(ml_dtypes.bfloat16)
    c["g4s"] = g4s.astype(ml_dtypes.bfloat16)
    bo = np.zeros((128, 128), np.float32)
    bo[:64, :64] = 1.0
    bo[64:, 64:] = 1.0
    c["bo_q"] = bo.astype(ml_dtypes.bfloat16)
    c["bo_k"] = (bo / 64).astype(ml_dtypes.bfloat16)
    c["ones_mean"] = np.full((128, 128), 1.0 / 256, np.float32).astype(ml_dtypes.bfloat16)
    c["ones_row"] = np.ones((1, 128), np.float32).astype(ml_dtypes.bfloat16)
    return c


CONST_SPECS = [
    ("ident", [128, 128], BF16), ("tri_incl", [128, 128], BF16), ("tri_suf", [128, 128], BF16),
    ("cmask4", [128, 512], F32), ("tri_all", [128, 128], BF16), ("tri_incl_s", [128, 128], BF16), ("tri_suf_s", [128, 128], BF16),
    ("cmask4_s", [128, 512], F32), ("onesH", [2, 128, 128], BF16), ("g4", [4, 512], BF16), ("g4s", [4, 512], BF16),
    ("bo_q", [128, 128], BF16), ("bo_k", [128, 128], BF16), ("ones_mean", [128, 128], BF16), ("ones_row", [1, 128], BF16),
]


class Prog:
    def __init__(self, debug=None):
        self.nc = nc = bass.Bass("TRN2", target_bir_lowering=False)
        self.s = Sched(nc)
        self.planning = False
        self.plan = []
        self.wscr = None
        self.next_x = None
        self.norm_done = False
        self.x_loaded = False
        self.debug = debug or {}
        self.dbg_outs = []
        self.din = {}
        self.dout = {}
        self._declare_io()
        self._alloc()

    def _in(self, name, shape, dt=F32):
        self.din[name] = self.nc.dram_tensor(name, list(shape), dt, kind="ExternalInput").ap()
        return self.din[name]

    def _out(self, name, shape, dt=F32):
        self.dout[name] = self.nc.dram_tensor(name, list(shape), dt, kind="ExternalOutput").ap()
        return self.dout[name]

    def _declare_io(self):
        i = self._in
        i("xp", [TOKC, D]); i("xprev", [TOKC, D]); i("xs", [TS, D])
        i("ck", [2, 128, 256]); i("cv", [2, 128, 256]); i("st", [2, 4, 128, 256])
        i("ffn1_up", [D, 2 * DFF]); i("ffn1_down", [DFF, D]); i("ffn2_up", [D, 2 * DFF]); i("ffn2_down", [DFF, D])
        i("w_in", [D, IN_W]); i("w_branch", [2048, D]); i("w_out", [D, D])
        i("w_alpha", [16, 512]); i("b_alpha", [1, 512])
        i("gains_fm", [128, 24])
        i("fgain_bc", [128, D])
        i("hgain", [128, 2])
        i("qgcol", [128, 1]); i("kgcol", [128, 1]); i("kgain_bc", [128, 64])
        i("sinkT", [2, 4, 128])
        i("halo_bias", [128, 1])
        i("biasp", [2, 128, 4, 512]); i("biass", [3, 128, 4, 256])
        for n, sh, dt in CONST_SPECS:
            i(n, sh, dt)
        o = self._out
        o("y", [TOKC, D]); o("ys", [TS, D]); o("nk", [128, 256]); o("nv", [128, 256]); o("ng", [4, 128, 256])
        o("nks", [TS, 256]); o("nvs", [TS, 256]); o("ngs", [2, 4, 128, 256])

    def sb(self, name, shape, dt):
        return self.nc.alloc_sbuf_tensor("s_" + name, list(shape), dt)

    def _alloc(self):
        nc = self.nc
        sb = self.sb
        self.c = {}
        for n, sh, dt in CONST_SPECS:
            if n.endswith("_s"):
                continue
            if len(sh) == 3:
                self.c[n] = sb("c_" + n, [sh[1], sh[0], sh[2]], dt)
            else:
                self.c[n] = sb("c_" + n, sh, dt)
        self.gains_fm = sb("gains_fm", [128, 24], F32)
        self.fgain_bc = sb("fgain_bc", [128, D], F32)
        self.hgain = sb("hgain", [128, 2], F32)
        self.qgcol = sb("qgcol", [128, 1], F32)
        self.kgcol = sb("kgcol", [128, 1], F32)
        self.kgain_bc = sb("kgain_bc", [128, 64], F32)
        self.sinkT = sb("sinkT", [4, 2, 128], F32)
        self.esinkT = sb("esinkT", [4, 2, 128], BF16)
        self.halo_bias = sb("halo_bias", [128, 1], F32)
        self.walpha = sb("walpha", [16, 512], BF16)
        self.balpha = sb("balpha", [1, 512], BF16)
        self.biasT = sb("biasT", [128, 2, 4, 512], BF16)
        self.biasTs = self.biasT[:, :, :, :].rearrange("p a j n -> p (a j n)")[:, 0:3072].rearrange("p (a j n) -> p a j n", a=3, j=4)
        self.xt = [sb("xt%d" % i, [128, D], F32) for i in range(4)]
        self.hball = sb("hball", [128, 4, D], BF16)
        self.hb = [self.hball[:, i, :] for i in range(4)]
        self.osT = [self.hball[:, 2 * i:2 * i + 2, :].rearrange("p a (g t) -> p (a g) t", g=2) for i in range(2)]
        self.ss = [sb("ss%d" % i, [128, 1], F32) for i in range(4)]
        self.rstd = [sb("rstd%d" % i, [128, 1], F32) for i in range(4)]
        self.hT = [sb("hT%d" % i, [128, T], BF16) for i in range(8)]
        self.arena0 = sb("arena0", [128, 22 * T], BF16)
        self.actT = [self.arena0[:, i * T:(i + 1) * T] for i in range(22)]
        self.e1T = self.arena0[:, 0:4096].bitcast(F32).rearrange("p (h t) -> p h t", h=4)
        self.e2T = self.arena0[:, 4096:8192].bitcast(F32).rearrange("p (h t) -> p h t", h=4)
        self.qeT = self.arena0[:, 8192:10240].rearrange("p (h t) -> p h t", h=4)
        self.arena1 = sb("arena1", [128, 10240], BF16)
        self.v_tok = [self.arena1[:, i * 1024:(i + 1) * 1024] for i in range(4)]
        self.kd = [self.arena1[:, 4096 + i * 512:4096 + (i + 1) * 512] for i in range(4)]
        self.es = [self.arena1[:, 6144 + i * 1024:6144 + (i + 1) * 1024].bitcast(F32) for i in range(4)]
        self.sigA = self.arena1[:, 0:4096].rearrange("p (c t) -> p c t", c=8)
        self.sigB = self.arena1[:, 4096:8192].rearrange("p (c t) -> p c t", c=8)
        self.keT = sb("keT", [128, 4, T], BF16)
        self.gaT = sb("gaT", [16, T], BF16)
        self.sp = [sb("sp%d" % i, [128, 512], BF16) for i in range(4)]
        self.dtot = sb("dtot", [128, 4, 1], F32)
        self.sgT = sb("sgT", [128, 8, T], BF16)
        self.ATm = sb("ATm", [128, 512], BF16)
        self.ATm2 = [self.ATm, sb("ATm1", [128, 512], BF16)]
        self.Sbf_alt = sb("Sbf_alt", [128, 4, 256], BF16)
        self.o_raw = sb("o_raw", [128, 8, 128], F32)
        self.osq = sb("osq", [128, 8, 128], BF16)
        self.rstdg = sb("rstdg", [128, 512], F32)
        self.o_raw2 = [self.o_raw, sb("o_raw1", [128, 8, 128], F32)]
        self.osq2 = [self.osq, sb("osq1", [128, 8, 128], BF16)]
        self.S = [sb("S0", [128, 4, 256], F32), self.xt[1][:, :].rearrange("p (h v) -> p h v", h=4)]
        self.Sbf = [sb("Sbf0", [128, 4, 256], BF16), self.xt[2][:, 0:512].bitcast(BF16).rearrange("p (h v) -> p h v", h=4)]
        self.sqr = [sb("sqr%d" % i, [128, T], F32) for i in range(1)]
        self.sqsq = [sb("sqsq%d" % i, [128, T], BF16) for i in range(1)]
        self.rsq = [sb("rsq%d" % i, [128, T], F32) for i in range(1)]
        self.qo = [sb("qo%d" % i, [128, 4 * T], BF16) for i in range(2)]
        self.knT_cur = sb("knT_cur", [128, 2, T], BF16)
        self.knT_prev = sb("knT_prev", [128, 2, 128], BF16)
        self.vz_cur = [sb("vz_cur%d" % i, [128, 512], BF16) for i in range(4)]
        self.vz_prev = sb("vz_prev", [128, 512], BF16)
        self.PT = [sb("PT%d" % i, [128, 512], BF16) for i in range(6)]
        self.rden = sb("rden", [128, 512], F32)
        self.ksq = self.sqr[0][:, 0:256]
        self.ktok = self.sqr[0][:, 256:512]
        self.kss = sb("kss", [128, 4], F32)
        self.knew = self.rsq[0][:, 0:256]
        self.vnew = self.rsq[0][:, 256:512]
        self.mtmp = [sb("mtmp%d" % i, [128, T], F32) for i in range(2)]
        self.sa = self.mtmp
        self.junk = self.mtmp[0][:, :].bitcast(BF16)
        self.ckf = sb("ckf", [128, 256], F32)
        self.ckb = sb("ckb", [128, 256], BF16)
        self.knT_c = [sb("knT_c%d" % i, [128, 2, 128], BF16) for i in range(2)]
        self.vz_c = [sb("vz_c%d" % i, [128, 512], BF16) for i in range(2)]
        self.NSLOT = 4
        self.slots = [sb("wslot%d" % i, [128, 4096], BF16) for i in range(self.NSLOT)]
        self.psf = [nc.alloc_psum_tensor("psf%d" % i, [128, 512], F32) for i in range(6)]
        self.psb = [nc.alloc_psum_tensor("psb%d" % i, [128, 1024], BF16) for i in range(2)]

    def reset_counters(self):
        self._ng_done = False
        self._early_done = False
        self.norm_done = False
        self.x_loaded = False
        self.next_x = None
        self._pf = 0
        self._pb = 0
        self._pt = 0
        self._piece = 0
        self._rr = 0

    def pf(self):
        self._pf += 1
        return self.psf[self._pf % 6]

    def pb(self):
        self._pb += 1
        return self.psb[self._pb % 2]

    def add(self, eng, fn, reads, writes, dma=False, group=None):
        if self.planning:
            return
        self.s.add(eng, fn, reads, writes, dma, group)

    def mm(self, out, lhsT, rhs, start=True, stop=True, rkeys=None):
        self.add("pe", lambda h: h.matmul(out, lhsT=lhsT, rhs=rhs, start=start, stop=stop), K(lhsT, rhs) if rkeys is None else rkeys, K(out))

    def tr(self, out, in_, ident, rkeys=None):
        self.add("pe", lambda h: h.transpose(out, in_, ident), K(in_, ident) if rkeys is None else rkeys, K(out))

    def act(self, out, in_, func, bias=None, scale=None, accum_out=None):
        kw = {}
        if bias is not None:
            kw["bias"] = bias
        if scale is not None:
            kw["scale"] = scale
        if accum_out is not None:
            kw["accum_out"] = accum_out
        self.add("act", lambda h: h.activation(out=out, in_=in_, func=func, **kw), K(in_, bias, scale), K(out, accum_out))

    def rsqrt(self, out, in_, scale, eps):
        self.act(out, in_, AF.Ln, bias=eps, scale=scale)
        self.act(out, out, AF.Exp, scale=-0.5)

    def amul(self, out, in_, mul):
        self.add("act", lambda h: h.mul(out=out, in_=in_, mul=mul), K(in_, mul), K(out))

    def cp(self, eng, out, in_):
        if eng == "act":
            self.add("act", lambda h: h.copy(out=out, in_=in_), K(in_), K(out))
        else:
            self.add(eng, lambda h: h.tensor_copy(out=out, in_=in_), K(in_), K(out))

    def tt(self, eng, out, in0, in1, op, rkeys=None, wkeys=None):
        self.add(eng, lambda h: h.tensor_tensor(out=out, in0=in0, in1=in1, op=op), K(in0, in1) if rkeys is None else rkeys, K(out) if wkeys is None else wkeys)

    def ts(self, eng, out, in0, s1, s2, op0, op1=None, rkeys=None, wkeys=None):
        if rkeys is not None:
            self.add(eng, lambda h: h.tensor_scalar(out=out, in0=in0, scalar1=s1, scalar2=0.0, op0=op0, op1=ALU.add), rkeys, wkeys)
            return
        if op1 is None:
            if op0 == ALU.pow:
                self.add(eng, lambda h: h.tensor_scalar(out=out, in0=in0, scalar1=0.0, scalar2=s1, op0=ALU.add, op1=ALU.pow), K(in0, s1), K(out))
            else:
                self.add(eng, lambda h: h.tensor_scalar(out=out, in0=in0, scalar1=s1, scalar2=0.0, op0=op0, op1=ALU.add), K(in0, s1), K(out))
        else:
            self.add(eng, lambda h: h.tensor_scalar(out=out, in0=in0, scalar1=s1, scalar2=s2, op0=op0, op1=op1), K(in0, s1, s2), K(out))

    def stt(self, eng, out, in0, scalar, in1, op0, op1):
        self.add(eng, lambda h: h.scalar_tensor_tensor(out=out, in0=in0, scalar=scalar, in1=in1, op0=op0, op1=op1),
                 K(in0, scalar, in1), K(out))

    def memset(self, eng, ap, val):
        self.add(eng, lambda h: h.memset(ap, val), [], K(ap))

    def dma(self, eng, out, in_, reads=(), writes=(), group=None):
        self.add(eng, lambda h: h.dma_start(out=out, in_=in_), K(in_) + list(reads), K(out) + list(writes), dma=True, group=group)

    def wget(self, tag, spec, wid, nel=4096):
        if self.planning:
            self.plan.append((tag, spec, wid, nel))
            return self.slots[0]
        if self._piece == 0:
            self.first = {}
            for j, pl in enumerate(self.plan):
                self.first.setdefault(pl[2], j)
            self.widx = {w: k for k, w in enumerate(self.first)}
            if self.wscr is None:
                self.wscr = self.nc.dram_tensor("wscr", [len(self.widx), 128, 4096], BF16).ap()
        i = self._piece
        assert self.plan[i][0] == tag, (self.plan[i][0], tag)
        LA = self.NSLOT - 2
        if i == 0:
            for j in range(min(LA, len(self.plan))):
                self._issue_piece(j)
        slot = self.slots[i % self.NSLOT]
        if self.first[wid] == i and self.nuse[wid] > 1:
            self.dma("pool", self.wscr[self.widx[wid], :, :nel], slot[:, :nel], writes=[("scr", wid)])
        if i + LA < len(self.plan):
            self._issue_piece(i + LA)
        self._piece += 1
        return slot

    def _issue_piece(self, j):
        slot = self.slots[j % self.NSLOT]
        tag, spec, wid, nel = self.plan[j]
        if self.first[wid] == j:
            for dst, src in spec(slot):
                self.dma("pool", dst, src, group=("piece", j))
        else:
            self.dma("sp", slot[:, :nel], self.wscr[self.widx[wid], :, :nel], reads=[("scr", wid)])

    def dbg(self, name, ap, shape, dt=F32):
        if name not in self.debug:
            return
        if self.planning:
            return
        d = self.nc.dram_tensor("dbg_" + name, list(shape), dt, kind="ExternalOutput").ap()
        self.dbg_outs.append("dbg_" + name)
        self.dma("sp", d, ap)

    def setup(self, first_x=None):
        c = self.c
        di = self.din
        self.dma("sp", c["ident"][:], di["ident"])
        self.dma("sp", self.gains_fm[:], di["gains_fm"])
        if first_x is not None:
            self.load_x(first_x, [(j * 128, 128) for j in range(4)])
        for n, sh, dt in CONST_SPECS:
            if n == "ident":
                continue
            if n.endswith("_s"):
                continue
            if len(sh) == 3:
                self.dma("sp", c[n][:], di[n].rearrange("a p f -> p a f"))
            else:
                self.dma("sp", c[n][:], di[n])
        for n in ("fgain_bc", "hgain", "qgcol", "kgcol", "kgain_bc", "halo_bias"):
            self.dma("sp", getattr(self, n)[:], di[n])
        self.dma("sp", self.sinkT[:], di["sinkT"].rearrange("a g m -> g a m"))
        self.dma("pool", self.walpha[:], di["w_alpha"])
        self.dma("pool", self.balpha[:], di["b_alpha"])
        self.dma("pool", self.biasT[:, 0:2], di["biasp"].rearrange("a p j n -> p a j n"))
        self.act(self.esinkT[:], self.sinkT[:], AF.Exp)
        for t_ in self.vz_cur + [self.vz_prev] + self.vz_c:
            self.memset("pool", t_[:], 0.0)
        self.memset("pool", self.knT_prev[:], 0.0)
        self.memset("dve", self.S[0][:], 0.0)
        self.memset("dve", self.Sbf[0][:], 0.0)
        for t_ in self.ss:
            self.memset("dve", t_[:], 0.0)
        if self.debug.get("delay"):
            self.memset("pool", self.slots[0][:], 0.0)
            for _ in range(int(self.debug["delay"])):
                self.cp("pool", self.slots[1][:], self.slots[0][:])

    def load_x(self, src, tts):
        for i, (o, P) in enumerate(tts):
            self.dma("sp", self.xt[i][:P, :], src[o:o + P, :])

    def norm_hT(self, tts, Tn, gcol0, src=None):
        self.norm_stats(tts, src)
        self.norm_tr(tts, Tn, gcol0, subtile_major=True)

    def norm_stats(self, tts, src=None):
        xt, hb = (self.xt if src is None else src), self.hb
        for i, (o, P) in enumerate(tts):
            self.act(self.junk[:P, :], xt[i][:P, :], AF.Square, accum_out=self.ss[i][:P, 0:1])
            self.rsqrt(self.rstd[i][:P, 0:1], self.ss[i][:P, 0:1], 1.0 / D, EPS)
            self.ts("pool" if i % 2 else "dve", hb[i][:P, :], xt[i][:P, :], self.rstd[i][:P, 0:1], None, ALU.mult,
                    rkeys=K(xt[i][:], self.rstd[i][:]) + ["s_hball"], wkeys=[("hb", i)])

    def norm_tr(self, tts, Tn, gcol0, subtile_major=False):
        hb = self.hb
        ident = self.c["ident"]
        if subtile_major and len(tts) > 1:
            regs = []
            b0, b1 = self.pb(), self.pb()
            f0, f1 = self.pf()[:, :].bitcast(BF16), self.pf()[:, :].bitcast(BF16)
            for bank in (b0, b1, f0, f1):
                regs += [bank[:, 0:512], bank[:, 512:1024]]
            for i, (o, P) in enumerate(tts):
                for cc in range(8):
                    self.tr(regs[cc][:, o:o + P], hb[i][:P, cc * 128:(cc + 1) * 128], ident[:P, :P],
                            rkeys=K(ident[:]) + [("hb", i), "s_hball"])
            for cc in (0, 2, 1, 3, 4, 6, 5, 7):
                g = self.gains_fm[:, gcol0 + cc:gcol0 + cc + 1]
                if (cc // 2) % 2 == 0:
                    self.amul(self.hT[cc][:, :Tn], regs[cc][:, :Tn], g)
                else:
                    self.ts("dve", self.hT[cc][:, :Tn], regs[cc][:, :Tn], g, None, ALU.mult)
            return
        for cc in range(8):
            pT = self.pb()
            for i, (o, P) in enumerate(tts):
                self.tr(pT[:, o:o + P], hb[i][:P, cc * 128:(cc + 1) * 128], ident[:P, :P],
                        rkeys=K(ident[:]) + [("hb", i), "s_hball"])
            g = self.gains_fm[:, gcol0 + cc:gcol0 + cc + 1]
            if cc % 2 == 0:
                self.amul(self.hT[cc][:, :Tn], pT[:, :Tn], g)
            else:
                self.ts("dve", self.hT[cc][:, :Tn], pT[:, :Tn], g, None, ALU.mult)

    def ffn(self, tts, Tn, wup, wdn, gcol0, tagp, mid_hook=None, mid_hook2=None):
        if self.norm_done:
            self.norm_done = False
        else:
            self.norm_hT(tts, Tn, gcol0)
        hT = self.hT
        upv = wup.rearrange("(k p) (ab c) -> p k ab c", p=128, ab=2)
        for g in range(11):
            def spec(slot, g=g):
                sv = slot[:, :].rearrange("p (k ab c) -> p k ab c", k=8, ab=2)
                return [(sv[:, :, ab, :], upv[:, :, ab, g * 256:(g + 1) * 256]) for ab in range(2)]
            slot = self.wget((tagp, "up", g), spec, (tagp[-1], "up", g))
            wv = slot[:, :].rearrange("p (k ab c) -> p k ab c", k=8, ab=2)
            for u in range(2):
                i = 2 * g + u
                pa = self.pf()
                pb_ = self.pf()
                for k in range(8):
                    self.mm(pa[:, :Tn], wv[:, k, 0, u * 128:(u + 1) * 128], hT[k][:, :Tn], k == 0, k == 7)
                for k in range(8):
                    self.mm(pb_[:, :Tn], wv[:, k, 1, u * 128:(u + 1) * 128], hT[k][:, :Tn], k == 0, k == 7)
                sa = self.sa[i % 2]
                self.act(sa[:, :Tn], pa[:, :Tn], AF.Silu)
                self.tt("dve", self.actT[i][:, :Tn], sa[:, :Tn], pb_[:, :Tn], ALU.mult,
                        rkeys=K(sa[:], pb_[:]) + ["s_arena0"], wkeys=[("actT", i)])
        if mid_hook is not None:
            mid_hook()
        dnv = wdn.rearrange("(i p) n -> p i n", p=128)
        for nh in range(2):
            accs = [self.pf() for _ in tts]
            for cg, (c0, c1) in enumerate(((0, 8), (8, 16), (16, 22))):
                def spec(slot, nh=nh, c0=c0, c1=c1):
                    return [(slot[:, :(c1 - c0) * 512].rearrange("p (c n) -> p c n", n=512), dnv[:, c0:c1, nh * 512:(nh + 1) * 512])]
                slot = self.wget((tagp, "down", nh, cg), spec, (tagp[-1], "down", nh, cg), (c1 - c0) * 512)
                wv = slot[:, :(c1 - c0) * 512].rearrange("p (c n) -> p c n", n=512)
                for i, (o, P) in enumerate(tts):
                    for cc in range(c0, c1):
                        self.mm(accs[i][:P, :], self.actT[cc][:, o:o + P], wv[:, cc - c0, :], cc == 0, cc == 21,
                                rkeys=K(wv[:, 0, :]) + ["s_arena0", ("actT", cc)])
            for i, (o, P) in enumerate(tts):
                xs_ = self.xt[i][:P, nh * 512:(nh + 1) * 512]
                self.stt("dve", xs_, accs[i][:P, :], 0.5, xs_, ALU.mult, ALU.add)
            if nh == 0 and mid_hook2 is not None:
                mid_hook2()

    def win_piece(self, tag, c0, w):
        wv_ = self.din["w_in"].rearrange("(k p) c -> p k c", p=128)

        def spec(slot):
            return [(slot[:, :8 * w].rearrange("p (k c) -> p k c", k=8), wv_[:, :, c0:c0 + w])]
        slot = self.wget(tag, spec, tag[1:], 8 * w)
        return slot[:, :8 * w].rearrange("p (k c) -> p k c", k=8)

    def fm_proj(self, wv, col0, Tn, lhs_view=None):
        p = self.pf()
        for k in range(8):
            lhs = wv[:, k, col0:col0 + 128] if lhs_view is None else lhs_view(k)
            self.mm(p[:, :Tn], lhs, self.hT[k][:, :Tn], k == 0, k == 7)
        return p

    def tm_proj(self, wv, c0, w, o, P):
        p = self.pf()
        for k in range(8):
            self.mm(p[:P, :w], self.hT[k][:, o:o + P], wv[:, k, c0:c0 + w], k == 0, k == 7)
        return p

    def qknorm(self, ps, out, gcol, bo, eps, Tn, idx, ntt=None):
        sqsq, sqr, rs = self.sqsq[0], self.sqr[0], self.rsq[0]
        self.act(sqsq[:, :Tn], ps, AF.Square)
        pm = self.pf()
        self.mm(pm[:, :Tn], bo[:, :], sqsq[:, :Tn])
        self.rsqrt(rs[:, :Tn], pm[:, :Tn], 1.0, eps)
        if ntt is None:
            self.stt("dve", out, ps, gcol, rs[:, :Tn], ALU.mult, ALU.mult)
        else:
            self.stt("dve", out, ps.rearrange("p (t q) -> p t q", t=ntt), gcol, rs[:, :Tn].rearrange("p (t q) -> p t q", t=ntt), ALU.mult, ALU.mult)

    def mixer(self, tts, Tn, mode, tagp, seqs, tile_idx, last_tile):
        c = self.c
        sample = mode == "sample"
        L = tts[0][1]
        tri_i = c["tri_incl"]
        tri_s = c["tri_suf"]
        cmask = c["cmask4"]
        self.norm_hT(tts, Tn, 8)
        if mode == "pre" and self.next_x is not None:
            self.load_x(self.next_x, tts)
        hT = self.hT
        e1T, e2T, qeT, keT = self.e1T, self.e2T, self.qeT, self.keT
        wv = self.win_piece((tagp, "ga"), C_GA, 16)
        p = self.pf()
        for k in range(8):
            self.mm(p[:16, :Tn], wv[:, k, 0:16], hT[k][:, :Tn], k == 0, k == 7)
        self.cp("act", self.gaT[:, :Tn], p[:16, :Tn])
        wk = self.win_piece((tagp, "gk"), C_GK, 512)
        n_t = len(tts)
        pzs = []
        for i, (o, P) in enumerate(tts):
            pz = self.pf()
            self.mm(pz[:P, :], self.gaT[:, o:o + P], self.walpha[:, :], True, False)
            self.mm(pz[:P, :], c["ones_row"][0:1, :P], self.balpha[0:1, :], False, True)
            pzs.append(pz)
        for i, (o, P) in enumerate(tts):
            self.act(self.es[i][:P, :], pzs[i][:P, :], AF.Exp, scale=-1.0)
        for i, (o, P) in enumerate(tts):
            self.act(self.sp[i][:P, :], self.es[i][:P, :], AF.Ln, bias=1.0)
        for i, (o, P) in enumerate(tts):
            pk = self.tm_proj(wk, 0, 512, o, P)
            self.cp("dve", self.kd[i][:P, :], pk[:P, :])
        if mode != "pre":
            for hd in range(4):
                p = self.fm_proj(wk, hd * 128, Tn)
                self.cp("dve", keT[:, hd, :Tn], p[:, :Tn])
            wq = self.win_piece((tagp, "gq"), C_GQ, 512)
            for hd in range(4):
                p = self.fm_proj(wq, hd * 128, Tn)
                self.cp("dve", qeT[:, hd, :Tn], p[:, :Tn])
        pbts = []
        for i, (o, P) in enumerate(tts):
            pbt = self.pf()
            for hd in range(4):
                self.mm(pbt[:, hd * P:(hd + 1) * P], self.sp[i][:P, hd * 128:(hd + 1) * 128], tri_i[:P, :P])
            pbts.append(pbt)
        for i, (o, P) in enumerate(tts):
            pv3 = pbts[i][:, :4 * P].rearrange("p (h t) -> p h t", h=4)
            self.act(e1T[:, :, o:o + P], pv3, AF.Exp)
            if mode != "pre":
                self.act(e2T[:, :, o:o + P], pv3, AF.Exp, scale=-1.0)
        for pc in range(2):
            wvv = self.win_piece((tagp, "gv", pc), C_GV + pc * 512, 512)
            for i, (o, P) in enumerate(tts):
                p = self.tm_proj(wvv, 0, 512, o, P)
                self.cp("dve", self.v_tok[i][:P, pc * 512:(pc + 1) * 512], p[:P, :])
        psufs = []
        for i, (o, P) in enumerate(tts):
            psuf = self.pf()
            whole = (mode == "pre")
            self.mm(psuf[:P, :], tri_s[:P, :P], self.sp[i][:P, :], True, not (whole and i < n_t - 1))
            if whole:
                for j in range(i + 1, n_t):
                    self.mm(psuf[:P, :], c["tri_all"][:P, :P], self.sp[j][:P, :], False, j == n_t - 1)
            psufs.append(psuf)
        for i, (o, P) in enumerate(tts):
            self.act(self.es[i][:P, :], psufs[i][:P, :], AF.Exp)
        if mode != "pre":
            for pc in range(2):
                wg = self.win_piece((tagp, "gr", pc), C_GR + pc * 512, 512)
                for u in range(4):
                    p = self.fm_proj(wg, u * 128, Tn)
                    self.act(self.sgT[:, pc * 4 + u, :Tn], p[:, :Tn], AF.Silu)
        for i, (o, P) in enumerate(tts):
            self.tt("dve", self.kd[i][:P, :], self.kd[i][:P, :], self.es[i][:P, :], ALU.mult)
        if mode != "pre":
            for hd in range(4):
                self.tt("dve", keT[:, hd, :Tn], keT[:, hd, :Tn], e2T[:, hd, :Tn], ALU.mult)
            for hd in range(4):
                self.stt("dve", qeT[:, hd, :Tn], qeT[:, hd, :Tn], 128 ** -0.5, e1T[:, hd, :Tn], ALU.mult, ALU.mult)
        mstop = self.debug.get("mstop") if sample else None
        gla_last_tail = None
        if mstop == "gpipe":
            return
        if mode == "pre":
            S, Sbf = seqs[0][2], seqs[0][3]
            pS = [self.pf(), self.pf()]
            for hd in range(4):
                for i, (o, P) in enumerate(tts):
                    self.mm(pS[hd // 2][:, (hd % 2) * 256:(hd % 2 + 1) * 256], self.kd[i][:P, hd * 128:(hd + 1) * 128],
                            self.v_tok[i][:P, hd * 256:(hd + 1) * 256], i == 0, i == n_t - 1)
            self.cp("dve", self.dtot[:, :, :], e1T[:, :, 127:128])
            for i in range(1, n_t):
                self.tt("dve", self.dtot[:, :, :], self.dtot[:, :, :], e1T[:, :, i * 128 + 127:i * 128 + 128], ALU.mult)
            for hd in range(4):
                self.stt("dve", S[:, hd, :], S[:, hd, :], self.dtot[:, hd, :], pS[hd // 2][:, (hd % 2) * 256:(hd % 2 + 1) * 256], ALU.mult, ALU.add)
            self.cp("act", Sbf[:, :, :], S[:, :, :])
        elif len(seqs) == 1:
            gla_last_tail = self.gla_pipelined(tts, seqs[0], cmask)
        else:
            for i, (o, P) in enumerate(tts):
                self.gla_chunk(i, o, P, mode, seqs, cmask)
        if mstop == "gla":
            return
        if mode == "full" and tile_idx == 0:
            self.dbg("oaT", self.sgT[:, :, :], [128, 8, T], BF16)
            self.dbg("qeT", self.qeT, [128, 4, T], BF16)
            self.dbg("keT", self.keT[:, :, :], [128, 4, T], BF16)
            self.dbg("e1T", self.e1T, [128, 4, T], F32)
            self.dbg("kd3", self.kd[3], [128, 512], BF16)
            self.dbg("vtok3", self.v_tok[3], [128, 1024], BF16)
            self.dbg("S", self.S[0][:, :, :], [128, 4, 256], F32)
        need_kv = (mode != "pre") or last_tile
        if mode != "pre":
            ntt = len(tts)
            w_in_v = self.din["w_in"].rearrange("(k p) c -> p k c", p=128)
            for pc in range(2):
                def spec(slot, pc=pc):
                    sv = slot[:, :].rearrange("p (k g two d) -> p k g two d", k=8, g=4, two=2)
                    return [(sv[:, k, :, two, :], w_in_v[:, k, C_SQ + pc * 512 + two * 256:C_SQ + pc * 512 + (two + 1) * 256].rearrange("p (g d) -> p g d", g=4))
                            for two in range(2) for k in range(8)]
                wq = self.wget((tagp, "sq", pc), spec, ("sq", pc))[:, :].rearrange("p (k c) -> p k c", k=8)
                qv = self.qo[pc][:, :ntt * 4 * L].rearrange("p (t g q) -> p t g q", t=ntt, g=4)
                for g in range(4):
                    p = self.fm_proj(wq, g * 128, Tn)
                    if gla_last_tail is not None and g == 1:
                        gla_last_tail()
                        gla_last_tail = None
                    self.qknorm(p[:, :Tn], qv[:, :, g, :], self.qgcol[:, 0:1], c["bo_q"], 64 * EPS, Tn, g, ntt)
        if need_kv:
            wkv = self.win_piece((tagp, "sksv"), C_SK, 512)
            for P2 in range(2):
                p = self.fm_proj(wkv, P2 * 128, Tn)
                self.qknorm(p[:, :Tn], self.knT_cur[:, P2, :Tn], self.kgcol[:, 0:1], c["bo_k"], EPS, Tn, P2)
            for i, (o, P) in enumerate(tts):
                pvv = self.tm_proj(wkv, 256, 256, o, P)
                vz4 = self.vz_cur[i][:, :].rearrange("p (jp par c) -> p jp par c", jp=2, par=2)
                pv4 = pvv[:, :256].rearrange("p (jp par d) -> p jp par d", jp=2, par=2)
                for par in range(2):
                    self.cp("act", vz4[:P, :, par, par * 64:(par + 1) * 64], pv4[:P, :, par, :])
                want_out = sample or (mode == "full" and last_tile and i == len(tts) - 1)
                if want_out:
                    self.cp("act", self.vnew[:P, :], pvv[:P, :256])
                    pkk = self.tm_proj(wkv, 0, 256, o, P)
                    self.act(self.ksq[:P, :], pkk[:P, :256], AF.Square)
                    self.add("dve", lambda h, P=P: h.tensor_reduce(out=self.kss[:P, 0:4], in_=self.ksq[:P, :].rearrange("p (j d) -> p j d", j=4),
                                                                    axis=AX.X, op=ALU.add), K(self.ksq[:]), K(self.kss[:]))
                    self.rsqrt(self.kss[:P, :], self.kss[:P, :], 1.0 / 64, EPS)
                    k3 = self.ktok[:P, :].rearrange("p (j d) -> p j d", j=4)
                    self.tt("dve", k3, pkk[:P, :256].rearrange("p (j d) -> p j d", j=4),
                            self.kss[:P, 0:4].unsqueeze(2).to_broadcast([P, 4, 64]), ALU.mult)
                    self.tt("dve", self.knew[:P, :].rearrange("p (j d) -> p j d", j=4), k3,
                            self.kgain_bc[:P, :].unsqueeze(1).to_broadcast([P, 4, 64]), ALU.mult)
                    if sample:
                        self.dma("sp", self.dout["nks"], self.knew[:P, :])
                        self.dma("sp", self.dout["nvs"], self.vnew[:P, :])
                    else:
                        self.dma("sp", self.dout["nk"], self.knew[:P, :])
                        self.dma("sp", self.dout["nv"], self.vnew[:P, :])
        if mode == "pre":
            if last_tile:
                self.cp("pool", self.knT_prev[:, :, :], self.knT_cur[:, :, Tn - 128:Tn])
                self.cp("pool", self.vz_prev[:, :], self.vz_cur[len(tts) - 1][:, :])
            if self.next_x is not None:
                self.norm_hT(tts, Tn, 0)
                self.norm_done = True
            return
        if mstop == "swaproj":
            return
        gate_groups = []
        gstate = {}
        for nm, c0, dst in (("gta", C_GTA, self.sigA), ("gtb", C_GTB, self.sigB)):
            for pc in range(2):
                for u in range(4):
                    def grp(nm=nm, c0=c0, dst=dst, pc=pc, u=u):
                        if u == 0:
                            gstate["w"] = self.win_piece((tagp, nm, pc), c0 + pc * 512, 512)
                        p = self.fm_proj(gstate["w"], u * 128, Tn)
                        self.act(dst[:, pc * 4 + u, :Tn], p[:, :Tn], AF.Tanh, scale=0.5)
                    gate_groups.append(grp)

        def run_gates(k):
            for _ in range(min(k, len(gate_groups))):
                gate_groups.pop(0)()
        if sample:
            kts = []
            for sq_ in range(2):
                kts.append(dict(knT=lambda P2, rows, sq_=sq_: self.knT_c[sq_][rows, P2, :], vz=self.vz_c[sq_], bias=self.biasTs[:, sq_, :, :], nk=128, hb=None))
            kts.append(dict(knT=lambda P2, rows: self.knT_cur[rows, P2, 0:64], vz=self.vz_cur[0], bias=self.biasTs[:, 2, :, :], nk=64, hb=None))
            self.swa_block(0, 0, 1, 64, kts, c["g4s"], between=lambda: run_gates(2))
        else:
            for i, (o, P) in enumerate(tts):
                kts = []
                if i == 0:
                    hb = self.halo_bias[:, 0:1] if tile_idx == 0 else None
                    kts.append(dict(knT=lambda P2, rows: self.knT_prev[rows, P2, :], vz=self.vz_prev, bias=self.biasT[:, 0, :, :], nk=128, hb=hb))
                else:
                    kts.append(dict(knT=lambda P2, rows, o=o: self.knT_cur[rows, P2, o - 128:o], vz=self.vz_cur[i - 1], bias=self.biasT[:, 0, :, :], nk=128, hb=None))
                kts.append(dict(knT=lambda P2, rows, o=o: self.knT_cur[rows, P2, o:o + 128], vz=self.vz_cur[i], bias=self.biasT[:, 1, :, :], nk=128, hb=None))
                self.swa_block(i, o, len(tts), 128, kts, c["g4"], between=lambda: run_gates(2))
            self.cp("pool", self.knT_prev[:, :, :], self.knT_cur[:, :, Tn - 128:Tn])
            self.cp("pool", self.vz_prev[:, :], self.vz_cur[len(tts) - 1][:, :])
        if mode == "full" and tile_idx == 0:
            self.dbg("osT0", self.osT[0], [128, 4, T], BF16)
            self.dbg("osT1", self.osT[1], [128, 4, T], BF16)
            self.dbg("knT", self.knT_cur[:, :, :], [128, 2, T], BF16)
        if mstop == "swa":
            return
        run_gates(len(gate_groups))
        if mode == "full" and tile_idx == 0:
            self.dbg("sgA", self.sigA, [128, 8, T], BF16)
            self.dbg("sgB", self.sigB, [128, 8, T], BF16)
        wbr = self.din["w_branch"]
        wbA = wbr[0:1024, :].rearrange("(f p) n -> p f n", p=128)
        wbB = wbr[1024:2048, :].rearrange("(P j g d) n -> d P j g n", P=2, j=2, g=4)
        for ng in range(2):
            def specA(slot, ng=ng):
                return [(slot[:, :].rearrange("p (f n) -> p f n", f=8), wbA[:, :, ng * 512:(ng + 1) * 512])]

            def specB(slot, ng=ng):
                return [(slot[half * 64:(half + 1) * 64, :].rearrange("p (P g n) -> p P g n", P=2, g=4)[:, P2],
                         wbB[:, P2, half, :, ng * 512:(ng + 1) * 512]) for half in range(2) for P2 in range(2)]
            sA = self.wget((tagp, "wbA", ng), specA, ("wbA", ng))[:, :].rearrange("p (f n) -> p f n", f=8)
            sB = self.wget((tagp, "wbB", ng), specB, ("wbB", ng))[:, :].rearrange("p (f n) -> p f n", f=8)
            for u in range(4):
                n = ng * 4 + u
                pa = self.pf()
                for f in range(8):
                    self.mm(pa[:, :Tn], sA[:, f, u * 128:(u + 1) * 128], self.sgT[:, f, :Tn], f == 0, f == 7)
                pb_ = self.pf()
                for f in range(8):
                    self.mm(pb_[:, :Tn], sB[:, f, u * 128:(u + 1) * 128], self.osT[f // 4][:, f % 4, :Tn], f == 0, f == 7)
                t0, t1 = self.mtmp
                self.stt("dve", t0[:, :Tn], self.sigA[:, n, :Tn], 1.0, pa[:, :Tn], ALU.add, ALU.mult)
                self.stt("dve", t1[:, :Tn], self.sigB[:, n, :Tn], 1.0, pb_[:, :Tn], ALU.add, ALU.mult)
                self.tt("pool", self.sigA[:, n, :Tn], t0[:, :Tn], t1[:, :Tn], ALU.add)
        if mode == "full" and tile_idx == 0:
            self.dbg("mT", self.sigA, [128, 8, T], BF16)
        wo = self.din["w_out"].rearrange("(f p) n -> p f n", p=128)
        for nh in range(2):
            def spec(slot, nh=nh):
                return [(slot[:, :].rearrange("p (f n) -> p f n", f=8), wo[:, :, nh * 512:(nh + 1) * 512])]
            so = self.wget((tagp, "wo", nh), spec, ("wo", nh))[:, :].rearrange("p (f n) -> p f n", f=8)
            for i, (o, P) in enumerate(tts):
                p = self.pf()
                for f in range(8):
                    self.mm(p[:P, :], self.sigA[:, f, o:o + P], so[:, f, :], f == 0, f == 7)
                xs_ = self.xt[i][:P, nh * 512:(nh + 1) * 512]
                self.stt("dve", xs_, p[:P, :], 0.5, xs_, ALU.mult, ALU.add)

    def gla_pipelined(self, tts, seq, cmask):
        c = self.c
        e1T, qeT, keT = self.e1T, self.qeT, self.keT
        so, nreal, S, Sbf0 = seq
        Sb = [Sbf0, self.Sbf_alt]
        L = tts[0][1]
        n_t = len(tts)
        assert n_t % 2 == 0

        def front(i, o):
            pS = [self.pf(), self.pf()]
            for hd in range(4):
                self.mm(pS[hd // 2][:, (hd % 2) * 256:(hd % 2 + 1) * 256], self.kd[i][:L, hd * 128:(hd + 1) * 128],
                        self.v_tok[i][:L, hd * 256:(hd + 1) * 256])
            pA = self.pf()
            for hd in range(4):
                self.mm(pA[:L, hd * L:(hd + 1) * L], keT[:, hd, o:o + L], qeT[:, hd, o:o + L])
            atm = self.ATm2[i % 2]
            self.tt("dve", atm[:L, :4 * L], pA[:L, :4 * L], cmask[:L, :4 * L], ALU.mult)
            dcol = o + L - 1
            for hd in range(4):
                self.stt("dve", S[:, hd, :], S[:, hd, :], e1T[:, hd, dcol:dcol + 1], pS[hd // 2][:, (hd % 2) * 256:(hd % 2 + 1) * 256], ALU.mult, ALU.add)
            self.cp("act", Sb[(i + 1) % 2][:, :, :], S[:, :, :])
            return atm

        def tail(j):
            oj = tts[j][0]
            o_raw, osq = self.o_raw2[j % 2], self.osq2[j % 2]
            pM = self.pf()
            for hd in range(4):
                self.mm(pM[:, hd * L:(hd + 1) * L], c["ones_mean"][:, :], osq[:, 2 * hd, :L], True, False)
                self.mm(pM[:, hd * L:(hd + 1) * L], c["ones_mean"][:, :], osq[:, 2 * hd + 1, :L], False, True)
            self.rsqrt(self.rstdg[:, :4 * L], pM[:, :4 * L], 1.0, EPS)
            r3 = self.rstdg[:, :4 * L].rearrange("p (h l) -> p h l", h=4)
            o4 = o_raw[:, :, :].rearrange("p (h v) l -> p h v l", v=2)
            for vc in range(2):
                self.stt("dve", o4[:, :, vc, :L], o4[:, :, vc, :L], self.hgain[:, vc:vc + 1], r3, ALU.mult, ALU.mult)
            self.tt("dve", self.sgT[:, :, oj:oj + L], o_raw[:, :, :L], self.sgT[:, :, oj:oj + L], ALU.mult)

        nxt = front(0, tts[0][0])
        for i, (o, P) in enumerate(tts):
            atm = nxt
            Sbf = Sb[i % 2]
            pO = [self.pf(), self.pf()]
            for hd in range(4):
                for vc in range(2):
                    reg = pO[hd // 2][:, ((hd % 2) * 2 + vc) * L:((hd % 2) * 2 + vc + 1) * L]
                    self.mm(reg, self.v_tok[i][:L, hd * 256 + vc * 128:hd * 256 + (vc + 1) * 128], atm[:L, hd * L:(hd + 1) * L], True, False)
                    self.mm(reg, Sbf[:, hd, vc * 128:(vc + 1) * 128], qeT[:, hd, o:o + L], False, True)
            o_raw, osq = self.o_raw2[i % 2], self.osq2[i % 2]
            for b in range(2):
                pv = pO[b][:, :4 * L].rearrange("p (c l) -> p c l", c=4)
                self.cp("act", o_raw[:, 4 * b:4 * b + 4, :L], pv)
                self.act(osq[:, 4 * b:4 * b + 4, :L], pv, AF.Square)
            if i + 1 < n_t:
                nxt = front(i + 1, tts[i + 1][0])
            if i >= 1:
                tail(i - 1)
        return lambda: tail(n_t - 1)

    def gla_chunk(self, i, o, L, mode, seqs, cmask):
        c = self.c
        e1T, qeT, keT = self.e1T, self.qeT, self.keT
        nseq = len(seqs)
        blk = L // nseq
        def state_mm(si):
            so = seqs[si][0]
            pS = [self.pf(), self.pf()]
            for hd in range(4):
                self.mm(pS[hd // 2][:, (hd % 2) * 256:(hd % 2 + 1) * 256], self.kd[i][so:so + blk, hd * 128:(hd + 1) * 128],
                        self.v_tok[i][so:so + blk, hd * 256:(hd + 1) * 256])
            return pS
        hoist = nseq == 1
        pSs = [state_mm(0)] if hoist else None
        pA = self.pf()
        for hd in range(4):
            self.mm(pA[:L, hd * L:(hd + 1) * L], keT[:, hd, o:o + L], qeT[:, hd, o:o + L])
        self.tt("dve", self.ATm[:L, :4 * L], pA[:L, :4 * L], cmask[:L, :4 * L], ALU.mult)
        pO = [self.pf(), self.pf()]
        for hd in range(4):
            for vc in range(2):
                reg = pO[hd // 2][:, ((hd % 2) * 2 + vc) * L:((hd % 2) * 2 + vc + 1) * L]
                self.mm(reg, self.v_tok[i][:L, hd * 256 + vc * 128:hd * 256 + (vc + 1) * 128], self.ATm[:L, hd * L:(hd + 1) * L], True, False)
                for si, (so, nreal, S, Sbf) in enumerate(seqs):
                    self.mm(reg[:, so:so + blk], Sbf[:, hd, vc * 128:(vc + 1) * 128], qeT[:, hd, o + so:o + so + blk], False, si == nseq - 1)
        for si, (so, nreal, S, Sbf) in enumerate(seqs):
            pS = pSs[si] if hoist else state_mm(si)
            dcol = o + so + nreal - 1
            for hd in range(4):
                self.stt("dve", S[:, hd, :], S[:, hd, :], e1T[:, hd, dcol:dcol + 1], pS[hd // 2][:, (hd % 2) * 256:(hd % 2 + 1) * 256], ALU.mult, ALU.add)
            self.cp("act", Sbf[:, :, :], S[:, :, :])
        for b in range(2):
            pv = pO[b][:, :4 * L].rearrange("p (c l) -> p c l", c=4)
            self.cp("act", self.o_raw[:, 4 * b:4 * b + 4, :L], pv)
            self.act(self.osq[:, 4 * b:4 * b + 4, :L], pv, AF.Square)
        pM = self.pf()
        for hd in range(4):
            self.mm(pM[:, hd * L:(hd + 1) * L], c["ones_mean"][:, :], self.osq[:, 2 * hd, :L], True, False)
            self.mm(pM[:, hd * L:(hd + 1) * L], c["ones_mean"][:, :], self.osq[:, 2 * hd + 1, :L], False, True)
        self.rsqrt(self.rstdg[:, :4 * L], pM[:, :4 * L], 1.0, EPS)
        r3 = self.rstdg[:, :4 * L].rearrange("p (h l) -> p h l", h=4)
        o4 = self.o_raw[:, :, :].rearrange("p (h v) l -> p h v l", v=2)
        for vc in range(2):
            self.stt("dve", o4[:, :, vc, :L], o4[:, :, vc, :L], self.hgain[:, vc:vc + 1], r3, ALU.mult, ALU.mult)
        self.tt("dve", self.sgT[:, :, o:o + L], self.o_raw[:, :, :L], self.sgT[:, :, o:o + L], ALU.mult)

    def swa_block(self, ti, q0, ntt, nq, kts, g4, between=None):
        c = self.c
        N = 4 * nq
        for P2 in range(2):
            if between is not None:
                between()
            pts = []
            for kt in kts:
                nk = kt["nk"]
                pss = []
                for half in range(2):
                    rows = slice(half * 64, half * 64 + 64)
                    rhs_q = self.qo[P2][rows, ti * 4 * nq:(ti + 1) * 4 * nq]
                    ps = self.pf()
                    self.mm(ps[:nk, :N], kt["knT"](P2, rows), rhs_q, True, False)
                    pss.append(ps)
                for half in range(2):
                    self.mm(pss[half][:nk, :N], c["ident"][:, :nk], kt["bias"][:, 2 * P2 + half, :N], False, True)
                for half in range(2):
                    ps = pss[half]
                    self._pt += 1
                    pt = self.PT[self._pt % 6]
                    if kt["hb"] is not None:
                        self.act(pt[:nk, :N], ps[:nk, :N], AF.Exp, bias=kt["hb"][:nk, :])
                    else:
                        self.act(pt[:nk, :N], ps[:nk, :N], AF.Exp)
                    pts.append((pt, kt, half))
            pO = self.pf()
            pD = self.pf()
            n = len(pts)
            for idx, (pt, kt, half) in enumerate(pts):
                nk = kt["nk"]
                vz4 = kt["vz"][:, :].rearrange("p (j c) -> p j c", j=4)
                self.mm(pO[:, :N], vz4[:nk, 2 * P2 + half, :], pt[:nk, :N], idx == 0, idx == n - 1)
            for idx, (pt, kt, half) in enumerate(pts):
                nk = kt["nk"]
                self.mm(pD[:, :N], c["onesH"][:nk, half, :], pt[:nk, :N], idx == 0, False)
            self.mm(pD[:, :N], self.esinkT[0:4, P2, :], g4[0:4, :N], False, True)
            self.add("dve", lambda h, pD=pD, N=N: h.reciprocal(out=self.rden[:, :N], in_=pD[:, :N]), K(pD[:]), K(self.rden[:]))
            self.tt("dve", self.osT[P2][:, :, q0:q0 + nq], pO[:, :N].rearrange("p (g q) -> p g q", g=4),
                    self.rden[:, :N].rearrange("p (g q) -> p g q", g=4), ALU.mult)

    def final_out(self, tts, dst):
        for i, (o, P) in enumerate(tts):
            self.act(self.junk[:P, :], self.xt[i][:P, :], AF.Square, accum_out=self.ss[i][:P, 0:1])
            self.rsqrt(self.rstd[i][:P, 0:1], self.ss[i][:P, 0:1], 1.0 / D, EPS)
            self.ts("dve", self.xt[i][:P, :], self.xt[i][:P, :], self.rstd[i][:P, 0:1], None, ALU.mult)
            self.tt("pool" if i % 2 else "dve", self.xt[i][:P, :], self.xt[i][:P, :], self.fgain_bc[:P, :], ALU.mult)
            self.dma("sp", dst[o:o + P, :], self.xt[i][:P, :])

    def body(self):
        di = self.din
        self.reset_counters()
        pre_on = not self.debug.get("skip_pre") and not self.debug.get("only_sample")
        self.setup(first_x=di["xprev"][0:T, :] if pre_on else None)
        tts = [(j * 128, 128) for j in range(4)]
        seqs_p = [(0, 128, self.S[0], self.Sbf[0])]
        stop = self.debug.get("stop")
        if not self.debug.get("skip_pre") and not self.debug.get("only_sample"):
            for t in range(NTILE):
                self.next_x = di["xprev"][(t + 1) * T:(t + 2) * T, :] if t + 1 < NTILE else di["xp"][0:T, :]
                self.ffn(tts, T, di["ffn1_up"], di["ffn1_down"], 0, ("pre", t, "f1"))
                self.mixer(tts, T, "pre", ("pre", t), seqs_p, t, t == NTILE - 1)
            self.next_x = None
        pre_ran = not self.debug.get("skip_pre") and not self.debug.get("only_sample")
        for t in range(NTILE if not self.debug.get("only_sample") else 0):
            if self.x_loaded:
                self.x_loaded = False
            elif not (t == 0 and pre_ran):
                self.load_x(di["xp"][t * T:(t + 1) * T, :], tts)
            self.ffn(tts, T, di["ffn1_up"], di["ffn1_down"], 0, ("own", t, "f1"))
            if stop == "ffn1":
                self.final_dbg_x(tts, t)
                continue
            self.mixer(tts, T, "full", ("own", t), seqs_p, t, t == NTILE - 1)
            if stop == "mixer":
                self.final_dbg_x(tts, t)
                continue
            if t == NTILE - 1 and not self.debug.get("no_sample") and not self.debug.get("no_ffn2"):
                self.dma("sp", self.dout["ng"].rearrange("h k v -> k h v"), self.S[0][:, :, :])
                self._ng_done = True
                self.sample_setup_early()
            staged = False
            if not self.debug.get("no_ffn2"):
                hook = None
                smp_next = (t == NTILE - 1 and not self.debug.get("no_sample") and not self.debug.get("no_final") and not stop)
                if smp_next:
                    tss_ = [(0, TS)]
                    xst = [self.arena1[:, 0:2048].bitcast(F32)]
                    self.dma("sp", xst[0][:TS, :], di["xs"])

                    def hook(xst=xst, tss_=tss_):
                        self.norm_stats(tss_, src=xst)

                    def hook2(tss_=tss_):
                        self.norm_tr(tss_, TS, 0)
                    staged = True
                if t + 1 < NTILE and not self.debug.get("no_final"):
                    xst = [self.arena1[:, i * 2048:(i + 1) * 2048].bitcast(F32) for i in range(4)]
                    nsrc = di["xp"][(t + 1) * T:(t + 2) * T, :]
                    for i, (o, P) in enumerate(tts):
                        self.dma("sp", xst[i][:P, :], nsrc[o:o + P, :])

                    def hook(xst=xst):
                        self.norm_stats(tts, src=xst)

                    def hook2():
                        self.norm_tr(tts, T, 0)
                    staged = True
                self.ffn(tts, T, di["ffn2_up"], di["ffn2_down"], 16, ("own", t, "f2"), mid_hook=hook, mid_hook2=hook2 if hook else None)
            if self.debug.get("no_final"):
                self.final_dbg_x(tts, t)
            else:
                self.final_out(tts, self.dout["y"][t * T:(t + 1) * T, :])
            if staged:
                for i, (o, P) in enumerate(tss_ if smp_next else tts):
                    self.cp("pool", self.xt[i][:P, :], xst[i][:P, :])
                self.norm_done = True
                self.x_loaded = True
        if not self._ng_done:
            self.dma("sp", self.dout["ng"].rearrange("h k v -> k h v"), self.S[0][:, :, :])
        if stop or self.debug.get("no_sample"):
            return
        if not self._early_done:
            self.sample_setup_early()
        tss = [(0, TS)]
        self.dma("sp", self.S[1][:, :, :], di["st"][1].rearrange("h k v -> k h v"))
        self.cp("pool", self.Sbf[1][:, :, :], self.S[1][:, :, :])
        seqs_s = [(0, 16, self.S[0], self.Sbf[0]), (32, 16, self.S[1], self.Sbf[1])]
        if self.x_loaded:
            self.x_loaded = False
        else:
            self.load_x(di["xs"], tss)
        self.sample_rest(tss, seqs_s)

    def sample_setup_early(self):
        di = self.din
        self._early_done = True
        self.dma("pool", self.biasTs, di["biass"].rearrange("a p j n -> p a j n"))
        for n in ("tri_incl", "tri_suf", "cmask4"):
            self.dma("sp", self.c[n][:], di[n + "_s"])
        self.dma("sp", self.S[0][:, :, :], di["st"][0].rearrange("h k v -> k h v"))
        self.cp("pool", self.Sbf[0][:, :, :], self.S[0][:, :, :])
        for sq_ in range(2):
            self.dma("sp", self.ckf[:, :], di["ck"][sq_])
            self.cp("dve", self.ckb[:, :], self.ckf[:, :])
            for P2 in range(2):
                pT = self.pb()
                self.tr(pT[:, 0:128], self.ckb[:, P2 * 128:(P2 + 1) * 128], self.c["ident"][:, :])
                self.cp("act", self.knT_c[sq_][:, P2, :], pT[:, 0:128])
            self.dma("sp", self.ckf[:, :], di["cv"][sq_])
            vz4 = self.vz_c[sq_][:, :].rearrange("p (jp par c) -> p jp par c", jp=2, par=2)
            cv4 = self.ckf[:, :].rearrange("p (jp par d) -> p jp par d", jp=2, par=2)
            for par in range(2):
                self.cp("dve", vz4[:, :, par, par * 64:(par + 1) * 64], cv4[:, :, par, :])

    def sample_rest(self, tss, seqs_s):
        di = self.din
        sstop = self.debug.get("sstop")
        self.ffn(tss, TS, di["ffn1_up"], di["ffn1_down"], 0, ("smp", "f1"))
        if sstop != "ffn1":
            self.mixer(tss, TS, "sample", ("smp",), seqs_s, 0, True)
            if sstop != "mixer":
                self.ffn(tss, TS, di["ffn2_up"], di["ffn2_down"], 16, ("smp", "f2"))
        self.final_out(tss, self.dout["ys"])
        for sq_ in range(2):
            self.dma("sp", self.dout["ngs"][sq_].rearrange("h k v -> k h v"), self.S[sq_][:, :, :])

    def final_dbg_x(self, tts, t):
        for i, (o, P) in enumerate(tts):
            self.dma("sp", self.dout["y"][t * T + o:t * T + o + P, :], self.xt[i][:P, :])

    def build(self):
        self.planning = True
        self.body()
        self.planning = False
        self.nuse = {}
        for pl in self.plan:
            self.nuse[pl[2]] = self.nuse.get(pl[2], 0) + 1
        self.body()
        self.s.emit()
        return self.nc


_CACHE = {}


def _get_prog(debug=None):
    key = repr(sorted((debug or {}).items()))
    if key not in _CACHE:
        p = Prog(debug)
        p.build()
        _CACHE[key] = p
    return _CACHE[key]


def make_in_maps(inp):
    f = lambda a: np.ascontiguousarray(np.asarray(a, dtype=np.float32))
    xp = f(inp["x_prompt"])
    xsm = f(inp["x_sample"])
    ck = f(inp["cache_swa_k"])[0].reshape(16, 128, 256)
    cv = f(inp["cache_swa_v"])[0].reshape(16, 128, 256)
    st = f(inp["state_gla"])[0]
    consts = _consts()
    bp, bs = _bias_tables(inp["rel_bias"])
    gains = np.stack([f(inp["ffn1_norm"])[0], f(inp["mix_norm"])[0], f(inp["ffn2_norm"])[0]])
    gains_fm = np.ascontiguousarray(gains.reshape(3, 8, 128).transpose(2, 0, 1).reshape(128, 24))
    fg = np.ascontiguousarray(np.broadcast_to(f(inp["final_norm"])[0][None, :], (128, D)))
    hg = np.ascontiguousarray(f(inp["gla_head_norm"])[0].reshape(2, 128).T)
    qg = np.ascontiguousarray(np.tile(f(inp["q_norm"])[0], 2)[:, None])
    kg = np.ascontiguousarray(np.tile(f(inp["k_norm"])[0], 2)[:, None])
    kgb = np.ascontiguousarray(np.broadcast_to(f(inp["k_norm"])[0][None, :], (128, 64)))
    sinks = f(inp["attn_sinks"])[0]
    sinkT = np.zeros((2, 4, 128), np.float32)
    for P2 in range(2):
        for g in range(4):
            sinkT[P2, g, :64] = sinks[4 * (2 * P2) + g]
            sinkT[P2, g, 64:] = sinks[4 * (2 * P2 + 1) + g]
    shared = {
        "ffn1_up": f(inp["ffn1_w_up"])[0], "ffn1_down": f(inp["ffn1_w_down"])[0],
        "ffn2_up": f(inp["ffn2_w_up"])[0], "ffn2_down": f(inp["ffn2_w_down"])[0],
        "w_in": f(inp["w_in"])[0], "w_branch": f(inp["w_branch"])[0], "w_out": f(inp["w_out"])[0],
        "w_alpha": f(inp["gla_w_alpha"])[0], "b_alpha": f(inp["gla_b_alpha"]).reshape(1, 512),
        "gains_fm": gains_fm, "fgain_bc": fg, "hgain": hg, "qgcol": qg, "kgcol": kg, "kgain_bc": kgb,
        "sinkT": sinkT, "biasp": bp, "biass": bs,
    }
    shared.update(consts)
    maps = []
    zeros_prev = np.zeros((TOKC, D), np.float32)
    for cidx in range(NCORES):
        b, half = cidx // 2, cidx % 2
        m = dict(shared)
        m["xp"] = np.ascontiguousarray(xp[b, half * TOKC:(half + 1) * TOKC])
        m["xprev"] = np.ascontiguousarray(xp[b, 0:TOKC]) if half == 1 else zeros_prev
        xs_ = np.zeros((TS, D), np.float32)
        for s_ in range(2):
            xs_[32 * s_:32 * s_ + 16] = xsm[2 * cidx + s_]
        m["xs"] = xs_
        m["ck"] = np.ascontiguousarray(ck[2 * cidx:2 * cidx + 2])
        m["cv"] = np.ascontiguousarray(cv[2 * cidx:2 * cidx + 2])
        m["st"] = np.ascontiguousarray(st[2 * cidx:2 * cidx + 2])
        m["halo_bias"] = np.full((128, 1), 0.0 if half == 1 else NEG, np.float32)
        maps.append(m)
    return maps


def run(inp, debug=None):
    prog = _get_prog(debug)
    maps = make_in_maps(inp)
    res = run_bass_kernel_spmd(prog.nc, maps, core_ids=list(range(NCORES)))
    return prog, res


def kernel(**inp):
    prog, res = run(inp)
    R = res.results
    y = np.zeros((4, 4096, D), np.float32)
    ys = np.zeros((16, 16, D), np.float32)
    nk = np.zeros((1, 4, 128, 4, 64), np.float32)
    nv = np.zeros((1, 4, 128, 4, 64), np.float32)
    ng = np.zeros((1, 4, 4, 128, 256), np.float32)
    nks = np.zeros((1, 16, 16, 4, 64), np.float32)
    nvs = np.zeros((1, 16, 16, 4, 64), np.float32)
    ngs = np.zeros((1, 16, 4, 128, 256), np.float32)
    for cidx in range(NCORES):
        b, half = cidx // 2, cidx % 2
        r = R[cidx]
        y[b, half * TOKC:(half + 1) * TOKC] = r["y"]
        if half == 1:
            nk[0, b] = r["nk"].reshape(128, 4, 64)
            nv[0, b] = r["nv"].reshape(128, 4, 64)
            ng[0, b] = r["ng"]
        for s_ in range(2):
            ys[2 * cidx + s_] = r["ys"][32 * s_:32 * s_ + 16]
            nks[0, 2 * cidx + s_] = r["nks"][32 * s_:32 * s_ + 16].reshape(16, 4, 64)
            nvs[0, 2 * cidx + s_] = r["nvs"][32 * s_:32 * s_ + 16].reshape(16, 4, 64)
            ngs[0, 2 * cidx + s_] = r["ngs"][s_]
    return (y, ys, nk, nv, ng, nks, nvs, ngs)
```

```python
import numpy as np
import ml_dtypes
import concourse.bass as bass
import concourse.mybir as mybir
from concourse.bass_utils import run_bass_kernel_spmd

F32 = mybir.dt.float32
BF16 = mybir.dt.bfloat16
AF = mybir.ActivationFunctionType
ALU = mybir.AluOpType
AX = mybir.AxisListType

ENGS = ("pe", "act", "dve", "pool", "sp")
NCORES = 8
D = 1024
DFF = 2816
TOKC = 2048
T = 512
NTILE = TOKC // T
TS = 64
EPS = 1e-6
NEG = -30000.0
IN_W = 6672
C_GQ, C_GK, C_GV, C_GR, C_GA, C_SQ, C_SK, C_SV, C_GTA, C_GTB = 0, 512, 1024, 2048, 3072, 3088, 4112, 4368, 4624, 5648


class Op:
    __slots__ = ("eng", "fn", "deps", "sig", "semval", "dma", "dsem", "dval", "idx", "group")


class Sched:
    def __init__(self, nc, n_dma_sems=48):
        self.nc = nc
        self.ops = []
        self.by_eng = {e: [] for e in ENGS}
        self.last_w = {}
        self.readers = {}
        self.esem = {e: nc.alloc_semaphore(name="cnt_" + e) for e in ENGS}
        self.dsems = [nc.alloc_semaphore(name="dma%d" % i) for i in range(n_dma_sems)]
        self.dcount = [0] * n_dma_sems
        self.dlast = [None] * n_dma_sems
        self.dnext = {"sw": 0, "hw": 0}

    def add(self, eng, fn, reads=(), writes=(), dma=False, group=None):
        op = Op()
        op.group = group
        op.eng = eng
        op.fn = fn
        op.sig = False
        op.semval = None
        op.dma = dma
        op.idx = len(self.ops)
        deps = {}

        def dep(d, raw):
            if d is None:
                return
            if not d.dma and not dma and d.eng == eng:
                if eng == "pe" or not raw:
                    return
            deps[d.idx] = d

        for k in reads:
            for w in self.last_w.get(k, ()):
                dep(w, True)
        for k in writes:
            for w in self.last_w.get(k, ()):
                if group is not None and w.group == group:
                    continue
                dep(w, False)
            for r in self.readers.get(k, ()):
                dep(r, False)
        if dma:
            half = len(self.dsems) // 2
            cls = "sw" if eng == "pool" else "hw"
            j = self.dnext[cls] + (half if cls == "sw" else 0)
            self.dnext[cls] = (self.dnext[cls] + 1) % half
            if self.dlast[j] is not None:
                deps[self.dlast[j].idx] = self.dlast[j]
            self.dcount[j] += 16
            op.dsem = j
            op.dval = self.dcount[j]
            self.dlast[j] = op
        for k in reads:
            lst = self.readers.setdefault(k, [])
            if not dma:
                lst[:] = [r for r in lst if r.dma or r.eng != eng]
            lst.append(op)
        for k in writes:
            lw = self.last_w.get(k, [])
            if group is not None and lw and lw[0].group == group:
                lw.append(op)
            else:
                self.last_w[k] = [op]
                self.readers[k] = []
        op.deps = list(deps.values())
        for d in op.deps:
            if not d.dma:
                d.sig = True
        self.ops.append(op)
        self.by_eng[eng].append(op)
        return op

    def emit(self):
        nc = self.nc
        for e in ENGS:
            c = 0
            for op in self.by_eng[e]:
                if not op.dma and op.sig:
                    c += 1
                    op.semval = c
        final_d = list(self.dcount)

        def replay(e, h):
            seen = {}
            for op in self.by_eng[e]:
                need = {}
                for d in op.deps:
                    if d.dma:
                        key, val, sem = ("d", d.dsem), d.dval, self.dsems[d.dsem]
                    else:
                        key, val, sem = ("e", d.eng), d.semval, self.esem[d.eng]
                    if val > need.get(key, (0, None))[0]:
                        need[key] = (val, sem)
                for key, (val, sem) in need.items():
                    if seen.get(key, 0) >= val:
                        continue
                    seen[key] = val
                    h.wait_ge(sem, val)
                ins = op.fn(h)
                if op.dma:
                    ins.then_inc(self.dsems[op.dsem], 16)
                elif op.sig:
                    ins.then_inc(self.esem[e], 1)
            if e == "sp":
                for j, v in enumerate(final_d):
                    if v > 0:
                        h.wait_ge(self.dsems[j], v)

        with nc.Block() as block:
            @block.tensor
            def _(h):
                replay("pe", h)

            @block.scalar
            def _(h):
                replay("act", h)

            @block.vector
            def _(h):
                replay("dve", h)

            @block.gpsimd
            def _(h):
                replay("pool", h)

            @block.sync
            def _(h):
                replay("sp", h)


def _is_dram(ap):
    return "DRam" in type(ap.tensor).__name__


def K(*aps):
    out = []
    for a in aps:
        if a is None or isinstance(a, (int, float)):
            continue
        if _is_dram(a):
            continue
        out.append(a.tensor.name)
    return out


def _t5_bucket(rel):
    nb = 16
    ret = np.where(rel > 0, nb, 0)
    n = np.abs(rel)
    max_exact = nb // 2
    nf = np.maximum(n, 1).astype(np.float32)
    large = max_exact + (np.log(nf / max_exact) / np.float32(np.log(128 / max_exact)) * (nb - max_exact)).astype(np.int32)
    large = np.minimum(large, nb - 1)
    return ret + np.where(n < max_exact, n, large)


def _bias_tables(rel_bias):
    rb = np.asarray(rel_bias, np.float32)
    q = np.arange(128)
    out = np.zeros((2, 128, 4, 4, 128), np.float32)
    for kt in range(2):
        kloc = np.arange(128) + (kt - 1) * 128
        rel = kloc[:, None] - q[None, :]
        bk = _t5_bucket(rel)
        kc = np.floor_divide(kloc, 64)[:, None]
        qc = (q // 64)[None, :]
        valid = (kc <= qc) & (kc >= qc - 2)
        g = rb[bk]
        g = np.where(valid[:, :, None], g, NEG)
        out[kt] = g.reshape(128, 128, 4, 4).transpose(0, 2, 3, 1)
    bp = out.reshape(2, 128, 4, 512)
    outs = np.full((3, 128, 4, 4, 64), NEG, np.float32)
    qpos = 2048 + np.arange(16)
    for s in range(2):
        kpos = 2048 - 128 + np.arange(128)
        bk = _t5_bucket(kpos[:, None] - qpos[None, :])
        g = rb[bk].reshape(128, 16, 4, 4).transpose(0, 2, 3, 1)
        outs[s, :, :, :, 32 * s:32 * s + 16] = g
        bk2 = _t5_bucket(qpos[:, None] - qpos[None, :])
        g2 = rb[bk2].reshape(16, 16, 4, 4).transpose(0, 2, 3, 1)
        outs[2, 32 * s:32 * s + 16, :, :, 32 * s:32 * s + 16] = g2
    bs = outs.reshape(3, 128, 4, 256)
    return np.ascontiguousarray(bp, np.float32), np.ascontiguousarray(bs, np.float32)


def _consts():
    c = {}
    c["ident"] = np.eye(128, dtype=np.float32).astype(ml_dtypes.bfloat16)
    s = np.arange(128)[:, None]
    t = np.arange(128)[None, :]
    c["tri_incl"] = np.where(s <= t, -1.0 / 16, 0.0).astype(np.float32).astype(ml_dtypes.bfloat16)
    c["tri_suf"] = np.where(s > t, -1.0 / 16, 0.0).astype(np.float32).astype(ml_dtypes.bfloat16)
    c["cmask4"] = np.tile(np.where(s <= t, 1.0, 0.0).astype(np.float32), (1, 4))
    c["tri_all"] = np.full((128, 128), -1.0 / 16, np.float32).astype(ml_dtypes.bfloat16)
    s6 = np.arange(64)[:, None]
    t6 = np.arange(64)[None, :]
    same = (s6 // 32) == (t6 // 32)
    real_s = (s6 % 32) < 16
    tis = np.zeros((128, 128), np.float32)
    tis[:64, :64] = np.where(same & (s6 <= t6), -1.0 / 16, 0.0)
    tss = np.zeros((128, 128), np.float32)
    tss[:64, :64] = np.where(same & (s6 > t6) & real_s, -1.0 / 16, 0.0)
    cms = np.zeros((128, 512), np.float32)
    cms[:64, :256] = np.tile(np.where(same & (s6 <= t6), 1.0, 0.0), (1, 4))
    c["tri_incl_s"] = tis.astype(ml_dtypes.bfloat16)
    c["tri_suf_s"] = tss.astype(ml_dtypes.bfloat16)
    c["cmask4_s"] = cms
    oh = np.zeros((2, 128, 128), np.float32)
    oh[0, :, :64] = 1.0
    oh[1, :, 64:] = 1.0
    c["onesH"] = oh.astype(ml_dtypes.bfloat16)
    g4 = np.zeros((4, 512), np.float32)
    g4s = np.zeros((4, 512), np.float32)
    for g in range(4):
        g4[g, g * 128:(g + 1) * 128] = 1.0
        g4s[g, g * 64:(g + 1) * 64] = 1.0
    c["g4"] = g4.astype(ml_dtypes.bfloat16)
    c["g4s"] = g4s.astype(ml_dtypes.bfloat16)
    bo = np.zeros((128, 128), np.float32)
    bo[:64, :64] = 1.0
    bo[64:, 64:] = 1.0
    c["bo_q"] = bo.astype(ml_dtypes.bfloat16)
    c["bo_k"] = (bo / 64).astype(ml_dtypes.bfloat16)
    c["ones_mean"] = np.full((128, 128), 1.0 / 256, np.float32).astype(ml_dtypes.bfloat16)
    c["ones_row"] = np.ones((1, 128), np.float32).astype(ml_dtypes.bfloat16)
    return c


CONST_SPECS = [
    ("ident", [128, 128], BF16), ("tri_incl", [128, 128], BF16), ("tri_suf", [128, 128], BF16),
    ("cmask4", [128, 512], F32), ("tri_all", [128, 128], BF16), ("tri_incl_s", [128, 128], BF16), ("tri_suf_s", [128, 128], BF16),
    ("cmask4_s", [128, 512], F32), ("onesH", [2, 128, 128], BF16), ("g4", [4, 512], BF16), ("g4s", [4, 512], BF16),
    ("bo_q", [128, 128], BF16), ("bo_k", [128, 128], BF16), ("ones_mean", [128, 128], BF16), ("ones_row", [1, 128], BF16),
]


class Prog:
    def __init__(self, debug=None):
        self.nc = nc = bass.Bass("TRN2", target_bir_lowering=False)
        self.s = Sched(nc)
        self.planning = False
        self.plan = []
        self.wscr = None
        self.next_x = None
        self.norm_done = False
        self.x_loaded = False
        self.debug = debug or {}
        self.dbg_outs = []
        self.din = {}
        self.dout = {}
        self._declare_io()
        self._alloc()

    def _in(self, name, shape, dt=F32):
        self.din[name] = self.nc.dram_tensor(name, list(shape), dt, kind="ExternalInput").ap()
        return self.din[name]

    def _out(self, name, shape, dt=F32):
        self.dout[name] = self.nc.dram_tensor(name, list(shape), dt, kind="ExternalOutput").ap()
        return self.dout[name]

    def _declare_io(self):
        i = self._in
        i("xp", [TOKC, D]); i("xprev", [TOKC, D]); i("xs", [TS, D])
        i("ck", [2, 128, 256]); i("cv", [2, 128, 256]); i("st", [2, 4, 128, 256])
        i("ffn1_up", [D, 2 * DFF]); i("ffn1_down", [DFF, D]); i("ffn2_up", [D, 2 * DFF]); i("ffn2_down", [DFF, D])
        i("w_in", [D, IN_W]); i("w_branch", [2048, D]); i("w_out", [D, D])
        i("w_alpha", [16, 512]); i("b_alpha", [1, 512])
        i("gains_fm", [128, 24])
        i("fgain_bc", [128, D])
        i("hgain", [128, 2])
        i("qgcol", [128, 1]); i("kgcol", [128, 1]); i("kgain_bc", [128, 64])
        i("sinkT", [2, 4, 128])
        i("halo_bias", [128, 1])
        i("biasp", [2, 128, 4, 512]); i("biass", [3, 128, 4, 256])
        for n, sh, dt in CONST_SPECS:
            i(n, sh, dt)
        o = self._out
        o("y", [TOKC, D]); o("ys", [TS, D]); o("nk", [128, 256]); o("nv", [128, 256]); o("ng", [4, 128, 256])
        o("nks", [TS, 256]); o("nvs", [TS, 256]); o("ngs", [2, 4, 128, 256])

    def sb(self, name, shape, dt):
        return self.nc.alloc_sbuf_tensor("s_" + name, list(shape), dt)

    def _alloc(self):
        nc = self.nc
        sb = self.sb
        self.c = {}
        for n, sh, dt in CONST_SPECS:
            if n.endswith("_s"):
                continue
            if len(sh) == 3:
                self.c[n] = sb("c_" + n, [sh[1], sh[0], sh[2]], dt)
            else:
                self.c[n] = sb("c_" + n, sh, dt)
        self.gains_fm = sb("gains_fm", [128, 24], F32)
        self.fgain_bc = sb("fgain_bc", [128, D], F32)
        self.hgain = sb("hgain", [128, 2], F32)
        self.qgcol = sb("qgcol", [128, 1], F32)
        self.kgcol = sb("kgcol", [128, 1], F32)
        self.kgain_bc = sb("kgain_bc", [128, 64], F32)
        self.sinkT = sb("sinkT", [4, 2, 128], F32)
        self.esinkT = sb("esinkT", [4, 2, 128], BF16)
        self.halo_bias = sb("halo_bias", [128, 1], F32)
        self.walpha = sb("walpha", [16, 512], BF16)
        self.balpha = sb("balpha", [1, 512], BF16)
        self.biasT = sb("biasT", [128, 2, 4, 512], BF16)
        self.biasTs = self.biasT[:, :, :, :].rearrange("p a j n -> p (a j n)")[:, 0:3072].rearrange("p (a j n) -> p a j n", a=3, j=4)
        self.xt = [sb("xt%d" % i, [128, D], F32) for i in range(4)]
        self.hball = sb("hball", [128, 4, D], BF16)
        self.hb = [self.hball[:, i, :] for i in range(4)]
        self.osT = [self.hball[:, 2 * i:2 * i + 2, :].rearrange("p a (g t) -> p (a g) t", g=2) for i in range(2)]
        self.ss = [sb("ss%d" % i, [128, 1], F32) for i in range(4)]
        self.rstd = [sb("rstd%d" % i, [128, 1], F32) for i in range(4)]
        self.hT = [sb("hT%d" % i, [128, T], BF16) for i in range(8)]
        self.arena0 = sb("arena0", [128, 22 * T], BF16)
        self.actT = [self.arena0[:, i * T:(i + 1) * T] for i in range(22)]
        self.e1T = self.arena0[:, 0:4096].bitcast(F32).rearrange("p (h t) -> p h t", h=4)
        self.e2T = self.arena0[:, 4096:8192].bitcast(F32).rearrange("p (h t) -> p h t", h=4)
        self.qeT = self.arena0[:, 8192:10240].rearrange("p (h t) -> p h t", h=4)
        self.arena1 = sb("arena1", [128, 10240], BF16)
        self.v_tok = [self.arena1[:, i * 1024:(i + 1) * 1024] for i in range(4)]
        self.kd = [self.arena1[:, 4096 + i * 512:4096 + (i + 1) * 512] for i in range(4)]
        self.es = [self.arena1[:, 6144 + i * 1024:6144 + (i + 1) * 1024].bitcast(F32) for i in range(4)]
        self.sigA = self.arena1[:, 0:4096].rearrange("p (c t) -> p c t", c=8)
        self.sigB = self.arena1[:, 4096:8192].rearrange("p (c t) -> p c t", c=8)
        self.keT = sb("keT", [128, 4, T], BF16)
        self.gaT = sb("gaT", [16, T], BF16)
        self.sp = [sb("sp%d" % i, [128, 512], BF16) for i in range(4)]
        self.dtot = sb("dtot", [128, 4, 1], F32)
        self.sgT = sb("sgT", [128, 8, T], BF16)
        self.ATm = sb("ATm", [128, 512], BF16)
        self.ATm2 = [self.ATm, sb("ATm1", [128, 512], BF16)]
        self.Sbf_alt = sb("Sbf_alt", [128, 4, 256], BF16)
        self.o_raw = sb("o_raw", [128, 8, 128], F32)
        self.osq = sb("osq", [128, 8, 128], BF16)
        self.rstdg = sb("rstdg", [128, 512], F32)
        self.o_raw2 = [self.o_raw, sb("o_raw1", [128, 8, 128], F32)]
        self.osq2 = [self.osq, sb("osq1", [128, 8, 128], BF16)]
        self.S = [sb("S0", [128, 4, 256], F32), self.xt[1][:, :].rearrange("p (h v) -> p h v", h=4)]
        self.Sbf = [sb("Sbf0", [128, 4, 256], BF16), self.xt[2][:, 0:512].bitcast(BF16).rearrange("p (h v) -> p h v", h=4)]
        self.sqr = [sb("sqr%d" % i, [128, T], F32) for i in range(1)]
        self.sqsq = [sb("sqsq%d" % i, [128, T], BF16) for i in range(1)]
        self.rsq = [sb("rsq%d" % i, [128, T], F32) for i in range(1)]
        self.qo = [sb("qo%d" % i, [128, 4 * T], BF16) for i in range(2)]
        self.knT_cur = sb("knT_cur", [128, 2, T], BF16)
        self.knT_prev = sb("knT_prev", [128, 2, 128], BF16)
        self.vz_cur = [sb("vz_cur%d" % i, [128, 512], BF16) for i in range(4)]
        self.vz_prev = sb("vz_prev", [128, 512], BF16)
        self.PT = [sb("PT%d" % i, [128, 512], BF16) for i in range(6)]
        self.rden = sb("rden", [128, 512], F32)
        self.ksq = self.sqr[0][:, 0:256]
        self.ktok = self.sqr[0][:, 256:512]
        self.kss = sb("kss", [128, 4], F32)
        self.knew = self.rsq[0][:, 0:256]
        self.vnew = self.rsq[0][:, 256:512]
        self.mtmp = [sb("mtmp%d" % i, [128, T], F32) for i in range(2)]
        self.sa = self.mtmp
        self.junk = self.mtmp[0][:, :].bitcast(BF16)
        self.ckf = sb("ckf", [128, 256], F32)
        self.ckb = sb("ckb", [128, 256], BF16)
        self.knT_c = [sb("knT_c%d" % i, [128, 2, 128], BF16) for i in range(2)]
        self.vz_c = [sb("vz_c%d" % i, [128, 512], BF16) for i in range(2)]
        self.NSLOT = 4
        self.slots = [sb("wslot%d" % i, [128, 4096], BF16) for i in range(self.NSLOT)]
        self.psf = [nc.alloc_psum_tensor("psf%d" % i, [128, 512], F32) for i in range(6)]
        self.psb = [nc.alloc_psum_tensor("psb%d" % i, [128, 1024], BF16) for i in range(2)]

    def reset_counters(self):
        self._ng_done = False
        self._early_done = False
        self.norm_done = False
        self.x_loaded = False
        self.next_x = None
        self._pf = 0
        self._pb = 0
        self._pt = 0
        self._piece = 0
        self._rr = 0

    def pf(self):
        self._pf += 1
        return self.psf[self._pf % 6]

    def pb(self):
        self._pb += 1
        return self.psb[self._pb % 2]

    def add(self, eng, fn, reads, writes, dma=False, group=None):
        if self.planning:
            return
        self.s.add(eng, fn, reads, writes, dma, group)

    def mm(self, out, lhsT, rhs, start=True, stop=True, rkeys=None):
        self.add("pe", lambda h: h.matmul(out, lhsT=lhsT, rhs=rhs, start=start, stop=stop), K(lhsT, rhs) if rkeys is None else rkeys, K(out))

    def tr(self, out, in_, ident, rkeys=None):
        self.add("pe", lambda h: h.transpose(out, in_, ident), K(in_, ident) if rkeys is None else rkeys, K(out))

    def act(self, out, in_, func, bias=None, scale=None, accum_out=None):
        kw = {}
        if bias is not None:
            kw["bias"] = bias
        if scale is not None:
            kw["scale"] = scale
        if accum_out is not None:
            kw["accum_out"] = accum_out
        self.add("act", lambda h: h.activation(out=out, in_=in_, func=func, **kw), K(in_, bias, scale), K(out, accum_out))

    def rsqrt(self, out, in_, scale, eps):
        self.act(out, in_, AF.Ln, bias=eps, scale=scale)
        self.act(out, out, AF.Exp, scale=-0.5)

    def amul(self, out, in_, mul):
        self.add("act", lambda h: h.mul(out=out, in_=in_, mul=mul), K(in_, mul), K(out))

    def cp(self, eng, out, in_):
        if eng == "act":
            self.add("act", lambda h: h.copy(out=out, in_=in_), K(in_), K(out))
        else:
            self.add(eng, lambda h: h.tensor_copy(out=out, in_=in_), K(in_), K(out))

    def tt(self, eng, out, in0, in1, op, rkeys=None, wkeys=None):
        self.add(eng, lambda h: h.tensor_tensor(out=out, in0=in0, in1=in1, op=op), K(in0, in1) if rkeys is None else rkeys, K(out) if wkeys is None else wkeys)

    def ts(self, eng, out, in0, s1, s2, op0, op1=None, rkeys=None, wkeys=None):
        if rkeys is not None:
            self.add(eng, lambda h: h.tensor_scalar(out=out, in0=in0, scalar1=s1, scalar2=0.0, op0=op0, op1=ALU.add), rkeys, wkeys)
            return
        if op1 is None:
            if op0 == ALU.pow:
                self.add(eng, lambda h: h.tensor_scalar(out=out, in0=in0, scalar1=0.0, scalar2=s1, op0=ALU.add, op1=ALU.pow), K(in0, s1), K(out))
            else:
                self.add(eng, lambda h: h.tensor_scalar(out=out, in0=in0, scalar1=s1, scalar2=0.0, op0=op0, op1=ALU.add), K(in0, s1), K(out))
        else:
            self.add(eng, lambda h: h.tensor_scalar(out=out, in0=in0, scalar1=s1, scalar2=s2, op0=op0, op1=op1), K(in0, s1, s2), K(out))

    def stt(self, eng, out, in0, scalar, in1, op0, op1):
        self.add(eng, lambda h: h.scalar_tensor_tensor(out=out, in0=in0, scalar=scalar, in1=in1, op0=op0, op1=op1),
                 K(in0, scalar, in1), K(out))

    def memset(self, eng, ap, val):
        self.add(eng, lambda h: h.memset(ap, val), [], K(ap))

    def dma(self, eng, out, in_, reads=(), writes=(), group=None):
        self.add(eng, lambda h: h.dma_start(out=out, in_=in_), K(in_) + list(reads), K(out) + list(writes), dma=True, group=group)

    def wget(self, tag, spec, wid, nel=4096):
        if self.planning:
            self.plan.append((tag, spec, wid, nel))
            return self.slots[0]
        if self._piece == 0:
            self.first = {}
            for j, pl in enumerate(self.plan):
                self.first.setdefault(pl[2], j)
            self.widx = {w: k for k, w in enumerate(self.first)}
            if self.wscr is None:
                self.wscr = self.nc.dram_tensor("wscr", [len(self.widx), 128, 4096], BF16).ap()
        i = self._piece
        assert self.plan[i][0] == tag, (self.plan[i][0], tag)
        LA = self.NSLOT - 2 if wid[0] == "wbB" else self.NSLOT - 1
        if i == 0:
            self._issue_piece(0)
            self._issued = 0
        slot = self.slots[i % self.NSLOT]
        if self.first[wid] == i and self.nuse[wid] > 1:
            self.dma("pool", self.wscr[self.widx[wid], :, :nel], slot[:, :nel], writes=[("scr", wid)])
        target = min(i + LA, len(self.plan) - 1)
        while self._issued < target:
            self._issued += 1
            self._issue_piece(self._issued)
        self._piece += 1
        return slot

    def _issue_piece(self, j):
        slot = self.slots[j % self.NSLOT]
        tag, spec, wid, nel = self.plan[j]
        if self.first[wid] == j:
            for dst, src in spec(slot):
                self.dma("pool", dst, src, group=("piece", j))
        else:
            self.dma("sp", slot[:, :nel], self.wscr[self.widx[wid], :, :nel], reads=[("scr", wid)])

    def dbg(self, name, ap, shape, dt=F32):
        if name not in self.debug:
            return
        if self.planning:
            return
        d = self.nc.dram_tensor("dbg_" + name, list(shape), dt, kind="ExternalOutput").ap()
        self.dbg_outs.append("dbg_" + name)
        self.dma("sp", d, ap)

    def setup(self, first_x=None):
        c = self.c
        di = self.din
        self.dma("sp", c["ident"][:], di["ident"])
        self.dma("sp", self.gains_fm[:], di["gains_fm"])
        if first_x is not None:
            self.load_x(first_x, [(j * 128, 128) for j in range(4)])
        for n, sh, dt in CONST_SPECS:
            if n == "ident":
                continue
            if n.endswith("_s"):
                continue
            if len(sh) == 3:
                self.dma("sp", c[n][:], di[n].rearrange("a p f -> p a f"))
            else:
                self.dma("sp", c[n][:], di[n])
        for n in ("fgain_bc", "hgain", "qgcol", "kgcol", "kgain_bc", "halo_bias"):
            self.dma("sp", getattr(self, n)[:], di[n])
        self.dma("sp", self.sinkT[:], di["sinkT"].rearrange("a g m -> g a m"))
        self.dma("pool", self.walpha[:], di["w_alpha"])
        self.dma("pool", self.balpha[:], di["b_alpha"])
        self.dma("pool", self.biasT[:, 0:2], di["biasp"].rearrange("a p j n -> p a j n"))
        self.act(self.esinkT[:], self.sinkT[:], AF.Exp)
        for t_ in self.vz_cur + [self.vz_prev] + self.vz_c:
            self.memset("pool", t_[:], 0.0)
        self.memset("pool", self.knT_prev[:], 0.0)
        self.memset("dve", self.S[0][:], 0.0)
        self.memset("dve", self.Sbf[0][:], 0.0)
        for t_ in self.ss:
            self.memset("dve", t_[:], 0.0)
        if self.debug.get("delay"):
            self.memset("pool", self.slots[0][:], 0.0)
            for _ in range(int(self.debug["delay"])):
                self.cp("pool", self.slots[1][:], self.slots[0][:])

    def load_x(self, src, tts):
        for i, (o, P) in enumerate(tts):
            self.dma("sp", self.xt[i][:P, :], src[o:o + P, :])

    def norm_hT(self, tts, Tn, gcol0, src=None):
        self.norm_stats(tts, src)
        self.norm_tr(tts, Tn, gcol0, subtile_major=True)

    def norm_stats(self, tts, src=None):
        xt, hb = (self.xt if src is None else src), self.hb
        for i, (o, P) in enumerate(tts):
            self.act(self.junk[:P, :], xt[i][:P, :], AF.Square, accum_out=self.ss[i][:P, 0:1])
            self.rsqrt(self.rstd[i][:P, 0:1], self.ss[i][:P, 0:1], 1.0 / D, EPS)
            self.ts("pool" if i % 2 else "dve", hb[i][:P, :], xt[i][:P, :], self.rstd[i][:P, 0:1], None, ALU.mult,
                    rkeys=K(xt[i][:], self.rstd[i][:]) + ["s_hball"], wkeys=[("hb", i)])

    def norm_tr(self, tts, Tn, gcol0, subtile_major=False):
        hb = self.hb
        ident = self.c["ident"]
        if subtile_major and len(tts) > 1:
            regs = []
            b0, b1 = self.pb(), self.pb()
            f0, f1 = self.pf()[:, :].bitcast(BF16), self.pf()[:, :].bitcast(BF16)
            for bank in (b0, b1, f0, f1):
                regs += [bank[:, 0:512], bank[:, 512:1024]]
            for i, (o, P) in enumerate(tts):
                for cc in range(8):
                    self.tr(regs[cc][:, o:o + P], hb[i][:P, cc * 128:(cc + 1) * 128], ident[:P, :P],
                            rkeys=K(ident[:]) + [("hb", i), "s_hball"])
            for cc in (0, 2, 1, 3, 4, 6, 5, 7):
                g = self.gains_fm[:, gcol0 + cc:gcol0 + cc + 1]
                if (cc // 2) % 2 == 0:
                    self.amul(self.hT[cc][:, :Tn], regs[cc][:, :Tn], g)
                else:
                    self.ts("dve", self.hT[cc][:, :Tn], regs[cc][:, :Tn], g, None, ALU.mult)
            return
        for cc in range(8):
            pT = self.pb()
            for i, (o, P) in enumerate(tts):
                self.tr(pT[:, o:o + P], hb[i][:P, cc * 128:(cc + 1) * 128], ident[:P, :P],
                        rkeys=K(ident[:]) + [("hb", i), "s_hball"])
            g = self.gains_fm[:, gcol0 + cc:gcol0 + cc + 1]
            if cc % 2 == 0:
                self.amul(self.hT[cc][:, :Tn], pT[:, :Tn], g)
            else:
                self.ts("dve", self.hT[cc][:, :Tn], pT[:, :Tn], g, None, ALU.mult)

    def ffn(self, tts, Tn, wup, wdn, gcol0, tagp, mid_hook=None, mid_hook2=None):
        if self.norm_done:
            self.norm_done = False
        else:
            self.norm_hT(tts, Tn, gcol0)
        hT = self.hT
        upv = wup.rearrange("(k p) (ab c) -> p k ab c", p=128, ab=2)
        for g in range(11):
            def spec(slot, g=g):
                sv = slot[:, :].rearrange("p (k ab c) -> p k ab c", k=8, ab=2)
                return [(sv[:, :, ab, :], upv[:, :, ab, g * 256:(g + 1) * 256]) for ab in range(2)]
            slot = self.wget((tagp, "up", g), spec, (tagp[-1], "up", g))
            wv = slot[:, :].rearrange("p (k ab c) -> p k ab c", k=8, ab=2)
            for u in range(2):
                i = 2 * g + u
                pa = self.pf()
                pb_ = self.pf()
                for k in range(8):
                    self.mm(pa[:, :Tn], wv[:, k, 0, u * 128:(u + 1) * 128], hT[k][:, :Tn], k == 0, k == 7)
                for k in range(8):
                    self.mm(pb_[:, :Tn], wv[:, k, 1, u * 128:(u + 1) * 128], hT[k][:, :Tn], k == 0, k == 7)
                sa = self.sa[i % 2]
                self.act(sa[:, :Tn], pa[:, :Tn], AF.Silu)
                self.tt("dve", self.actT[i][:, :Tn], sa[:, :Tn], pb_[:, :Tn], ALU.mult,
                        rkeys=K(sa[:], pb_[:]) + ["s_arena0"], wkeys=[("actT", i)])
        if mid_hook is not None:
            mid_hook()
        dnv = wdn.rearrange("(i p) n -> p i n", p=128)
        for nh in range(2):
            accs = [self.pf() for _ in tts]
            for cg, (c0, c1) in enumerate(((0, 8), (8, 16), (16, 22))):
                def spec(slot, nh=nh, c0=c0, c1=c1):
                    return [(slot[:, :(c1 - c0) * 512].rearrange("p (c n) -> p c n", n=512), dnv[:, c0:c1, nh * 512:(nh + 1) * 512])]
                slot = self.wget((tagp, "down", nh, cg), spec, (tagp[-1], "down", nh, cg), (c1 - c0) * 512)
                wv = slot[:, :(c1 - c0) * 512].rearrange("p (c n) -> p c n", n=512)
                for i, (o, P) in enumerate(tts):
                    for cc in range(c0, c1):
                        self.mm(accs[i][:P, :], self.actT[cc][:, o:o + P], wv[:, cc - c0, :], cc == 0, cc == 21,
                                rkeys=K(wv[:, 0, :]) + ["s_arena0", ("actT", cc)])
            for i, (o, P) in enumerate(tts):
                xs_ = self.xt[i][:P, nh * 512:(nh + 1) * 512]
                self.stt("dve", xs_, accs[i][:P, :], 0.5, xs_, ALU.mult, ALU.add)
            if nh == 0 and mid_hook2 is not None:
                mid_hook2()

    def win_piece(self, tag, c0, w):
        wv_ = self.din["w_in"].rearrange("(k p) c -> p k c", p=128)

        def spec(slot):
            return [(slot[:, :8 * w].rearrange("p (k c) -> p k c", k=8), wv_[:, :, c0:c0 + w])]
        slot = self.wget(tag, spec, tag[1:], 8 * w)
        return slot[:, :8 * w].rearrange("p (k c) -> p k c", k=8)

    def fm_proj(self, wv, col0, Tn, lhs_view=None):
        p = self.pf()
        for k in range(8):
            lhs = wv[:, k, col0:col0 + 128] if lhs_view is None else lhs_view(k)
            self.mm(p[:, :Tn], lhs, self.hT[k][:, :Tn], k == 0, k == 7)
        return p

    def tm_proj(self, wv, c0, w, o, P):
        p = self.pf()
        for k in range(8):
            self.mm(p[:P, :w], self.hT[k][:, o:o + P], wv[:, k, c0:c0 + w], k == 0, k == 7)
        return p

    def qknorm(self, ps, out, gcol, bo, eps, Tn, idx, ntt=None):
        sqsq, sqr, rs = self.sqsq[0], self.sqr[0], self.rsq[0]
        self.act(sqsq[:, :Tn], ps, AF.Square)
        pm = self.pf()
        self.mm(pm[:, :Tn], bo[:, :], sqsq[:, :Tn])
        self.rsqrt(rs[:, :Tn], pm[:, :Tn], 1.0, eps)
        if ntt is None:
            self.stt("dve", out, ps, gcol, rs[:, :Tn], ALU.mult, ALU.mult)
        else:
            self.stt("dve", out, ps.rearrange("p (t q) -> p t q", t=ntt), gcol, rs[:, :Tn].rearrange("p (t q) -> p t q", t=ntt), ALU.mult, ALU.mult)

    def mixer(self, tts, Tn, mode, tagp, seqs, tile_idx, last_tile):
        c = self.c
        sample = mode == "sample"
        L = tts[0][1]
        tri_i = c["tri_incl"]
        tri_s = c["tri_suf"]
        cmask = c["cmask4"]
        self.norm_hT(tts, Tn, 8)
        if mode == "pre" and self.next_x is not None:
            self.load_x(self.next_x, tts)
        hT = self.hT
        e1T, e2T, qeT, keT = self.e1T, self.e2T, self.qeT, self.keT
        wv = self.win_piece((tagp, "ga"), C_GA, 16)
        p = self.pf()
        for k in range(8):
            self.mm(p[:16, :Tn], wv[:, k, 0:16], hT[k][:, :Tn], k == 0, k == 7)
        self.cp("act", self.gaT[:, :Tn], p[:16, :Tn])
        wk = self.win_piece((tagp, "gk"), C_GK, 512)
        n_t = len(tts)
        pzs = []
        for i, (o, P) in enumerate(tts):
            pz = self.pf()
            self.mm(pz[:P, :], self.gaT[:, o:o + P], self.walpha[:, :], True, False)
            self.mm(pz[:P, :], c["ones_row"][0:1, :P], self.balpha[0:1, :], False, True)
            pzs.append(pz)
        for i, (o, P) in enumerate(tts):
            self.act(self.es[i][:P, :], pzs[i][:P, :], AF.Exp, scale=-1.0)
        for i, (o, P) in enumerate(tts):
            self.act(self.sp[i][:P, :], self.es[i][:P, :], AF.Ln, bias=1.0)
        for i, (o, P) in enumerate(tts):
            pk = self.tm_proj(wk, 0, 512, o, P)
            self.cp("dve", self.kd[i][:P, :], pk[:P, :])
        if mode != "pre":
            for hd in range(4):
                p = self.fm_proj(wk, hd * 128, Tn)
                self.cp("dve", keT[:, hd, :Tn], p[:, :Tn])
            wq = self.win_piece((tagp, "gq"), C_GQ, 512)
            for hd in range(4):
                p = self.fm_proj(wq, hd * 128, Tn)
                self.cp("dve", qeT[:, hd, :Tn], p[:, :Tn])
        pbts = []
        for i, (o, P) in enumerate(tts):
            pbt = self.pf()
            for hd in range(4):
                self.mm(pbt[:, hd * P:(hd + 1) * P], self.sp[i][:P, hd * 128:(hd + 1) * 128], tri_i[:P, :P])
            pbts.append(pbt)
        for i, (o, P) in enumerate(tts):
            pv3 = pbts[i][:, :4 * P].rearrange("p (h t) -> p h t", h=4)
            self.act(e1T[:, :, o:o + P], pv3, AF.Exp)
            if mode != "pre":
                self.act(e2T[:, :, o:o + P], pv3, AF.Exp, scale=-1.0)
        for pc in range(2):
            wvv = self.win_piece((tagp, "gv", pc), C_GV + pc * 512, 512)
            for i, (o, P) in enumerate(tts):
                p = self.tm_proj(wvv, 0, 512, o, P)
                self.cp("dve", self.v_tok[i][:P, pc * 512:(pc + 1) * 512], p[:P, :])
        psufs = []
        for i, (o, P) in enumerate(tts):
            psuf = self.pf()
            whole = (mode == "pre")
            self.mm(psuf[:P, :], tri_s[:P, :P], self.sp[i][:P, :], True, not (whole and i < n_t - 1))
            if whole:
                for j in range(i + 1, n_t):
                    self.mm(psuf[:P, :], c["tri_all"][:P, :P], self.sp[j][:P, :], False, j == n_t - 1)
            psufs.append(psuf)
        for i, (o, P) in enumerate(tts):
            self.act(self.es[i][:P, :], psufs[i][:P, :], AF.Exp)
        if mode != "pre":
            for pc in range(2):
                wg = self.win_piece((tagp, "gr", pc), C_GR + pc * 512, 512)
                for u in range(4):
                    p = self.fm_proj(wg, u * 128, Tn)
                    self.act(self.sgT[:, pc * 4 + u, :Tn], p[:, :Tn], AF.Silu)
        for i, (o, P) in enumerate(tts):
            self.tt("dve", self.kd[i][:P, :], self.kd[i][:P, :], self.es[i][:P, :], ALU.mult)
        if mode != "pre":
            for hd in range(4):
                self.tt("dve", keT[:, hd, :Tn], keT[:, hd, :Tn], e2T[:, hd, :Tn], ALU.mult)
            for hd in range(4):
                self.stt("dve", qeT[:, hd, :Tn], qeT[:, hd, :Tn], 128 ** -0.5, e1T[:, hd, :Tn], ALU.mult, ALU.mult)
        mstop = self.debug.get("mstop") if sample else None
        if mstop == "gpipe":
            return
        if mode == "pre":
            S, Sbf = seqs[0][2], seqs[0][3]
            pS = [self.pf(), self.pf()]
            for hd in range(4):
                for i, (o, P) in enumerate(tts):
                    self.mm(pS[hd // 2][:, (hd % 2) * 256:(hd % 2 + 1) * 256], self.kd[i][:P, hd * 128:(hd + 1) * 128],
                            self.v_tok[i][:P, hd * 256:(hd + 1) * 256], i == 0, i == n_t - 1)
            self.cp("dve", self.dtot[:, :, :], e1T[:, :, 127:128])
            for i in range(1, n_t):
                self.tt("dve", self.dtot[:, :, :], self.dtot[:, :, :], e1T[:, :, i * 128 + 127:i * 128 + 128], ALU.mult)
            for hd in range(4):
                self.stt("dve", S[:, hd, :], S[:, hd, :], self.dtot[:, hd, :], pS[hd // 2][:, (hd % 2) * 256:(hd % 2 + 1) * 256], ALU.mult, ALU.add)
            self.cp("act", Sbf[:, :, :], S[:, :, :])
        elif len(seqs) == 1:
            self.gla_pipelined(tts, seqs[0], cmask)
        else:
            for i, (o, P) in enumerate(tts):
                self.gla_chunk(i, o, P, mode, seqs, cmask)
        if mstop == "gla":
            return
        if mode == "full" and tile_idx == 0:
            self.dbg("oaT", self.sgT[:, :, :], [128, 8, T], BF16)
            self.dbg("qeT", self.qeT, [128, 4, T], BF16)
            self.dbg("keT", self.keT[:, :, :], [128, 4, T], BF16)
            self.dbg("e1T", self.e1T, [128, 4, T], F32)
            self.dbg("kd3", self.kd[3], [128, 512], BF16)
            self.dbg("vtok3", self.v_tok[3], [128, 1024], BF16)
            self.dbg("S", self.S[0][:, :, :], [128, 4, 256], F32)
        need_kv = (mode != "pre") or last_tile
        if mode != "pre":
            ntt = len(tts)
            w_in_v = self.din["w_in"].rearrange("(k p) c -> p k c", p=128)
            for pc in range(2):
                def spec(slot, pc=pc):
                    sv = slot[:, :].rearrange("p (k g two d) -> p k g two d", k=8, g=4, two=2)
                    return [(sv[:, k, :, two, :], w_in_v[:, k, C_SQ + pc * 512 + two * 256:C_SQ + pc * 512 + (two + 1) * 256].rearrange("p (g d) -> p g d", g=4))
                            for two in range(2) for k in range(8)]
                wq = self.wget((tagp, "sq", pc), spec, ("sq", pc))[:, :].rearrange("p (k c) -> p k c", k=8)
                qv = self.qo[pc][:, :ntt * 4 * L].rearrange("p (t g q) -> p t g q", t=ntt, g=4)
                for g in range(4):
                    p = self.fm_proj(wq, g * 128, Tn)
                    self.qknorm(p[:, :Tn], qv[:, :, g, :], self.qgcol[:, 0:1], c["bo_q"], 64 * EPS, Tn, g, ntt)
        if need_kv:
            wkv = self.win_piece((tagp, "sksv"), C_SK, 512)
            for P2 in range(2):
                p = self.fm_proj(wkv, P2 * 128, Tn)
                self.qknorm(p[:, :Tn], self.knT_cur[:, P2, :Tn], self.kgcol[:, 0:1], c["bo_k"], EPS, Tn, P2)
            for i, (o, P) in enumerate(tts):
                pvv = self.tm_proj(wkv, 256, 256, o, P)
                vz4 = self.vz_cur[i][:, :].rearrange("p (jp par c) -> p jp par c", jp=2, par=2)
                pv4 = pvv[:, :256].rearrange("p (jp par d) -> p jp par d", jp=2, par=2)
                for par in range(2):
                    self.cp("act", vz4[:P, :, par, par * 64:(par + 1) * 64], pv4[:P, :, par, :])
                want_out = sample or (mode == "full" and last_tile and i == len(tts) - 1)
                if want_out:
                    self.cp("act", self.vnew[:P, :], pvv[:P, :256])
                    pkk = self.tm_proj(wkv, 0, 256, o, P)
                    self.act(self.ksq[:P, :], pkk[:P, :256], AF.Square)
                    self.add("dve", lambda h, P=P: h.tensor_reduce(out=self.kss[:P, 0:4], in_=self.ksq[:P, :].rearrange("p (j d) -> p j d", j=4),
                                                                    axis=AX.X, op=ALU.add), K(self.ksq[:]), K(self.kss[:]))
                    self.rsqrt(self.kss[:P, :], self.kss[:P, :], 1.0 / 64, EPS)
                    k3 = self.ktok[:P, :].rearrange("p (j d) -> p j d", j=4)
                    self.tt("dve", k3, pkk[:P, :256].rearrange("p (j d) -> p j d", j=4),
                            self.kss[:P, 0:4].unsqueeze(2).to_broadcast([P, 4, 64]), ALU.mult)
                    self.tt("dve", self.knew[:P, :].rearrange("p (j d) -> p j d", j=4), k3,
                            self.kgain_bc[:P, :].unsqueeze(1).to_broadcast([P, 4, 64]), ALU.mult)
                    if sample:
                        self.dma("sp", self.dout["nks"], self.knew[:P, :])
                        self.dma("sp", self.dout["nvs"], self.vnew[:P, :])
                    else:
                        self.dma("sp", self.dout["nk"], self.knew[:P, :])
                        self.dma("sp", self.dout["nv"], self.vnew[:P, :])
        if mode == "pre":
            if last_tile:
                self.cp("pool", self.knT_prev[:, :, :], self.knT_cur[:, :, Tn - 128:Tn])
                self.cp("pool", self.vz_prev[:, :], self.vz_cur[len(tts) - 1][:, :])
            if self.next_x is not None:
                self.norm_hT(tts, Tn, 0)
                self.norm_done = True
            return
        if mstop == "swaproj":
            return
        gate_groups = []
        gstate = {}
        for nm, c0, dst in (("gta", C_GTA, self.sigA), ("gtb", C_GTB, self.sigB)):
            for pc in range(2):
                for u in range(4):
                    def grp(nm=nm, c0=c0, dst=dst, pc=pc, u=u):
                        if u == 0:
                            gstate["w"] = self.win_piece((tagp, nm, pc), c0 + pc * 512, 512)
                        p = self.fm_proj(gstate["w"], u * 128, Tn)
                        self.act(dst[:, pc * 4 + u, :Tn], p[:, :Tn], AF.Tanh, scale=0.5)
                    gate_groups.append(grp)

        def run_gates(k):
            for _ in range(min(k, len(gate_groups))):
                gate_groups.pop(0)()
        if sample:
            kts = []
            for sq_ in range(2):
                kts.append(dict(knT=lambda P2, rows, sq_=sq_: self.knT_c[sq_][rows, P2, :], vz=self.vz_c[sq_], bias=self.biasTs[:, sq_, :, :], nk=128, hb=None))
            kts.append(dict(knT=lambda P2, rows: self.knT_cur[rows, P2, 0:64], vz=self.vz_cur[0], bias=self.biasTs[:, 2, :, :], nk=64, hb=None))
            self.swa_block(0, 0, 1, 64, kts, c["g4s"], between=lambda: run_gates(2))
        else:
            for i, (o, P) in enumerate(tts):
                kts = []
                if i == 0:
                    hb = self.halo_bias[:, 0:1] if tile_idx == 0 else None
                    kts.append(dict(knT=lambda P2, rows: self.knT_prev[rows, P2, :], vz=self.vz_prev, bias=self.biasT[:, 0, :, :], nk=128, hb=hb))
                else:
                    kts.append(dict(knT=lambda P2, rows, o=o: self.knT_cur[rows, P2, o - 128:o], vz=self.vz_cur[i - 1], bias=self.biasT[:, 0, :, :], nk=128, hb=None))
                kts.append(dict(knT=lambda P2, rows, o=o: self.knT_cur[rows, P2, o:o + 128], vz=self.vz_cur[i], bias=self.biasT[:, 1, :, :], nk=128, hb=None))
                self.swa_block(i, o, len(tts), 128, kts, c["g4"], between=lambda: run_gates(2))
            self.cp("pool", self.knT_prev[:, :, :], self.knT_cur[:, :, Tn - 128:Tn])
            self.cp("pool", self.vz_prev[:, :], self.vz_cur[len(tts) - 1][:, :])
        if mode == "full" and tile_idx == 0:
            self.dbg("osT0", self.osT[0], [128, 4, T], BF16)
            self.dbg("osT1", self.osT[1], [128, 4, T], BF16)
            self.dbg("knT", self.knT_cur[:, :, :], [128, 2, T], BF16)
        if mstop == "swa":
            return
        run_gates(len(gate_groups))
        if mode == "full" and tile_idx == 0:
            self.dbg("sgA", self.sigA, [128, 8, T], BF16)
            self.dbg("sgB", self.sigB, [128, 8, T], BF16)
        wbr = self.din["w_branch"]
        wbA = wbr[0:1024, :].rearrange("(f p) n -> p f n", p=128)
        wbB = wbr[1024:2048, :].rearrange("(P j g d) n -> d P j g n", P=2, j=2, g=4)
        for ng in range(2):
            def specA(slot, ng=ng):
                return [(slot[:, :].rearrange("p (f n) -> p f n", f=8), wbA[:, :, ng * 512:(ng + 1) * 512])]

            def specB(slot, ng=ng):
                return [(slot[half * 64:(half + 1) * 64, :].rearrange("p (P g n) -> p P g n", P=2, g=4)[:, P2],
                         wbB[:, P2, half, :, ng * 512:(ng + 1) * 512]) for half in range(2) for P2 in range(2)]
            sA = self.wget((tagp, "wbA", ng), specA, ("wbA", ng))[:, :].rearrange("p (f n) -> p f n", f=8)
            sB = self.wget((tagp, "wbB", ng), specB, ("wbB", ng))[:, :].rearrange("p (f n) -> p f n", f=8)
            for u in range(4):
                n = ng * 4 + u
                pa = self.pf()
                for f in range(8):
                    self.mm(pa[:, :Tn], sA[:, f, u * 128:(u + 1) * 128], self.sgT[:, f, :Tn], f == 0, f == 7)
                pb_ = self.pf()
                for f in range(8):
                    self.mm(pb_[:, :Tn], sB[:, f, u * 128:(u + 1) * 128], self.osT[f // 4][:, f % 4, :Tn], f == 0, f == 7)
                t0, t1 = self.mtmp
                self.stt("dve", t0[:, :Tn], self.sigA[:, n, :Tn], 1.0, pa[:, :Tn], ALU.add, ALU.mult)
                self.stt("dve", t1[:, :Tn], self.sigB[:, n, :Tn], 1.0, pb_[:, :Tn], ALU.add, ALU.mult)
                self.tt("pool", self.sigA[:, n, :Tn], t0[:, :Tn], t1[:, :Tn], ALU.add)
        if mode == "full" and tile_idx == 0:
            self.dbg("mT", self.sigA, [128, 8, T], BF16)
        wo = self.din["w_out"].rearrange("(f p) n -> p f n", p=128)
        for nh in range(2):
            def spec(slot, nh=nh):
                return [(slot[:, :].rearrange("p (f n) -> p f n", f=8), wo[:, :, nh * 512:(nh + 1) * 512])]
            so = self.wget((tagp, "wo", nh), spec, ("wo", nh))[:, :].rearrange("p (f n) -> p f n", f=8)
            for i, (o, P) in enumerate(tts):
                p = self.pf()
                for f in range(8):
                    self.mm(p[:P, :], self.sigA[:, f, o:o + P], so[:, f, :], f == 0, f == 7)
                xs_ = self.xt[i][:P, nh * 512:(nh + 1) * 512]
                self.stt("dve", xs_, p[:P, :], 0.5, xs_, ALU.mult, ALU.add)

    def gla_pipelined(self, tts, seq, cmask):
        c = self.c
        e1T, qeT, keT = self.e1T, self.qeT, self.keT
        so, nreal, S, Sbf0 = seq
        Sb = [Sbf0, self.Sbf_alt]
        L = tts[0][1]
        n_t = len(tts)
        assert n_t % 2 == 0

        def front(i, o):
            pS = [self.pf(), self.pf()]
            for hd in range(4):
                self.mm(pS[hd // 2][:, (hd % 2) * 256:(hd % 2 + 1) * 256], self.kd[i][:L, hd * 128:(hd + 1) * 128],
                        self.v_tok[i][:L, hd * 256:(hd + 1) * 256])
            pA = self.pf()
            for hd in range(4):
                self.mm(pA[:L, hd * L:(hd + 1) * L], keT[:, hd, o:o + L], qeT[:, hd, o:o + L])
            atm = self.ATm2[i % 2]
            self.tt("dve", atm[:L, :4 * L], pA[:L, :4 * L], cmask[:L, :4 * L], ALU.mult)
            dcol = o + L - 1
            for hd in range(4):
                self.stt("dve", S[:, hd, :], S[:, hd, :], e1T[:, hd, dcol:dcol + 1], pS[hd // 2][:, (hd % 2) * 256:(hd % 2 + 1) * 256], ALU.mult, ALU.add)
            self.cp("act", Sb[(i + 1) % 2][:, :, :], S[:, :, :])
            return atm

        def tail(j):
            oj = tts[j][0]
            o_raw, osq = self.o_raw2[j % 2], self.osq2[j % 2]
            pM = self.pf()
            for hd in range(4):
                self.mm(pM[:, hd * L:(hd + 1) * L], c["ones_mean"][:, :], osq[:, 2 * hd, :L], True, False)
                self.mm(pM[:, hd * L:(hd + 1) * L], c["ones_mean"][:, :], osq[:, 2 * hd + 1, :L], False, True)
            self.rsqrt(self.rstdg[:, :4 * L], pM[:, :4 * L], 1.0, EPS)
            r3 = self.rstdg[:, :4 * L].rearrange("p (h l) -> p h l", h=4)
            o4 = o_raw[:, :, :].rearrange("p (h v) l -> p h v l", v=2)
            for vc in range(2):
                self.stt("dve", o4[:, :, vc, :L], o4[:, :, vc, :L], self.hgain[:, vc:vc + 1], r3, ALU.mult, ALU.mult)
            self.tt("dve", self.sgT[:, :, oj:oj + L], o_raw[:, :, :L], self.sgT[:, :, oj:oj + L], ALU.mult)

        nxt = front(0, tts[0][0])
        for i, (o, P) in enumerate(tts):
            atm = nxt
            Sbf = Sb[i % 2]
            pO = [self.pf(), self.pf()]
            for hd in range(4):
                for vc in range(2):
                    reg = pO[hd // 2][:, ((hd % 2) * 2 + vc) * L:((hd % 2) * 2 + vc + 1) * L]
                    self.mm(reg, self.v_tok[i][:L, hd * 256 + vc * 128:hd * 256 + (vc + 1) * 128], atm[:L, hd * L:(hd + 1) * L], True, False)
                    self.mm(reg, Sbf[:, hd, vc * 128:(vc + 1) * 128], qeT[:, hd, o:o + L], False, True)
            o_raw, osq = self.o_raw2[i % 2], self.osq2[i % 2]
            for b in range(2):
                pv = pO[b][:, :4 * L].rearrange("p (c l) -> p c l", c=4)
                self.cp("act", o_raw[:, 4 * b:4 * b + 4, :L], pv)
                self.act(osq[:, 4 * b:4 * b + 4, :L], pv, AF.Square)
            if i + 1 < n_t:
                nxt = front(i + 1, tts[i + 1][0])
            if i >= 1:
                tail(i - 1)
        tail(n_t - 1)

    def gla_chunk(self, i, o, L, mode, seqs, cmask):
        c = self.c
        e1T, qeT, keT = self.e1T, self.qeT, self.keT
        nseq = len(seqs)
        blk = L // nseq
        def state_mm(si):
            so = seqs[si][0]
            pS = [self.pf(), self.pf()]
            for hd in range(4):
                self.mm(pS[hd // 2][:, (hd % 2) * 256:(hd % 2 + 1) * 256], self.kd[i][so:so + blk, hd * 128:(hd + 1) * 128],
                        self.v_tok[i][so:so + blk, hd * 256:(hd + 1) * 256])
            return pS
        hoist = nseq == 1
        pSs = [state_mm(0)] if hoist else None
        pA = self.pf()
        for hd in range(4):
            self.mm(pA[:L, hd * L:(hd + 1) * L], keT[:, hd, o:o + L], qeT[:, hd, o:o + L])
        self.tt("dve", self.ATm[:L, :4 * L], pA[:L, :4 * L], cmask[:L, :4 * L], ALU.mult)
        pO = [self.pf(), self.pf()]
        for hd in range(4):
            for vc in range(2):
                reg = pO[hd // 2][:, ((hd % 2) * 2 + vc) * L:((hd % 2) * 2 + vc + 1) * L]
                self.mm(reg, self.v_tok[i][:L, hd * 256 + vc * 128:hd * 256 + (vc + 1) * 128], self.ATm[:L, hd * L:(hd + 1) * L], True, False)
                for si, (so, nreal, S, Sbf) in enumerate(seqs):
                    self.mm(reg[:, so:so + blk], Sbf[:, hd, vc * 128:(vc + 1) * 128], qeT[:, hd, o + so:o + so + blk], False, si == nseq - 1)
        for si, (so, nreal, S, Sbf) in enumerate(seqs):
            pS = pSs[si] if hoist else state_mm(si)
            dcol = o + so + nreal - 1
            for hd in range(4):
                self.stt("dve", S[:, hd, :], S[:, hd, :], e1T[:, hd, dcol:dcol + 1], pS[hd // 2][:, (hd % 2) * 256:(hd % 2 + 1) * 256], ALU.mult, ALU.add)
            self.cp("act", Sbf[:, :, :], S[:, :, :])
        for b in range(2):
            pv = pO[b][:, :4 * L].rearrange("p (c l) -> p c l", c=4)
            self.cp("act", self.o_raw[:, 4 * b:4 * b + 4, :L], pv)
            self.act(self.osq[:, 4 * b:4 * b + 4, :L], pv, AF.Square)
        pM = self.pf()
        for hd in range(4):
            self.mm(pM[:, hd * L:(hd + 1) * L], c["ones_mean"][:, :], self.osq[:, 2 * hd, :L], True, False)
            self.mm(pM[:, hd * L:(hd + 1) * L], c["ones_mean"][:, :], self.osq[:, 2 * hd + 1, :L], False, True)
        self.rsqrt(self.rstdg[:, :4 * L], pM[:, :4 * L], 1.0, EPS)
        r3 = self.rstdg[:, :4 * L].rearrange("p (h l) -> p h l", h=4)
        o4 = self.o_raw[:, :, :].rearrange("p (h v) l -> p h v l", v=2)
        for vc in range(2):
            self.stt("dve", o4[:, :, vc, :L], o4[:, :, vc, :L], self.hgain[:, vc:vc + 1], r3, ALU.mult, ALU.mult)
        self.tt("dve", self.sgT[:, :, o:o + L], self.o_raw[:, :, :L], self.sgT[:, :, o:o + L], ALU.mult)

    def swa_block(self, ti, q0, ntt, nq, kts, g4, between=None):
        c = self.c
        N = 4 * nq
        for P2 in range(2):
            if between is not None:
                between()
            pts = []
            for kt in kts:
                nk = kt["nk"]
                pss = []
                for half in range(2):
                    rows = slice(half * 64, half * 64 + 64)
                    rhs_q = self.qo[P2][rows, ti * 4 * nq:(ti + 1) * 4 * nq]
                    ps = self.pf()
                    self.mm(ps[:nk, :N], kt["knT"](P2, rows), rhs_q, True, False)
                    pss.append(ps)
                for half in range(2):
                    self.mm(pss[half][:nk, :N], c["ident"][:, :nk], kt["bias"][:, 2 * P2 + half, :N], False, True)
                for half in range(2):
                    ps = pss[half]
                    self._pt += 1
                    pt = self.PT[self._pt % 6]
                    if kt["hb"] is not None:
                        self.act(pt[:nk, :N], ps[:nk, :N], AF.Exp, bias=kt["hb"][:nk, :])
                    else:
                        self.act(pt[:nk, :N], ps[:nk, :N], AF.Exp)
                    pts.append((pt, kt, half))
            pO = self.pf()
            pD = self.pf()
            n = len(pts)
            for idx, (pt, kt, half) in enumerate(pts):
                nk = kt["nk"]
                vz4 = kt["vz"][:, :].rearrange("p (j c) -> p j c", j=4)
                self.mm(pO[:, :N], vz4[:nk, 2 * P2 + half, :], pt[:nk, :N], idx == 0, idx == n - 1)
            for idx, (pt, kt, half) in enumerate(pts):
                nk = kt["nk"]
                self.mm(pD[:, :N], c["onesH"][:nk, half, :], pt[:nk, :N], idx == 0, False)
            self.mm(pD[:, :N], self.esinkT[0:4, P2, :], g4[0:4, :N], False, True)
            self.add("dve", lambda h, pD=pD, N=N: h.reciprocal(out=self.rden[:, :N], in_=pD[:, :N]), K(pD[:]), K(self.rden[:]))
            self.tt("dve", self.osT[P2][:, :, q0:q0 + nq], pO[:, :N].rearrange("p (g q) -> p g q", g=4),
                    self.rden[:, :N].rearrange("p (g q) -> p g q", g=4), ALU.mult)

    def final_out(self, tts, dst):
        for i, (o, P) in enumerate(tts):
            self.act(self.junk[:P, :], self.xt[i][:P, :], AF.Square, accum_out=self.ss[i][:P, 0:1])
            self.rsqrt(self.rstd[i][:P, 0:1], self.ss[i][:P, 0:1], 1.0 / D, EPS)
            self.ts("dve", self.xt[i][:P, :], self.xt[i][:P, :], self.rstd[i][:P, 0:1], None, ALU.mult)
            self.tt("pool" if i % 2 else "dve", self.xt[i][:P, :], self.xt[i][:P, :], self.fgain_bc[:P, :], ALU.mult)
            self.dma("sp", dst[o:o + P, :], self.xt[i][:P, :])

    def body(self):
        di = self.din
        self.reset_counters()
        pre_on = not self.debug.get("skip_pre") and not self.debug.get("only_sample")
        self.setup(first_x=di["xprev"][0:T, :] if pre_on else None)
        tts = [(j * 128, 128) for j in range(4)]
        seqs_p = [(0, 128, self.S[0], self.Sbf[0])]
        stop = self.debug.get("stop")
        if not self.debug.get("skip_pre") and not self.debug.get("only_sample"):
            for t in range(NTILE):
                self.next_x = di["xprev"][(t + 1) * T:(t + 2) * T, :] if t + 1 < NTILE else di["xp"][0:T, :]
                self.ffn(tts, T, di["ffn1_up"], di["ffn1_down"], 0, ("pre", t, "f1"))
                self.mixer(tts, T, "pre", ("pre", t), seqs_p, t, t == NTILE - 1)
            self.next_x = None
        pre_ran = not self.debug.get("skip_pre") and not self.debug.get("only_sample")
        for t in range(NTILE if not self.debug.get("only_sample") else 0):
            if self.x_loaded:
                self.x_loaded = False
            elif not (t == 0 and pre_ran):
                self.load_x(di["xp"][t * T:(t + 1) * T, :], tts)
            self.ffn(tts, T, di["ffn1_up"], di["ffn1_down"], 0, ("own", t, "f1"))
            if stop == "ffn1":
                self.final_dbg_x(tts, t)
                continue
            self.mixer(tts, T, "full", ("own", t), seqs_p, t, t == NTILE - 1)
            if stop == "mixer":
                self.final_dbg_x(tts, t)
                continue
            if t == NTILE - 1 and not self.debug.get("no_sample") and not self.debug.get("no_ffn2"):
                self.dma("sp", self.dout["ng"].rearrange("h k v -> k h v"), self.S[0][:, :, :])
                self._ng_done = True
                self.sample_setup_early()
            staged = False
            if not self.debug.get("no_ffn2"):
                hook = None
                smp_next = (t == NTILE - 1 and not self.debug.get("no_sample") and not self.debug.get("no_final") and not stop)
                if smp_next:
                    tss_ = [(0, TS)]
                    xst = [self.arena1[:, 0:2048].bitcast(F32)]
                    self.dma("sp", xst[0][:TS, :], di["xs"])

                    def hook(xst=xst, tss_=tss_):
                        self.norm_stats(tss_, src=xst)

                    def hook2(tss_=tss_):
                        self.norm_tr(tss_, TS, 0)
                    staged = True
                if t + 1 < NTILE and not self.debug.get("no_final"):
                    xst = [self.arena1[:, i * 2048:(i + 1) * 2048].bitcast(F32) for i in range(4)]
                    nsrc = di["xp"][(t + 1) * T:(t + 2) * T, :]
                    for i, (o, P) in enumerate(tts):
                        self.dma("sp", xst[i][:P, :], nsrc[o:o + P, :])

                    def hook(xst=xst):
                        self.norm_stats(tts, src=xst)

                    def hook2():
                        self.norm_tr(tts, T, 0)
                    staged = True
                self.ffn(tts, T, di["ffn2_up"], di["ffn2_down"], 16, ("own", t, "f2"), mid_hook=hook, mid_hook2=hook2 if hook else None)
            if self.debug.get("no_final"):
                self.final_dbg_x(tts, t)
            else:
                self.final_out(tts, self.dout["y"][t * T:(t + 1) * T, :])
            if staged:
                for i, (o, P) in enumerate(tss_ if smp_next else tts):
                    self.cp("pool", self.xt[i][:P, :], xst[i][:P, :])
                self.norm_done = True
                self.x_loaded = True
        if not self._ng_done:
            self.dma("sp", self.dout["ng"].rearrange("h k v -> k h v"), self.S[0][:, :, :])
        if stop or self.debug.get("no_sample"):
            return
        if not self._early_done:
            self.sample_setup_early()
        tss = [(0, TS)]
        self.dma("sp", self.S[1][:, :, :], di["st"][1].rearrange("h k v -> k h v"))
        self.cp("pool", self.Sbf[1][:, :, :], self.S[1][:, :, :])
        seqs_s = [(0, 16, self.S[0], self.Sbf[0]), (32, 16, self.S[1], self.Sbf[1])]
        if self.x_loaded:
            self.x_loaded = False
        else:
            self.load_x(di["xs"], tss)
        self.sample_rest(tss, seqs_s)

    def sample_setup_early(self):
        di = self.din
        self._early_done = True
        self.dma("pool", self.biasTs, di["biass"].rearrange("a p j n -> p a j n"))
        for n in ("tri_incl", "tri_suf", "cmask4"):
            self.dma("sp", self.c[n][:], di[n + "_s"])
        self.dma("sp", self.S[0][:, :, :], di["st"][0].rearrange("h k v -> k h v"))
        self.cp("pool", self.Sbf[0][:, :, :], self.S[0][:, :, :])
        for sq_ in range(2):
            self.dma("sp", self.ckf[:, :], di["ck"][sq_])
            self.cp("dve", self.ckb[:, :], self.ckf[:, :])
            for P2 in range(2):
                pT = self.pb()
                self.tr(pT[:, 0:128], self.ckb[:, P2 * 128:(P2 + 1) * 128], self.c["ident"][:, :])
                self.cp("act", self.knT_c[sq_][:, P2, :], pT[:, 0:128])
            self.dma("sp", self.ckf[:, :], di["cv"][sq_])
            vz4 = self.vz_c[sq_][:, :].rearrange("p (jp par c) -> p jp par c", jp=2, par=2)
            cv4 = self.ckf[:, :].rearrange("p (jp par d) -> p jp par d", jp=2, par=2)
            for par in range(2):
                self.cp("dve", vz4[:, :, par, par * 64:(par + 1) * 64], cv4[:, :, par, :])

    def sample_rest(self, tss, seqs_s):
        di = self.din
        sstop = self.debug.get("sstop")
        self.ffn(tss, TS, di["ffn1_up"], di["ffn1_down"], 0, ("smp", "f1"))
        if sstop != "ffn1":
            self.mixer(tss, TS, "sample", ("smp",), seqs_s, 0, True)
            if sstop != "mixer":
                self.ffn(tss, TS, di["ffn2_up"], di["ffn2_down"], 16, ("smp", "f2"))
        self.final_out(tss, self.dout["ys"])
        for sq_ in range(2):
            self.dma("sp", self.dout["ngs"][sq_].rearrange("h k v -> k h v"), self.S[sq_][:, :, :])

    def final_dbg_x(self, tts, t):
        for i, (o, P) in enumerate(tts):
            self.dma("sp", self.dout["y"][t * T + o:t * T + o + P, :], self.xt[i][:P, :])

    def build(self):
        self.planning = True
        self.body()
        self.planning = False
        self.nuse = {}
        for pl in self.plan:
            self.nuse[pl[2]] = self.nuse.get(pl[2], 0) + 1
        self.body()
        self.s.emit()
        return self.nc


_CACHE = {}


def _get_prog(debug=None):
    key = repr(sorted((debug or {}).items()))
    if key not in _CACHE:
        p = Prog(debug)
        p.build()
        _CACHE[key] = p
    return _CACHE[key]


def make_in_maps(inp):
    f = lambda a: np.ascontiguousarray(np.asarray(a, dtype=np.float32))
    xp = f(inp["x_prompt"])
    xsm = f(inp["x_sample"])
    ck = f(inp["cache_swa_k"])[0].reshape(16, 128, 256)
    cv = f(inp["cache_swa_v"])[0].reshape(16, 128, 256)
    st = f(inp["state_gla"])[0]
    consts = _consts()
    bp, bs = _bias_tables(inp["rel_bias"])
    gains = np.stack([f(inp["ffn1_norm"])[0], f(inp["mix_norm"])[0], f(inp["ffn2_norm"])[0]])
    gains_fm = np.ascontiguousarray(gains.reshape(3, 8, 128).transpose(2, 0, 1).reshape(128, 24))
    fg = np.ascontiguousarray(np.broadcast_to(f(inp["final_norm"])[0][None, :], (128, D)))
    hg = np.ascontiguousarray(f(inp["gla_head_norm"])[0].reshape(2, 128).T)
    qg = np.ascontiguousarray(np.tile(f(inp["q_norm"])[0], 2)[:, None])
    kg = np.ascontiguousarray(np.tile(f(inp["k_norm"])[0], 2)[:, None])
    kgb = np.ascontiguousarray(np.broadcast_to(f(inp["k_norm"])[0][None, :], (128, 64)))
    sinks = f(inp["attn_sinks"])[0]
    sinkT = np.zeros((2, 4, 128), np.float32)
    for P2 in range(2):
        for g in range(4):
            sinkT[P2, g, :64] = sinks[4 * (2 * P2) + g]
            sinkT[P2, g, 64:] = sinks[4 * (2 * P2 + 1) + g]
    shared = {
        "ffn1_up": f(inp["ffn1_w_up"])[0], "ffn1_down": f(inp["ffn1_w_down"])[0],
        "ffn2_up": f(inp["ffn2_w_up"])[0], "ffn2_down": f(inp["ffn2_w_down"])[0],
        "w_in": f(inp["w_in"])[0], "w_branch": f(inp["w_branch"])[0], "w_out": f(inp["w_out"])[0],
        "w_alpha": f(inp["gla_w_alpha"])[0], "b_alpha": f(inp["gla_b_alpha"]).reshape(1, 512),
        "gains_fm": gains_fm, "fgain_bc": fg, "hgain": hg, "qgcol": qg, "kgcol": kg, "kgain_bc": kgb,
        "sinkT": sinkT, "biasp": bp, "biass": bs,
    }
    shared.update(consts)
    maps = []
    zeros_prev = np.zeros((TOKC, D), np.float32)
    for cidx in range(NCORES):
        b, half = cidx // 2, cidx % 2
        m = dict(shared)
        m["xp"] = np.ascontiguousarray(xp[b, half * TOKC:(half + 1) * TOKC])
        m["xprev"] = np.ascontiguousarray(xp[b, 0:TOKC]) if half == 1 else zeros_prev
        xs_ = np.zeros((TS, D), np.float32)
        for s_ in range(2):
            xs_[32 * s_:32 * s_ + 16] = xsm[2 * cidx + s_]
        m["xs"] = xs_
        m["ck"] = np.ascontiguousarray(ck[2 * cidx:2 * cidx + 2])
        m["cv"] = np.ascontiguousarray(cv[2 * cidx:2 * cidx + 2])
        m["st"] = np.ascontiguousarray(st[2 * cidx:2 * cidx + 2])
        m["halo_bias"] = np.full((128, 1), 0.0 if half == 1 else NEG, np.float32)
        maps.append(m)
    return maps


def run(inp, debug=None):
    prog = _get_prog(debug)
    maps = make_in_maps(inp)
    res = run_bass_kernel_spmd(prog.nc, maps, core_ids=list(range(NCORES)))
    return prog, res


def kernel(**inp):
    prog, res = run(inp)
    R = res.results
    y = np.zeros((4, 4096, D), np.float32)
    ys = np.zeros((16, 16, D), np.float32)
    nk = np.zeros((1, 4, 128, 4, 64), np.float32)
    nv = np.zeros((1, 4, 128, 4, 64), np.float32)
    ng = np.zeros((1, 4, 4, 128, 256), np.float32)
    nks = np.zeros((1, 16, 16, 4, 64), np.float32)
    nvs = np.zeros((1, 16, 16, 4, 64), np.float32)
    ngs = np.zeros((1, 16, 4, 128, 256), np.float32)
    for cidx in range(NCORES):
        b, half = cidx // 2, cidx % 2
        r = R[cidx]
        y[b, half * TOKC:(half + 1) * TOKC] = r["y"]
        if half == 1:
            nk[0, b] = r["nk"].reshape(128, 4, 64)
            nv[0, b] = r["nv"].reshape(128, 4, 64)
            ng[0, b] = r["ng"]
        for s_ in range(2):
            ys[2 * cidx + s_] = r["ys"][32 * s_:32 * s_ + 16]
            nks[0, 2 * cidx + s_] = r["nks"][32 * s_:32 * s_ + 16].reshape(16, 4, 64)
            nvs[0, 2 * cidx + s_] = r["nvs"][32 * s_:32 * s_ + 16].reshape(16, 4, 64)
            ngs[0, 2 * cidx + s_] = r["ngs"][s_]
    return (y, ys, nk, nv, ng, nks, nvs, ngs)
```

```python
import numpy as np
import ml_dtypes
import concourse.bass as bass
import concourse.mybir as mybir
from concourse.bass_utils import run_bass_kernel_spmd

F32 = mybir.dt.float32
BF16 = mybir.dt.bfloat16
AF = mybir.ActivationFunctionType
ALU = mybir.AluOpType
AX = mybir.AxisListType

ENGS = ("pe", "act", "dve", "pool", "sp")
NCORES = 8
D = 1024
DFF = 2816
TOKC = 2048
T = 512
NTILE = TOKC // T
TS = 64
EPS = 1e-6
NEG = -30000.0
IN_W = 6672
C_GQ, C_GK, C_GV, C_GR, C_GA, C_SQ, C_SK, C_SV, C_GTA, C_GTB = 0, 512, 1024, 2048, 3072, 3088, 4112, 4368, 4624, 5648


class Op:
    __slots__ = ("eng", "fn", "deps", "sig", "semval", "dma", "dsem", "dval", "idx", "group")


class Sched:
    def __init__(self, nc, n_dma_sems=48):
        self.nc = nc
        self.ops = []
        self.by_eng = {e: [] for e in ENGS}
        self.last_w = {}
        self.readers = {}
        self.esem = {e: nc.alloc_semaphore(name="cnt_" + e) for e in ENGS}
        self.dsems = [nc.alloc_semaphore(name="dma%d" % i) for i in range(n_dma_sems)]
        self.dcount = [0] * n_dma_sems
        self.dlast = [None] * n_dma_sems
        self.dnext = {"sw": 0, "hw": 0}

    def add(self, eng, fn, reads=(), writes=(), dma=False, group=None):
        op = Op()
        op.group = group
        op.eng = eng
        op.fn = fn
        op.sig = False
        op.semval = None
        op.dma = dma
        op.idx = len(self.ops)
        deps = {}

        def dep(d, raw):
            if d is None:
                return
            if not d.dma and not dma and d.eng == eng:
                if eng == "pe" or not raw:
                    return
            deps[d.idx] = d

        for k in reads:
            for w in self.last_w.get(k, ()):
                dep(w, True)
        for k in writes:
            for w in self.last_w.get(k, ()):
                if group is not None and w.group == group:
                    continue
                dep(w, False)
            for r in self.readers.get(k, ()):
                dep(r, False)
        if dma:
            half = len(self.dsems) // 2
            cls = "sw" if eng == "pool" else "hw"
            j = self.dnext[cls] + (half if cls == "sw" else 0)
            self.dnext[cls] = (self.dnext[cls] + 1) % half
            if self.dlast[j] is not None:
                deps[self.dlast[j].idx] = self.dlast[j]
            self.dcount[j] += 16
            op.dsem = j
            op.dval = self.dcount[j]
            self.dlast[j] = op
        for k in reads:
            lst = self.readers.setdefault(k, [])
            if not dma:
                lst[:] = [r for r in lst if r.dma or r.eng != eng]
            lst.append(op)
        for k in writes:
            lw = self.last_w.get(k, [])
            if group is not None and lw and lw[0].group == group:
                lw.append(op)
            else:
                self.last_w[k] = [op]
                self.readers[k] = []
        op.deps = list(deps.values())
        for d in op.deps:
            if not d.dma:
                d.sig = True
        self.ops.append(op)
        self.by_eng[eng].append(op)
        return op

    def emit(self):
        nc = self.nc
        for e in ENGS:
            c = 0
            for op in self.by_eng[e]:
                if not op.dma and op.sig:
                    c += 1
                    op.semval = c
        final_d = list(self.dcount)

        def replay(e, h):
            seen = {}
            for op in self.by_eng[e]:
                need = {}
                for d in op.deps:
                    if d.dma:
                        key, val, sem = ("d", d.dsem), d.dval, self.dsems[d.dsem]
                    else:
                        key, val, sem = ("e", d.eng), d.semval, self.esem[d.eng]
                    if val > need.get(key, (0, None))[0]:
                        need[key] = (val, sem)
                for key, (val, sem) in need.items():
                    if seen.get(key, 0) >= val:
                        continue
                    seen[key] = val
                    h.wait_ge(sem, val)
                ins = op.fn(h)
                if op.dma:
                    ins.then_inc(self.dsems[op.dsem], 16)
                elif op.sig:
                    ins.then_inc(self.esem[e], 1)
            if e == "sp":
                for j, v in enumerate(final_d):
                    if v > 0:
                        h.wait_ge(self.dsems[j], v)

        with nc.Block() as block:
            @block.tensor
            def _(h):
                replay("pe", h)

            @block.scalar
            def _(h):
                replay("act", h)

            @block.vector
            def _(h):
                replay("dve", h)

            @block.gpsimd
            def _(h):
                replay("pool", h)

            @block.sync
            def _(h):
                replay("sp", h)


def _is_dram(ap):
    return "DRam" in type(ap.tensor).__name__


def K(*aps):
    out = []
    for a in aps:
        if a is None or isinstance(a, (int, float)):
            continue
        if _is_dram(a):
            continue
        out.append(a.tensor.name)
    return out


def _t5_bucket(rel):
    nb = 16
    ret = np.where(rel > 0, nb, 0)
    n = np.abs(rel)
    max_exact = nb // 2
    nf = np.maximum(n, 1).astype(np.float32)
    large = max_exact + (np.log(nf / max_exact) / np.float32(np.log(128 / max_exact)) * (nb - max_exact)).astype(np.int32)
    large = np.minimum(large, nb - 1)
    return ret + np.where(n < max_exact, n, large)


def _bias_tables(rel_bias):
    rb = np.asarray(rel_bias, np.float32)
    q = np.arange(128)
    out = np.zeros((2, 128, 4, 4, 128), np.float32)
    for kt in range(2):
        kloc = np.arange(128) + (kt - 1) * 128
        rel = kloc[:, None] - q[None, :]
        bk = _t5_bucket(rel)
        kc = np.floor_divide(kloc, 64)[:, None]
        qc = (q // 64)[None, :]
        valid = (kc <= qc) & (kc >= qc - 2)
        g = rb[bk]
        g = np.where(valid[:, :, None], g, NEG)
        out[kt] = g.reshape(128, 128, 4, 4).transpose(0, 2, 3, 1)
    bp = out.reshape(2, 128, 4, 512)
    outs = np.full((3, 128, 4, 4, 64), NEG, np.float32)
    qpos = 2048 + np.arange(16)
    for s in range(2):
        kpos = 2048 - 128 + np.arange(128)
        bk = _t5_bucket(kpos[:, None] - qpos[None, :])
        g = rb[bk].reshape(128, 16, 4, 4).transpose(0, 2, 3, 1)
        outs[s, :, :, :, 32 * s:32 * s + 16] = g
        bk2 = _t5_bucket(qpos[:, None] - qpos[None, :])
        g2 = rb[bk2].reshape(16, 16, 4, 4).transpose(0, 2, 3, 1)
        outs[2, 32 * s:32 * s + 16, :, :, 32 * s:32 * s + 16] = g2
    bs = outs.reshape(3, 128, 4, 256)
    return np.ascontiguousarray(bp, np.float32), np.ascontiguousarray(bs, np.float32)


def _consts():
    c = {}
    c["ident"] = np.eye(128, dtype=np.float32).astype(ml_dtypes.bfloat16)
    s = np.arange(128)[:, None]
    t = np.arange(128)[None, :]
    c["tri_incl"] = np.where(s <= t, -1.0 / 16, 0.0).astype(np.float32).astype(ml_dtypes.bfloat16)
    c["tri_suf"] = np.where(s > t, -1.0 / 16, 0.0).astype(np.float32).astype(ml_dtypes.bfloat16)
    c["cmask4"] = np.tile(np.where(s <= t, 1.0, 0.0).astype(np.float32), (1, 4))
    c["tri_all"] = np.full((128, 128), -1.0 / 16, np.float32).astype(ml_dtypes.bfloat16)
    s6 = np.arange(64)[:, None]
    t6 = np.arange(64)[None, :]
    same = (s6 // 32) == (t6 // 32)
    real_s = (s6 % 32) < 16
    tis = np.zeros((128, 128), np.float32)
    tis[:64, :64] = np.where(same & (s6 <= t6), -1.0 / 16, 0.0)
    tss = np.zeros((128, 128), np.float32)
    tss[:64, :64] = np.where(same & (s6 > t6) & real_s, -1.0 / 16, 0.0)
    cms = np.zeros((128, 512), np.float32)
    cms[:64, :256] = np.tile(np.where(same & (s6 <= t6), 1.0, 0.0), (1, 4))
    c["tri_incl_s"] = tis.astype(ml_dtypes.bfloat16)
    c["tri_suf_s"] = tss.astype(ml_dtypes.bfloat16)
    c["cmask4_s"] = cms
    oh = np.zeros((2, 128, 128), np.float32)
    oh[0, :, :64] = 1.0
    oh[1, :, 64:] = 1.0
    c["onesH"] = oh.astype(ml_dtypes.bfloat16)
    g4 = np.zeros((4, 512), np.float32)
    g4s = np.zeros((4, 512), np.float32)
    for g in range(4):
        g4[g, g * 128:(g + 1) * 128] = 1.0
        g4s[g, g * 64:(g + 1) * 64] = 1.0
    c["g4"] = g4.astype(ml_dtypes.bfloat16)
    c["g4s"] = g4s.astype(ml_dtypes.bfloat16)
    bo = np.zeros((128, 128), np.float32)
    bo[:64, :64] = 1.0
    bo[64:, 64:] = 1.0
    c["bo_q"] = bo.astype(ml_dtypes.bfloat16)
    c["bo_k"] = (bo / 64).astype(ml_dtypes.bfloat16)
    c["ones_mean"] = np.full((128, 128), 1.0 / 256, np.float32).astype(ml_dtypes.bfloat16)
    c["ones_row"] = np.ones((1, 128), np.float32).astype(ml_dtypes.bfloat16)
    return c


CONST_SPECS = [
    ("ident", [128, 128], BF16), ("tri_incl", [128, 128], BF16), ("tri_suf", [128, 128], BF16),
    ("cmask4", [128, 512], F32), ("tri_all", [128, 128], BF16), ("tri_incl_s", [128, 128], BF16), ("tri_suf_s", [128, 128], BF16),
    ("cmask4_s", [128, 512], F32), ("onesH", [2, 128, 128], BF16), ("g4", [4, 512], BF16), ("g4s", [4, 512], BF16),
    ("bo_q", [128, 128], BF16), ("bo_k", [128, 128], BF16), ("ones_mean", [128, 128], BF16), ("ones_row", [1, 128], BF16),
]


class Prog:
    def __init__(self, debug=None):
        self.nc = nc = bass.Bass("TRN2", target_bir_lowering=False)
        self.s = Sched(nc)
        self.planning = False
        self.plan = []
        self.wscr = None
        self.next_x = None
        self.norm_done = False
        self.x_loaded = False
        self.debug = debug or {}
        self.dbg_outs = []
        self.din = {}
        self.dout = {}
        self._declare_io()
        self._alloc()

    def _in(self, name, shape, dt=F32):
        self.din[name] = self.nc.dram_tensor(name, list(shape), dt, kind="ExternalInput").ap()
        return self.din[name]

    def _out(self, name, shape, dt=F32):
        self.dout[name] = self.nc.dram_tensor(name, list(shape), dt, kind="ExternalOutput").ap()
        return self.dout[name]

    def _declare_io(self):
        i = self._in
        i("xp", [TOKC, D]); i("xprev", [TOKC, D]); i("xs", [TS, D])
        i("ck", [2, 128, 256]); i("cv", [2, 128, 256]); i("st", [2, 4, 128, 256])
        i("ffn1_up", [D, 2 * DFF]); i("ffn1_down", [DFF, D]); i("ffn2_up", [D, 2 * DFF]); i("ffn2_down", [DFF, D])
        i("w_in", [D, IN_W]); i("w_branch", [2048, D]); i("w_out", [D, D])
        i("w_alpha", [16, 512]); i("b_alpha", [1, 512])
        i("gains_fm", [128, 24])
        i("fgain_bc", [128, D])
        i("hgain", [128, 2])
        i("qgcol", [128, 1]); i("kgcol", [128, 1]); i("kgain_bc", [128, 64])
        i("sinkT", [2, 4, 128])
        i("halo_bias", [128, 1])
        i("biasp", [2, 128, 4, 512]); i("biass", [3, 128, 4, 256])
        for n, sh, dt in CONST_SPECS:
            i(n, sh, dt)
        o = self._out
        o("y", [TOKC, D]); o("ys", [TS, D]); o("nk", [128, 256]); o("nv", [128, 256]); o("ng", [4, 128, 256])
        o("nks", [TS, 256]); o("nvs", [TS, 256]); o("ngs", [2, 4, 128, 256])

    def sb(self, name, shape, dt):
        return self.nc.alloc_sbuf_tensor("s_" + name, list(shape), dt)

    def _alloc(self):
        nc = self.nc
        sb = self.sb
        self.c = {}
        for n, sh, dt in CONST_SPECS:
            if n.endswith("_s"):
                continue
            if len(sh) == 3:
                self.c[n] = sb("c_" + n, [sh[1], sh[0], sh[2]], dt)
            else:
                self.c[n] = sb("c_" + n, sh, dt)
        self.gains_fm = sb("gains_fm", [128, 24], F32)
        self.fgain_bc = sb("fgain_bc", [128, D], F32)
        self.hgain = sb("hgain", [128, 2], F32)
        self.qgcol = sb("qgcol", [128, 1], F32)
        self.kgcol = sb("kgcol", [128, 1], F32)
        self.kgain_bc = sb("kgain_bc", [128, 64], F32)
        self.sinkT = sb("sinkT", [4, 2, 128], F32)
        self.esinkT = sb("esinkT", [4, 2, 128], BF16)
        self.halo_bias = sb("halo_bias", [128, 1], F32)
        self.walpha = sb("walpha", [16, 512], BF16)
        self.balpha = sb("balpha", [1, 512], BF16)
        self.biasT = sb("biasT", [128, 2, 4, 512], BF16)
        self.biasTs = self.biasT[:, :, :, :].rearrange("p a j n -> p (a j n)")[:, 0:3072].rearrange("p (a j n) -> p a j n", a=3, j=4)
        self.xt = [sb("xt%d" % i, [128, D], F32) for i in range(4)]
        self.hball = sb("hball", [128, 4, D], BF16)
        self.hb = [self.hball[:, i, :] for i in range(4)]
        self.osT = [self.hball[:, 2 * i:2 * i + 2, :].rearrange("p a (g t) -> p (a g) t", g=2) for i in range(2)]
        self.ss = [sb("ss%d" % i, [128, 1], F32) for i in range(4)]
        self.rstd = [sb("rstd%d" % i, [128, 1], F32) for i in range(4)]
        self.hT = [sb("hT%d" % i, [128, T], BF16) for i in range(8)]
        self.arena0 = sb("arena0", [128, 22 * T], BF16)
        self.actT = [self.arena0[:, i * T:(i + 1) * T] for i in range(22)]
        self.e1T = self.arena0[:, 0:4096].bitcast(F32).rearrange("p (h t) -> p h t", h=4)
        self.e2T = self.arena0[:, 4096:8192].bitcast(F32).rearrange("p (h t) -> p h t", h=4)
        self.qeT = self.arena0[:, 8192:10240].rearrange("p (h t) -> p h t", h=4)
        self.arena1 = sb("arena1", [128, 10240], BF16)
        self.v_tok = [self.arena1[:, i * 1024:(i + 1) * 1024] for i in range(4)]
        self.kd = [self.arena1[:, 4096 + i * 512:4096 + (i + 1) * 512] for i in range(4)]
        self.es = [self.arena1[:, 6144 + i * 1024:6144 + (i + 1) * 1024].bitcast(F32) for i in range(4)]
        self.sigA = self.arena1[:, 0:4096].rearrange("p (c t) -> p c t", c=8)
        self.sigB = self.arena1[:, 4096:8192].rearrange("p (c t) -> p c t", c=8)
        self.keT = sb("keT", [128, 4, T], BF16)
        self.gaT = sb("gaT", [16, T], BF16)
        self.sp = [sb("sp%d" % i, [128, 512], BF16) for i in range(4)]
        self.dtot = sb("dtot", [128, 4, 1], F32)
        self.sgT = sb("sgT", [128, 8, T], BF16)
        self.ATm = sb("ATm", [128, 512], BF16)
        self.ATm2 = [self.ATm, sb("ATm1", [128, 512], BF16)]
        self.Sbf_alt = sb("Sbf_alt", [128, 4, 256], BF16)
        self.o_raw = sb("o_raw", [128, 8, 128], F32)
        self.osq = sb("osq", [128, 8, 128], BF16)
        self.rstdg = sb("rstdg", [128, 512], F32)
        self.o_raw2 = [self.o_raw, sb("o_raw1", [128, 8, 128], F32)]
        self.osq2 = [self.osq, sb("osq1", [128, 8, 128], BF16)]
        self.S = [sb("S0", [128, 4, 256], F32), self.xt[1][:, :].rearrange("p (h v) -> p h v", h=4)]
        self.Sbf = [sb("Sbf0", [128, 4, 256], BF16), self.xt[2][:, 0:512].bitcast(BF16).rearrange("p (h v) -> p h v", h=4)]
        self.sqr = [sb("sqr%d" % i, [128, T], F32) for i in range(1)]
        self.sqsq = [sb("sqsq%d" % i, [128, T], BF16) for i in range(1)]
        self.rsq = [sb("rsq%d" % i, [128, T], F32) for i in range(1)]
        self.qo = [sb("qo%d" % i, [128, 4 * T], BF16) for i in range(2)]
        self.knT_cur = sb("knT_cur", [128, 2, T], BF16)
        self.knT_prev = sb("knT_prev", [128, 2, 128], BF16)
        self.vz_cur = [sb("vz_cur%d" % i, [128, 512], BF16) for i in range(4)]
        self.vz_prev = sb("vz_prev", [128, 512], BF16)
        self.PT = [sb("PT%d" % i, [128, 512], BF16) for i in range(6)]
        self.rden = sb("rden", [128, 512], F32)
        self.ksq = self.sqr[0][:, 0:256]
        self.ktok = self.sqr[0][:, 256:512]
        self.kss = sb("kss", [128, 4], F32)
        self.knew = self.rsq[0][:, 0:256]
        self.vnew = self.rsq[0][:, 256:512]
        self.mtmp = [sb("mtmp%d" % i, [128, T], F32) for i in range(2)]
        self.sa = self.mtmp
        self.junk = self.mtmp[0][:, :].bitcast(BF16)
        self.ckf = sb("ckf", [128, 256], F32)
        self.ckb = sb("ckb", [128, 256], BF16)
        self.knT_c = [sb("knT_c%d" % i, [128, 2, 128], BF16) for i in range(2)]
        self.vz_c = [sb("vz_c%d" % i, [128, 512], BF16) for i in range(2)]
        self.NSLOT = 4
        self.slots = [sb("wslot%d" % i, [128, 4096], BF16) for i in range(self.NSLOT)]
        self.psf = [nc.alloc_psum_tensor("psf%d" % i, [128, 512], F32) for i in range(6)]
        self.psb = [nc.alloc_psum_tensor("psb%d" % i, [128, 1024], BF16) for i in range(2)]

    def reset_counters(self):
        self._ng_done = False
        self._early_done = False
        self.norm_done = False
        self.x_loaded = False
        self.next_x = None
        self._pf = 0
        self._pb = 0
        self._pt = 0
        self._piece = 0
        self._rr = 0

    def pf(self):
        self._pf += 1
        return self.psf[self._pf % 6]

    def pb(self):
        self._pb += 1
        return self.psb[self._pb % 2]

    def add(self, eng, fn, reads, writes, dma=False, group=None):
        if self.planning:
            return
        self.s.add(eng, fn, reads, writes, dma, group)

    def mm(self, out, lhsT, rhs, start=True, stop=True, rkeys=None):
        self.add("pe", lambda h: h.matmul(out, lhsT=lhsT, rhs=rhs, start=start, stop=stop), K(lhsT, rhs) if rkeys is None else rkeys, K(out))

    def tr(self, out, in_, ident, rkeys=None):
        self.add("pe", lambda h: h.transpose(out, in_, ident), K(in_, ident) if rkeys is None else rkeys, K(out))

    def act(self, out, in_, func, bias=None, scale=None, accum_out=None):
        kw = {}
        if bias is not None:
            kw["bias"] = bias
        if scale is not None:
            kw["scale"] = scale
        if accum_out is not None:
            kw["accum_out"] = accum_out
        self.add("act", lambda h: h.activation(out=out, in_=in_, func=func, **kw), K(in_, bias, scale), K(out, accum_out))

    def rsqrt(self, out, in_, scale, eps):
        self.act(out, in_, AF.Ln, bias=eps, scale=scale)
        self.act(out, out, AF.Exp, scale=-0.5)

    def amul(self, out, in_, mul):
        self.add("act", lambda h: h.mul(out=out, in_=in_, mul=mul), K(in_, mul), K(out))

    def cp(self, eng, out, in_):
        if eng == "act":
            self.add("act", lambda h: h.copy(out=out, in_=in_), K(in_), K(out))
        else:
            self.add(eng, lambda h: h.tensor_copy(out=out, in_=in_), K(in_), K(out))

    def tt(self, eng, out, in0, in1, op, rkeys=None, wkeys=None):
        self.add(eng, lambda h: h.tensor_tensor(out=out, in0=in0, in1=in1, op=op), K(in0, in1) if rkeys is None else rkeys, K(out) if wkeys is None else wkeys)

    def ts(self, eng, out, in0, s1, s2, op0, op1=None, rkeys=None, wkeys=None):
        if rkeys is not None:
            self.add(eng, lambda h: h.tensor_scalar(out=out, in0=in0, scalar1=s1, scalar2=0.0, op0=op0, op1=ALU.add), rkeys, wkeys)
            return
        if op1 is None:
            if op0 == ALU.pow:
                self.add(eng, lambda h: h.tensor_scalar(out=out, in0=in0, scalar1=0.0, scalar2=s1, op0=ALU.add, op1=ALU.pow), K(in0, s1), K(out))
            else:
                self.add(eng, lambda h: h.tensor_scalar(out=out, in0=in0, scalar1=s1, scalar2=0.0, op0=op0, op1=ALU.add), K(in0, s1), K(out))
        else:
            self.add(eng, lambda h: h.tensor_scalar(out=out, in0=in0, scalar1=s1, scalar2=s2, op0=op0, op1=op1), K(in0, s1, s2), K(out))

    def stt(self, eng, out, in0, scalar, in1, op0, op1):
        self.add(eng, lambda h: h.scalar_tensor_tensor(out=out, in0=in0, scalar=scalar, in1=in1, op0=op0, op1=op1),
                 K(in0, scalar, in1), K(out))

    def memset(self, eng, ap, val):
        self.add(eng, lambda h: h.memset(ap, val), [], K(ap))

    def dma(self, eng, out, in_, reads=(), writes=(), group=None):
        self.add(eng, lambda h: h.dma_start(out=out, in_=in_), K(in_) + list(reads), K(out) + list(writes), dma=True, group=group)

    def wget(self, tag, spec, wid, nel=4096):
        if self.planning:
            self.plan.append((tag, spec, wid, nel))
            return self.slots[0]
        if self._piece == 0:
            self.first = {}
            for j, pl in enumerate(self.plan):
                self.first.setdefault(pl[2], j)
            self.widx = {w: k for k, w in enumerate(self.first)}
            if self.wscr is None:
                self.wscr = self.nc.dram_tensor("wscr", [len(self.widx), 128, 4096], BF16).ap()
        i = self._piece
        assert self.plan[i][0] == tag, (self.plan[i][0], tag)
        LA = self.NSLOT - 2 if wid[0] == "wbB" else self.NSLOT - 1
        if i == 0:
            self._issue_piece(0)
            self._issued = 0
        slot = self.slots[i % self.NSLOT]
        if self.first[wid] == i and self.nuse[wid] > 1:
            self.dma("pool", self.wscr[self.widx[wid], :, :nel], slot[:, :nel], writes=[("scr", wid)])
        target = min(i + LA, len(self.plan) - 1)
        while self._issued < target:
            self._issued += 1
            self._issue_piece(self._issued)
        self._piece += 1
        return slot

    def _issue_piece(self, j):
        slot = self.slots[j % self.NSLOT]
        tag, spec, wid, nel = self.plan[j]
        if self.first[wid] == j:
            for dst, src in spec(slot):
                self.dma("pool", dst, src, group=("piece", j))
        else:
            self.dma("sp", slot[:, :nel], self.wscr[self.widx[wid], :, :nel], reads=[("scr", wid)])

    def dbg(self, name, ap, shape, dt=F32):
        if name not in self.debug:
            return
        if self.planning:
            return
        d = self.nc.dram_tensor("dbg_" + name, list(shape), dt, kind="ExternalOutput").ap()
        self.dbg_outs.append("dbg_" + name)
        self.dma("sp", d, ap)

    def setup(self, first_x=None):
        c = self.c
        di = self.din
        self.dma("sp", c["ident"][:], di["ident"])
        self.dma("sp", self.gains_fm[:], di["gains_fm"])
        if first_x is not None:
            self.load_x(first_x, [(j * 128, 128) for j in range(4)])
        for n, sh, dt in CONST_SPECS:
            if n == "ident":
                continue
            if n.endswith("_s"):
                continue
            if len(sh) == 3:
                self.dma("sp", c[n][:], di[n].rearrange("a p f -> p a f"))
            else:
                self.dma("sp", c[n][:], di[n])
        for n in ("fgain_bc", "hgain", "qgcol", "kgcol", "kgain_bc", "halo_bias"):
            self.dma("sp", getattr(self, n)[:], di[n])
        self.dma("sp", self.sinkT[:], di["sinkT"].rearrange("a g m -> g a m"))
        self.dma("pool", self.walpha[:], di["w_alpha"])
        self.dma("pool", self.balpha[:], di["b_alpha"])
        self.dma("pool", self.biasT[:, 0:2], di["biasp"].rearrange("a p j n -> p a j n"))
        self.act(self.esinkT[:], self.sinkT[:], AF.Exp)
        for t_ in self.vz_cur + [self.vz_prev] + self.vz_c:
            self.memset("pool", t_[:], 0.0)
        self.memset("pool", self.knT_prev[:], 0.0)
        self.memset("dve", self.S[0][:], 0.0)
        self.memset("dve", self.Sbf[0][:], 0.0)
        for t_ in self.ss:
            self.memset("dve", t_[:], 0.0)
        if self.debug.get("delay"):
            self.memset("pool", self.slots[0][:], 0.0)
            for _ in range(int(self.debug["delay"])):
                self.cp("pool", self.slots[1][:], self.slots[0][:])

    def load_x(self, src, tts):
        for i, (o, P) in enumerate(tts):
            self.dma("sp", self.xt[i][:P, :], src[o:o + P, :])

    def norm_hT(self, tts, Tn, gcol0, src=None):
        self.norm_stats(tts, src)
        self.norm_tr(tts, Tn, gcol0, subtile_major=True)

    def norm_stats(self, tts, src=None):
        xt, hb = (self.xt if src is None else src), self.hb
        for i, (o, P) in enumerate(tts):
            self.act(self.junk[:P, :], xt[i][:P, :], AF.Square, accum_out=self.ss[i][:P, 0:1])
            self.rsqrt(self.rstd[i][:P, 0:1], self.ss[i][:P, 0:1], 1.0 / D, EPS)
            self.ts("pool" if i % 2 else "dve", hb[i][:P, :], xt[i][:P, :], self.rstd[i][:P, 0:1], None, ALU.mult,
                    rkeys=K(xt[i][:], self.rstd[i][:]) + ["s_hball"], wkeys=[("hb", i)])

    def norm_tr(self, tts, Tn, gcol0, subtile_major=False):
        hb = self.hb
        ident = self.c["ident"]
        if subtile_major and len(tts) > 1:
            regs = []
            b0, b1 = self.pb(), self.pb()
            f0, f1 = self.pf()[:, :].bitcast(BF16), self.pf()[:, :].bitcast(BF16)
            for bank in (b0, b1, f0, f1):
                regs += [bank[:, 0:512], bank[:, 512:1024]]
            for i, (o, P) in enumerate(tts):
                for cc in range(8):
                    self.tr(regs[cc][:, o:o + P], hb[i][:P, cc * 128:(cc + 1) * 128], ident[:P, :P],
                            rkeys=K(ident[:]) + [("hb", i), "s_hball"])
            for cc in (0, 2, 1, 3, 4, 6, 5, 7):
                g = self.gains_fm[:, gcol0 + cc:gcol0 + cc + 1]
                if (cc // 2) % 2 == 0:
                    self.amul(self.hT[cc][:, :Tn], regs[cc][:, :Tn], g)
                else:
                    self.ts("dve", self.hT[cc][:, :Tn], regs[cc][:, :Tn], g, None, ALU.mult)
            return
        for cc in range(8):
            pT = self.pb()
            for i, (o, P) in enumerate(tts):
                self.tr(pT[:, o:o + P], hb[i][:P, cc * 128:(cc + 1) * 128], ident[:P, :P],
                        rkeys=K(ident[:]) + [("hb", i), "s_hball"])
            g = self.gains_fm[:, gcol0 + cc:gcol0 + cc + 1]
            if cc % 2 == 0:
                self.amul(self.hT[cc][:, :Tn], pT[:, :Tn], g)
            else:
                self.ts("dve", self.hT[cc][:, :Tn], pT[:, :Tn], g, None, ALU.mult)

    def ffn(self, tts, Tn, wup, wdn, gcol0, tagp, mid_hook=None, mid_hook2=None):
        if self.norm_done:
            self.norm_done = False
        else:
            self.norm_hT(tts, Tn, gcol0)
        hT = self.hT
        upv = wup.rearrange("(k p) (ab c) -> p k ab c", p=128, ab=2)
        for g in range(11):
            def spec(slot, g=g):
                sv = slot[:, :].rearrange("p (k ab c) -> p k ab c", k=8, ab=2)
                return [(sv[:, :, ab, :], upv[:, :, ab, g * 256:(g + 1) * 256]) for ab in range(2)]
            slot = self.wget((tagp, "up", g), spec, (tagp[-1], "up", g))
            wv = slot[:, :].rearrange("p (k ab c) -> p k ab c", k=8, ab=2)
            for u in range(2):
                i = 2 * g + u
                pa = self.pf()
                pb_ = self.pf()
                for k in range(8):
                    self.mm(pa[:, :Tn], wv[:, k, 0, u * 128:(u + 1) * 128], hT[k][:, :Tn], k == 0, k == 7)
                for k in range(8):
                    self.mm(pb_[:, :Tn], wv[:, k, 1, u * 128:(u + 1) * 128], hT[k][:, :Tn], k == 0, k == 7)
                sa = self.sa[i % 2]
                self.act(sa[:, :Tn], pa[:, :Tn], AF.Silu)
                self.tt("dve", self.actT[i][:, :Tn], sa[:, :Tn], pb_[:, :Tn], ALU.mult,
                        rkeys=K(sa[:], pb_[:]) + ["s_arena0"], wkeys=[("actT", i)])
        if mid_hook is not None:
            mid_hook()
        dnv = wdn.rearrange("(i p) n -> p i n", p=128)
        for nh in range(2):
            accs = [self.pf() for _ in tts]
            for cg, (c0, c1) in enumerate(((0, 8), (8, 16), (16, 22))):
                def spec(slot, nh=nh, c0=c0, c1=c1):
                    return [(slot[:, :(c1 - c0) * 512].rearrange("p (c n) -> p c n", n=512), dnv[:, c0:c1, nh * 512:(nh + 1) * 512])]
                slot = self.wget((tagp, "down", nh, cg), spec, (tagp[-1], "down", nh, cg), (c1 - c0) * 512)
                wv = slot[:, :(c1 - c0) * 512].rearrange("p (c n) -> p c n", n=512)
                for i, (o, P) in enumerate(tts):
                    for cc in range(c0, c1):
                        self.mm(accs[i][:P, :], self.actT[cc][:, o:o + P], wv[:, cc - c0, :], cc == 0, cc == 21,
                                rkeys=K(wv[:, 0, :]) + ["s_arena0", ("actT", cc)])
            for i, (o, P) in enumerate(tts):
                xs_ = self.xt[i][:P, nh * 512:(nh + 1) * 512]
                self.stt("dve", xs_, accs[i][:P, :], 0.5, xs_, ALU.mult, ALU.add)
            if nh == 0 and mid_hook2 is not None:
                mid_hook2()

    def win_piece(self, tag, c0, w):
        wv_ = self.din["w_in"].rearrange("(k p) c -> p k c", p=128)

        def spec(slot):
            return [(slot[:, :8 * w].rearrange("p (k c) -> p k c", k=8), wv_[:, :, c0:c0 + w])]
        slot = self.wget(tag, spec, tag[1:], 8 * w)
        return slot[:, :8 * w].rearrange("p (k c) -> p k c", k=8)

    def fm_proj(self, wv, col0, Tn, lhs_view=None):
        p = self.pf()
        for k in range(8):
            lhs = wv[:, k, col0:col0 + 128] if lhs_view is None else lhs_view(k)
            self.mm(p[:, :Tn], lhs, self.hT[k][:, :Tn], k == 0, k == 7)
        return p

    def tm_proj(self, wv, c0, w, o, P):
        p = self.pf()
        for k in range(8):
            self.mm(p[:P, :w], self.hT[k][:, o:o + P], wv[:, k, c0:c0 + w], k == 0, k == 7)
        return p

    def qknorm(self, ps, out, gcol, bo, eps, Tn, idx, ntt=None):
        sqsq, sqr, rs = self.sqsq[0], self.sqr[0], self.rsq[0]
        self.act(sqsq[:, :Tn], ps, AF.Square)
        pm = self.pf()
        self.mm(pm[:, :Tn], bo[:, :], sqsq[:, :Tn])
        self.rsqrt(rs[:, :Tn], pm[:, :Tn], 1.0, eps)
        if ntt is None:
            self.stt("dve", out, ps, gcol, rs[:, :Tn], ALU.mult, ALU.mult)
        else:
            self.stt("dve", out, ps.rearrange("p (t q) -> p t q", t=ntt), gcol, rs[:, :Tn].rearrange("p (t q) -> p t q", t=ntt), ALU.mult, ALU.mult)

    def mixer(self, tts, Tn, mode, tagp, seqs, tile_idx, last_tile):
        c = self.c
        sample = mode == "sample"
        L = tts[0][1]
        tri_i = c["tri_incl"]
        tri_s = c["tri_suf"]
        cmask = c["cmask4"]
        self.norm_hT(tts, Tn, 8)
        if mode == "pre" and self.next_x is not None:
            self.load_x(self.next_x, tts)
        hT = self.hT
        e1T, e2T, qeT, keT = self.e1T, self.e2T, self.qeT, self.keT
        wv = self.win_piece((tagp, "ga"), C_GA, 16)
        p = self.pf()
        for k in range(8):
            self.mm(p[:16, :Tn], wv[:, k, 0:16], hT[k][:, :Tn], k == 0, k == 7)
        self.cp("act", self.gaT[:, :Tn], p[:16, :Tn])
        wk = self.win_piece((tagp, "gk"), C_GK, 512)
        n_t = len(tts)
        pzs = []
        for i, (o, P) in enumerate(tts):
            pz = self.pf()
            self.mm(pz[:P, :], self.gaT[:, o:o + P], self.walpha[:, :], True, False)
            self.mm(pz[:P, :], c["ones_row"][0:1, :P], self.balpha[0:1, :], False, True)
            pzs.append(pz)
        for i, (o, P) in enumerate(tts):
            self.act(self.es[i][:P, :], pzs[i][:P, :], AF.Exp, scale=-1.0)
        for i, (o, P) in enumerate(tts):
            self.act(self.sp[i][:P, :], self.es[i][:P, :], AF.Ln, bias=1.0)
        for i, (o, P) in enumerate(tts):
            pk = self.tm_proj(wk, 0, 512, o, P)
            self.cp("dve", self.kd[i][:P, :], pk[:P, :])
        if mode != "pre":
            for hd in range(4):
                p = self.fm_proj(wk, hd * 128, Tn)
                self.cp("dve", keT[:, hd, :Tn], p[:, :Tn])
            wq = self.win_piece((tagp, "gq"), C_GQ, 512)
            for hd in range(4):
                p = self.fm_proj(wq, hd * 128, Tn)
                self.cp("dve", qeT[:, hd, :Tn], p[:, :Tn])
        pbts = []
        for i, (o, P) in enumerate(tts):
            pbt = self.pf()
            for hd in range(4):
                self.mm(pbt[:, hd * P:(hd + 1) * P], self.sp[i][:P, hd * 128:(hd + 1) * 128], tri_i[:P, :P])
            pbts.append(pbt)
        for i, (o, P) in enumerate(tts):
            pv3 = pbts[i][:, :4 * P].rearrange("p (h t) -> p h t", h=4)
            self.act(e1T[:, :, o:o + P], pv3, AF.Exp)
            if mode != "pre":
                self.act(e2T[:, :, o:o + P], pv3, AF.Exp, scale=-1.0)
        for pc in range(2):
            wvv = self.win_piece((tagp, "gv", pc), C_GV + pc * 512, 512)
            for i, (o, P) in enumerate(tts):
                p = self.tm_proj(wvv, 0, 512, o, P)
                self.cp("dve", self.v_tok[i][:P, pc * 512:(pc + 1) * 512], p[:P, :])
        psufs = []
        for i, (o, P) in enumerate(tts):
            psuf = self.pf()
            whole = (mode == "pre")
            self.mm(psuf[:P, :], tri_s[:P, :P], self.sp[i][:P, :], True, not (whole and i < n_t - 1))
            if whole:
                for j in range(i + 1, n_t):
                    self.mm(psuf[:P, :], c["tri_all"][:P, :P], self.sp[j][:P, :], False, j == n_t - 1)
            psufs.append(psuf)
        for i, (o, P) in enumerate(tts):
            self.act(self.es[i][:P, :], psufs[i][:P, :], AF.Exp)
        if mode != "pre":
            for pc in range(2):
                wg = self.win_piece((tagp, "gr", pc), C_GR + pc * 512, 512)
                for u in range(4):
                    p = self.fm_proj(wg, u * 128, Tn)
                    self.act(self.sgT[:, pc * 4 + u, :Tn], p[:, :Tn], AF.Silu)
        for i, (o, P) in enumerate(tts):
            self.tt("dve", self.kd[i][:P, :], self.kd[i][:P, :], self.es[i][:P, :], ALU.mult)
        if mode != "pre":
            for hd in range(4):
                self.tt("dve", keT[:, hd, :Tn], keT[:, hd, :Tn], e2T[:, hd, :Tn], ALU.mult)
            for hd in range(4):
                self.stt("dve", qeT[:, hd, :Tn], qeT[:, hd, :Tn], 128 ** -0.5, e1T[:, hd, :Tn], ALU.mult, ALU.mult)
        mstop = self.debug.get("mstop") if sample else None
        gla_last_tail = None
        if mstop == "gpipe":
            return
        if mode == "pre":
            S, Sbf = seqs[0][2], seqs[0][3]
            pS = [self.pf(), self.pf()]
            for hd in range(4):
                for i, (o, P) in enumerate(tts):
                    self.mm(pS[hd // 2][:, (hd % 2) * 256:(hd % 2 + 1) * 256], self.kd[i][:P, hd * 128:(hd + 1) * 128],
                            self.v_tok[i][:P, hd * 256:(hd + 1) * 256], i == 0, i == n_t - 1)
            self.cp("dve", self.dtot[:, :, :], e1T[:, :, 127:128])
            for i in range(1, n_t):
                self.tt("dve", self.dtot[:, :, :], self.dtot[:, :, :], e1T[:, :, i * 128 + 127:i * 128 + 128], ALU.mult)
            for hd in range(4):
                self.stt("dve", S[:, hd, :], S[:, hd, :], self.dtot[:, hd, :], pS[hd // 2][:, (hd % 2) * 256:(hd % 2 + 1) * 256], ALU.mult, ALU.add)
            self.cp("act", Sbf[:, :, :], S[:, :, :])
        elif len(seqs) == 1:
            gla_last_tail = self.gla_pipelined(tts, seqs[0], cmask)
        else:
            for i, (o, P) in enumerate(tts):
                self.gla_chunk(i, o, P, mode, seqs, cmask)
        if mstop == "gla":
            return
        if mode == "full" and tile_idx == 0:
            self.dbg("oaT", self.sgT[:, :, :], [128, 8, T], BF16)
            self.dbg("qeT", self.qeT, [128, 4, T], BF16)
            self.dbg("keT", self.keT[:, :, :], [128, 4, T], BF16)
            self.dbg("e1T", self.e1T, [128, 4, T], F32)
            self.dbg("kd3", self.kd[3], [128, 512], BF16)
            self.dbg("vtok3", self.v_tok[3], [128, 1024], BF16)
            self.dbg("S", self.S[0][:, :, :], [128, 4, 256], F32)
        need_kv = (mode != "pre") or last_tile
        if mode != "pre":
            ntt = len(tts)
            w_in_v = self.din["w_in"].rearrange("(k p) c -> p k c", p=128)
            for pc in range(2):
                def spec(slot, pc=pc):
                    sv = slot[:, :].rearrange("p (k g two d) -> p k g two d", k=8, g=4, two=2)
                    return [(sv[:, k, :, two, :], w_in_v[:, k, C_SQ + pc * 512 + two * 256:C_SQ + pc * 512 + (two + 1) * 256].rearrange("p (g d) -> p g d", g=4))
                            for two in range(2) for k in range(8)]
                wq = self.wget((tagp, "sq", pc), spec, ("sq", pc))[:, :].rearrange("p (k c) -> p k c", k=8)
                qv = self.qo[pc][:, :ntt * 4 * L].rearrange("p (t g q) -> p t g q", t=ntt, g=4)
                for g in range(4):
                    p = self.fm_proj(wq, g * 128, Tn)
                    if gla_last_tail is not None and g == 1:
                        gla_last_tail()
                        gla_last_tail = None
                    self.qknorm(p[:, :Tn], qv[:, :, g, :], self.qgcol[:, 0:1], c["bo_q"], 64 * EPS, Tn, g, ntt)
        if need_kv:
            wkv = self.win_piece((tagp, "sksv"), C_SK, 512)
            for P2 in range(2):
                p = self.fm_proj(wkv, P2 * 128, Tn)
                self.qknorm(p[:, :Tn], self.knT_cur[:, P2, :Tn], self.kgcol[:, 0:1], c["bo_k"], EPS, Tn, P2)
            for i, (o, P) in enumerate(tts):
                pvv = self.tm_proj(wkv, 256, 256, o, P)
                vz4 = self.vz_cur[i][:, :].rearrange("p (jp par c) -> p jp par c", jp=2, par=2)
                pv4 = pvv[:, :256].rearrange("p (jp par d) -> p jp par d", jp=2, par=2)
                for par in range(2):
                    self.cp("act", vz4[:P, :, par, par * 64:(par + 1) * 64], pv4[:P, :, par, :])
                want_out = sample or (mode == "full" and last_tile and i == len(tts) - 1)
                if want_out:
                    self.cp("act", self.vnew[:P, :], pvv[:P, :256])
                    pkk = self.tm_proj(wkv, 0, 256, o, P)
                    self.act(self.ksq[:P, :], pkk[:P, :256], AF.Square)
                    self.add("dve", lambda h, P=P: h.tensor_reduce(out=self.kss[:P, 0:4], in_=self.ksq[:P, :].rearrange("p (j d) -> p j d", j=4),
                                                                    axis=AX.X, op=ALU.add), K(self.ksq[:]), K(self.kss[:]))
                    self.rsqrt(self.kss[:P, :], self.kss[:P, :], 1.0 / 64, EPS)
                    k3 = self.ktok[:P, :].rearrange("p (j d) -> p j d", j=4)
                    self.tt("dve", k3, pkk[:P, :256].rearrange("p (j d) -> p j d", j=4),
                            self.kss[:P, 0:4].unsqueeze(2).to_broadcast([P, 4, 64]), ALU.mult)
                    self.tt("dve", self.knew[:P, :].rearrange("p (j d) -> p j d", j=4), k3,
                            self.kgain_bc[:P, :].unsqueeze(1).to_broadcast([P, 4, 64]), ALU.mult)
                    if sample:
                        self.dma("sp", self.dout["nks"], self.knew[:P, :])
                        self.dma("sp", self.dout["nvs"], self.vnew[:P, :])
                    else:
                        self.dma("sp", self.dout["nk"], self.knew[:P, :])
                        self.dma("sp", self.dout["nv"], self.vnew[:P, :])
        if mode == "pre":
            if last_tile:
                self.cp("pool", self.knT_prev[:, :, :], self.knT_cur[:, :, Tn - 128:Tn])
                self.cp("pool", self.vz_prev[:, :], self.vz_cur[len(tts) - 1][:, :])
            if self.next_x is not None:
                self.norm_hT(tts, Tn, 0)
                self.norm_done = True
            return
        if mstop == "swaproj":
            return
        gate_groups = []
        gstate = {}
        for nm, c0, dst in (("gta", C_GTA, self.sigA), ("gtb", C_GTB, self.sigB)):
            for pc in range(2):
                for u in range(4):
                    def grp(nm=nm, c0=c0, dst=dst, pc=pc, u=u):
                        if u == 0:
                            gstate["w"] = self.win_piece((tagp, nm, pc), c0 + pc * 512, 512)
                        p = self.fm_proj(gstate["w"], u * 128, Tn)
                        self.act(dst[:, pc * 4 + u, :Tn], p[:, :Tn], AF.Tanh, scale=0.5)
                    gate_groups.append(grp)

        def run_gates(k):
            for _ in range(min(k, len(gate_groups))):
                gate_groups.pop(0)()
        if sample:
            kts = []
            for sq_ in range(2):
                kts.append(dict(knT=lambda P2, rows, sq_=sq_: self.knT_c[sq_][rows, P2, :], vz=self.vz_c[sq_], bias=self.biasTs[:, sq_, :, :], nk=128, hb=None))
            kts.append(dict(knT=lambda P2, rows: self.knT_cur[rows, P2, 0:64], vz=self.vz_cur[0], bias=self.biasTs[:, 2, :, :], nk=64, hb=None))
            self.swa_block(0, 0, 1, 64, kts, c["g4s"], between=lambda: run_gates(2))
        else:
            for i, (o, P) in enumerate(tts):
                kts = []
                if i == 0:
                    hb = self.halo_bias[:, 0:1] if tile_idx == 0 else None
                    kts.append(dict(knT=lambda P2, rows: self.knT_prev[rows, P2, :], vz=self.vz_prev, bias=self.biasT[:, 0, :, :], nk=128, hb=hb))
                else:
                    kts.append(dict(knT=lambda P2, rows, o=o: self.knT_cur[rows, P2, o - 128:o], vz=self.vz_cur[i - 1], bias=self.biasT[:, 0, :, :], nk=128, hb=None))
                kts.append(dict(knT=lambda P2, rows, o=o: self.knT_cur[rows, P2, o:o + 128], vz=self.vz_cur[i], bias=self.biasT[:, 1, :, :], nk=128, hb=None))
                self.swa_block(i, o, len(tts), 128, kts, c["g4"], between=lambda: run_gates(2))
            self.cp("pool", self.knT_prev[:, :, :], self.knT_cur[:, :, Tn - 128:Tn])
            self.cp("pool", self.vz_prev[:, :], self.vz_cur[len(tts) - 1][:, :])
        if mode == "full" and tile_idx == 0:
            self.dbg("osT0", self.osT[0], [128, 4, T], BF16)
            self.dbg("osT1", self.osT[1], [128, 4, T], BF16)
            self.dbg("knT", self.knT_cur[:, :, :], [128, 2, T], BF16)
        if mstop == "swa":
            return
        run_gates(len(gate_groups))
        if mode == "full" and tile_idx == 0:
            self.dbg("sgA", self.sigA, [128, 8, T], BF16)
            self.dbg("sgB", self.sigB, [128, 8, T], BF16)
        wbr = self.din["w_branch"]
        wbA = wbr[0:1024, :].rearrange("(f p) n -> p f n", p=128)
        wbB = wbr[1024:2048, :].rearrange("(P j g d) n -> d P j g n", P=2, j=2, g=4)
        for ng in range(2):
            def specA(slot, ng=ng):
                return [(slot[:, :].rearrange("p (f n) -> p f n", f=8), wbA[:, :, ng * 512:(ng + 1) * 512])]

            def specB(slot, ng=ng):
                return [(slot[half * 64:(half + 1) * 64, :].rearrange("p (P g n) -> p P g n", P=2, g=4)[:, P2],
                         wbB[:, P2, half, :, ng * 512:(ng + 1) * 512]) for half in range(2) for P2 in range(2)]
            sA = self.wget((tagp, "wbA", ng), specA, ("wbA", ng))[:, :].rearrange("p (f n) -> p f n", f=8)
            sB = self.wget((tagp, "wbB", ng), specB, ("wbB", ng))[:, :].rearrange("p (f n) -> p f n", f=8)
            for u in range(4):
                n = ng * 4 + u
                pa = self.pf()
                for f in range(8):
                    self.mm(pa[:, :Tn], sA[:, f, u * 128:(u + 1) * 128], self.sgT[:, f, :Tn], f == 0, f == 7)
                pb_ = self.pf()
                for f in range(8):
                    self.mm(pb_[:, :Tn], sB[:, f, u * 128:(u + 1) * 128], self.osT[f // 4][:, f % 4, :Tn], f == 0, f == 7)
                t0, t1 = self.mtmp
                self.stt("dve", t0[:, :Tn], self.sigA[:, n, :Tn], 1.0, pa[:, :Tn], ALU.add, ALU.mult)
                self.stt("dve", t1[:, :Tn], self.sigB[:, n, :Tn], 1.0, pb_[:, :Tn], ALU.add, ALU.mult)
                self.tt("pool", self.sigA[:, n, :Tn], t0[:, :Tn], t1[:, :Tn], ALU.add)
        if mode == "full" and tile_idx == 0:
            self.dbg("mT", self.sigA, [128, 8, T], BF16)
        wo = self.din["w_out"].rearrange("(f p) n -> p f n", p=128)
        for nh in range(2):
            def spec(slot, nh=nh):
                return [(slot[:, :].rearrange("p (f n) -> p f n", f=8), wo[:, :, nh * 512:(nh + 1) * 512])]
            so = self.wget((tagp, "wo", nh), spec, ("wo", nh))[:, :].rearrange("p (f n) -> p f n", f=8)
            for i, (o, P) in enumerate(tts):
                p = self.pf()
                for f in range(8):
                    self.mm(p[:P, :], self.sigA[:, f, o:o + P], so[:, f, :], f == 0, f == 7)
                xs_ = self.xt[i][:P, nh * 512:(nh + 1) * 512]
                self.stt("dve", xs_, p[:P, :], 0.5, xs_, ALU.mult, ALU.add)

    def gla_pipelined(self, tts, seq, cmask):
        c = self.c
        e1T, qeT, keT = self.e1T, self.qeT, self.keT
        so, nreal, S, Sbf0 = seq
        Sb = [Sbf0, self.Sbf_alt]
        L = tts[0][1]
        n_t = len(tts)
        assert n_t % 2 == 0

        def front(i, o):
            pS = [self.pf(), self.pf()]
            for hd in range(4):
                self.mm(pS[hd // 2][:, (hd % 2) * 256:(hd % 2 + 1) * 256], self.kd[i][:L, hd * 128:(hd + 1) * 128],
                        self.v_tok[i][:L, hd * 256:(hd + 1) * 256])
            pA = self.pf()
            for hd in range(4):
                self.mm(pA[:L, hd * L:(hd + 1) * L], keT[:, hd, o:o + L], qeT[:, hd, o:o + L])
            atm = self.ATm2[i % 2]
            self.tt("dve", atm[:L, :4 * L], pA[:L, :4 * L], cmask[:L, :4 * L], ALU.mult)
            dcol = o + L - 1
            for hd in range(4):
                self.stt("dve", S[:, hd, :], S[:, hd, :], e1T[:, hd, dcol:dcol + 1], pS[hd // 2][:, (hd % 2) * 256:(hd % 2 + 1) * 256], ALU.mult, ALU.add)
            self.cp("act", Sb[(i + 1) % 2][:, :, :], S[:, :, :])
            return atm

        def tail(j):
            oj = tts[j][0]
            o_raw, osq = self.o_raw2[j % 2], self.osq2[j % 2]
            pM = self.pf()
            for hd in range(4):
                self.mm(pM[:, hd * L:(hd + 1) * L], c["ones_mean"][:, :], osq[:, 2 * hd, :L], True, False)
                self.mm(pM[:, hd * L:(hd + 1) * L], c["ones_mean"][:, :], osq[:, 2 * hd + 1, :L], False, True)
            self.rsqrt(self.rstdg[:, :4 * L], pM[:, :4 * L], 1.0, EPS)
            r3 = self.rstdg[:, :4 * L].rearrange("p (h l) -> p h l", h=4)
            o4 = o_raw[:, :, :].rearrange("p (h v) l -> p h v l", v=2)
            for vc in range(2):
                self.stt("dve", o4[:, :, vc, :L], o4[:, :, vc, :L], self.hgain[:, vc:vc + 1], r3, ALU.mult, ALU.mult)
            self.tt("dve", self.sgT[:, :, oj:oj + L], o_raw[:, :, :L], self.sgT[:, :, oj:oj + L], ALU.mult)

        nxt = front(0, tts[0][0])
        for i, (o, P) in enumerate(tts):
            atm = nxt
            Sbf = Sb[i % 2]
            pO = [self.pf(), self.pf()]
            for hd in range(4):
                for vc in range(2):
                    reg = pO[hd // 2][:, ((hd % 2) * 2 + vc) * L:((hd % 2) * 2 + vc + 1) * L]
                    self.mm(reg, self.v_tok[i][:L, hd * 256 + vc * 128:hd * 256 + (vc + 1) * 128], atm[:L, hd * L:(hd + 1) * L], True, False)
                    self.mm(reg, Sbf[:, hd, vc * 128:(vc + 1) * 128], qeT[:, hd, o:o + L], False, True)
            o_raw, osq = self.o_raw2[i % 2], self.osq2[i % 2]
            for b in range(2):
                pv = pO[b][:, :4 * L].rearrange("p (c l) -> p c l", c=4)
                self.cp("act", o_raw[:, 4 * b:4 * b + 4, :L], pv)
                self.act(osq[:, 4 * b:4 * b + 4, :L], pv, AF.Square)
            if i + 1 < n_t:
                nxt = front(i + 1, tts[i + 1][0])
            if i >= 1:
                tail(i - 1)
        return lambda: tail(n_t - 1)

    def gla_chunk(self, i, o, L, mode, seqs, cmask):
        c = self.c
        e1T, qeT, keT = self.e1T, self.qeT, self.keT
        nseq = len(seqs)
        blk = L // nseq
        def state_mm(si):
            so = seqs[si][0]
            pS = [self.pf(), self.pf()]
            for hd in range(4):
                self.mm(pS[hd // 2][:, (hd % 2) * 256:(hd % 2 + 1) * 256], self.kd[i][so:so + blk, hd * 128:(hd + 1) * 128],
                        self.v_tok[i][so:so + blk, hd * 256:(hd + 1) * 256])
            return pS
        hoist = nseq == 1
        pSs = [state_mm(0)] if hoist else None
        pA = self.pf()
        for hd in range(4):
            self.mm(pA[:L, hd * L:(hd + 1) * L], keT[:, hd, o:o + L], qeT[:, hd, o:o + L])
        self.tt("dve", self.ATm[:L, :4 * L], pA[:L, :4 * L], cmask[:L, :4 * L], ALU.mult)
        pO = [self.pf(), self.pf()]
        for hd in range(4):
            for vc in range(2):
                reg = pO[hd // 2][:, ((hd % 2) * 2 + vc) * L:((hd % 2) * 2 + vc + 1) * L]
                self.mm(reg, self.v_tok[i][:L, hd * 256 + vc * 128:hd * 256 + (vc + 1) * 128], self.ATm[:L, hd * L:(hd + 1) * L], True, False)
                for si, (so, nreal, S, Sbf) in enumerate(seqs):
                    self.mm(reg[:, so:so + blk], Sbf[:, hd, vc * 128:(vc + 1) * 128], qeT[:, hd, o + so:o + so + blk], False, si == nseq - 1)
        for si, (so, nreal, S, Sbf) in enumerate(seqs):
            pS = pSs[si] if hoist else state_mm(si)
            dcol = o + so + nreal - 1
            for hd in range(4):
                self.stt("dve", S[:, hd, :], S[:, hd, :], e1T[:, hd, dcol:dcol + 1], pS[hd // 2][:, (hd % 2) * 256:(hd % 2 + 1) * 256], ALU.mult, ALU.add)
            self.cp("act", Sbf[:, :, :], S[:, :, :])
        for b in range(2):
            pv = pO[b][:, :4 * L].rearrange("p (c l) -> p c l", c=4)
            self.cp("act", self.o_raw[:, 4 * b:4 * b + 4, :L], pv)
            self.act(self.osq[:, 4 * b:4 * b + 4, :L], pv, AF.Square)
        pM = self.pf()
        for hd in range(4):
            self.mm(pM[:, hd * L:(hd + 1) * L], c["ones_mean"][:, :], self.osq[:, 2 * hd, :L], True, False)
            self.mm(pM[:, hd * L:(hd + 1) * L], c["ones_mean"][:, :], self.osq[:, 2 * hd + 1, :L], False, True)
        self.rsqrt(self.rstdg[:, :4 * L], pM[:, :4 * L], 1.0, EPS)
        r3 = self.rstdg[:, :4 * L].rearrange("p (h l) -> p h l", h=4)
        o4 = self.o_raw[:, :, :].rearrange("p (h v) l -> p h v l", v=2)
        for vc in range(2):
            self.stt("dve", o4[:, :, vc, :L], o4[:, :, vc, :L], self.hgain[:, vc:vc + 1], r3, ALU.mult, ALU.mult)
        self.tt("dve", self.sgT[:, :, o:o + L], self.o_raw[:, :, :L], self.sgT[:, :, o:o + L], ALU.mult)

    def swa_block(self, ti, q0, ntt, nq, kts, g4, between=None):
        c = self.c
        N = 4 * nq
        for P2 in range(2):
            if between is not None:
                between()
            pts = []
            for kt in kts:
                nk = kt["nk"]
                pss = []
                for half in range(2):
                    rows = slice(half * 64, half * 64 + 64)
                    rhs_q = self.qo[P2][rows, ti * 4 * nq:(ti + 1) * 4 * nq]
                    ps = self.pf()
                    self.mm(ps[:nk, :N], kt["knT"](P2, rows), rhs_q, True, False)
                    pss.append(ps)
                for half in range(2):
                    self.mm(pss[half][:nk, :N], c["ident"][:, :nk], kt["bias"][:, 2 * P2 + half, :N], False, True)
                for half in range(2):
                    ps = pss[half]
                    self._pt += 1
                    pt = self.PT[self._pt % 6]
                    if kt["hb"] is not None:
                        self.act(pt[:nk, :N], ps[:nk, :N], AF.Exp, bias=kt["hb"][:nk, :])
                    else:
                        self.act(pt[:nk, :N], ps[:nk, :N], AF.Exp)
                    pts.append((pt, kt, half))
            pO = self.pf()
            pD = self.pf()
            n = len(pts)
            for idx, (pt, kt, half) in enumerate(pts):
                nk = kt["nk"]
                vz4 = kt["vz"][:, :].rearrange("p (j c) -> p j c", j=4)
                self.mm(pO[:, :N], vz4[:nk, 2 * P2 + half, :], pt[:nk, :N], idx == 0, idx == n - 1)
            for idx, (pt, kt, half) in enumerate(pts):
                nk = kt["nk"]
                self.mm(pD[:, :N], c["onesH"][:nk, half, :], pt[:nk, :N], idx == 0, False)
            self.mm(pD[:, :N], self.esinkT[0:4, P2, :], g4[0:4, :N], False, True)
            self.add("dve", lambda h, pD=pD, N=N: h.reciprocal(out=self.rden[:, :N], in_=pD[:, :N]), K(pD[:]), K(self.rden[:]))
            self.tt("dve", self.osT[P2][:, :, q0:q0 + nq], pO[:, :N].rearrange("p (g q) -> p g q", g=4),
                    self.rden[:, :N].rearrange("p (g q) -> p g q", g=4), ALU.mult)

    def final_out(self, tts, dst):
        for i, (o, P) in enumerate(tts):
            self.act(self.junk[:P, :], self.xt[i][:P, :], AF.Square, accum_out=self.ss[i][:P, 0:1])
            self.rsqrt(self.rstd[i][:P, 0:1], self.ss[i][:P, 0:1], 1.0 / D, EPS)
            self.ts("dve", self.xt[i][:P, :], self.xt[i][:P, :], self.rstd[i][:P, 0:1], None, ALU.mult)
            self.tt("pool" if i % 2 else "dve", self.xt[i][:P, :], self.xt[i][:P, :], self.fgain_bc[:P, :], ALU.mult)
            self.dma("sp", dst[o:o + P, :], self.xt[i][:P, :])

    def body(self):
        di = self.din
        self.reset_counters()
        pre_on = not self.debug.get("skip_pre") and not self.debug.get("only_sample")
        self.setup(first_x=di["xprev"][0:T, :] if pre_on else None)
        tts = [(j * 128, 128) for j in range(4)]
        seqs_p = [(0, 128, self.S[0], self.Sbf[0])]
        stop = self.debug.get("stop")
        if not self.debug.get("skip_pre") and not self.debug.get("only_sample"):
            for t in range(NTILE):
                self.next_x = di["xprev"][(t + 1) * T:(t + 2) * T, :] if t + 1 < NTILE else di["xp"][0:T, :]
                self.ffn(tts, T, di["ffn1_up"], di["ffn1_down"], 0, ("pre", t, "f1"))
                self.mixer(tts, T, "pre", ("pre", t), seqs_p, t, t == NTILE - 1)
            self.next_x = None
        pre_ran = not self.debug.get("skip_pre") and not self.debug.get("only_sample")
        for t in range(NTILE if not self.debug.get("only_sample") else 0):
            if self.x_loaded:
                self.x_loaded = False
            elif not (t == 0 and pre_ran):
                self.load_x(di["xp"][t * T:(t + 1) * T, :], tts)
            self.ffn(tts, T, di["ffn1_up"], di["ffn1_down"], 0, ("own", t, "f1"))
            if stop == "ffn1":
                self.final_dbg_x(tts, t)
                continue
            self.mixer(tts, T, "full", ("own", t), seqs_p, t, t == NTILE - 1)
            if stop == "mixer":
                self.final_dbg_x(tts, t)
                continue
            if t == NTILE - 1 and not self.debug.get("no_sample") and not self.debug.get("no_ffn2"):
                self.dma("sp", self.dout["ng"].rearrange("h k v -> k h v"), self.S[0][:, :, :])
                self._ng_done = True
                self.sample_setup_early()
            staged = False
            if not self.debug.get("no_ffn2"):
                hook = None
                smp_next = (t == NTILE - 1 and not self.debug.get("no_sample") and not self.debug.get("no_final") and not stop)
                if smp_next:
                    tss_ = [(0, TS)]
                    xst = [self.arena1[:, 0:2048].bitcast(F32)]
                    self.dma("sp", xst[0][:TS, :], di["xs"])

                    def hook(xst=xst, tss_=tss_):
                        self.norm_stats(tss_, src=xst)

                    def hook2(tss_=tss_):
                        self.norm_tr(tss_, TS, 0)
                    staged = True
                if t + 1 < NTILE and not self.debug.get("no_final"):
                    xst = [self.arena1[:, i * 2048:(i + 1) * 2048].bitcast(F32) for i in range(4)]
                    nsrc = di["xp"][(t + 1) * T:(t + 2) * T, :]
                    for i, (o, P) in enumerate(tts):
                        self.dma("sp", xst[i][:P, :], nsrc[o:o + P, :])

                    def hook(xst=xst):
                        self.norm_stats(tts, src=xst)

                    def hook2():
                        self.norm_tr(tts, T, 0)
                    staged = True
                self.ffn(tts, T, di["ffn2_up"], di["ffn2_down"], 16, ("own", t, "f2"), mid_hook=hook, mid_hook2=hook2 if hook else None)
            if self.debug.get("no_final"):
                self.final_dbg_x(tts, t)
            else:
                self.final_out(tts, self.dout["y"][t * T:(t + 1) * T, :])
            if staged:
                for i, (o, P) in enumerate(tss_ if smp_next else tts):
                    self.cp("pool", self.xt[i][:P, :], xst[i][:P, :])
                self.norm_done = True
                self.x_loaded = True
        if not self._ng_done:
            self.dma("sp", self.dout["ng"].rearrange("h k v -> k h v"), self.S[0][:, :, :])
        if stop or self.debug.get("no_sample"):
            return
        if not self._early_done:
            self.sample_setup_early()
        tss = [(0, TS)]
        self.dma("sp", self.S[1][:, :, :], di["st"][1].rearrange("h k v -> k h v"))
        self.cp("pool", self.Sbf[1][:, :, :], self.S[1][:, :, :])
        seqs_s = [(0, 16, self.S[0], self.Sbf[0]), (32, 16, self.S[1], self.Sbf[1])]
        if self.x_loaded:
            self.x_loaded = False
        else:
            self.load_x(di["xs"], tss)
        self.sample_rest(tss, seqs_s)

    def sample_setup_early(self):
        di = self.din
        self._early_done = True
        self.dma("pool", self.biasTs, di["biass"].rearrange("a p j n -> p a j n"))
        for n in ("tri_incl", "tri_suf", "cmask4"):
            self.dma("sp", self.c[n][:], di[n + "_s"])
        self.dma("sp", self.S[0][:, :, :], di["st"][0].rearrange("h k v -> k h v"))
        self.cp("pool", self.Sbf[0][:, :, :], self.S[0][:, :, :])
        for sq_ in range(2):
            self.dma("sp", self.ckf[:, :], di["ck"][sq_])
            self.cp("dve", self.ckb[:, :], self.ckf[:, :])
            for P2 in range(2):
                pT = self.pb()
                self.tr(pT[:, 0:128], self.ckb[:, P2 * 128:(P2 + 1) * 128], self.c["ident"][:, :])
                self.cp("act", self.knT_c[sq_][:, P2, :], pT[:, 0:128])
            self.dma("sp", self.ckf[:, :], di["cv"][sq_])
            vz4 = self.vz_c[sq_][:, :].rearrange("p (jp par c) -> p jp par c", jp=2, par=2)
            cv4 = self.ckf[:, :].rearrange("p (jp par d) -> p jp par d", jp=2, par=2)
            for par in range(2):
                self.cp("dve", vz4[:, :, par, par * 64:(par + 1) * 64], cv4[:, :, par, :])

    def sample_rest(self, tss, seqs_s):
        di = self.din
        sstop = self.debug.get("sstop")
        self.ffn(tss, TS, di["ffn1_up"], di["ffn1_down"], 0, ("smp", "f1"))
        if sstop != "ffn1":
            self.mixer(tss, TS, "sample", ("smp",), seqs_s, 0, True)
            if sstop != "mixer":
                self.ffn(tss, TS, di["ffn2_up"], di["ffn2_down"], 16, ("smp", "f2"))
        self.final_out(tss, self.dout["ys"])
        for sq_ in range(2):
            self.dma("sp", self.dout["ngs"][sq_].rearrange("h k v -> k h v"), self.S[sq_][:, :, :])

    def final_dbg_x(self, tts, t):
        for i, (o, P) in enumerate(tts):
            self.dma("sp", self.dout["y"][t * T + o:t * T + o + P, :], self.xt[i][:P, :])

    def build(self):
        self.planning = True
        self.body()
        self.planning = False
        self.nuse = {}
        for pl in self.plan:
            self.nuse[pl[2]] = self.nuse.get(pl[2], 0) + 1
        self.body()
        self.s.emit()
        return self.nc


_CACHE = {}


def _get_prog(debug=None):
    key = repr(sorted((debug or {}).items()))
    if key not in _CACHE:
        p = Prog(debug)
        p.build()
        _CACHE[key] = p
    return _CACHE[key]


def make_in_maps(inp):
    f = lambda a: np.ascontiguousarray(np.asarray(a, dtype=np.float32))
    xp = f(inp["x_prompt"])
    xsm = f(inp["x_sample"])
    ck = f(inp["cache_swa_k"])[0].reshape(16, 128, 256)
    cv = f(inp["cache_swa_v"])[0].reshape(16, 128, 256)
    st = f(inp["state_gla"])[0]
    consts = _consts()
    bp, bs = _bias_tables(inp["rel_bias"])
    gains = np.stack([f(inp["ffn1_norm"])[0], f(inp["mix_norm"])[0], f(inp["ffn2_norm"])[0]])
    gains_fm = np.ascontiguousarray(gains.reshape(3, 8, 128).transpose(2, 0, 1).reshape(128, 24))
    fg = np.ascontiguousarray(np.broadcast_to(f(inp["final_norm"])[0][None, :], (128, D)))
    hg = np.ascontiguousarray(f(inp["gla_head_norm"])[0].reshape(2, 128).T)
    qg = np.ascontiguousarray(np.tile(f(inp["q_norm"])[0], 2)[:, None])
    kg = np.ascontiguousarray(np.tile(f(inp["k_norm"])[0], 2)[:, None])
    kgb = np.ascontiguousarray(np.broadcast_to(f(inp["k_norm"])[0][None, :], (128, 64)))
    sinks = f(inp["attn_sinks"])[0]
    sinkT = np.zeros((2, 4, 128), np.float32)
    for P2 in range(2):
        for g in range(4):
            sinkT[P2, g, :64] = sinks[4 * (2 * P2) + g]
            sinkT[P2, g, 64:] = sinks[4 * (2 * P2 + 1) + g]
    shared = {
        "ffn1_up": f(inp["ffn1_w_up"])[0], "ffn1_down": f(inp["ffn1_w_down"])[0],
        "ffn2_up": f(inp["ffn2_w_up"])[0], "ffn2_down": f(inp["ffn2_w_down"])[0],
        "w_in": f(inp["w_in"])[0], "w_branch": f(inp["w_branch"])[0], "w_out": f(inp["w_out"])[0],
        "w_alpha": f(inp["gla_w_alpha"])[0], "b_alpha": f(inp["gla_b_alpha"]).reshape(1, 512),
        "gains_fm": gains_fm, "fgain_bc": fg, "hgain": hg, "qgcol": qg, "kgcol": kg, "kgain_bc": kgb,
        "sinkT": sinkT, "biasp": bp, "biass": bs,
    }
    shared.update(consts)
    maps = []
    zeros_prev = np.zeros((TOKC, D), np.float32)
    for cidx in range(NCORES):
        b, half = cidx // 2, cidx % 2
        m = dict(shared)
        m["xp"] = np.ascontiguousarray(xp[b, half * TOKC:(half + 1) * TOKC])
        m["xprev"] = np.ascontiguousarray(xp[b, 0:TOKC]) if half == 1 else zeros_prev
        xs_ = np.zeros((TS, D), np.float32)
        for s_ in range(2):
            xs_[32 * s_:32 * s_ + 16] = xsm[2 * cidx + s_]
        m["xs"] = xs_
        m["ck"] = np.ascontiguousarray(ck[2 * cidx:2 * cidx + 2])
        m["cv"] = np.ascontiguousarray(cv[2 * cidx:2 * cidx + 2])
        m["st"] = np.ascontiguousarray(st[2 * cidx:2 * cidx + 2])
        m["halo_bias"] = np.full((128, 1), 0.0 if half == 1 else NEG, np.float32)
        maps.append(m)
    return maps


def run(inp, debug=None):
    prog = _get_prog(debug)
    maps = make_in_maps(inp)
    res = run_bass_kernel_spmd(prog.nc, maps, core_ids=list(range(NCORES)))
    return prog, res


def kernel(**inp):
    prog, res = run(inp)
    R = res.results
    y = np.zeros((4, 4096, D), np.float32)
    ys = np.zeros((16, 16, D), np.float32)
    nk = np.zeros((1, 4, 128, 4, 64), np.float32)
    nv = np.zeros((1, 4, 128, 4, 64), np.float32)
    ng = np.zeros((1, 4, 4, 128, 256), np.float32)
    nks = np.zeros((1, 16, 16, 4, 64), np.float32)
    nvs = np.zeros((1, 16, 16, 4, 64), np.float32)
    ngs = np.zeros((1, 16, 4, 128, 256), np.float32)
    for cidx in range(NCORES):
        b, half = cidx // 2, cidx % 2
        r = R[cidx]
        y[b, half * TOKC:(half + 1) * TOKC] = r["y"]
        if half == 1:
            nk[0, b] = r["nk"].reshape(128, 4, 64)
            nv[0, b] = r["nv"].reshape(128, 4, 64)
            ng[0, b] = r["ng"]
        for s_ in range(2):
            ys[2 * cidx + s_] = r["ys"][32 * s_:32 * s_ + 16]
            nks[0, 2 * cidx + s_] = r["nks"][32 * s_:32 * s_ + 16].reshape(16, 4, 64)
            nvs[0, 2 * cidx + s_] = r["nvs"][32 * s_:32 * s_ + 16].reshape(16, 4, 64)
            ngs[0, 2 * cidx + s_] = r["ngs"][s_]
    return (y, ys, nk, nv, ng, nks, nvs, ngs)
```

```python
import numpy as np
import ml_dtypes
import concourse.bass as bass
import concourse.mybir as mybir
from concourse.bass_utils import run_bass_kernel_spmd

F32 = mybir.dt.float32
BF16 = mybir.dt.bfloat16
AF = mybir.ActivationFunctionType
ALU = mybir.AluOpType
AX = mybir.AxisListType

ENGS = ("pe", "act", "dve", "pool", "sp")
NCORES = 8
D = 1024
DFF = 2816
TOKC = 2048
T = 512
NTILE = TOKC // T
TS = 64
EPS = 1e-6
NEG = -30000.0
IN_W = 6672
C_GQ, C_GK, C_GV, C_GR, C_GA, C_SQ, C_SK, C_SV, C_GTA, C_GTB = 0, 512, 1024, 2048, 3072, 3088, 4112, 4368, 4624, 5648


class Op:
    __slots__ = ("eng", "fn", "deps", "sig", "semval", "dma", "dsem", "dval", "idx", "group")


class Sched:
    def __init__(self, nc, n_dma_sems=48):
        self.nc = nc
        self.ops = []
        self.by_eng = {e: [] for e in ENGS}
        self.last_w = {}
        self.readers = {}
        self.esem = {e: nc.alloc_semaphore(name="cnt_" + e) for e in ENGS}
        self.dsems = [nc.alloc_semaphore(name="dma%d" % i) for i in range(n_dma_sems)]
        self.dcount = [0] * n_dma_sems
        self.dlast = [None] * n_dma_sems
        self.dnext = {"sw": 0, "hw": 0}

    def add(self, eng, fn, reads=(), writes=(), dma=False, group=None):
        op = Op()
        op.group = group
        op.eng = eng
        op.fn = fn
        op.sig = False
        op.semval = None
        op.dma = dma
        op.idx = len(self.ops)
        deps = {}

        def dep(d, raw):
            if d is None:
                return
            if not d.dma and not dma and d.eng == eng:
                if eng == "pe" or not raw:
                    return
            deps[d.idx] = d

        for k in reads:
            for w in self.last_w.get(k, ()):
                dep(w, True)
        for k in writes:
            for w in self.last_w.get(k, ()):
                if group is not None and w.group == group:
                    continue
                dep(w, False)
            for r in self.readers.get(k, ()):
                dep(r, False)
        if dma:
            half = len(self.dsems) // 2
            cls = "sw" if eng == "pool" else "hw"
            j = self.dnext[cls] + (half if cls == "sw" else 0)
            self.dnext[cls] = (self.dnext[cls] + 1) % half
            if self.dlast[j] is not None:
                deps[self.dlast[j].idx] = self.dlast[j]
            self.dcount[j] += 16
            op.dsem = j
            op.dval = self.dcount[j]
            self.dlast[j] = op
        for k in reads:
            lst = self.readers.setdefault(k, [])
            if not dma:
                lst[:] = [r for r in lst if r.dma or r.eng != eng]
            lst.append(op)
        for k in writes:
            lw = self.last_w.get(k, [])
            if group is not None and lw and lw[0].group == group:
                lw.append(op)
            else:
                self.last_w[k] = [op]
                self.readers[k] = []
        op.deps = list(deps.values())
        for d in op.deps:
            if not d.dma:
                d.sig = True
        self.ops.append(op)
        self.by_eng[eng].append(op)
        return op

    def emit(self):
        nc = self.nc
        for e in ENGS:
            c = 0
            for op in self.by_eng[e]:
                if not op.dma and op.sig:
                    c += 1
                    op.semval = c
        final_d = list(self.dcount)

        def replay(e, h):
            seen = {}
            for op in self.by_eng[e]:
                need = {}
                for d in op.deps:
                    if d.dma:
                        key, val, sem = ("d", d.dsem), d.dval, self.dsems[d.dsem]
                    else:
                        key, val, sem = ("e", d.eng), d.semval, self.esem[d.eng]
                    if val > need.get(key, (0, None))[0]:
                        need[key] = (val, sem)
                for key, (val, sem) in need.items():
                    if seen.get(key, 0) >= val:
                        continue
                    seen[key] = val
                    h.wait_ge(sem, val)
                ins = op.fn(h)
                if op.dma:
                    ins.then_inc(self.dsems[op.dsem], 16)
                elif op.sig:
                    ins.then_inc(self.esem[e], 1)
            if e == "sp":
                for j, v in enumerate(final_d):
                    if v > 0:
                        h.wait_ge(self.dsems[j], v)

        with nc.Block() as block:
            @block.tensor
            def _(h):
                replay("pe", h)

            @block.scalar
            def _(h):
                replay("act", h)

            @block.vector
            def _(h):
                replay("dve", h)

            @block.gpsimd
            def _(h):
                replay("pool", h)

            @block.sync
            def _(h):
                replay("sp", h)


def _is_dram(ap):
    return "DRam" in type(ap.tensor).__name__


def K(*aps):
    out = []
    for a in aps:
        if a is None or isinstance(a, (int, float)):
            continue
        if _is_dram(a):
            continue
        out.append(a.tensor.name)
    return out


def _t5_bucket(rel):
    nb = 16
    ret = np.where(rel > 0, nb, 0)
    n = np.abs(rel)
    max_exact = nb // 2
    nf = np.maximum(n, 1).astype(np.float32)
    large = max_exact + (np.log(nf / max_exact) / np.float32(np.log(128 / max_exact)) * (nb - max_exact)).astype(np.int32)
    large = np.minimum(large, nb - 1)
    return ret + np.where(n < max_exact, n, large)


def _bias_tables(rel_bias):
    rb = np.asarray(rel_bias, np.float32)
    q = np.arange(128)
    out = np.zeros((2, 128, 4, 4, 128), np.float32)
    for kt in range(2):
        kloc = np.arange(128) + (kt - 1) * 128
        rel = kloc[:, None] - q[None, :]
        bk = _t5_bucket(rel)
        kc = np.floor_divide(kloc, 64)[:, None]
        qc = (q // 64)[None, :]
        valid = (kc <= qc) & (kc >= qc - 2)
        g = rb[bk]
        g = np.where(valid[:, :, None], g, NEG)
        out[kt] = g.reshape(128, 128, 4, 4).transpose(0, 2, 3, 1)
    bp = out.reshape(2, 128, 4, 512)
    outs = np.full((3, 128, 4, 4, 64), NEG, np.float32)
    qpos = 2048 + np.arange(16)
    for s in range(2):
        kpos = 2048 - 128 + np.arange(128)
        bk = _t5_bucket(kpos[:, None] - qpos[None, :])
        g = rb[bk].reshape(128, 16, 4, 4).transpose(0, 2, 3, 1)
        outs[s, :, :, :, 32 * s:32 * s + 16] = g
        bk2 = _t5_bucket(qpos[:, None] - qpos[None, :])
        g2 = rb[bk2].reshape(16, 16, 4, 4).transpose(0, 2, 3, 1)
        outs[2, 32 * s:32 * s + 16, :, :, 32 * s:32 * s + 16] = g2
    bs = outs.reshape(3, 128, 4, 256)
    return np.ascontiguousarray(bp, np.float32), np.ascontiguousarray(bs, np.float32)


def _consts():
    c = {}
    c["ident"] = np.eye(128, dtype=np.float32).astype(ml_dtypes.bfloat16)
    s = np.arange(128)[:, None]
    t = np.arange(128)[None, :]
    c["tri_incl"] = np.where(s <= t, -1.0 / 16, 0.0).astype(np.float32).astype(ml_dtypes.bfloat16)
    c["tri_suf"] = np.where(s > t, -1.0 / 16, 0.0).astype(np.float32).astype(ml_dtypes.bfloat16)
    c["cmask4"] = np.tile(np.where(s <= t, 1.0, 0.0).astype(np.float32), (1, 4))
    c["tri_all"] = np.full((128, 128), -1.0 / 16, np.float32).astype(ml_dtypes.bfloat16)
    s6 = np.arange(64)[:, None]
    t6 = np.arange(64)[None, :]
    same = (s6 // 32) == (t6 // 32)
    real_s = (s6 % 32) < 16
    tis = np.zeros((128, 128), np.float32)
    tis[:64, :64] = np.where(same & (s6 <= t6), -1.0 / 16, 0.0)
    tss = np.zeros((128, 128), np.float32)
    tss[:64, :64] = np.where(same & (s6 > t6) & real_s, -1.0 / 16, 0.0)
    cms = np.zeros((128, 512), np.float32)
    cms[:64, :256] = np.tile(np.where(same & (s6 <= t6), 1.0, 0.0), (1, 4))
    c["tri_incl_s"] = tis.astype(ml_dtypes.bfloat16)
    c["tri_suf_s"] = tss.astype(ml_dtypes.bfloat16)
    c["cmask4_s"] = cms
    oh = np.zeros((2, 128, 128), np.float32)
    oh[0, :, :64] = 1.0
    oh[1, :, 64:] = 1.0
    c["onesH"] = oh.astype(ml_dtypes.bfloat16)
    g4 = np.zeros((4, 512), np.float32)
    g4s = np.zeros((4, 512), np.float32)
    for g in range(4):
        g4[g, g * 128:(g + 1) * 128] = 1.0
        g4s[g, g * 64:(g + 1) * 64] = 1.0
    c["g4"] = g4.astype(ml_dtypes.bfloat16)
    c["g4s"] = g4s.astype(ml_dtypes.bfloat16)
    bo = np.zeros((128, 128), np.float32)
    bo[:64, :64] = 1.0
    bo[64:, 64:] = 1.0
    c["bo_q"] = bo.astype(ml_dtypes.bfloat16)
    c["bo_k"] = (bo / 64).astype(ml_dtypes.bfloat16)
    c["ones_mean"] = np.full((128, 128), 1.0 / 256, np.float32).astype(ml_dtypes.bfloat16)
    c["ones_row"] = np.ones((1, 128), np.float32).astype(ml_dtypes.bfloat16)
    return c


CONST_SPECS = [
    ("ident", [128, 128], BF16), ("tri_incl", [128, 128], BF16), ("tri_suf", [128, 128], BF16),
    ("cmask4", [128, 512], F32), ("tri_all", [128, 128], BF16), ("tri_incl_s", [128, 128], BF16), ("tri_suf_s", [128, 128], BF16),
    ("cmask4_s", [128, 512], F32), ("onesH", [2, 128, 128], BF16), ("g4", [4, 512], BF16), ("g4s", [4, 512], BF16),
    ("bo_q", [128, 128], BF16), ("bo_k", [128, 128], BF16), ("ones_mean", [128, 128], BF16), ("ones_row", [1, 128], BF16),
]


class Prog:
    def __init__(self, debug=None):
        self.nc = nc = bass.Bass("TRN2", target_bir_lowering=False)
        self.s = Sched(nc)
        self.planning = False
        self.plan = []
        self.wscr = None
        self.next_x = None
        self.norm_done = False
        self.x_loaded = False
        self.debug = debug or {}
        self.dbg_outs = []
        self.din = {}
        self.dout = {}
        self._declare_io()
        self._alloc()

    def _in(self, name, shape, dt=F32):
        self.din[name] = self.nc.dram_tensor(name, list(shape), dt, kind="ExternalInput").ap()
        return self.din[name]

    def _out(self, name, shape, dt=F32):
        self.dout[name] = self.nc.dram_tensor(name, list(shape), dt, kind="ExternalOutput").ap()
        return self.dout[name]

    def _declare_io(self):
        i = self._in
        i("xp", [TOKC, D]); i("xprev", [TOKC, D]); i("xs", [TS, D])
        i("ck", [2, 128, 256]); i("cv", [2, 128, 256]); i("st", [2, 4, 128, 256])
        i("ffn1_up", [D, 2 * DFF]); i("ffn1_down", [DFF, D]); i("ffn2_up", [D, 2 * DFF]); i("ffn2_down", [DFF, D])
        i("w_in", [D, IN_W]); i("w_branch", [2048, D]); i("w_out", [D, D])
        i("w_alpha", [16, 512]); i("b_alpha", [1, 512])
        i("gains_fm", [128, 24])
        i("fgain_bc", [128, D])
        i("hgain", [128, 2])
        i("qgcol", [128, 1]); i("kgcol", [128, 1]); i("kgain_bc", [128, 64])
        i("sinkT", [2, 4, 128])
        i("halo_bias", [128, 1])
        i("biasp", [2, 128, 4, 512]); i("biass", [3, 128, 4, 256])
        for n, sh, dt in CONST_SPECS:
            i(n, sh, dt)
        o = self._out
        o("y", [TOKC, D]); o("ys", [TS, D]); o("nk", [128, 256]); o("nv", [128, 256]); o("ng", [4, 128, 256])
        o("nks", [TS, 256]); o("nvs", [TS, 256]); o("ngs", [2, 4, 128, 256])

    def sb(self, name, shape, dt):
        return self.nc.alloc_sbuf_tensor("s_" + name, list(shape), dt)

    def _alloc(self):
        nc = self.nc
        sb = self.sb
        self.c = {}
        for n, sh, dt in CONST_SPECS:
            if n.endswith("_s"):
                continue
            if len(sh) == 3:
                self.c[n] = sb("c_" + n, [sh[1], sh[0], sh[2]], dt)
            else:
                self.c[n] = sb("c_" + n, sh, dt)
        self.gains_fm = sb("gains_fm", [128, 24], F32)
        self.fgain_bc = sb("fgain_bc", [128, D], F32)
        self.hgain = sb("hgain", [128, 2], F32)
        self.qgcol = sb("qgcol", [128, 1], F32)
        self.kgcol = sb("kgcol", [128, 1], F32)
        self.kgain_bc = sb("kgain_bc", [128, 64], F32)
        self.sinkT = sb("sinkT", [4, 2, 128], F32)
        self.esinkT = sb("esinkT", [4, 2, 128], BF16)
        self.halo_bias = sb("halo_bias", [128, 1], F32)
        self.walpha = sb("walpha", [16, 512], BF16)
        self.balpha = sb("balpha", [1, 512], BF16)
        self.biasT = sb("biasT", [128, 2, 4, 512], BF16)
        self.biasTs = self.biasT[:, :, :, :].rearrange("p a j n -> p (a j n)")[:, 0:3072].rearrange("p (a j n) -> p a j n", a=3, j=4)
        self.xt = [sb("xt%d" % i, [128, D], F32) for i in range(4)]
        self.hball = sb("hball", [128, 4, D], BF16)
        self.hb = [self.hball[:, i, :] for i in range(4)]
        self.osT = [self.hball[:, 2 * i:2 * i + 2, :].rearrange("p a (g t) -> p (a g) t", g=2) for i in range(2)]
        self.ss = [sb("ss%d" % i, [128, 1], F32) for i in range(4)]
        self.rstd = [sb("rstd%d" % i, [128, 1], F32) for i in range(4)]
        self.hT = [sb("hT%d" % i, [128, T], BF16) for i in range(8)]
        self.arena0 = sb("arena0", [128, 22 * T], BF16)
        self.actT = [self.arena0[:, i * T:(i + 1) * T] for i in range(22)]
        self.e1T = self.arena0[:, 0:4096].bitcast(F32).rearrange("p (h t) -> p h t", h=4)
        self.e2T = self.arena0[:, 4096:8192].bitcast(F32).rearrange("p (h t) -> p h t", h=4)
        self.qeT = self.arena0[:, 8192:10240].rearrange("p (h t) -> p h t", h=4)
        self.arena1 = sb("arena1", [128, 10240], BF16)
        self.v_tok = [self.arena1[:, i * 1024:(i + 1) * 1024] for i in range(4)]
        self.kd = [self.arena1[:, 4096 + i * 512:4096 + (i + 1) * 512] for i in range(4)]
        self.es = [self.arena1[:, 6144 + i * 1024:6144 + (i + 1) * 1024].bitcast(F32) for i in range(4)]
        self.sigA = self.arena1[:, 0:4096].rearrange("p (c t) -> p c t", c=8)
        self.sigB = self.arena1[:, 4096:8192].rearrange("p (c t) -> p c t", c=8)
        self.keT = sb("keT", [128, 4, T], BF16)
        self.gaT = sb("gaT", [16, T], BF16)
        self.sp = [sb("sp%d" % i, [128, 512], BF16) for i in range(4)]
        self.dtot = sb("dtot", [128, 4, 1], F32)
        self.sgT = sb("sgT", [128, 8, T], BF16)
        self.ATm = sb("ATm", [128, 512], BF16)
        self.ATm2 = [self.ATm, sb("ATm1", [128, 512], BF16)]
        self.Sbf_alt = sb("Sbf_alt", [128, 4, 256], BF16)
        self.o_raw = sb("o_raw", [128, 8, 128], F32)
        self.osq = sb("osq", [128, 8, 128], BF16)
        self.rstdg = sb("rstdg", [128, 512], F32)
        self.o_raw2 = [self.o_raw, sb("o_raw1", [128, 8, 128], F32)]
        self.osq2 = [self.osq, sb("osq1", [128, 8, 128], BF16)]
        self.S = [sb("S0", [128, 4, 256], F32), self.xt[1][:, :].rearrange("p (h v) -> p h v", h=4)]
        self.Sbf = [sb("Sbf0", [128, 4, 256], BF16), self.xt[2][:, 0:512].bitcast(BF16).rearrange("p (h v) -> p h v", h=4)]
        self.sqr = [sb("sqr%d" % i, [128, T], F32) for i in range(1)]
        self.sqsq = [sb("sqsq%d" % i, [128, T], BF16) for i in range(1)]
        self.rsq = [sb("rsq%d" % i, [128, T], F32) for i in range(1)]
        self.qo = [sb("qo%d" % i, [128, 4 * T], BF16) for i in range(2)]
        self.knT_cur = sb("knT_cur", [128, 2, T], BF16)
        self.knT_prev = sb("knT_prev", [128, 2, 128], BF16)
        self.vz_cur = [sb("vz_cur%d" % i, [128, 512], BF16) for i in range(4)]
        self.vz_prev = sb("vz_prev", [128, 512], BF16)
        self.PT = [sb("PT%d" % i, [128, 512], BF16) for i in range(6)]
        self.rden = sb("rden", [128, 512], F32)
        self.ksq = self.sqr[0][:, 0:256]
        self.ktok = self.sqr[0][:, 256:512]
        self.kss = sb("kss", [128, 4], F32)
        self.knew = self.rsq[0][:, 0:256]
        self.vnew = self.rsq[0][:, 256:512]
        self.mtmp = [sb("mtmp%d" % i, [128, T], F32) for i in range(2)]
        self.sa = self.mtmp
        self.junk = self.mtmp[0][:, :].bitcast(BF16)
        self.ckf = sb("ckf", [128, 256], F32)
        self.ckb = sb("ckb", [128, 256], BF16)
        self.knT_c = [sb("knT_c%d" % i, [128, 2, 128], BF16) for i in range(2)]
        self.vz_c = [sb("vz_c%d" % i, [128, 512], BF16) for i in range(2)]
        self.NSLOT = 4
        self.slots = [sb("wslot%d" % i, [128, 4096], BF16) for i in range(self.NSLOT)]
        self.psf = [nc.alloc_psum_tensor("psf%d" % i, [128, 512], F32) for i in range(6)]
        self.psb = [nc.alloc_psum_tensor("psb%d" % i, [128, 1024], BF16) for i in range(2)]

    def reset_counters(self):
        self._ng_done = False
        self._early_done = False
        self.norm_done = False
        self.x_loaded = False
        self.next_x = None
        self._pf = 0
        self._pb = 0
        self._pt = 0
        self._piece = 0
        self._rr = 0

    def pf(self):
        self._pf += 1
        return self.psf[self._pf % 6]

    def pb(self):
        self._pb += 1
        return self.psb[self._pb % 2]

    def add(self, eng, fn, reads, writes, dma=False, group=None):
        if self.planning:
            return
        self.s.add(eng, fn, reads, writes, dma, group)

    def mm(self, out, lhsT, rhs, start=True, stop=True, rkeys=None):
        self.add("pe", lambda h: h.matmul(out, lhsT=lhsT, rhs=rhs, start=start, stop=stop), K(lhsT, rhs) if rkeys is None else rkeys, K(out))

    def tr(self, out, in_, ident, rkeys=None):
        self.add("pe", lambda h: h.transpose(out, in_, ident), K(in_, ident) if rkeys is None else rkeys, K(out))

    def act(self, out, in_, func, bias=None, scale=None, accum_out=None):
        kw = {}
        if bias is not None:
            kw["bias"] = bias
        if scale is not None:
            kw["scale"] = scale
        if accum_out is not None:
            kw["accum_out"] = accum_out
        self.add("act", lambda h: h.activation(out=out, in_=in_, func=func, **kw), K(in_, bias, scale), K(out, accum_out))

    def rsqrt(self, out, in_, scale, eps):
        self.act(out, in_, AF.Ln, bias=eps, scale=scale)
        self.act(out, out, AF.Exp, scale=-0.5)

    def amul(self, out, in_, mul):
        self.add("act", lambda h: h.mul(out=out, in_=in_, mul=mul), K(in_, mul), K(out))

    def cp(self, eng, out, in_):
        if eng == "act":
            self.add("act", lambda h: h.copy(out=out, in_=in_), K(in_), K(out))
        else:
            self.add(eng, lambda h: h.tensor_copy(out=out, in_=in_), K(in_), K(out))

    def tt(self, eng, out, in0, in1, op, rkeys=None, wkeys=None):
        self.add(eng, lambda h: h.tensor_tensor(out=out, in0=in0, in1=in1, op=op), K(in0, in1) if rkeys is None else rkeys, K(out) if wkeys is None else wkeys)

    def ts(self, eng, out, in0, s1, s2, op0, op1=None, rkeys=None, wkeys=None):
        if rkeys is not None:
            self.add(eng, lambda h: h.tensor_scalar(out=out, in0=in0, scalar1=s1, scalar2=0.0, op0=op0, op1=ALU.add), rkeys, wkeys)
            return
        if op1 is None:
            if op0 == ALU.pow:
                self.add(eng, lambda h: h.tensor_scalar(out=out, in0=in0, scalar1=0.0, scalar2=s1, op0=ALU.add, op1=ALU.pow), K(in0, s1), K(out))
            else:
                self.add(eng, lambda h: h.tensor_scalar(out=out, in0=in0, scalar1=s1, scalar2=0.0, op0=op0, op1=ALU.add), K(in0, s1), K(out))
        else:
            self.add(eng, lambda h: h.tensor_scalar(out=out, in0=in0, scalar1=s1, scalar2=s2, op0=op0, op1=op1), K(in0, s1, s2), K(out))

    def stt(self, eng, out, in0, scalar, in1, op0, op1):
        self.add(eng, lambda h: h.scalar_tensor_tensor(out=out, in0=in0, scalar=scalar, in1=in1, op0=op0, op1=op1),
                 K(in0, scalar, in1), K(out))

    def memset(self, eng, ap, val):
        self.add(eng, lambda h: h.memset(ap, val), [], K(ap))

    def dma(self, eng, out, in_, reads=(), writes=(), group=None):
        self.add(eng, lambda h: h.dma_start(out=out, in_=in_), K(in_) + list(reads), K(out) + list(writes), dma=True, group=group)

    def wget(self, tag, spec, wid, nel=4096):
        if self.planning:
            self.plan.append((tag, spec, wid, nel))
            return self.slots[0]
        if self._piece == 0:
            self.first = {}
            for j, pl in enumerate(self.plan):
                self.first.setdefault(pl[2], j)
            self.widx = {w: k for k, w in enumerate(self.first)}
            if self.wscr is None:
                self.wscr = self.nc.dram_tensor("wscr", [len(self.widx), 128, 4096], BF16).ap()
        i = self._piece
        assert self.plan[i][0] == tag, (self.plan[i][0], tag)
        LA = self.NSLOT - 2 if wid[0] == "wbB" else self.NSLOT - 1
        if i == 0:
            self._issue_piece(0)
            self._issued = 0
        slot = self.slots[i % self.NSLOT]
        if self.first[wid] == i and self.nuse[wid] > 1:
            self.dma("pool", self.wscr[self.widx[wid], :, :nel], slot[:, :nel], writes=[("scr", wid)])
        target = min(i + LA, len(self.plan) - 1)
        while self._issued < target:
            self._issued += 1
            self._issue_piece(self._issued)
        self._piece += 1
        return slot

    def _issue_piece(self, j):
        slot = self.slots[j % self.NSLOT]
        tag, spec, wid, nel = self.plan[j]
        if self.first[wid] == j:
            for dst, src in spec(slot):
                self.dma("pool", dst, src, group=("piece", j))
        else:
            self.dma("sp", slot[:, :nel], self.wscr[self.widx[wid], :, :nel], reads=[("scr", wid)])

    def dbg(self, name, ap, shape, dt=F32):
        if name not in self.debug:
            return
        if self.planning:
            return
        d = self.nc.dram_tensor("dbg_" + name, list(shape), dt, kind="ExternalOutput").ap()
        self.dbg_outs.append("dbg_" + name)
        self.dma("sp", d, ap)

    def setup(self, first_x=None):
        c = self.c
        di = self.din
        self.dma("sp", c["ident"][:], di["ident"])
        self.dma("sp", self.gains_fm[:], di["gains_fm"])
        if first_x is not None:
            self.load_x(first_x, [(j * 128, 128) for j in range(4)])
        for n, sh, dt in CONST_SPECS:
            if n == "ident":
                continue
            if n.endswith("_s"):
                continue
            if len(sh) == 3:
                self.dma("sp", c[n][:], di[n].rearrange("a p f -> p a f"))
            else:
                self.dma("sp", c[n][:], di[n])
        for n in ("fgain_bc", "hgain", "qgcol", "kgcol", "kgain_bc", "halo_bias"):
            self.dma("sp", getattr(self, n)[:], di[n])
        self.dma("sp", self.sinkT[:], di["sinkT"].rearrange("a g m -> g a m"))
        self.dma("pool", self.walpha[:], di["w_alpha"])
        self.dma("pool", self.balpha[:], di["b_alpha"])
        self.dma("pool", self.biasT[:, 0:2], di["biasp"].rearrange("a p j n -> p a j n"))
        self.act(self.esinkT[:], self.sinkT[:], AF.Exp)
        for t_ in self.vz_cur + [self.vz_prev] + self.vz_c:
            self.memset("pool", t_[:], 0.0)
        self.memset("pool", self.knT_prev[:], 0.0)
        self.memset("dve", self.S[0][:], 0.0)
        self.memset("dve", self.Sbf[0][:], 0.0)
        for t_ in self.ss:
            self.memset("dve", t_[:], 0.0)
        if self.debug.get("delay"):
            self.memset("pool", self.slots[0][:], 0.0)
            for _ in range(int(self.debug["delay"])):
                self.cp("pool", self.slots[1][:], self.slots[0][:])

    def load_x(self, src, tts):
        for i, (o, P) in enumerate(tts):
            self.dma("sp", self.xt[i][:P, :], src[o:o + P, :])

    def norm_hT(self, tts, Tn, gcol0, src=None):
        self.norm_stats(tts, src)
        self.norm_tr(tts, Tn, gcol0, subtile_major=True)

    def norm_stats(self, tts, src=None):
        xt, hb = (self.xt if src is None else src), self.hb
        for i, (o, P) in enumerate(tts):
            self.act(self.junk[:P, :], xt[i][:P, :], AF.Square, accum_out=self.ss[i][:P, 0:1])
            self.rsqrt(self.rstd[i][:P, 0:1], self.ss[i][:P, 0:1], 1.0 / D, EPS)
            self.ts("pool" if i % 2 else "dve", hb[i][:P, :], xt[i][:P, :], self.rstd[i][:P, 0:1], None, ALU.mult,
                    rkeys=K(xt[i][:], self.rstd[i][:]) + ["s_hball"], wkeys=[("hb", i)])

    def norm_tr(self, tts, Tn, gcol0, subtile_major=False):
        hb = self.hb
        ident = self.c["ident"]
        if subtile_major and len(tts) > 1:
            regs = []
            b0, b1 = self.pb(), self.pb()
            f0, f1 = self.pf()[:, :].bitcast(BF16), self.pf()[:, :].bitcast(BF16)
            for bank in (b0, b1, f0, f1):
                regs += [bank[:, 0:512], bank[:, 512:1024]]
            for i, (o, P) in enumerate(tts):
                for cc in range(8):
                    self.tr(regs[cc][:, o:o + P], hb[i][:P, cc * 128:(cc + 1) * 128], ident[:P, :P],
                            rkeys=K(ident[:]) + [("hb", i), "s_hball"])
            for cc in (0, 2, 1, 3, 4, 6, 5, 7):
                g = self.gains_fm[:, gcol0 + cc:gcol0 + cc + 1]
                if (cc // 2) % 2 == 0:
                    self.amul(self.hT[cc][:, :Tn], regs[cc][:, :Tn], g)
                else:
                    self.ts("dve", self.hT[cc][:, :Tn], regs[cc][:, :Tn], g, None, ALU.mult)
            return
        for cc in range(8):
            pT = self.pb()
            for i, (o, P) in enumerate(tts):
                self.tr(pT[:, o:o + P], hb[i][:P, cc * 128:(cc + 1) * 128], ident[:P, :P],
                        rkeys=K(ident[:]) + [("hb", i), "s_hball"])
            g = self.gains_fm[:, gcol0 + cc:gcol0 + cc + 1]
            if cc % 2 == 0:
                self.amul(self.hT[cc][:, :Tn], pT[:, :Tn], g)
            else:
                self.ts("dve", self.hT[cc][:, :Tn], pT[:, :Tn], g, None, ALU.mult)

    def ffn(self, tts, Tn, wup, wdn, gcol0, tagp, mid_hook=None, mid_hook2=None):
        if self.norm_done:
            self.norm_done = False
        else:
            self.norm_hT(tts, Tn, gcol0)
        hT = self.hT
        upv = wup.rearrange("(k p) (ab c) -> p k ab c", p=128, ab=2)
        for g in range(11):
            def spec(slot, g=g):
                sv = slot[:, :].rearrange("p (k ab c) -> p k ab c", k=8, ab=2)
                return [(sv[:, :, ab, :], upv[:, :, ab, g * 256:(g + 1) * 256]) for ab in range(2)]
            slot = self.wget((tagp, "up", g), spec, (tagp[-1], "up", g))
            wv = slot[:, :].rearrange("p (k ab c) -> p k ab c", k=8, ab=2)
            for u in range(2):
                i = 2 * g + u
                pa = self.pf()
                pb_ = self.pf()
                for k in range(8):
                    self.mm(pa[:, :Tn], wv[:, k, 0, u * 128:(u + 1) * 128], hT[k][:, :Tn], k == 0, k == 7)
                for k in range(8):
                    self.mm(pb_[:, :Tn], wv[:, k, 1, u * 128:(u + 1) * 128], hT[k][:, :Tn], k == 0, k == 7)
                sa = self.sa[i % 2]
                self.act(sa[:, :Tn], pa[:, :Tn], AF.Silu)
                self.tt("dve", self.actT[i][:, :Tn], sa[:, :Tn], pb_[:, :Tn], ALU.mult,
                        rkeys=K(sa[:], pb_[:]) + ["s_arena0"], wkeys=[("actT", i)])
        if mid_hook is not None:
            mid_hook()
        dnv = wdn.rearrange("(i p) n -> p i n", p=128)
        for nh in range(2):
            accs = [self.pf() for _ in tts]
            for cg, (c0, c1) in enumerate(((0, 8), (8, 16), (16, 22))):
                def spec(slot, nh=nh, c0=c0, c1=c1):
                    return [(slot[:, :(c1 - c0) * 512].rearrange("p (c n) -> p c n", n=512), dnv[:, c0:c1, nh * 512:(nh + 1) * 512])]
                slot = self.wget((tagp, "down", nh, cg), spec, (tagp[-1], "down", nh, cg), (c1 - c0) * 512)
                wv = slot[:, :(c1 - c0) * 512].rearrange("p (c n) -> p c n", n=512)
                for i, (o, P) in enumerate(tts):
                    for cc in range(c0, c1):
                        self.mm(accs[i][:P, :], self.actT[cc][:, o:o + P], wv[:, cc - c0, :], cc == 0, cc == 21,
                                rkeys=K(wv[:, 0, :]) + ["s_arena0", ("actT", cc)])
            for i, (o, P) in enumerate(tts):
                xs_ = self.xt[i][:P, nh * 512:(nh + 1) * 512]
                self.stt("dve", xs_, accs[i][:P, :], 0.5, xs_, ALU.mult, ALU.add)
            if nh == 0 and mid_hook2 is not None:
                mid_hook2()

    def win_piece(self, tag, c0, w):
        wv_ = self.din["w_in"].rearrange("(k p) c -> p k c", p=128)

        def spec(slot):
            return [(slot[:, :8 * w].rearrange("p (k c) -> p k c", k=8), wv_[:, :, c0:c0 + w])]
        slot = self.wget(tag, spec, tag[1:], 8 * w)
        return slot[:, :8 * w].rearrange("p (k c) -> p k c", k=8)

    def fm_proj(self, wv, col0, Tn, lhs_view=None):
        p = self.pf()
        for k in range(8):
            lhs = wv[:, k, col0:col0 + 128] if lhs_view is None else lhs_view(k)
            self.mm(p[:, :Tn], lhs, self.hT[k][:, :Tn], k == 0, k == 7)
        return p

    def tm_proj(self, wv, c0, w, o, P):
        p = self.pf()
        for k in range(8):
            self.mm(p[:P, :w], self.hT[k][:, o:o + P], wv[:, k, c0:c0 + w], k == 0, k == 7)
        return p

    def qknorm(self, ps, out, gcol, bo, eps, Tn, idx, ntt=None):
        sqsq, sqr, rs = self.sqsq[0], self.sqr[0], self.rsq[0]
        self.act(sqsq[:, :Tn], ps, AF.Square)
        pm = self.pf()
        self.mm(pm[:, :Tn], bo[:, :], sqsq[:, :Tn])
        self.rsqrt(rs[:, :Tn], pm[:, :Tn], 1.0, eps)
        if ntt is None:
            self.stt("dve", out, ps, gcol, rs[:, :Tn], ALU.mult, ALU.mult)
        else:
            self.stt("dve", out, ps.rearrange("p (t q) -> p t q", t=ntt), gcol, rs[:, :Tn].rearrange("p (t q) -> p t q", t=ntt), ALU.mult, ALU.mult)

    def mixer(self, tts, Tn, mode, tagp, seqs, tile_idx, last_tile):
        c = self.c
        sample = mode == "sample"
        L = tts[0][1]
        tri_i = c["tri_incl"]
        tri_s = c["tri_suf"]
        cmask = c["cmask4"]
        self.norm_hT(tts, Tn, 8)
        if mode == "pre" and self.next_x is not None:
            self.load_x(self.next_x, tts)
        hT = self.hT
        e1T, e2T, qeT, keT = self.e1T, self.e2T, self.qeT, self.keT
        wv = self.win_piece((tagp, "ga"), C_GA, 16)
        p = self.pf()
        for k in range(8):
            self.mm(p[:16, :Tn], wv[:, k, 0:16], hT[k][:, :Tn], k == 0, k == 7)
        self.cp("act", self.gaT[:, :Tn], p[:16, :Tn])
        wk = self.win_piece((tagp, "gk"), C_GK, 512)
        n_t = len(tts)
        pzs = []
        for i, (o, P) in enumerate(tts):
            pz = self.pf()
            self.mm(pz[:P, :], self.gaT[:, o:o + P], self.walpha[:, :], True, False)
            self.mm(pz[:P, :], c["ones_row"][0:1, :P], self.balpha[0:1, :], False, True)
            pzs.append(pz)
        for i, (o, P) in enumerate(tts):
            self.act(self.es[i][:P, :], pzs[i][:P, :], AF.Exp, scale=-1.0)
        for i, (o, P) in enumerate(tts):
            self.act(self.sp[i][:P, :], self.es[i][:P, :], AF.Ln, bias=1.0)
        for i, (o, P) in enumerate(tts):
            pk = self.tm_proj(wk, 0, 512, o, P)
            self.cp("dve", self.kd[i][:P, :], pk[:P, :])
        early_norm = mode == "pre" and self.next_x is not None
        if early_norm:
            self.norm_stats(tts)
        if mode != "pre":
            for hd in range(4):
                p = self.fm_proj(wk, hd * 128, Tn)
                self.cp("dve", keT[:, hd, :Tn], p[:, :Tn])
            wq = self.win_piece((tagp, "gq"), C_GQ, 512)
            for hd in range(4):
                p = self.fm_proj(wq, hd * 128, Tn)
                self.cp("dve", qeT[:, hd, :Tn], p[:, :Tn])
        pbts = []
        for i, (o, P) in enumerate(tts):
            pbt = self.pf()
            for hd in range(4):
                self.mm(pbt[:, hd * P:(hd + 1) * P], self.sp[i][:P, hd * 128:(hd + 1) * 128], tri_i[:P, :P])
            pbts.append(pbt)
        for i, (o, P) in enumerate(tts):
            pv3 = pbts[i][:, :4 * P].rearrange("p (h t) -> p h t", h=4)
            self.act(e1T[:, :, o:o + P], pv3, AF.Exp)
            if mode != "pre":
                self.act(e2T[:, :, o:o + P], pv3, AF.Exp, scale=-1.0)
        for pc in range(2):
            wvv = self.win_piece((tagp, "gv", pc), C_GV + pc * 512, 512)
            for i, (o, P) in enumerate(tts):
                p = self.tm_proj(wvv, 0, 512, o, P)
                self.cp("dve", self.v_tok[i][:P, pc * 512:(pc + 1) * 512], p[:P, :])
        if early_norm and not last_tile:
            self.norm_tr(tts, Tn, 0, subtile_major=True)
            self.norm_done = True
        psufs = []
        for i, (o, P) in enumerate(tts):
            psuf = self.pf()
            whole = (mode == "pre")
            self.mm(psuf[:P, :], tri_s[:P, :P], self.sp[i][:P, :], True, not (whole and i < n_t - 1))
            if whole:
                for j in range(i + 1, n_t):
                    self.mm(psuf[:P, :], c["tri_all"][:P, :P], self.sp[j][:P, :], False, j == n_t - 1)
            psufs.append(psuf)
        for i, (o, P) in enumerate(tts):
            self.act(self.es[i][:P, :], psufs[i][:P, :], AF.Exp)
        if mode != "pre":
            for pc in range(2):
                wg = self.win_piece((tagp, "gr", pc), C_GR + pc * 512, 512)
                for u in range(4):
                    p = self.fm_proj(wg, u * 128, Tn)
                    self.act(self.sgT[:, pc * 4 + u, :Tn], p[:, :Tn], AF.Silu)
        for i, (o, P) in enumerate(tts):
            self.tt("dve", self.kd[i][:P, :], self.kd[i][:P, :], self.es[i][:P, :], ALU.mult)
        if mode != "pre":
            for hd in range(4):
                self.tt("dve", keT[:, hd, :Tn], keT[:, hd, :Tn], e2T[:, hd, :Tn], ALU.mult)
            for hd in range(4):
                self.stt("dve", qeT[:, hd, :Tn], qeT[:, hd, :Tn], 128 ** -0.5, e1T[:, hd, :Tn], ALU.mult, ALU.mult)
        mstop = self.debug.get("mstop") if sample else None
        gla_last_tail = None
        if mstop == "gpipe":
            return
        if mode == "pre":
            S, Sbf = seqs[0][2], seqs[0][3]
            pS = [self.pf(), self.pf()]
            for hd in range(4):
                for i, (o, P) in enumerate(tts):
                    self.mm(pS[hd // 2][:, (hd % 2) * 256:(hd % 2 + 1) * 256], self.kd[i][:P, hd * 128:(hd + 1) * 128],
                            self.v_tok[i][:P, hd * 256:(hd + 1) * 256], i == 0, i == n_t - 1)
            self.cp("dve", self.dtot[:, :, :], e1T[:, :, 127:128])
            for i in range(1, n_t):
                self.tt("dve", self.dtot[:, :, :], self.dtot[:, :, :], e1T[:, :, i * 128 + 127:i * 128 + 128], ALU.mult)
            for hd in range(4):
                self.stt("dve", S[:, hd, :], S[:, hd, :], self.dtot[:, hd, :], pS[hd // 2][:, (hd % 2) * 256:(hd % 2 + 1) * 256], ALU.mult, ALU.add)
            self.cp("act", Sbf[:, :, :], S[:, :, :])
        elif len(seqs) == 1:
            gla_last_tail = self.gla_pipelined(tts, seqs[0], cmask)
        else:
            for i, (o, P) in enumerate(tts):
                self.gla_chunk(i, o, P, mode, seqs, cmask)
        if mstop == "gla":
            return
        if mode == "full" and tile_idx == 0:
            self.dbg("oaT", self.sgT[:, :, :], [128, 8, T], BF16)
            self.dbg("qeT", self.qeT, [128, 4, T], BF16)
            self.dbg("keT", self.keT[:, :, :], [128, 4, T], BF16)
            self.dbg("e1T", self.e1T, [128, 4, T], F32)
            self.dbg("kd3", self.kd[3], [128, 512], BF16)
            self.dbg("vtok3", self.v_tok[3], [128, 1024], BF16)
            self.dbg("S", self.S[0][:, :, :], [128, 4, 256], F32)
        need_kv = (mode != "pre") or last_tile
        if mode != "pre":
            ntt = len(tts)
            w_in_v = self.din["w_in"].rearrange("(k p) c -> p k c", p=128)
            for pc in range(2):
                def spec(slot, pc=pc):
                    sv = slot[:, :].rearrange("p (k g two d) -> p k g two d", k=8, g=4, two=2)
                    return [(sv[:, k, :, two, :], w_in_v[:, k, C_SQ + pc * 512 + two * 256:C_SQ + pc * 512 + (two + 1) * 256].rearrange("p (g d) -> p g d", g=4))
                            for two in range(2) for k in range(8)]
                wq = self.wget((tagp, "sq", pc), spec, ("sq", pc))[:, :].rearrange("p (k c) -> p k c", k=8)
                qv = self.qo[pc][:, :ntt * 4 * L].rearrange("p (t g q) -> p t g q", t=ntt, g=4)
                for g in range(4):
                    p = self.fm_proj(wq, g * 128, Tn)
                    if gla_last_tail is not None and g == 1:
                        gla_last_tail()
                        gla_last_tail = None
                    self.qknorm(p[:, :Tn], qv[:, :, g, :], self.qgcol[:, 0:1], c["bo_q"], 64 * EPS, Tn, g, ntt)
        if need_kv:
            wkv = self.win_piece((tagp, "sksv"), C_SK, 512)
            for P2 in range(2):
                p = self.fm_proj(wkv, P2 * 128, Tn)
                self.qknorm(p[:, :Tn], self.knT_cur[:, P2, :Tn], self.kgcol[:, 0:1], c["bo_k"], EPS, Tn, P2)
            for i, (o, P) in enumerate(tts):
                pvv = self.tm_proj(wkv, 256, 256, o, P)
                vz4 = self.vz_cur[i][:, :].rearrange("p (jp par c) -> p jp par c", jp=2, par=2)
                pv4 = pvv[:, :256].rearrange("p (jp par d) -> p jp par d", jp=2, par=2)
                for par in range(2):
                    self.cp("act", vz4[:P, :, par, par * 64:(par + 1) * 64], pv4[:P, :, par, :])
                want_out = sample or (mode == "full" and last_tile and i == len(tts) - 1)
                if want_out:
                    self.cp("act", self.vnew[:P, :], pvv[:P, :256])
                    pkk = self.tm_proj(wkv, 0, 256, o, P)
                    self.act(self.ksq[:P, :], pkk[:P, :256], AF.Square)
                    self.add("dve", lambda h, P=P: h.tensor_reduce(out=self.kss[:P, 0:4], in_=self.ksq[:P, :].rearrange("p (j d) -> p j d", j=4),
                                                                    axis=AX.X, op=ALU.add), K(self.ksq[:]), K(self.kss[:]))
                    self.rsqrt(self.kss[:P, :], self.kss[:P, :], 1.0 / 64, EPS)
                    k3 = self.ktok[:P, :].rearrange("p (j d) -> p j d", j=4)
                    self.tt("dve", k3, pkk[:P, :256].rearrange("p (j d) -> p j d", j=4),
                            self.kss[:P, 0:4].unsqueeze(2).to_broadcast([P, 4, 64]), ALU.mult)
                    self.tt("dve", self.knew[:P, :].rearrange("p (j d) -> p j d", j=4), k3,
                            self.kgain_bc[:P, :].unsqueeze(1).to_broadcast([P, 4, 64]), ALU.mult)
                    if sample:
                        self.dma("sp", self.dout["nks"], self.knew[:P, :])
                        self.dma("sp", self.dout["nvs"], self.vnew[:P, :])
                    else:
                        self.dma("sp", self.dout["nk"], self.knew[:P, :])
                        self.dma("sp", self.dout["nv"], self.vnew[:P, :])
        if mode == "pre":
            if last_tile:
                self.cp("pool", self.knT_prev[:, :, :], self.knT_cur[:, :, Tn - 128:Tn])
                self.cp("pool", self.vz_prev[:, :], self.vz_cur[len(tts) - 1][:, :])
            if self.next_x is not None and not self.norm_done:
                self.norm_tr(tts, Tn, 0, subtile_major=True)
                self.norm_done = True
            return
        if mstop == "swaproj":
            return
        gate_groups = []
        gstate = {}
        for nm, c0, dst in (("gta", C_GTA, self.sigA), ("gtb", C_GTB, self.sigB)):
            for pc in range(2):
                for u in range(4):
                    def grp(nm=nm, c0=c0, dst=dst, pc=pc, u=u):
                        if u == 0:
                            gstate["w"] = self.win_piece((tagp, nm, pc), c0 + pc * 512, 512)
                        p = self.fm_proj(gstate["w"], u * 128, Tn)
                        self.act(dst[:, pc * 4 + u, :Tn], p[:, :Tn], AF.Tanh, scale=0.5)
                    gate_groups.append(grp)

        def run_gates(k):
            for _ in range(min(k, len(gate_groups))):
                gate_groups.pop(0)()
        if sample:
            kts = []
            for sq_ in range(2):
                kts.append(dict(knT=lambda P2, rows, sq_=sq_: self.knT_c[sq_][rows, P2, :], vz=self.vz_c[sq_], bias=self.biasTs[:, sq_, :, :], nk=128, hb=None))
            kts.append(dict(knT=lambda P2, rows: self.knT_cur[rows, P2, 0:64], vz=self.vz_cur[0], bias=self.biasTs[:, 2, :, :], nk=64, hb=None))
            self.swa_block(0, 0, 1, 64, kts, c["g4s"], between=lambda: run_gates(2))
        else:
            for i, (o, P) in enumerate(tts):
                kts = []
                if i == 0:
                    hb = self.halo_bias[:, 0:1] if tile_idx == 0 else None
                    kts.append(dict(knT=lambda P2, rows: self.knT_prev[rows, P2, :], vz=self.vz_prev, bias=self.biasT[:, 0, :, :], nk=128, hb=hb))
                else:
                    kts.append(dict(knT=lambda P2, rows, o=o: self.knT_cur[rows, P2, o - 128:o], vz=self.vz_cur[i - 1], bias=self.biasT[:, 0, :, :], nk=128, hb=None))
                kts.append(dict(knT=lambda P2, rows, o=o: self.knT_cur[rows, P2, o:o + 128], vz=self.vz_cur[i], bias=self.biasT[:, 1, :, :], nk=128, hb=None))
                self.swa_block(i, o, len(tts), 128, kts, c["g4"], between=lambda: run_gates(2))
            self.cp("pool", self.knT_prev[:, :, :], self.knT_cur[:, :, Tn - 128:Tn])
            self.cp("pool", self.vz_prev[:, :], self.vz_cur[len(tts) - 1][:, :])
        if mode == "full" and tile_idx == 0:
            self.dbg("osT0", self.osT[0], [128, 4, T], BF16)
            self.dbg("osT1", self.osT[1], [128, 4, T], BF16)
            self.dbg("knT", self.knT_cur[:, :, :], [128, 2, T], BF16)
        if mstop == "swa":
            return
        run_gates(len(gate_groups))
        if mode == "full" and tile_idx == 0:
            self.dbg("sgA", self.sigA, [128, 8, T], BF16)
            self.dbg("sgB", self.sigB, [128, 8, T], BF16)
        wbr = self.din["w_branch"]
        wbA = wbr[0:1024, :].rearrange("(f p) n -> p f n", p=128)
        wbB = wbr[1024:2048, :].rearrange("(P j g d) n -> d P j g n", P=2, j=2, g=4)
        for ng in range(2):
            def specA(slot, ng=ng):
                return [(slot[:, :].rearrange("p (f n) -> p f n", f=8), wbA[:, :, ng * 512:(ng + 1) * 512])]

            def specB(slot, ng=ng):
                return [(slot[half * 64:(half + 1) * 64, :].rearrange("p (P g n) -> p P g n", P=2, g=4)[:, P2],
                         wbB[:, P2, half, :, ng * 512:(ng + 1) * 512]) for half in range(2) for P2 in range(2)]
            sA = self.wget((tagp, "wbA", ng), specA, ("wbA", ng))[:, :].rearrange("p (f n) -> p f n", f=8)
            sB = self.wget((tagp, "wbB", ng), specB, ("wbB", ng))[:, :].rearrange("p (f n) -> p f n", f=8)
            for u in range(4):
                n = ng * 4 + u
                pa = self.pf()
                for f in range(8):
                    self.mm(pa[:, :Tn], sA[:, f, u * 128:(u + 1) * 128], self.sgT[:, f, :Tn], f == 0, f == 7)
                pb_ = self.pf()
                for f in range(8):
                    self.mm(pb_[:, :Tn], sB[:, f, u * 128:(u + 1) * 128], self.osT[f // 4][:, f % 4, :Tn], f == 0, f == 7)
                t0, t1 = self.mtmp
                self.stt("dve", t0[:, :Tn], self.sigA[:, n, :Tn], 1.0, pa[:, :Tn], ALU.add, ALU.mult)
                self.stt("dve", t1[:, :Tn], self.sigB[:, n, :Tn], 1.0, pb_[:, :Tn], ALU.add, ALU.mult)
                self.tt("pool", self.sigA[:, n, :Tn], t0[:, :Tn], t1[:, :Tn], ALU.add)
        if mode == "full" and tile_idx == 0:
            self.dbg("mT", self.sigA, [128, 8, T], BF16)
        wo = self.din["w_out"].rearrange("(f p) n -> p f n", p=128)
        for nh in range(2):
            def spec(slot, nh=nh):
                return [(slot[:, :].rearrange("p (f n) -> p f n", f=8), wo[:, :, nh * 512:(nh + 1) * 512])]
            so = self.wget((tagp, "wo", nh), spec, ("wo", nh))[:, :].rearrange("p (f n) -> p f n", f=8)
            for i, (o, P) in enumerate(tts):
                p = self.pf()
                for f in range(8):
                    self.mm(p[:P, :], self.sigA[:, f, o:o + P], so[:, f, :], f == 0, f == 7)
                xs_ = self.xt[i][:P, nh * 512:(nh + 1) * 512]
                self.stt("dve", xs_, p[:P, :], 0.5, xs_, ALU.mult, ALU.add)

    def gla_pipelined(self, tts, seq, cmask):
        c = self.c
        e1T, qeT, keT = self.e1T, self.qeT, self.keT
        so, nreal, S, Sbf0 = seq
        Sb = [Sbf0, self.Sbf_alt]
        L = tts[0][1]
        n_t = len(tts)
        assert n_t % 2 == 0

        def front(i, o):
            pS = [self.pf(), self.pf()]
            for hd in range(4):
                self.mm(pS[hd // 2][:, (hd % 2) * 256:(hd % 2 + 1) * 256], self.kd[i][:L, hd * 128:(hd + 1) * 128],
                        self.v_tok[i][:L, hd * 256:(hd + 1) * 256])
            pA = self.pf()
            for hd in range(4):
                self.mm(pA[:L, hd * L:(hd + 1) * L], keT[:, hd, o:o + L], qeT[:, hd, o:o + L])
            atm = self.ATm2[i % 2]
            self.tt("dve", atm[:L, :4 * L], pA[:L, :4 * L], cmask[:L, :4 * L], ALU.mult)
            dcol = o + L - 1
            for hd in range(4):
                self.stt("dve", S[:, hd, :], S[:, hd, :], e1T[:, hd, dcol:dcol + 1], pS[hd // 2][:, (hd % 2) * 256:(hd % 2 + 1) * 256], ALU.mult, ALU.add)
            self.cp("act", Sb[(i + 1) % 2][:, :, :], S[:, :, :])
            return atm

        def tail(j):
            oj = tts[j][0]
            o_raw, osq = self.o_raw2[j % 2], self.osq2[j % 2]
            pM = self.pf()
            for hd in range(4):
                self.mm(pM[:, hd * L:(hd + 1) * L], c["ones_mean"][:, :], osq[:, 2 * hd, :L], True, False)
                self.mm(pM[:, hd * L:(hd + 1) * L], c["ones_mean"][:, :], osq[:, 2 * hd + 1, :L], False, True)
            self.rsqrt(self.rstdg[:, :4 * L], pM[:, :4 * L], 1.0, EPS)
            r3 = self.rstdg[:, :4 * L].rearrange("p (h l) -> p h l", h=4)
            o4 = o_raw[:, :, :].rearrange("p (h v) l -> p h v l", v=2)
            for vc in range(2):
                self.stt("dve", o4[:, :, vc, :L], o4[:, :, vc, :L], self.hgain[:, vc:vc + 1], r3, ALU.mult, ALU.mult)
            self.tt("dve", self.sgT[:, :, oj:oj + L], o_raw[:, :, :L], self.sgT[:, :, oj:oj + L], ALU.mult)

        nxt = front(0, tts[0][0])
        for i, (o, P) in enumerate(tts):
            atm = nxt
            Sbf = Sb[i % 2]
            pO = [self.pf(), self.pf()]
            for hd in range(4):
                for vc in range(2):
                    reg = pO[hd // 2][:, ((hd % 2) * 2 + vc) * L:((hd % 2) * 2 + vc + 1) * L]
                    self.mm(reg, self.v_tok[i][:L, hd * 256 + vc * 128:hd * 256 + (vc + 1) * 128], atm[:L, hd * L:(hd + 1) * L], True, False)
                    self.mm(reg, Sbf[:, hd, vc * 128:(vc + 1) * 128], qeT[:, hd, o:o + L], False, True)
            o_raw, osq = self.o_raw2[i % 2], self.osq2[i % 2]
            for b in range(2):
                pv = pO[b][:, :4 * L].rearrange("p (c l) -> p c l", c=4)
                self.cp("act", o_raw[:, 4 * b:4 * b + 4, :L], pv)
                self.act(osq[:, 4 * b:4 * b + 4, :L], pv, AF.Square)
            if i + 1 < n_t:
                nxt = front(i + 1, tts[i + 1][0])
            if i >= 1:
                tail(i - 1)
        return lambda: tail(n_t - 1)

    def gla_chunk(self, i, o, L, mode, seqs, cmask):
        c = self.c
        e1T, qeT, keT = self.e1T, self.qeT, self.keT
        nseq = len(seqs)
        blk = L // nseq
        def state_mm(si):
            so = seqs[si][0]
            pS = [self.pf(), self.pf()]
            for hd in range(4):
                self.mm(pS[hd // 2][:, (hd % 2) * 256:(hd % 2 + 1) * 256], self.kd[i][so:so + blk, hd * 128:(hd + 1) * 128],
                        self.v_tok[i][so:so + blk, hd * 256:(hd + 1) * 256])
            return pS
        hoist = nseq == 1
        pSs = [state_mm(0)] if hoist else None
        pA = self.pf()
        for hd in range(4):
            self.mm(pA[:L, hd * L:(hd + 1) * L], keT[:, hd, o:o + L], qeT[:, hd, o:o + L])
        self.tt("dve", self.ATm[:L, :4 * L], pA[:L, :4 * L], cmask[:L, :4 * L], ALU.mult)
        pO = [self.pf(), self.pf()]
        for hd in range(4):
            for vc in range(2):
                reg = pO[hd // 2][:, ((hd % 2) * 2 + vc) * L:((hd % 2) * 2 + vc + 1) * L]
                self.mm(reg, self.v_tok[i][:L, hd * 256 + vc * 128:hd * 256 + (vc + 1) * 128], self.ATm[:L, hd * L:(hd + 1) * L], True, False)
                for si, (so, nreal, S, Sbf) in enumerate(seqs):
                    self.mm(reg[:, so:so + blk], Sbf[:, hd, vc * 128:(vc + 1) * 128], qeT[:, hd, o + so:o + so + blk], False, si == nseq - 1)
        for si, (so, nreal, S, Sbf) in enumerate(seqs):
            pS = pSs[si] if hoist else state_mm(si)
            dcol = o + so + nreal - 1
            for hd in range(4):
                self.stt("dve", S[:, hd, :], S[:, hd, :], e1T[:, hd, dcol:dcol + 1], pS[hd // 2][:, (hd % 2) * 256:(hd % 2 + 1) * 256], ALU.mult, ALU.add)
            self.cp("act", Sbf[:, :, :], S[:, :, :])
        for b in range(2):
            pv = pO[b][:, :4 * L].rearrange("p (c l) -> p c l", c=4)
            self.cp("act", self.o_raw[:, 4 * b:4 * b + 4, :L], pv)
            self.act(self.osq[:, 4 * b:4 * b + 4, :L], pv, AF.Square)
        pM = self.pf()
        for hd in range(4):
            self.mm(pM[:, hd * L:(hd + 1) * L], c["ones_mean"][:, :], self.osq[:, 2 * hd, :L], True, False)
            self.mm(pM[:, hd * L:(hd + 1) * L], c["ones_mean"][:, :], self.osq[:, 2 * hd + 1, :L], False, True)
        self.rsqrt(self.rstdg[:, :4 * L], pM[:, :4 * L], 1.0, EPS)
        r3 = self.rstdg[:, :4 * L].rearrange("p (h l) -> p h l", h=4)
        o4 = self.o_raw[:, :, :].rearrange("p (h v) l -> p h v l", v=2)
        for vc in range(2):
            self.stt("dve", o4[:, :, vc, :L], o4[:, :, vc, :L], self.hgain[:, vc:vc + 1], r3, ALU.mult, ALU.mult)
        self.tt("dve", self.sgT[:, :, o:o + L], self.o_raw[:, :, :L], self.sgT[:, :, o:o + L], ALU.mult)

    def swa_block(self, ti, q0, ntt, nq, kts, g4, between=None):
        c = self.c
        N = 4 * nq
        for P2 in range(2):
            if between is not None:
                between()
            pts = []
            for kt in kts:
                nk = kt["nk"]
                pss = []
                for half in range(2):
                    rows = slice(half * 64, half * 64 + 64)
                    rhs_q = self.qo[P2][rows, ti * 4 * nq:(ti + 1) * 4 * nq]
                    ps = self.pf()
                    self.mm(ps[:nk, :N], kt["knT"](P2, rows), rhs_q, True, False)
                    pss.append(ps)
                for half in range(2):
                    self.mm(pss[half][:nk, :N], c["ident"][:, :nk], kt["bias"][:, 2 * P2 + half, :N], False, True)
                for half in range(2):
                    ps = pss[half]
                    self._pt += 1
                    pt = self.PT[self._pt % 6]
                    if kt["hb"] is not None:
                        self.act(pt[:nk, :N], ps[:nk, :N], AF.Exp, bias=kt["hb"][:nk, :])
                    else:
                        self.act(pt[:nk, :N], ps[:nk, :N], AF.Exp)
                    pts.append((pt, kt, half))
            pO = self.pf()
            pD = self.pf()
            n = len(pts)
            for idx, (pt, kt, half) in enumerate(pts):
                nk = kt["nk"]
                vz4 = kt["vz"][:, :].rearrange("p (j c) -> p j c", j=4)
                self.mm(pO[:, :N], vz4[:nk, 2 * P2 + half, :], pt[:nk, :N], idx == 0, idx == n - 1)
            for idx, (pt, kt, half) in enumerate(pts):
                nk = kt["nk"]
                self.mm(pD[:, :N], c["onesH"][:nk, half, :], pt[:nk, :N], idx == 0, False)
            self.mm(pD[:, :N], self.esinkT[0:4, P2, :], g4[0:4, :N], False, True)
            self.add("dve", lambda h, pD=pD, N=N: h.reciprocal(out=self.rden[:, :N], in_=pD[:, :N]), K(pD[:]), K(self.rden[:]))
            self.tt("dve", self.osT[P2][:, :, q0:q0 + nq], pO[:, :N].rearrange("p (g q) -> p g q", g=4),
                    self.rden[:, :N].rearrange("p (g q) -> p g q", g=4), ALU.mult)

    def final_out(self, tts, dst):
        for i, (o, P) in enumerate(tts):
            self.act(self.junk[:P, :], self.xt[i][:P, :], AF.Square, accum_out=self.ss[i][:P, 0:1])
            self.rsqrt(self.rstd[i][:P, 0:1], self.ss[i][:P, 0:1], 1.0 / D, EPS)
            self.ts("dve", self.xt[i][:P, :], self.xt[i][:P, :], self.rstd[i][:P, 0:1], None, ALU.mult)
            self.tt("pool" if i % 2 else "dve", self.xt[i][:P, :], self.xt[i][:P, :], self.fgain_bc[:P, :], ALU.mult)
            self.dma("sp", dst[o:o + P, :], self.xt[i][:P, :])

    def body(self):
        di = self.din
        self.reset_counters()
        pre_on = not self.debug.get("skip_pre") and not self.debug.get("only_sample")
        self.setup(first_x=di["xprev"][0:T, :] if pre_on else None)
        tts = [(j * 128, 128) for j in range(4)]
        seqs_p = [(0, 128, self.S[0], self.Sbf[0])]
        stop = self.debug.get("stop")
        if not self.debug.get("skip_pre") and not self.debug.get("only_sample"):
            for t in range(NTILE):
                self.next_x = di["xprev"][(t + 1) * T:(t + 2) * T, :] if t + 1 < NTILE else di["xp"][0:T, :]
                self.ffn(tts, T, di["ffn1_up"], di["ffn1_down"], 0, ("pre", t, "f1"))
                self.mixer(tts, T, "pre", ("pre", t), seqs_p, t, t == NTILE - 1)
            self.next_x = None
        pre_ran = not self.debug.get("skip_pre") and not self.debug.get("only_sample")
        for t in range(NTILE if not self.debug.get("only_sample") else 0):
            if self.x_loaded:
                self.x_loaded = False
            elif not (t == 0 and pre_ran):
                self.load_x(di["xp"][t * T:(t + 1) * T, :], tts)
            self.ffn(tts, T, di["ffn1_up"], di["ffn1_down"], 0, ("own", t, "f1"))
            if stop == "ffn1":
                self.final_dbg_x(tts, t)
                continue
            self.mixer(tts, T, "full", ("own", t), seqs_p, t, t == NTILE - 1)
            if stop == "mixer":
                self.final_dbg_x(tts, t)
                continue
            if t == NTILE - 1 and not self.debug.get("no_sample") and not self.debug.get("no_ffn2"):
                self.dma("sp", self.dout["ng"].rearrange("h k v -> k h v"), self.S[0][:, :, :])
                self._ng_done = True
                self.sample_setup_early()
            staged = False
            if not self.debug.get("no_ffn2"):
                hook = None
                smp_next = (t == NTILE - 1 and not self.debug.get("no_sample") and not self.debug.get("no_final") and not stop)
                if smp_next:
                    tss_ = [(0, TS)]
                    xst = [self.arena1[:, 0:2048].bitcast(F32)]
                    self.dma("sp", xst[0][:TS, :], di["xs"])

                    def hook(xst=xst, tss_=tss_):
                        self.norm_stats(tss_, src=xst)

                    def hook2(tss_=tss_):
                        self.norm_tr(tss_, TS, 0)
                    staged = True
                if t + 1 < NTILE and not self.debug.get("no_final"):
                    xst = [self.arena1[:, i * 2048:(i + 1) * 2048].bitcast(F32) for i in range(4)]
                    nsrc = di["xp"][(t + 1) * T:(t + 2) * T, :]
                    for i, (o, P) in enumerate(tts):
                        self.dma("sp", xst[i][:P, :], nsrc[o:o + P, :])

                    def hook(xst=xst):
                        self.norm_stats(tts, src=xst)

                    def hook2():
                        self.norm_tr(tts, T, 0)
                    staged = True
                self.ffn(tts, T, di["ffn2_up"], di["ffn2_down"], 16, ("own", t, "f2"), mid_hook=hook, mid_hook2=hook2 if hook else None)
            if self.debug.get("no_final"):
                self.final_dbg_x(tts, t)
            else:
                self.final_out(tts, self.dout["y"][t * T:(t + 1) * T, :])
            if staged:
                for i, (o, P) in enumerate(tss_ if smp_next else tts):
                    self.cp("pool", self.xt[i][:P, :], xst[i][:P, :])
                self.norm_done = True
                self.x_loaded = True
        if not self._ng_done:
            self.dma("sp", self.dout["ng"].rearrange("h k v -> k h v"), self.S[0][:, :, :])
        if stop or self.debug.get("no_sample"):
            return
        if not self._early_done:
            self.sample_setup_early()
        tss = [(0, TS)]
        self.dma("sp", self.S[1][:, :, :], di["st"][1].rearrange("h k v -> k h v"))
        self.cp("pool", self.Sbf[1][:, :, :], self.S[1][:, :, :])
        seqs_s = [(0, 16, self.S[0], self.Sbf[0]), (32, 16, self.S[1], self.Sbf[1])]
        if self.x_loaded:
            self.x_loaded = False
        else:
            self.load_x(di["xs"], tss)
        self.sample_rest(tss, seqs_s)

    def sample_setup_early(self):
        di = self.din
        self._early_done = True
        self.dma("pool", self.biasTs, di["biass"].rearrange("a p j n -> p a j n"))
        for n in ("tri_incl", "tri_suf", "cmask4"):
            self.dma("sp", self.c[n][:], di[n + "_s"])
        self.dma("sp", self.S[0][:, :, :], di["st"][0].rearrange("h k v -> k h v"))
        self.cp("pool", self.Sbf[0][:, :, :], self.S[0][:, :, :])
        for sq_ in range(2):
            self.dma("sp", self.ckf[:, :], di["ck"][sq_])
            self.cp("dve", self.ckb[:, :], self.ckf[:, :])
            for P2 in range(2):
                pT = self.pb()
                self.tr(pT[:, 0:128], self.ckb[:, P2 * 128:(P2 + 1) * 128], self.c["ident"][:, :])
                self.cp("act", self.knT_c[sq_][:, P2, :], pT[:, 0:128])
            self.dma("sp", self.ckf[:, :], di["cv"][sq_])
            vz4 = self.vz_c[sq_][:, :].rearrange("p (jp par c) -> p jp par c", jp=2, par=2)
            cv4 = self.ckf[:, :].rearrange("p (jp par d) -> p jp par d", jp=2, par=2)
            for par in range(2):
                self.cp("dve", vz4[:, :, par, par * 64:(par + 1) * 64], cv4[:, :, par, :])

    def sample_rest(self, tss, seqs_s):
        di = self.din
        sstop = self.debug.get("sstop")
        self.ffn(tss, TS, di["ffn1_up"], di["ffn1_down"], 0, ("smp", "f1"))
        if sstop != "ffn1":
            self.mixer(tss, TS, "sample", ("smp",), seqs_s, 0, True)
            if sstop != "mixer":
                self.ffn(tss, TS, di["ffn2_up"], di["ffn2_down"], 16, ("smp", "f2"))
        self.final_out(tss, self.dout["ys"])
        for sq_ in range(2):
            self.dma("sp", self.dout["ngs"][sq_].rearrange("h k v -> k h v"), self.S[sq_][:, :, :])

    def final_dbg_x(self, tts, t):
        for i, (o, P) in enumerate(tts):
            self.dma("sp", self.dout["y"][t * T + o:t * T + o + P, :], self.xt[i][:P, :])

    def build(self):
        self.planning = True
        self.body()
        self.planning = False
        self.nuse = {}
        for pl in self.plan:
            self.nuse[pl[2]] = self.nuse.get(pl[2], 0) + 1
        self.body()
        self.s.emit()
        return self.nc


_CACHE = {}


def _get_prog(debug=None):
    key = repr(sorted((debug or {}).items()))
    if key not in _CACHE:
        p = Prog(debug)
        p.build()
        _CACHE[key] = p
    return _CACHE[key]


def make_in_maps(inp):
    f = lambda a: np.ascontiguousarray(np.asarray(a, dtype=np.float32))
    xp = f(inp["x_prompt"])
    xsm = f(inp["x_sample"])
    ck = f(inp["cache_swa_k"])[0].reshape(16, 128, 256)
    cv = f(inp["cache_swa_v"])[0].reshape(16, 128, 256)
    st = f(inp["state_gla"])[0]
    consts = _consts()
    bp, bs = _bias_tables(inp["rel_bias"])
    gains = np.stack([f(inp["ffn1_norm"])[0], f(inp["mix_norm"])[0], f(inp["ffn2_norm"])[0]])
    gains_fm = np.ascontiguousarray(gains.reshape(3, 8, 128).transpose(2, 0, 1).reshape(128, 24))
    fg = np.ascontiguousarray(np.broadcast_to(f(inp["final_norm"])[0][None, :], (128, D)))
    hg = np.ascontiguousarray(f(inp["gla_head_norm"])[0].reshape(2, 128).T)
    qg = np.ascontiguousarray(np.tile(f(inp["q_norm"])[0], 2)[:, None])
    kg = np.ascontiguousarray(np.tile(f(inp["k_norm"])[0], 2)[:, None])
    kgb = np.ascontiguousarray(np.broadcast_to(f(inp["k_norm"])[0][None, :], (128, 64)))
    sinks = f(inp["attn_sinks"])[0]
    sinkT = np.zeros((2, 4, 128), np.float32)
    for P2 in range(2):
        for g in range(4):
            sinkT[P2, g, :64] = sinks[4 * (2 * P2) + g]
            sinkT[P2, g, 64:] = sinks[4 * (2 * P2 + 1) + g]
    shared = {
        "ffn1_up": f(inp["ffn1_w_up"])[0], "ffn1_down": f(inp["ffn1_w_down"])[0],
        "ffn2_up": f(inp["ffn2_w_up"])[0], "ffn2_down": f(inp["ffn2_w_down"])[0],
        "w_in": f(inp["w_in"])[0], "w_branch": f(inp["w_branch"])[0], "w_out": f(inp["w_out"])[0],
        "w_alpha": f(inp["gla_w_alpha"])[0], "b_alpha": f(inp["gla_b_alpha"]).reshape(1, 512),
        "gains_fm": gains_fm, "fgain_bc": fg, "hgain": hg, "qgcol": qg, "kgcol": kg, "kgain_bc": kgb,
        "sinkT": sinkT, "biasp": bp, "biass": bs,
    }
    shared.update(consts)
    maps = []
    zeros_prev = np.zeros((TOKC, D), np.float32)
    for cidx in range(NCORES):
        b, half = cidx // 2, cidx % 2
        m = dict(shared)
        m["xp"] = np.ascontiguousarray(xp[b, half * TOKC:(half + 1) * TOKC])
        m["xprev"] = np.ascontiguousarray(xp[b, 0:TOKC]) if half == 1 else zeros_prev
        xs_ = np.zeros((TS, D), np.float32)
        for s_ in range(2):
            xs_[32 * s_:32 * s_ + 16] = xsm[2 * cidx + s_]
        m["xs"] = xs_
        m["ck"] = np.ascontiguousarray(ck[2 * cidx:2 * cidx + 2])
        m["cv"] = np.ascontiguousarray(cv[2 * cidx:2 * cidx + 2])
        m["st"] = np.ascontiguousarray(st[2 * cidx:2 * cidx + 2])
        m["halo_bias"] = np.full((128, 1), 0.0 if half == 1 else NEG, np.float32)
        maps.append(m)
    return maps


def run(inp, debug=None):
    prog = _get_prog(debug)
    maps = make_in_maps(inp)
    res = run_bass_kernel_spmd(prog.nc, maps, core_ids=list(range(NCORES)))
    return prog, res


def kernel(**inp):
    prog, res = run(inp)
    R = res.results
    y = np.zeros((4, 4096, D), np.float32)
    ys = np.zeros((16, 16, D), np.float32)
    nk = np.zeros((1, 4, 128, 4, 64), np.float32)
    nv = np.zeros((1, 4, 128, 4, 64), np.float32)
    ng = np.zeros((1, 4, 4, 128, 256), np.float32)
    nks = np.zeros((1, 16, 16, 4, 64), np.float32)
    nvs = np.zeros((1, 16, 16, 4, 64), np.float32)
    ngs = np.zeros((1, 16, 4, 128, 256), np.float32)
    for cidx in range(NCORES):
        b, half = cidx // 2, cidx % 2
        r = R[cidx]
        y[b, half * TOKC:(half + 1) * TOKC] = r["y"]
        if half == 1:
            nk[0, b] = r["nk"].reshape(128, 4, 64)
            nv[0, b] = r["nv"].reshape(128, 4, 64)
            ng[0, b] = r["ng"]
        for s_ in range(2):
            ys[2 * cidx + s_] = r["ys"][32 * s_:32 * s_ + 16]
            nks[0, 2 * cidx + s_] = r["nks"][32 * s_:32 * s_ + 16].reshape(16, 4, 64)
            nvs[0, 2 * cidx + s_] = r["nvs"][32 * s_:32 * s_ + 16].reshape(16, 4, 64)
            ngs[0, 2 * cidx + s_] = r["ngs"][s_]
    return (y, ys, nk, nv, ng, nks, nvs, ngs)
```

```python
import numpy as np
import ml_dtypes
import concourse.bass as bass
import concourse.mybir as mybir
from concourse.bass_utils import run_bass_kernel_spmd

F32 = mybir.dt.float32
BF16 = mybir.dt.bfloat16
AF = mybir.ActivationFunctionType
ALU = mybir.AluOpType
AX = mybir.AxisListType

ENGS = ("pe", "act", "dve", "pool", "sp")
NCORES = 8
D = 1024
DFF = 2816
TOKC = 2048
T = 512
NTILE = TOKC // T
TS = 64
EPS = 1e-6
NEG = -30000.0
IN_W = 6672
C_GQ, C_GK, C_GV, C_GR, C_GA, C_SQ, C_SK, C_SV, C_GTA, C_GTB = 0, 512, 1024, 2048, 3072, 3088, 4112, 4368, 4624, 5648


class Op:
    __slots__ = ("eng", "fn", "deps", "sig", "semval", "dma", "dsem", "dval", "idx", "group")


class Sched:
    def __init__(self, nc, n_dma_sems=48):
        self.nc = nc
        self.ops = []
        self.by_eng = {e: [] for e in ENGS}
        self.last_w = {}
        self.readers = {}
        self.esem = {e: nc.alloc_semaphore(name="cnt_" + e) for e in ENGS}
        self.dsems = [nc.alloc_semaphore(name="dma%d" % i) for i in range(n_dma_sems)]
        self.dcount = [0] * n_dma_sems
        self.dlast = [None] * n_dma_sems
        self.dnext = {"sw": 0, "hw": 0}

    def add(self, eng, fn, reads=(), writes=(), dma=False, group=None):
        op = Op()
        op.group = group
        op.eng = eng
        op.fn = fn
        op.sig = False
        op.semval = None
        op.dma = dma
        op.idx = len(self.ops)
        deps = {}

        def dep(d, raw):
            if d is None:
                return
            if not d.dma and not dma and d.eng == eng:
                if eng == "pe" or not raw:
                    return
            deps[d.idx] = d

        for k in reads:
            for w in self.last_w.get(k, ()):
                dep(w, True)
        for k in writes:
            for w in self.last_w.get(k, ()):
                if group is not None and w.group == group:
                    continue
                dep(w, False)
            for r in self.readers.get(k, ()):
                dep(r, False)
        if dma:
            half = len(self.dsems) // 2
            cls = "sw" if eng == "pool" else "hw"
            j = self.dnext[cls] + (half if cls == "sw" else 0)
            self.dnext[cls] = (self.dnext[cls] + 1) % half
            if self.dlast[j] is not None:
                deps[self.dlast[j].idx] = self.dlast[j]
            self.dcount[j] += 16
            op.dsem = j
            op.dval = self.dcount[j]
            self.dlast[j] = op
        for k in reads:
            lst = self.readers.setdefault(k, [])
            if not dma:
                lst[:] = [r for r in lst if r.dma or r.eng != eng]
            lst.append(op)
        for k in writes:
            lw = self.last_w.get(k, [])
            if group is not None and lw and lw[0].group == group:
                lw.append(op)
            else:
                self.last_w[k] = [op]
                self.readers[k] = []
        op.deps = list(deps.values())
        for d in op.deps:
            if not d.dma:
                d.sig = True
        self.ops.append(op)
        self.by_eng[eng].append(op)
        return op

    def emit(self):
        nc = self.nc
        for e in ENGS:
            c = 0
            for op in self.by_eng[e]:
                if not op.dma and op.sig:
                    c += 1
                    op.semval = c
        final_d = list(self.dcount)

        def replay(e, h):
            seen = {}
            for op in self.by_eng[e]:
                need = {}
                for d in op.deps:
                    if d.dma:
                        key, val, sem = ("d", d.dsem), d.dval, self.dsems[d.dsem]
                    else:
                        key, val, sem = ("e", d.eng), d.semval, self.esem[d.eng]
                    if val > need.get(key, (0, None))[0]:
                        need[key] = (val, sem)
                for key, (val, sem) in need.items():
                    if seen.get(key, 0) >= val:
                        continue
                    seen[key] = val
                    h.wait_ge(sem, val)
                ins = op.fn(h)
                if op.dma:
                    ins.then_inc(self.dsems[op.dsem], 16)
                elif op.sig:
                    ins.then_inc(self.esem[e], 1)
            if e == "sp":
                for j, v in enumerate(final_d):
                    if v > 0:
                        h.wait_ge(self.dsems[j], v)

        with nc.Block() as block:
            @block.tensor
            def _(h):
                replay("pe", h)

            @block.scalar
            def _(h):
                replay("act", h)

            @block.vector
            def _(h):
                replay("dve", h)

            @block.gpsimd
            def _(h):
                replay("pool", h)

            @block.sync
            def _(h):
                replay("sp", h)


def _is_dram(ap):
    return "DRam" in type(ap.tensor).__name__


def K(*aps):
    out = []
    for a in aps:
        if a is None or isinstance(a, (int, float)):
            continue
        if _is_dram(a):
            continue
        out.append(a.tensor.name)
    return out


def _t5_bucket(rel):
    nb = 16
    ret = np.where(rel > 0, nb, 0)
    n = np.abs(rel)
    max_exact = nb // 2
    nf = np.maximum(n, 1).astype(np.float32)
    large = max_exact + (np.log(nf / max_exact) / np.float32(np.log(128 / max_exact)) * (nb - max_exact)).astype(np.int32)
    large = np.minimum(large, nb - 1)
    return ret + np.where(n < max_exact, n, large)


def _bias_tables(rel_bias):
    rb = np.asarray(rel_bias, np.float32)
    q = np.arange(128)
    out = np.zeros((2, 128, 4, 4, 128), np.float32)
    for kt in range(2):
        kloc = np.arange(128) + (kt - 1) * 128
        rel = kloc[:, None] - q[None, :]
        bk = _t5_bucket(rel)
        kc = np.floor_divide(kloc, 64)[:, None]
        qc = (q // 64)[None, :]
        valid = (kc <= qc) & (kc >= qc - 2)
        g = rb[bk]
        g = np.where(valid[:, :, None], g, NEG)
        out[kt] = g.reshape(128, 128, 4, 4).transpose(0, 2, 3, 1)
    bp = out.reshape(2, 128, 4, 512)
    outs = np.full((3, 128, 4, 4, 64), NEG, np.float32)
    qpos = 2048 + np.arange(16)
    for s in range(2):
        kpos = 2048 - 128 + np.arange(128)
        bk = _t5_bucket(kpos[:, None] - qpos[None, :])
        g = rb[bk].reshape(128, 16, 4, 4).transpose(0, 2, 3, 1)
        outs[s, :, :, :, 32 * s:32 * s + 16] = g
        bk2 = _t5_bucket(qpos[:, None] - qpos[None, :])
        g2 = rb[bk2].reshape(16, 16, 4, 4).transpose(0, 2, 3, 1)
        outs[2, 32 * s:32 * s + 16, :, :, 32 * s:32 * s + 16] = g2
    bs = outs.reshape(3, 128, 4, 256)
    return np.ascontiguousarray(bp, np.float32), np.ascontiguousarray(bs, np.float32)


def _consts():
    c = {}
    c["ident"] = np.eye(128, dtype=np.float32).astype(ml_dtypes.bfloat16)
    s = np.arange(128)[:, None]
    t = np.arange(128)[None, :]
    c["tri_incl"] = np.where(s <= t, -1.0 / 16, 0.0).astype(np.float32).astype(ml_dtypes.bfloat16)
    c["tri_suf"] = np.where(s > t, -1.0 / 16, 0.0).astype(np.float32).astype(ml_dtypes.bfloat16)
    c["cmask4"] = np.tile(np.where(s <= t, 1.0, 0.0).astype(np.float32), (1, 4))
    c["tri_all"] = np.full((128, 128), -1.0 / 16, np.float32).astype(ml_dtypes.bfloat16)
    s6 = np.arange(64)[:, None]
    t6 = np.arange(64)[None, :]
    same = (s6 // 32) == (t6 // 32)
    real_s = (s6 % 32) < 16
    tis = np.zeros((128, 128), np.float32)
    tis[:64, :64] = np.where(same & (s6 <= t6), -1.0 / 16, 0.0)
    tss = np.zeros((128, 128), np.float32)
    tss[:64, :64] = np.where(same & (s6 > t6) & real_s, -1.0 / 16, 0.0)
    cms = np.zeros((128, 512), np.float32)
    cms[:64, :256] = np.tile(np.where(same & (s6 <= t6), 1.0, 0.0), (1, 4))
    c["tri_incl_s"] = tis.astype(ml_dtypes.bfloat16)
    c["tri_suf_s"] = tss.astype(ml_dtypes.bfloat16)
    c["cmask4_s"] = cms
    oh = np.zeros((2, 128, 128), np.float32)
    oh[0, :, :64] = 1.0
    oh[1, :, 64:] = 1.0
    c["onesH"] = oh.astype(ml_dtypes.bfloat16)
    g4 = np.zeros((4, 512), np.float32)
    g4s = np.zeros((4, 512), np.float32)
    for g in range(4):
        g4[g, g * 128:(g + 1) * 128] = 1.0
        g4s[g, g * 64:(g + 1) * 64] = 1.0
    c["g4"] = g4.astype(ml_dtypes.bfloat16)
    c["g4s"] = g4s.astype(ml_dtypes.bfloat16)
    bo = np.zeros((128, 128), np.float32)
    bo[:64, :64] = 1.0
    bo[64:, 64:] = 1.0
    c["bo_q"] = bo.astype(ml_dtypes.bfloat16)
    c["bo_k"] = (bo / 64).astype(ml_dtypes.bfloat16)
    c["ones_mean"] = np.full((128, 128), 1.0 / 256, np.float32).astype(ml_dtypes.bfloat16)
    c["ones_row"] = np.ones((1, 128), np.float32).astype(ml_dtypes.bfloat16)
    return c


CONST_SPECS = [
    ("ident", [128, 128], BF16), ("tri_incl", [128, 128], BF16), ("tri_suf", [128, 128], BF16),
    ("cmask4", [128, 512], F32), ("tri_all", [128, 128], BF16), ("tri_incl_s", [128, 128], BF16), ("tri_suf_s", [128, 128], BF16),
    ("cmask4_s", [128, 512], F32), ("onesH", [2, 128, 128], BF16), ("g4", [4, 512], BF16), ("g4s", [4, 512], BF16),
    ("bo_q", [128, 128], BF16), ("bo_k", [128, 128], BF16), ("ones_mean", [128, 128], BF16), ("ones_row", [1, 128], BF16),
]


class Prog:
    def __init__(self, debug=None):
        self.nc = nc = bass.Bass("TRN2", target_bir_lowering=False)
        self.s = Sched(nc)
        self.planning = False
        self.plan = []
        self.wscr = None
        self.next_x = None
        self.norm_done = False
        self.x_loaded = False
        self.debug = debug or {}
        self.dbg_outs = []
        self.din = {}
        self.dout = {}
        self._declare_io()
        self._alloc()

    def _in(self, name, shape, dt=F32):
        self.din[name] = self.nc.dram_tensor(name, list(shape), dt, kind="ExternalInput").ap()
        return self.din[name]

    def _out(self, name, shape, dt=F32):
        self.dout[name] = self.nc.dram_tensor(name, list(shape), dt, kind="ExternalOutput").ap()
        return self.dout[name]

    def _declare_io(self):
        i = self._in
        i("xp", [TOKC, D]); i("xprev", [TOKC, D]); i("xs", [TS, D])
        i("ck", [2, 128, 256]); i("cv", [2, 128, 256]); i("st", [2, 4, 128, 256])
        i("ffn1_up", [D, 2 * DFF]); i("ffn1_down", [DFF, D]); i("ffn2_up", [D, 2 * DFF]); i("ffn2_down", [DFF, D])
        i("w_in", [D, IN_W]); i("w_branch", [2048, D]); i("w_out", [D, D])
        i("w_alpha", [16, 512]); i("b_alpha", [1, 512])
        i("gains_fm", [128, 24])
        i("fgain_bc", [128, D])
        i("hgain", [128, 2])
        i("qgcol", [128, 1]); i("kgcol", [128, 1]); i("kgain_bc", [128, 64])
        i("sinkT", [2, 4, 128])
        i("halo_bias", [128, 1])
        i("biasp", [2, 128, 4, 512]); i("biass", [3, 128, 4, 256])
        for n, sh, dt in CONST_SPECS:
            i(n, sh, dt)
        o = self._out
        o("y", [TOKC, D]); o("ys", [TS, D]); o("nk", [128, 256]); o("nv", [128, 256]); o("ng", [4, 128, 256])
        o("nks", [TS, 256]); o("nvs", [TS, 256]); o("ngs", [2, 4, 128, 256])

    def sb(self, name, shape, dt):
        return self.nc.alloc_sbuf_tensor("s_" + name, list(shape), dt)

    def _alloc(self):
        nc = self.nc
        sb = self.sb
        self.c = {}
        for n, sh, dt in CONST_SPECS:
            if n.endswith("_s"):
                continue
            if len(sh) == 3:
                self.c[n] = sb("c_" + n, [sh[1], sh[0], sh[2]], dt)
            else:
                self.c[n] = sb("c_" + n, sh, dt)
        self.gains_fm = sb("gains_fm", [128, 24], F32)
        self.fgain_bc = sb("fgain_bc", [128, D], F32)
        self.hgain = sb("hgain", [128, 2], F32)
        self.qgcol = sb("qgcol", [128, 1], F32)
        self.kgcol = sb("kgcol", [128, 1], F32)
        self.kgain_bc = sb("kgain_bc", [128, 64], F32)
        self.sinkT = sb("sinkT", [4, 2, 128], F32)
        self.esinkT = sb("esinkT", [4, 2, 128], BF16)
        self.halo_bias = sb("halo_bias", [128, 1], F32)
        self.walpha = sb("walpha", [16, 512], BF16)
        self.balpha = sb("balpha", [1, 512], BF16)
        self.biasT = sb("biasT", [128, 2, 4, 512], BF16)
        self.biasTs = self.biasT[:, :, :, :].rearrange("p a j n -> p (a j n)")[:, 0:3072].rearrange("p (a j n) -> p a j n", a=3, j=4)
        self.xt = [sb("xt%d" % i, [128, D], F32) for i in range(4)]
        self.hball = sb("hball", [128, 4, D], BF16)
        self.hb = [self.hball[:, i, :] for i in range(4)]
        self.osT = [self.hball[:, 2 * i:2 * i + 2, :].rearrange("p a (g t) -> p (a g) t", g=2) for i in range(2)]
        self.ss = [sb("ss%d" % i, [128, 1], F32) for i in range(4)]
        self.rstd = [sb("rstd%d" % i, [128, 1], F32) for i in range(4)]
        self.hT = [sb("hT%d" % i, [128, T], BF16) for i in range(8)]
        self.arena0 = sb("arena0", [128, 22 * T], BF16)
        self.actT = [self.arena0[:, i * T:(i + 1) * T] for i in range(22)]
        self.e1T = self.arena0[:, 0:4096].bitcast(F32).rearrange("p (h t) -> p h t", h=4)
        self.e2T = self.arena0[:, 4096:8192].bitcast(F32).rearrange("p (h t) -> p h t", h=4)
        self.qeT = self.arena0[:, 8192:10240].rearrange("p (h t) -> p h t", h=4)
        self.arena1 = sb("arena1", [128, 10240], BF16)
        self.v_tok = [self.arena1[:, i * 1024:(i + 1) * 1024] for i in range(4)]
        self.kd = [self.arena1[:, 4096 + i * 512:4096 + (i + 1) * 512] for i in range(4)]
        self.es = [self.arena1[:, 6144 + i * 1024:6144 + (i + 1) * 1024].bitcast(F32) for i in range(4)]
        self.sigA = self.arena1[:, 0:4096].rearrange("p (c t) -> p c t", c=8)
        self.sigB = self.arena1[:, 4096:8192].rearrange("p (c t) -> p c t", c=8)
        self.keT = sb("keT", [128, 4, T], BF16)
        self.gaT = sb("gaT", [16, T], BF16)
        self.sp = [sb("sp%d" % i, [128, 512], BF16) for i in range(4)]
        self.dtot = sb("dtot", [128, 4, 1], F32)
        self.sgT = sb("sgT", [128, 8, T], BF16)
        self.ATm = sb("ATm", [128, 512], BF16)
        self.ATm2 = [self.ATm, sb("ATm1", [128, 512], BF16)]
        self.Sbf_alt = sb("Sbf_alt", [128, 4, 256], BF16)
        self.o_raw = sb("o_raw", [128, 8, 128], F32)
        self.osq = sb("osq", [128, 8, 128], BF16)
        self.rstdg = sb("rstdg", [128, 512], F32)
        self.o_raw2 = [self.o_raw, sb("o_raw1", [128, 8, 128], F32)]
        self.osq2 = [self.osq, sb("osq1", [128, 8, 128], BF16)]
        self.S = [sb("S0", [128, 4, 256], F32), self.xt[1][:, :].rearrange("p (h v) -> p h v", h=4)]
        self.Sbf = [sb("Sbf0", [128, 4, 256], BF16), self.xt[2][:, 0:512].bitcast(BF16).rearrange("p (h v) -> p h v", h=4)]
        self.sqr = [sb("sqr%d" % i, [128, T], F32) for i in range(1)]
        self.sqsq = [sb("sqsq%d" % i, [128, T], BF16) for i in range(1)]
        self.rsq = [sb("rsq%d" % i, [128, T], F32) for i in range(1)]
        self.qo = [sb("qo%d" % i, [128, 4 * T], BF16) for i in range(2)]
        self.knT_cur = sb("knT_cur", [128, 2, T], BF16)
        self.knT_prev = sb("knT_prev", [128, 2, 128], BF16)
        self.vz_cur = [sb("vz_cur%d" % i, [128, 512], BF16) for i in range(4)]
        self.vz_prev = sb("vz_prev", [128, 512], BF16)
        self.PT = [sb("PT%d" % i, [128, 512], BF16) for i in range(6)]
        self.rden = sb("rden", [128, 512], F32)
        self.ksq = self.sqr[0][:, 0:256]
        self.ktok = self.sqr[0][:, 256:512]
        self.kss = sb("kss", [128, 4], F32)
        self.knew = self.rsq[0][:, 0:256]
        self.vnew = self.rsq[0][:, 256:512]
        self.mtmp = [sb("mtmp%d" % i, [128, T], F32) for i in range(2)]
        self.sa = self.mtmp
        self.junk = self.mtmp[0][:, :].bitcast(BF16)
        self.ckf = sb("ckf", [128, 256], F32)
        self.ckb = sb("ckb", [128, 256], BF16)
        self.knT_c = [sb("knT_c%d" % i, [128, 2, 128], BF16) for i in range(2)]
        self.vz_c = [sb("vz_c%d" % i, [128, 512], BF16) for i in range(2)]
        self.NSLOT = 4
        self.slots = [sb("wslot%d" % i, [128, 4096], BF16) for i in range(self.NSLOT)]
        self.psf = [nc.alloc_psum_tensor("psf%d" % i, [128, 512], F32) for i in range(6)]
        self.psb = [nc.alloc_psum_tensor("psb%d" % i, [128, 1024], BF16) for i in range(2)]

    def reset_counters(self):
        self._ng_done = False
        self._early_done = False
        self.norm_done = False
        self.x_loaded = False
        self.next_x = None
        self._pf = 0
        self._pb = 0
        self._pt = 0
        self._piece = 0
        self._rr = 0

    def pf(self):
        self._pf += 1
        return self.psf[self._pf % 6]

    def pb(self):
        self._pb += 1
        return self.psb[self._pb % 2]

    def add(self, eng, fn, reads, writes, dma=False, group=None):
        if self.planning:
            return
        self.s.add(eng, fn, reads, writes, dma, group)

    def mm(self, out, lhsT, rhs, start=True, stop=True, rkeys=None):
        self.add("pe", lambda h: h.matmul(out, lhsT=lhsT, rhs=rhs, start=start, stop=stop), K(lhsT, rhs) if rkeys is None else rkeys, K(out))

    def tr(self, out, in_, ident, rkeys=None):
        self.add("pe", lambda h: h.transpose(out, in_, ident), K(in_, ident) if rkeys is None else rkeys, K(out))

    def act(self, out, in_, func, bias=None, scale=None, accum_out=None):
        kw = {}
        if bias is not None:
            kw["bias"] = bias
        if scale is not None:
            kw["scale"] = scale
        if accum_out is not None:
            kw["accum_out"] = accum_out
        self.add("act", lambda h: h.activation(out=out, in_=in_, func=func, **kw), K(in_, bias, scale), K(out, accum_out))

    def rsqrt(self, out, in_, scale, eps):
        self.act(out, in_, AF.Ln, bias=eps, scale=scale)
        self.act(out, out, AF.Exp, scale=-0.5)

    def amul(self, out, in_, mul):
        self.add("act", lambda h: h.mul(out=out, in_=in_, mul=mul), K(in_, mul), K(out))

    def cp(self, eng, out, in_):
        if eng == "act":
            self.add("act", lambda h: h.copy(out=out, in_=in_), K(in_), K(out))
        else:
            self.add(eng, lambda h: h.tensor_copy(out=out, in_=in_), K(in_), K(out))

    def tt(self, eng, out, in0, in1, op, rkeys=None, wkeys=None):
        self.add(eng, lambda h: h.tensor_tensor(out=out, in0=in0, in1=in1, op=op), K(in0, in1) if rkeys is None else rkeys, K(out) if wkeys is None else wkeys)

    def ts(self, eng, out, in0, s1, s2, op0, op1=None, rkeys=None, wkeys=None):
        if rkeys is not None:
            self.add(eng, lambda h: h.tensor_scalar(out=out, in0=in0, scalar1=s1, scalar2=0.0, op0=op0, op1=ALU.add), rkeys, wkeys)
            return
        if op1 is None:
            if op0 == ALU.pow:
                self.add(eng, lambda h: h.tensor_scalar(out=out, in0=in0, scalar1=0.0, scalar2=s1, op0=ALU.add, op1=ALU.pow), K(in0, s1), K(out))
            else:
                self.add(eng, lambda h: h.tensor_scalar(out=out, in0=in0, scalar1=s1, scalar2=0.0, op0=op0, op1=ALU.add), K(in0, s1), K(out))
        else:
            self.add(eng, lambda h: h.tensor_scalar(out=out, in0=in0, scalar1=s1, scalar2=s2, op0=op0, op1=op1), K(in0, s1, s2), K(out))

    def stt(self, eng, out, in0, scalar, in1, op0, op1):
        self.add(eng, lambda h: h.scalar_tensor_tensor(out=out, in0=in0, scalar=scalar, in1=in1, op0=op0, op1=op1),
                 K(in0, scalar, in1), K(out))

    def memset(self, eng, ap, val):
        self.add(eng, lambda h: h.memset(ap, val), [], K(ap))

    def dma(self, eng, out, in_, reads=(), writes=(), group=None):
        self.add(eng, lambda h: h.dma_start(out=out, in_=in_), K(in_) + list(reads), K(out) + list(writes), dma=True, group=group)

    def wget(self, tag, spec, wid, nel=4096):
        if self.planning:
            self.plan.append((tag, spec, wid, nel))
            return self.slots[0]
        if self._piece == 0:
            self.first = {}
            for j, pl in enumerate(self.plan):
                self.first.setdefault(pl[2], j)
            self.widx = {w: k for k, w in enumerate(self.first)}
            if self.wscr is None:
                self.wscr = self.nc.dram_tensor("wscr", [len(self.widx), 128, 4096], BF16).ap()
        i = self._piece
        assert self.plan[i][0] == tag, (self.plan[i][0], tag)
        LA = self.NSLOT - 2 if wid[0] == "wbB" else self.NSLOT - 1
        if i == 0:
            self._issue_piece(0)
            self._issued = 0
        slot = self.slots[i % self.NSLOT]
        if self.first[wid] == i and self.nuse[wid] > 1:
            self.dma("pool", self.wscr[self.widx[wid], :, :nel], slot[:, :nel], writes=[("scr", wid)])
        target = min(i + LA, len(self.plan) - 1)
        while self._issued < target:
            self._issued += 1
            self._issue_piece(self._issued)
        self._piece += 1
        return slot

    def _issue_piece(self, j):
        slot = self.slots[j % self.NSLOT]
        tag, spec, wid, nel = self.plan[j]
        if self.first[wid] == j:
            for dst, src in spec(slot):
                self.dma("pool", dst, src, group=("piece", j))
        else:
            self.dma("sp", slot[:, :nel], self.wscr[self.widx[wid], :, :nel], reads=[("scr", wid)])

    def dbg(self, name, ap, shape, dt=F32):
        if name not in self.debug:
            return
        if self.planning:
            return
        d = self.nc.dram_tensor("dbg_" + name, list(shape), dt, kind="ExternalOutput").ap()
        self.dbg_outs.append("dbg_" + name)
        self.dma("sp", d, ap)

    def setup(self, first_x=None):
        c = self.c
        di = self.din
        self.dma("sp", c["ident"][:], di["ident"])
        self.dma("sp", self.gains_fm[:], di["gains_fm"])
        if first_x is not None:
            self.load_x(first_x, [(j * 128, 128) for j in range(4)])
        for n, sh, dt in CONST_SPECS:
            if n == "ident":
                continue
            if n.endswith("_s"):
                continue
            if len(sh) == 3:
                self.dma("sp", c[n][:], di[n].rearrange("a p f -> p a f"))
            else:
                self.dma("sp", c[n][:], di[n])
        for n in ("fgain_bc", "hgain", "qgcol", "kgcol", "kgain_bc", "halo_bias"):
            self.dma("sp", getattr(self, n)[:], di[n])
        self.dma("sp", self.sinkT[:], di["sinkT"].rearrange("a g m -> g a m"))
        self.dma("pool", self.walpha[:], di["w_alpha"])
        self.dma("pool", self.balpha[:], di["b_alpha"])
        self.dma("pool", self.biasT[:, 0:2], di["biasp"].rearrange("a p j n -> p a j n"))
        self.act(self.esinkT[:], self.sinkT[:], AF.Exp)
        for t_ in self.vz_cur + [self.vz_prev] + self.vz_c:
            self.memset("pool", t_[:], 0.0)
        self.memset("pool", self.knT_prev[:], 0.0)
        self.memset("dve", self.S[0][:], 0.0)
        self.memset("dve", self.Sbf[0][:], 0.0)
        for t_ in self.ss:
            self.memset("dve", t_[:], 0.0)
        if self.debug.get("delay"):
            self.memset("pool", self.slots[0][:], 0.0)
            for _ in range(int(self.debug["delay"])):
                self.cp("pool", self.slots[1][:], self.slots[0][:])

    def load_x(self, src, tts):
        for i, (o, P) in enumerate(tts):
            self.dma("sp", self.xt[i][:P, :], src[o:o + P, :])

    def norm_hT(self, tts, Tn, gcol0, src=None):
        self.norm_stats(tts, src)
        self.norm_tr(tts, Tn, gcol0, subtile_major=True)

    def norm_stats(self, tts, src=None):
        xt, hb = (self.xt if src is None else src), self.hb
        for i, (o, P) in enumerate(tts):
            self.act(self.junk[:P, :], xt[i][:P, :], AF.Square, accum_out=self.ss[i][:P, 0:1])
            self.rsqrt(self.rstd[i][:P, 0:1], self.ss[i][:P, 0:1], 1.0 / D, EPS)
            self.ts("pool" if i % 2 else "dve", hb[i][:P, :], xt[i][:P, :], self.rstd[i][:P, 0:1], None, ALU.mult,
                    rkeys=K(xt[i][:], self.rstd[i][:]) + ["s_hball"], wkeys=[("hb", i)])

    def norm_tr(self, tts, Tn, gcol0, subtile_major=False):
        hb = self.hb
        ident = self.c["ident"]
        if subtile_major and len(tts) > 1:
            regs = []
            b0, b1 = self.pb(), self.pb()
            f0, f1 = self.pf()[:, :].bitcast(BF16), self.pf()[:, :].bitcast(BF16)
            for bank in (b0, b1, f0, f1):
                regs += [bank[:, 0:512], bank[:, 512:1024]]
            for i, (o, P) in enumerate(tts):
                for cc in range(8):
                    self.tr(regs[cc][:, o:o + P], hb[i][:P, cc * 128:(cc + 1) * 128], ident[:P, :P],
                            rkeys=K(ident[:]) + [("hb", i), "s_hball"])
            for cc in (0, 2, 1, 3, 4, 6, 5, 7):
                g = self.gains_fm[:, gcol0 + cc:gcol0 + cc + 1]
                if (cc // 2) % 2 == 0:
                    self.amul(self.hT[cc][:, :Tn], regs[cc][:, :Tn], g)
                else:
                    self.ts("dve", self.hT[cc][:, :Tn], regs[cc][:, :Tn], g, None, ALU.mult)
            return
        for cc in range(8):
            pT = self.pb()
            for i, (o, P) in enumerate(tts):
                self.tr(pT[:, o:o + P], hb[i][:P, cc * 128:(cc + 1) * 128], ident[:P, :P],
                        rkeys=K(ident[:]) + [("hb", i), "s_hball"])
            g = self.gains_fm[:, gcol0 + cc:gcol0 + cc + 1]
            if cc % 2 == 0:
                self.amul(self.hT[cc][:, :Tn], pT[:, :Tn], g)
            else:
                self.ts("dve", self.hT[cc][:, :Tn], pT[:, :Tn], g, None, ALU.mult)

    def ffn(self, tts, Tn, wup, wdn, gcol0, tagp, mid_hook=None, mid_hook2=None):
        if self.norm_done:
            self.norm_done = False
        else:
            self.norm_hT(tts, Tn, gcol0)
        hT = self.hT
        upv = wup.rearrange("(k p) (ab c) -> p k ab c", p=128, ab=2)
        for g in range(11):
            def spec(slot, g=g):
                sv = slot[:, :].rearrange("p (k ab c) -> p k ab c", k=8, ab=2)
                return [(sv[:, :, ab, :], upv[:, :, ab, g * 256:(g + 1) * 256]) for ab in range(2)]
            slot = self.wget((tagp, "up", g), spec, (tagp[-1], "up", g))
            wv = slot[:, :].rearrange("p (k ab c) -> p k ab c", k=8, ab=2)
            for u in range(2):
                i = 2 * g + u
                pa = self.pf()
                pb_ = self.pf()
                for k in range(8):
                    self.mm(pa[:, :Tn], wv[:, k, 0, u * 128:(u + 1) * 128], hT[k][:, :Tn], k == 0, k == 7)
                for k in range(8):
                    self.mm(pb_[:, :Tn], wv[:, k, 1, u * 128:(u + 1) * 128], hT[k][:, :Tn], k == 0, k == 7)
                sa = self.sa[i % 2]
                self.act(sa[:, :Tn], pa[:, :Tn], AF.Silu)
                self.tt("dve", self.actT[i][:, :Tn], sa[:, :Tn], pb_[:, :Tn], ALU.mult,
                        rkeys=K(sa[:], pb_[:]) + ["s_arena0"], wkeys=[("actT", i)])
        if mid_hook is not None:
            mid_hook()
        dnv = wdn.rearrange("(i p) n -> p i n", p=128)
        for nh in range(2):
            accs = [self.pf() for _ in tts]
            for cg, (c0, c1) in enumerate(((0, 8), (8, 16), (16, 22))):
                def spec(slot, nh=nh, c0=c0, c1=c1):
                    return [(slot[:, :(c1 - c0) * 512].rearrange("p (c n) -> p c n", n=512), dnv[:, c0:c1, nh * 512:(nh + 1) * 512])]
                slot = self.wget((tagp, "down", nh, cg), spec, (tagp[-1], "down", nh, cg), (c1 - c0) * 512)
                wv = slot[:, :(c1 - c0) * 512].rearrange("p (c n) -> p c n", n=512)
                for i, (o, P) in enumerate(tts):
                    for cc in range(c0, c1):
                        self.mm(accs[i][:P, :], self.actT[cc][:, o:o + P], wv[:, cc - c0, :], cc == 0, cc == 21,
                                rkeys=K(wv[:, 0, :]) + ["s_arena0", ("actT", cc)])
            for i, (o, P) in enumerate(tts):
                xs_ = self.xt[i][:P, nh * 512:(nh + 1) * 512]
                self.stt("dve", xs_, accs[i][:P, :], 0.5, xs_, ALU.mult, ALU.add)
            if nh == 0 and mid_hook2 is not None:
                mid_hook2()

    def win_piece(self, tag, c0, w):
        wv_ = self.din["w_in"].rearrange("(k p) c -> p k c", p=128)

        def spec(slot):
            return [(slot[:, :8 * w].rearrange("p (k c) -> p k c", k=8), wv_[:, :, c0:c0 + w])]
        slot = self.wget(tag, spec, tag[1:], 8 * w)
        return slot[:, :8 * w].rearrange("p (k c) -> p k c", k=8)

    def fm_proj(self, wv, col0, Tn, lhs_view=None):
        p = self.pf()
        for k in range(8):
            lhs = wv[:, k, col0:col0 + 128] if lhs_view is None else lhs_view(k)
            self.mm(p[:, :Tn], lhs, self.hT[k][:, :Tn], k == 0, k == 7)
        return p

    def tm_proj(self, wv, c0, w, o, P):
        p = self.pf()
        for k in range(8):
            self.mm(p[:P, :w], self.hT[k][:, o:o + P], wv[:, k, c0:c0 + w], k == 0, k == 7)
        return p

    def qknorm(self, ps, out, gcol, bo, eps, Tn, idx, ntt=None):
        sqsq, sqr, rs = self.sqsq[0], self.sqr[0], self.rsq[0]
        self.act(sqsq[:, :Tn], ps, AF.Square)
        pm = self.pf()
        self.mm(pm[:, :Tn], bo[:, :], sqsq[:, :Tn])
        self.rsqrt(rs[:, :Tn], pm[:, :Tn], 1.0, eps)
        if ntt is None:
            self.stt("dve", out, ps, gcol, rs[:, :Tn], ALU.mult, ALU.mult)
        else:
            self.stt("dve", out, ps.rearrange("p (t q) -> p t q", t=ntt), gcol, rs[:, :Tn].rearrange("p (t q) -> p t q", t=ntt), ALU.mult, ALU.mult)

    def mixer(self, tts, Tn, mode, tagp, seqs, tile_idx, last_tile):
        c = self.c
        sample = mode == "sample"
        L = tts[0][1]
        tri_i = c["tri_incl"]
        tri_s = c["tri_suf"]
        cmask = c["cmask4"]
        self.norm_hT(tts, Tn, 8)
        if mode == "pre" and self.next_x is not None:
            self.load_x(self.next_x, tts)
        hT = self.hT
        e1T, e2T, qeT, keT = self.e1T, self.e2T, self.qeT, self.keT
        wv = self.win_piece((tagp, "ga"), C_GA, 16)
        p = self.pf()
        for k in range(8):
            self.mm(p[:16, :Tn], wv[:, k, 0:16], hT[k][:, :Tn], k == 0, k == 7)
        self.cp("act", self.gaT[:, :Tn], p[:16, :Tn])
        wk = self.win_piece((tagp, "gk"), C_GK, 512)
        n_t = len(tts)
        pzs = []
        for i, (o, P) in enumerate(tts):
            pz = self.pf()
            self.mm(pz[:P, :], self.gaT[:, o:o + P], self.walpha[:, :], True, False)
            self.mm(pz[:P, :], c["ones_row"][0:1, :P], self.balpha[0:1, :], False, True)
            pzs.append(pz)
        for i, (o, P) in enumerate(tts):
            self.act(self.es[i][:P, :], pzs[i][:P, :], AF.Exp, scale=-1.0)
        for i, (o, P) in enumerate(tts):
            self.act(self.sp[i][:P, :], self.es[i][:P, :], AF.Ln, bias=1.0)
        for i, (o, P) in enumerate(tts):
            pk = self.tm_proj(wk, 0, 512, o, P)
            self.cp("dve", self.kd[i][:P, :], pk[:P, :])
        early_norm = mode == "pre" and self.next_x is not None
        if early_norm:
            self.norm_stats(tts)
        if mode != "pre":
            for hd in range(4):
                p = self.fm_proj(wk, hd * 128, Tn)
                self.cp("dve", keT[:, hd, :Tn], p[:, :Tn])
            wq = self.win_piece((tagp, "gq"), C_GQ, 512)
            for hd in range(4):
                p = self.fm_proj(wq, hd * 128, Tn)
                self.cp("dve", qeT[:, hd, :Tn], p[:, :Tn])
        pbts = []
        for i, (o, P) in enumerate(tts):
            pbt = self.pf()
            for hd in range(4):
                self.mm(pbt[:, hd * P:(hd + 1) * P], self.sp[i][:P, hd * 128:(hd + 1) * 128], tri_i[:P, :P])
            pbts.append(pbt)
        for i, (o, P) in enumerate(tts):
            pv3 = pbts[i][:, :4 * P].rearrange("p (h t) -> p h t", h=4)
            self.act(e1T[:, :, o:o + P], pv3, AF.Exp)
            if mode != "pre":
                self.act(e2T[:, :, o:o + P], pv3, AF.Exp, scale=-1.0)
        psufs = []
        for i, (o, P) in enumerate(tts):
            psuf = self.pf()
            whole = (mode == "pre")
            self.mm(psuf[:P, :], tri_s[:P, :P], self.sp[i][:P, :], True, not (whole and i < n_t - 1))
            if whole:
                for j in range(i + 1, n_t):
                    self.mm(psuf[:P, :], c["tri_all"][:P, :P], self.sp[j][:P, :], False, j == n_t - 1)
            psufs.append(psuf)
        for i, (o, P) in enumerate(tts):
            self.act(self.es[i][:P, :], psufs[i][:P, :], AF.Exp)
        for i, (o, P) in enumerate(tts):
            self.tt("dve", self.kd[i][:P, :], self.kd[i][:P, :], self.es[i][:P, :], ALU.mult)
        for pc in range(2):
            wvv = self.win_piece((tagp, "gv", pc), C_GV + pc * 512, 512)
            for i, (o, P) in enumerate(tts):
                p = self.tm_proj(wvv, 0, 512, o, P)
                self.cp("dve", self.v_tok[i][:P, pc * 512:(pc + 1) * 512], p[:P, :])
        if early_norm and not last_tile:
            self.norm_tr(tts, Tn, 0, subtile_major=True)
            self.norm_done = True
        if mode != "pre":
            for pc in range(2):
                wg = self.win_piece((tagp, "gr", pc), C_GR + pc * 512, 512)
                for u in range(4):
                    p = self.fm_proj(wg, u * 128, Tn)
                    self.act(self.sgT[:, pc * 4 + u, :Tn], p[:, :Tn], AF.Silu)
        if mode != "pre":
            for hd in range(4):
                self.tt("dve", keT[:, hd, :Tn], keT[:, hd, :Tn], e2T[:, hd, :Tn], ALU.mult)
            for hd in range(4):
                self.stt("dve", qeT[:, hd, :Tn], qeT[:, hd, :Tn], 128 ** -0.5, e1T[:, hd, :Tn], ALU.mult, ALU.mult)
        mstop = self.debug.get("mstop") if sample else None
        gla_last_tail = None
        if mstop == "gpipe":
            return
        if mode == "pre":
            S, Sbf = seqs[0][2], seqs[0][3]
            pS = [self.pf(), self.pf()]
            for hd in range(4):
                for i, (o, P) in enumerate(tts):
                    self.mm(pS[hd // 2][:, (hd % 2) * 256:(hd % 2 + 1) * 256], self.kd[i][:P, hd * 128:(hd + 1) * 128],
                            self.v_tok[i][:P, hd * 256:(hd + 1) * 256], i == 0, i == n_t - 1)
            self.cp("dve", self.dtot[:, :, :], e1T[:, :, 127:128])
            for i in range(1, n_t):
                self.tt("dve", self.dtot[:, :, :], self.dtot[:, :, :], e1T[:, :, i * 128 + 127:i * 128 + 128], ALU.mult)
            for hd in range(4):
                self.stt("dve", S[:, hd, :], S[:, hd, :], self.dtot[:, hd, :], pS[hd // 2][:, (hd % 2) * 256:(hd % 2 + 1) * 256], ALU.mult, ALU.add)
            self.cp("act", Sbf[:, :, :], S[:, :, :])
        elif len(seqs) == 1:
            gla_last_tail = self.gla_pipelined(tts, seqs[0], cmask)
        else:
            for i, (o, P) in enumerate(tts):
                self.gla_chunk(i, o, P, mode, seqs, cmask)
        if mstop == "gla":
            return
        if mode == "full" and tile_idx == 0:
            self.dbg("oaT", self.sgT[:, :, :], [128, 8, T], BF16)
            self.dbg("qeT", self.qeT, [128, 4, T], BF16)
            self.dbg("keT", self.keT[:, :, :], [128, 4, T], BF16)
            self.dbg("e1T", self.e1T, [128, 4, T], F32)
            self.dbg("kd3", self.kd[3], [128, 512], BF16)
            self.dbg("vtok3", self.v_tok[3], [128, 1024], BF16)
            self.dbg("S", self.S[0][:, :, :], [128, 4, 256], F32)
        need_kv = (mode != "pre") or last_tile
        if mode != "pre":
            ntt = len(tts)
            w_in_v = self.din["w_in"].rearrange("(k p) c -> p k c", p=128)
            for pc in range(2):
                def spec(slot, pc=pc):
                    sv = slot[:, :].rearrange("p (k g two d) -> p k g two d", k=8, g=4, two=2)
                    return [(sv[:, k, :, two, :], w_in_v[:, k, C_SQ + pc * 512 + two * 256:C_SQ + pc * 512 + (two + 1) * 256].rearrange("p (g d) -> p g d", g=4))
                            for two in range(2) for k in range(8)]
                wq = self.wget((tagp, "sq", pc), spec, ("sq", pc))[:, :].rearrange("p (k c) -> p k c", k=8)
                qv = self.qo[pc][:, :ntt * 4 * L].rearrange("p (t g q) -> p t g q", t=ntt, g=4)
                for g in range(4):
                    p = self.fm_proj(wq, g * 128, Tn)
                    if gla_last_tail is not None and g == 1:
                        gla_last_tail()
                        gla_last_tail = None
                    self.qknorm(p[:, :Tn], qv[:, :, g, :], self.qgcol[:, 0:1], c["bo_q"], 64 * EPS, Tn, g, ntt)
        if need_kv:
            wkv = self.win_piece((tagp, "sksv"), C_SK, 512)
            for P2 in range(2):
                p = self.fm_proj(wkv, P2 * 128, Tn)
                self.qknorm(p[:, :Tn], self.knT_cur[:, P2, :Tn], self.kgcol[:, 0:1], c["bo_k"], EPS, Tn, P2)
            for i, (o, P) in enumerate(tts):
                pvv = self.tm_proj(wkv, 256, 256, o, P)
                vz4 = self.vz_cur[i][:, :].rearrange("p (jp par c) -> p jp par c", jp=2, par=2)
                pv4 = pvv[:, :256].rearrange("p (jp par d) -> p jp par d", jp=2, par=2)
                for par in range(2):
                    self.cp("act", vz4[:P, :, par, par * 64:(par + 1) * 64], pv4[:P, :, par, :])
                want_out = sample or (mode == "full" and last_tile and i == len(tts) - 1)
                if want_out:
                    self.cp("act", self.vnew[:P, :], pvv[:P, :256])
                    pkk = self.tm_proj(wkv, 0, 256, o, P)
                    self.act(self.ksq[:P, :], pkk[:P, :256], AF.Square)
                    self.add("dve", lambda h, P=P: h.tensor_reduce(out=self.kss[:P, 0:4], in_=self.ksq[:P, :].rearrange("p (j d) -> p j d", j=4),
                                                                    axis=AX.X, op=ALU.add), K(self.ksq[:]), K(self.kss[:]))
                    self.rsqrt(self.kss[:P, :], self.kss[:P, :], 1.0 / 64, EPS)
                    k3 = self.ktok[:P, :].rearrange("p (j d) -> p j d", j=4)
                    self.tt("dve", k3, pkk[:P, :256].rearrange("p (j d) -> p j d", j=4),
                            self.kss[:P, 0:4].unsqueeze(2).to_broadcast([P, 4, 64]), ALU.mult)
                    self.tt("dve", self.knew[:P, :].rearrange("p (j d) -> p j d", j=4), k3,
                            self.kgain_bc[:P, :].unsqueeze(1).to_broadcast([P, 4, 64]), ALU.mult)
                    if sample:
                        self.dma("sp", self.dout["nks"], self.knew[:P, :])
                        self.dma("sp", self.dout["nvs"], self.vnew[:P, :])
                    else:
                        self.dma("sp", self.dout["nk"], self.knew[:P, :])
                        self.dma("sp", self.dout["nv"], self.vnew[:P, :])
        if mode == "pre":
            if last_tile:
                self.cp("pool", self.knT_prev[:, :, :], self.knT_cur[:, :, Tn - 128:Tn])
                self.cp("pool", self.vz_prev[:, :], self.vz_cur[len(tts) - 1][:, :])
            if self.next_x is not None and not self.norm_done:
                self.norm_tr(tts, Tn, 0, subtile_major=True)
                self.norm_done = True
            return
        if mstop == "swaproj":
            return
        gate_groups = []
        gstate = {}
        for nm, c0, dst in (("gta", C_GTA, self.sigA), ("gtb", C_GTB, self.sigB)):
            for pc in range(2):
                for u in range(4):
                    def grp(nm=nm, c0=c0, dst=dst, pc=pc, u=u):
                        if u == 0:
                            gstate["w"] = self.win_piece((tagp, nm, pc), c0 + pc * 512, 512)
                        p = self.fm_proj(gstate["w"], u * 128, Tn)
                        self.act(dst[:, pc * 4 + u, :Tn], p[:, :Tn], AF.Tanh, scale=0.5)
                    gate_groups.append(grp)

        def run_gates(k):
            for _ in range(min(k, len(gate_groups))):
                gate_groups.pop(0)()
        if sample:
            kts = []
            for sq_ in range(2):
                kts.append(dict(knT=lambda P2, rows, sq_=sq_: self.knT_c[sq_][rows, P2, :], vz=self.vz_c[sq_], bias=self.biasTs[:, sq_, :, :], nk=128, hb=None))
            kts.append(dict(knT=lambda P2, rows: self.knT_cur[rows, P2, 0:64], vz=self.vz_cur[0], bias=self.biasTs[:, 2, :, :], nk=64, hb=None))
            self.swa_block(0, 0, 1, 64, kts, c["g4s"], between=lambda: run_gates(2))
        else:
            for i, (o, P) in enumerate(tts):
                kts = []
                if i == 0:
                    hb = self.halo_bias[:, 0:1] if tile_idx == 0 else None
                    kts.append(dict(knT=lambda P2, rows: self.knT_prev[rows, P2, :], vz=self.vz_prev, bias=self.biasT[:, 0, :, :], nk=128, hb=hb))
                else:
                    kts.append(dict(knT=lambda P2, rows, o=o: self.knT_cur[rows, P2, o - 128:o], vz=self.vz_cur[i - 1], bias=self.biasT[:, 0, :, :], nk=128, hb=None))
                kts.append(dict(knT=lambda P2, rows, o=o: self.knT_cur[rows, P2, o:o + 128], vz=self.vz_cur[i], bias=self.biasT[:, 1, :, :], nk=128, hb=None))
                self.swa_block(i, o, len(tts), 128, kts, c["g4"], between=lambda: run_gates(2))
            self.cp("pool", self.knT_prev[:, :, :], self.knT_cur[:, :, Tn - 128:Tn])
            self.cp("pool", self.vz_prev[:, :], self.vz_cur[len(tts) - 1][:, :])
        if mode == "full" and tile_idx == 0:
            self.dbg("osT0", self.osT[0], [128, 4, T], BF16)
            self.dbg("osT1", self.osT[1], [128, 4, T], BF16)
            self.dbg("knT", self.knT_cur[:, :, :], [128, 2, T], BF16)
        if mstop == "swa":
            return
        run_gates(len(gate_groups))
        if mode == "full" and tile_idx == 0:
            self.dbg("sgA", self.sigA, [128, 8, T], BF16)
            self.dbg("sgB", self.sigB, [128, 8, T], BF16)
        wbr = self.din["w_branch"]
        wbA = wbr[0:1024, :].rearrange("(f p) n -> p f n", p=128)
        wbB = wbr[1024:2048, :].rearrange("(P j g d) n -> d P j g n", P=2, j=2, g=4)
        for ng in range(2):
            def specA(slot, ng=ng):
                return [(slot[:, :].rearrange("p (f n) -> p f n", f=8), wbA[:, :, ng * 512:(ng + 1) * 512])]

            def specB(slot, ng=ng):
                return [(slot[half * 64:(half + 1) * 64, :].rearrange("p (P g n) -> p P g n", P=2, g=4)[:, P2],
                         wbB[:, P2, half, :, ng * 512:(ng + 1) * 512]) for half in range(2) for P2 in range(2)]
            sA = self.wget((tagp, "wbA", ng), specA, ("wbA", ng))[:, :].rearrange("p (f n) -> p f n", f=8)
            sB = self.wget((tagp, "wbB", ng), specB, ("wbB", ng))[:, :].rearrange("p (f n) -> p f n", f=8)
            for u in range(4):
                n = ng * 4 + u
                pa = self.pf()
                for f in range(8):
                    self.mm(pa[:, :Tn], sA[:, f, u * 128:(u + 1) * 128], self.sgT[:, f, :Tn], f == 0, f == 7)
                pb_ = self.pf()
                for f in range(8):
                    self.mm(pb_[:, :Tn], sB[:, f, u * 128:(u + 1) * 128], self.osT[f // 4][:, f % 4, :Tn], f == 0, f == 7)
                t0, t1 = self.mtmp
                self.stt("dve", t0[:, :Tn], self.sigA[:, n, :Tn], 1.0, pa[:, :Tn], ALU.add, ALU.mult)
                self.stt("dve", t1[:, :Tn], self.sigB[:, n, :Tn], 1.0, pb_[:, :Tn], ALU.add, ALU.mult)
                self.tt("pool", self.sigA[:, n, :Tn], t0[:, :Tn], t1[:, :Tn], ALU.add)
        if mode == "full" and tile_idx == 0:
            self.dbg("mT", self.sigA, [128, 8, T], BF16)
        wo = self.din["w_out"].rearrange("(f p) n -> p f n", p=128)
        for nh in range(2):
            def spec(slot, nh=nh):
                return [(slot[:, :].rearrange("p (f n) -> p f n", f=8), wo[:, :, nh * 512:(nh + 1) * 512])]
            so = self.wget((tagp, "wo", nh), spec, ("wo", nh))[:, :].rearrange("p (f n) -> p f n", f=8)
            for i, (o, P) in enumerate(tts):
                p = self.pf()
                for f in range(8):
                    self.mm(p[:P, :], self.sigA[:, f, o:o + P], so[:, f, :], f == 0, f == 7)
                xs_ = self.xt[i][:P, nh * 512:(nh + 1) * 512]
                self.stt("dve", xs_, p[:P, :], 0.5, xs_, ALU.mult, ALU.add)

    def gla_pipelined(self, tts, seq, cmask):
        c = self.c
        e1T, qeT, keT = self.e1T, self.qeT, self.keT
        so, nreal, S, Sbf0 = seq
        Sb = [Sbf0, self.Sbf_alt]
        L = tts[0][1]
        n_t = len(tts)
        assert n_t % 2 == 0

        def front(i, o):
            pS = [self.pf(), self.pf()]
            for hd in range(4):
                self.mm(pS[hd // 2][:, (hd % 2) * 256:(hd % 2 + 1) * 256], self.kd[i][:L, hd * 128:(hd + 1) * 128],
                        self.v_tok[i][:L, hd * 256:(hd + 1) * 256])
            pA = self.pf()
            for hd in range(4):
                self.mm(pA[:L, hd * L:(hd + 1) * L], keT[:, hd, o:o + L], qeT[:, hd, o:o + L])
            atm = self.ATm2[i % 2]
            self.tt("dve", atm[:L, :4 * L], pA[:L, :4 * L], cmask[:L, :4 * L], ALU.mult)
            dcol = o + L - 1
            for hd in range(4):
                self.stt("dve", S[:, hd, :], S[:, hd, :], e1T[:, hd, dcol:dcol + 1], pS[hd // 2][:, (hd % 2) * 256:(hd % 2 + 1) * 256], ALU.mult, ALU.add)
            self.cp("act", Sb[(i + 1) % 2][:, :, :], S[:, :, :])
            return atm

        def tail(j):
            oj = tts[j][0]
            o_raw, osq = self.o_raw2[j % 2], self.osq2[j % 2]
            pM = self.pf()
            for hd in range(4):
                self.mm(pM[:, hd * L:(hd + 1) * L], c["ones_mean"][:, :], osq[:, 2 * hd, :L], True, False)
                self.mm(pM[:, hd * L:(hd + 1) * L], c["ones_mean"][:, :], osq[:, 2 * hd + 1, :L], False, True)
            self.rsqrt(self.rstdg[:, :4 * L], pM[:, :4 * L], 1.0, EPS)
            r3 = self.rstdg[:, :4 * L].rearrange("p (h l) -> p h l", h=4)
            o4 = o_raw[:, :, :].rearrange("p (h v) l -> p h v l", v=2)
            for vc in range(2):
                self.stt("dve", o4[:, :, vc, :L], o4[:, :, vc, :L], self.hgain[:, vc:vc + 1], r3, ALU.mult, ALU.mult)
            self.tt("dve", self.sgT[:, :, oj:oj + L], o_raw[:, :, :L], self.sgT[:, :, oj:oj + L], ALU.mult)

        nxt = front(0, tts[0][0])
        for i, (o, P) in enumerate(tts):
            atm = nxt
            Sbf = Sb[i % 2]
            pO = [self.pf(), self.pf()]
            for hd in range(4):
                for vc in range(2):
                    reg = pO[hd // 2][:, ((hd % 2) * 2 + vc) * L:((hd % 2) * 2 + vc + 1) * L]
                    self.mm(reg, self.v_tok[i][:L, hd * 256 + vc * 128:hd * 256 + (vc + 1) * 128], atm[:L, hd * L:(hd + 1) * L], True, False)
                    self.mm(reg, Sbf[:, hd, vc * 128:(vc + 1) * 128], qeT[:, hd, o:o + L], False, True)
            o_raw, osq = self.o_raw2[i % 2], self.osq2[i % 2]
            for b in range(2):
                pv = pO[b][:, :4 * L].rearrange("p (c l) -> p c l", c=4)
                self.cp("act", o_raw[:, 4 * b:4 * b + 4, :L], pv)
                self.act(osq[:, 4 * b:4 * b + 4, :L], pv, AF.Square)
            if i + 1 < n_t:
                nxt = front(i + 1, tts[i + 1][0])
            if i >= 1:
                tail(i - 1)
        return lambda: tail(n_t - 1)

    def gla_chunk(self, i, o, L, mode, seqs, cmask):
        c = self.c
        e1T, qeT, keT = self.e1T, self.qeT, self.keT
        nseq = len(seqs)
        blk = L // nseq
        def state_mm(si):
            so = seqs[si][0]
            pS = [self.pf(), self.pf()]
            for hd in range(4):
                self.mm(pS[hd // 2][:, (hd % 2) * 256:(hd % 2 + 1) * 256], self.kd[i][so:so + blk, hd * 128:(hd + 1) * 128],
                        self.v_tok[i][so:so + blk, hd * 256:(hd + 1) * 256])
            return pS
        hoist = nseq == 1
        pSs = [state_mm(0)] if hoist else None
        pA = self.pf()
        for hd in range(4):
            self.mm(pA[:L, hd * L:(hd + 1) * L], keT[:, hd, o:o + L], qeT[:, hd, o:o + L])
        self.tt("dve", self.ATm[:L, :4 * L], pA[:L, :4 * L], cmask[:L, :4 * L], ALU.mult)
        pO = [self.pf(), self.pf()]
        for hd in range(4):
            for vc in range(2):
                reg = pO[hd // 2][:, ((hd % 2) * 2 + vc) * L:((hd % 2) * 2 + vc + 1) * L]
                self.mm(reg, self.v_tok[i][:L, hd * 256 + vc * 128:hd * 256 + (vc + 1) * 128], self.ATm[:L, hd * L:(hd + 1) * L], True, False)
                for si, (so, nreal, S, Sbf) in enumerate(seqs):
                    self.mm(reg[:, so:so + blk], Sbf[:, hd, vc * 128:(vc + 1) * 128], qeT[:, hd, o + so:o + so + blk], False, si == nseq - 1)
        for si, (so, nreal, S, Sbf) in enumerate(seqs):
            pS = pSs[si] if hoist else state_mm(si)
            dcol = o + so + nreal - 1
            for hd in range(4):
                self.stt("dve", S[:, hd, :], S[:, hd, :], e1T[:, hd, dcol:dcol + 1], pS[hd // 2][:, (hd % 2) * 256:(hd % 2 + 1) * 256], ALU.mult, ALU.add)
            self.cp("act", Sbf[:, :, :], S[:, :, :])
        for b in range(2):
            pv = pO[b][:, :4 * L].rearrange("p (c l) -> p c l", c=4)
            self.cp("act", self.o_raw[:, 4 * b:4 * b + 4, :L], pv)
            self.act(self.osq[:, 4 * b:4 * b + 4, :L], pv, AF.Square)
        pM = self.pf()
        for hd in range(4):
            self.mm(pM[:, hd * L:(hd + 1) * L], c["ones_mean"][:, :], self.osq[:, 2 * hd, :L], True, False)
            self.mm(pM[:, hd * L:(hd + 1) * L], c["ones_mean"][:, :], self.osq[:, 2 * hd + 1, :L], False, True)
        self.rsqrt(self.rstdg[:, :4 * L], pM[:, :4 * L], 1.0, EPS)
        r3 = self.rstdg[:, :4 * L].rearrange("p (h l) -> p h l", h=4)
        o4 = self.o_raw[:, :, :].rearrange("p (h v) l -> p h v l", v=2)
        for vc in range(2):
            self.stt("dve", o4[:, :, vc, :L], o4[:, :, vc, :L], self.hgain[:, vc:vc + 1], r3, ALU.mult, ALU.mult)
        self.tt("dve", self.sgT[:, :, o:o + L], self.o_raw[:, :, :L], self.sgT[:, :, o:o + L], ALU.mult)

    def swa_block(self, ti, q0, ntt, nq, kts, g4, between=None):
        c = self.c
        N = 4 * nq
        for P2 in range(2):
            if between is not None:
                between()
            pts = []
            for kt in kts:
                nk = kt["nk"]
                pss = []
                for half in range(2):
                    rows = slice(half * 64, half * 64 + 64)
                    rhs_q = self.qo[P2][rows, ti * 4 * nq:(ti + 1) * 4 * nq]
                    ps = self.pf()
                    self.mm(ps[:nk, :N], kt["knT"](P2, rows), rhs_q, True, False)
                    pss.append(ps)
                for half in range(2):
                    self.mm(pss[half][:nk, :N], c["ident"][:, :nk], kt["bias"][:, 2 * P2 + half, :N], False, True)
                for half in range(2):
                    ps = pss[half]
                    self._pt += 1
                    pt = self.PT[self._pt % 6]
                    if kt["hb"] is not None:
                        self.act(pt[:nk, :N], ps[:nk, :N], AF.Exp, bias=kt["hb"][:nk, :])
                    else:
                        self.act(pt[:nk, :N], ps[:nk, :N], AF.Exp)
                    pts.append((pt, kt, half))
            pO = self.pf()
            pD = self.pf()
            n = len(pts)
            for idx, (pt, kt, half) in enumerate(pts):
                nk = kt["nk"]
                vz4 = kt["vz"][:, :].rearrange("p (j c) -> p j c", j=4)
                self.mm(pO[:, :N], vz4[:nk, 2 * P2 + half, :], pt[:nk, :N], idx == 0, idx == n - 1)
            for idx, (pt, kt, half) in enumerate(pts):
                nk = kt["nk"]
                self.mm(pD[:, :N], c["onesH"][:nk, half, :], pt[:nk, :N], idx == 0, False)
            self.mm(pD[:, :N], self.esinkT[0:4, P2, :], g4[0:4, :N], False, True)
            self.add("dve", lambda h, pD=pD, N=N: h.reciprocal(out=self.rden[:, :N], in_=pD[:, :N]), K(pD[:]), K(self.rden[:]))
            self.tt("dve", self.osT[P2][:, :, q0:q0 + nq], pO[:, :N].rearrange("p (g q) -> p g q", g=4),
                    self.rden[:, :N].rearrange("p (g q) -> p g q", g=4), ALU.mult)

    def final_out(self, tts, dst):
        for i, (o, P) in enumerate(tts):
            self.act(self.junk[:P, :], self.xt[i][:P, :], AF.Square, accum_out=self.ss[i][:P, 0:1])
            self.rsqrt(self.rstd[i][:P, 0:1], self.ss[i][:P, 0:1], 1.0 / D, EPS)
            self.ts("dve", self.xt[i][:P, :], self.xt[i][:P, :], self.rstd[i][:P, 0:1], None, ALU.mult)
            self.tt("pool" if i % 2 else "dve", self.xt[i][:P, :], self.xt[i][:P, :], self.fgain_bc[:P, :], ALU.mult)
            self.dma("sp", dst[o:o + P, :], self.xt[i][:P, :])

    def body(self):
        di = self.din
        self.reset_counters()
        pre_on = not self.debug.get("skip_pre") and not self.debug.get("only_sample")
        self.setup(first_x=di["xprev"][0:T, :] if pre_on else None)
        tts = [(j * 128, 128) for j in range(4)]
        seqs_p = [(0, 128, self.S[0], self.Sbf[0])]
        stop = self.debug.get("stop")
        if not self.debug.get("skip_pre") and not self.debug.get("only_sample"):
            for t in range(NTILE):
                self.next_x = di["xprev"][(t + 1) * T:(t + 2) * T, :] if t + 1 < NTILE else di["xp"][0:T, :]
                self.ffn(tts, T, di["ffn1_up"], di["ffn1_down"], 0, ("pre", t, "f1"))
                self.mixer(tts, T, "pre", ("pre", t), seqs_p, t, t == NTILE - 1)
            self.next_x = None
        pre_ran = not self.debug.get("skip_pre") and not self.debug.get("only_sample")
        for t in range(NTILE if not self.debug.get("only_sample") else 0):
            if self.x_loaded:
                self.x_loaded = False
            elif not (t == 0 and pre_ran):
                self.load_x(di["xp"][t * T:(t + 1) * T, :], tts)
            self.ffn(tts, T, di["ffn1_up"], di["ffn1_down"], 0, ("own", t, "f1"))
            if stop == "ffn1":
                self.final_dbg_x(tts, t)
                continue
            self.mixer(tts, T, "full", ("own", t), seqs_p, t, t == NTILE - 1)
            if stop == "mixer":
                self.final_dbg_x(tts, t)
                continue
            if t == NTILE - 1 and not self.debug.get("no_sample") and not self.debug.get("no_ffn2"):
                self.dma("sp", self.dout["ng"].rearrange("h k v -> k h v"), self.S[0][:, :, :])
                self._ng_done = True
                self.sample_setup_early()
            staged = False
            if not self.debug.get("no_ffn2"):
                hook = None
                smp_next = (t == NTILE - 1 and not self.debug.get("no_sample") and not self.debug.get("no_final") and not stop)
                if smp_next:
                    tss_ = [(0, TS)]
                    xst = [self.arena1[:, 0:2048].bitcast(F32)]
                    self.dma("sp", xst[0][:TS, :], di["xs"])

                    def hook(xst=xst, tss_=tss_):
                        self.norm_stats(tss_, src=xst)

                    def hook2(tss_=tss_):
                        self.norm_tr(tss_, TS, 0)
                    staged = True
                if t + 1 < NTILE and not self.debug.get("no_final"):
                    xst = [self.arena1[:, i * 2048:(i + 1) * 2048].bitcast(F32) for i in range(4)]
                    nsrc = di["xp"][(t + 1) * T:(t + 2) * T, :]
                    for i, (o, P) in enumerate(tts):
                        self.dma("sp", xst[i][:P, :], nsrc[o:o + P, :])

                    def hook(xst=xst):
                        self.norm_stats(tts, src=xst)

                    def hook2():
                        self.norm_tr(tts, T, 0)
                    staged = True
                self.ffn(tts, T, di["ffn2_up"], di["ffn2_down"], 16, ("own", t, "f2"), mid_hook=hook, mid_hook2=hook2 if hook else None)
            if self.debug.get("no_final"):
                self.final_dbg_x(tts, t)
            else:
                self.final_out(tts, self.dout["y"][t * T:(t + 1) * T, :])
            if staged:
                for i, (o, P) in enumerate(tss_ if smp_next else tts):
                    self.cp("pool", self.xt[i][:P, :], xst[i][:P, :])
                self.norm_done = True
                self.x_loaded = True
        if not self._ng_done:
            self.dma("sp", self.dout["ng"].rearrange("h k v -> k h v"), self.S[0][:, :, :])
        if stop or self.debug.get("no_sample"):
            return
        if not self._early_done:
            self.sample_setup_early()
        tss = [(0, TS)]
        self.dma("sp", self.S[1][:, :, :], di["st"][1].rearrange("h k v -> k h v"))
        self.cp("pool", self.Sbf[1][:, :, :], self.S[1][:, :, :])
        seqs_s = [(0, 16, self.S[0], self.Sbf[0]), (32, 16, self.S[1], self.Sbf[1])]
        if self.x_loaded:
            self.x_loaded = False
        else:
            self.load_x(di["xs"], tss)
        self.sample_rest(tss, seqs_s)

    def sample_setup_early(self):
        di = self.din
        self._early_done = True
        self.dma("pool", self.biasTs, di["biass"].rearrange("a p j n -> p a j n"))
        for n in ("tri_incl", "tri_suf", "cmask4"):
            self.dma("sp", self.c[n][:], di[n + "_s"])
        self.dma("sp", self.S[0][:, :, :], di["st"][0].rearrange("h k v -> k h v"))
        self.cp("pool", self.Sbf[0][:, :, :], self.S[0][:, :, :])
        for sq_ in range(2):
            self.dma("sp", self.ckf[:, :], di["ck"][sq_])
            self.cp("dve", self.ckb[:, :], self.ckf[:, :])
            for P2 in range(2):
                pT = self.pb()
                self.tr(pT[:, 0:128], self.ckb[:, P2 * 128:(P2 + 1) * 128], self.c["ident"][:, :])
                self.cp("act", self.knT_c[sq_][:, P2, :], pT[:, 0:128])
            self.dma("sp", self.ckf[:, :], di["cv"][sq_])
            vz4 = self.vz_c[sq_][:, :].rearrange("p (jp par c) -> p jp par c", jp=2, par=2)
            cv4 = self.ckf[:, :].rearrange("p (jp par d) -> p jp par d", jp=2, par=2)
            for par in range(2):
                self.cp("dve", vz4[:, :, par, par * 64:(par + 1) * 64], cv4[:, :, par, :])

    def sample_rest(self, tss, seqs_s):
        di = self.din
        sstop = self.debug.get("sstop")
        self.ffn(tss, TS, di["ffn1_up"], di["ffn1_down"], 0, ("smp", "f1"))
        if sstop != "ffn1":
            self.mixer(tss, TS, "sample", ("smp",), seqs_s, 0, True)
            if sstop != "mixer":
                self.ffn(tss, TS, di["ffn2_up"], di["ffn2_down"], 16, ("smp", "f2"))
        self.final_out(tss, self.dout["ys"])
        for sq_ in range(2):
            self.dma("sp", self.dout["ngs"][sq_].rearrange("h k v -> k h v"), self.S[sq_][:, :, :])

    def final_dbg_x(self, tts, t):
        for i, (o, P) in enumerate(tts):
            self.dma("sp", self.dout["y"][t * T + o:t * T + o + P, :], self.xt[i][:P, :])

    def build(self):
        self.planning = True
        self.body()
        self.planning = False
        self.nuse = {}
        for pl in self.plan:
            self.nuse[pl[2]] = self.nuse.get(pl[2], 0) + 1
        self.body()
        self.s.emit()
        return self.nc


_CACHE = {}


def _get_prog(debug=None):
    key = repr(sorted((debug or {}).items()))
    if key not in _CACHE:
        p = Prog(debug)
        p.build()
        _CACHE[key] = p
    return _CACHE[key]


def make_in_maps(inp):
    f = lambda a: np.ascontiguousarray(np.asarray(a, dtype=np.float32))
    xp = f(inp["x_prompt"])
    xsm = f(inp["x_sample"])
    ck = f(inp["cache_swa_k"])[0].reshape(16, 128, 256)
    cv = f(inp["cache_swa_v"])[0].reshape(16, 128, 256)
    st = f(inp["state_gla"])[0]
    consts = _consts()
    bp, bs = _bias_tables(inp["rel_bias"])
    gains = np.stack([f(inp["ffn1_norm"])[0], f(inp["mix_norm"])[0], f(inp["ffn2_norm"])[0]])
    gains_fm = np.ascontiguousarray(gains.reshape(3, 8, 128).transpose(2, 0, 1).reshape(128, 24))
    fg = np.ascontiguousarray(np.broadcast_to(f(inp["final_norm"])[0][None, :], (128, D)))
    hg = np.ascontiguousarray(f(inp["gla_head_norm"])[0].reshape(2, 128).T)
    qg = np.ascontiguousarray(np.tile(f(inp["q_norm"])[0], 2)[:, None])
    kg = np.ascontiguousarray(np.tile(f(inp["k_norm"])[0], 2)[:, None])
    kgb = np.ascontiguousarray(np.broadcast_to(f(inp["k_norm"])[0][None, :], (128, 64)))
    sinks = f(inp["attn_sinks"])[0]
    sinkT = np.zeros((2, 4, 128), np.float32)
    for P2 in range(2):
        for g in range(4):
            sinkT[P2, g, :64] = sinks[4 * (2 * P2) + g]
            sinkT[P2, g, 64:] = sinks[4 * (2 * P2 + 1) + g]
    shared = {
        "ffn1_up": f(inp["ffn1_w_up"])[0], "ffn1_down": f(inp["ffn1_w_down"])[0],
        "ffn2_up": f(inp["ffn2_w_up"])[0], "ffn2_down": f(inp["ffn2_w_down"])[0],
        "w_in": f(inp["w_in"])[0], "w_branch": f(inp["w_branch"])[0], "w_out": f(inp["w_out"])[0],
        "w_alpha": f(inp["gla_w_alpha"])[0], "b_alpha": f(inp["gla_b_alpha"]).reshape(1, 512),
        "gains_fm": gains_fm, "fgain_bc": fg, "hgain": hg, "qgcol": qg, "kgcol": kg, "kgain_bc": kgb,
        "sinkT": sinkT, "biasp": bp, "biass": bs,
    }
    shared.update(consts)
    maps = []
    zeros_prev = np.zeros((TOKC, D), np.float32)
    for cidx in range(NCORES):
        b, half = cidx // 2, cidx % 2
        m = dict(shared)
        m["xp"] = np.ascontiguousarray(xp[b, half * TOKC:(half + 1) * TOKC])
        m["xprev"] = np.ascontiguousarray(xp[b, 0:TOKC]) if half == 1 else zeros_prev
        xs_ = np.zeros((TS, D), np.float32)
        for s_ in range(2):
            xs_[32 * s_:32 * s_ + 16] = xsm[2 * cidx + s_]
        m["xs"] = xs_
        m["ck"] = np.ascontiguousarray(ck[2 * cidx:2 * cidx + 2])
        m["cv"] = np.ascontiguousarray(cv[2 * cidx:2 * cidx + 2])
        m["st"] = np.ascontiguousarray(st[2 * cidx:2 * cidx + 2])
        m["halo_bias"] = np.full((128, 1), 0.0 if half == 1 else NEG, np.float32)
        maps.append(m)
    return maps


def run(inp, debug=None):
    prog = _get_prog(debug)
    maps = make_in_maps(inp)
    res = run_bass_kernel_spmd(prog.nc, maps, core_ids=list(range(NCORES)))
    return prog, res


def kernel(**inp):
    prog, res = run(inp)
    R = res.results
    y = np.zeros((4, 4096, D), np.float32)
    ys = np.zeros((16, 16, D), np.float32)
    nk = np.zeros((1, 4, 128, 4, 64), np.float32)
    nv = np.zeros((1, 4, 128, 4, 64), np.float32)
    ng = np.zeros((1, 4, 4, 128, 256), np.float32)
    nks = np.zeros((1, 16, 16, 4, 64), np.float32)
    nvs = np.zeros((1, 16, 16, 4, 64), np.float32)
    ngs = np.zeros((1, 16, 4, 128, 256), np.float32)
    for cidx in range(NCORES):
        b, half = cidx // 2, cidx % 2
        r = R[cidx]
        y[b, half * TOKC:(half + 1) * TOKC] = r["y"]
        if half == 1:
            nk[0, b] = r["nk"].reshape(128, 4, 64)
            nv[0, b] = r["nv"].reshape(128, 4, 64)
            ng[0, b] = r["ng"]
        for s_ in range(2):
            ys[2 * cidx + s_] = r["ys"][32 * s_:32 * s_ + 16]
            nks[0, 2 * cidx + s_] = r["nks"][32 * s_:32 * s_ + 16].reshape(16, 4, 64)
            nvs[0, 2 * cidx + s_] = r["nvs"][32 * s_:32 * s_ + 16].reshape(16, 4, 64)
            ngs[0, 2 * cidx + s_] = r["ngs"][s_]
    return (y, ys, nk, nv, ng, nks, nvs, ngs)
```
